# Optimizing a Trainium2 kernel written in Bass

```python
import math
import jax, jax.numpy as jnp
from jax import lax
import numpy as np

D_MODEL = 1024
BATCH = 16
SEQ = 256
DEPTH = 2
DEC_BATCH = 4
DEC_SEQ = 2048
PAST_LEN = 256

GRID_W = 64
ROPE_BASE = 10000.0
EPS = 1e-6
NEG_INF = -1e30
Q_BLOCK = 128

A_HEADS = 8
A_KV_HEADS = 2
A_GROUP = A_HEADS // A_KV_HEADS
A_HEAD_DIM = 64
A_WINDOW = 128
A_SCALE = A_HEAD_DIM ** -0.5
B_HEADS = 8
B_Q_LORA = 384
B_KV_LORA = 256
B_NOPE = 64
B_ROPE = 32
B_V = 64
MLA_SCALE = (B_NOPE + B_ROPE) ** -0.5
C_HEADS = 4
C_DK = 128
C_DV = 128
C_CONV = 3
C_CHUNK = 64
D_FF = 4 * D_MODEL

N_BRANCH = 3
BRANCH_W = 512
A_Q_W = A_HEADS * A_HEAD_DIM
A_KV_W = A_KV_HEADS * A_HEAD_DIM
C_QK_W = C_HEADS * C_DK
C_V_W = C_HEADS * C_DV
C_QKV_W = 2 * C_QK_W + C_V_W
IN_SPLITS = (A_Q_W, A_KV_W, A_KV_W, B_Q_LORA, B_KV_LORA, B_ROPE, C_QKV_W, 2 * C_HEADS, 2 * C_HEADS, C_V_W, N_BRANCH * D_MODEL)
D_IN = A_Q_W + 2 * A_KV_W + B_Q_LORA + B_KV_LORA + B_ROPE + C_QKV_W + 4 * C_HEADS + C_V_W + N_BRANCH * D_MODEL

kernel_name = 'hybrid_diffusion_prefix_trunk_step'


def rmsnorm(x, g):
    xf = x.astype(jnp.float32)
    y = xf * lax.rsqrt(jnp.mean(xf * xf, axis=-1, keepdims=True) + EPS)
    return (y * g.astype(jnp.float32)).astype(x.dtype)


def l2norm(x):
    xf = x.astype(jnp.float32)
    return (xf * lax.rsqrt(jnp.sum(xf * xf, axis=-1, keepdims=True) + EPS)).astype(x.dtype)


def axial_rope(x):
    n, r = x.shape[1], x.shape[-1]
    rows = n // GRID_W
    row = jnp.repeat(jnp.arange(rows), GRID_W)
    col = jnp.tile(jnp.arange(GRID_W), rows)
    half = r // 2
    quarter = half // 2
    inv_freq = jnp.power(ROPE_BASE, -jnp.arange(quarter, dtype=jnp.float32) / quarter)

    def rotate(xa, pos):
        ang = pos.astype(jnp.float32)[:, None] * inv_freq[None, :]
        cos = jnp.cos(ang)[:, None, :]
        sin = jnp.sin(ang)[:, None, :]
        x1 = xa[..., :quarter].astype(jnp.float32)
        x2 = xa[..., quarter:].astype(jnp.float32)
        return jnp.concatenate([x1 * cos - x2 * sin, x1 * sin + x2 * cos], axis=-1)

    out = jnp.concatenate([rotate(x[..., :half], row), rotate(x[..., half:], col)], axis=-1)
    return out.astype(x.dtype)


def softmax_with_sink(s, sink):
    m = jnp.maximum(jnp.max(s, axis=-1, keepdims=True), sink)
    e = jnp.exp(s - m)
    return e / (jnp.sum(e, axis=-1, keepdims=True) + jnp.exp(sink - m))


def blocked_attention(q, k, v, scale, sink=None):
    bsz, nq, hk, g, dk = q.shape
    nb = nq // Q_BLOCK
    qb = jnp.moveaxis(q.reshape(bsz, nb, Q_BLOCK, hk, g, dk), 1, 0)

    def one_block(qi):
        s = jnp.einsum('bqhgd,bkhd->bhgqk', qi, k).astype(jnp.float32) * scale
        if sink is None:
            p = jax.nn.softmax(s, axis=-1)
        else:
            p = softmax_with_sink(s, sink.astype(jnp.float32).reshape(1, hk, g, 1, 1))
        return jnp.einsum('bhgqk,bkhd->bqhgd', p.astype(v.dtype), v)

    out = lax.map(one_block, qb)
    return jnp.moveaxis(out, 0, 1).reshape(bsz, nq, hk, g, v.shape[-1])


def window_attention(q, k, v, k_ctx, v_ctx, sink, scale):
    bsz, n, hk, g, d = q.shape
    nb = n // Q_BLOCK
    qb = q.reshape(bsz, nb, Q_BLOCK, hk, g, d)
    pad = ((0, 0), (Q_BLOCK, Q_BLOCK), (0, 0), (0, 0))
    kp = jnp.pad(k, pad).reshape(bsz, nb + 2, Q_BLOCK, hk, d)
    vp = jnp.pad(v, pad).reshape(bsz, nb + 2, Q_BLOCK, hk, v.shape[-1])
    kw = jnp.concatenate([kp[:, :-2], kp[:, 1:-1], kp[:, 2:]], axis=2)
    vw = jnp.concatenate([vp[:, :-2], vp[:, 1:-1], vp[:, 2:]], axis=2)
    qpos = jnp.arange(n).reshape(nb, Q_BLOCK)
    kpos = (jnp.arange(nb) - 1)[:, None] * Q_BLOCK + jnp.arange(3 * Q_BLOCK)[None, :]
    valid = ((jnp.abs(qpos[:, :, None] - kpos[:, None, :]) <= A_WINDOW)
             & (kpos >= 0)[:, None, :] & (kpos < n)[:, None, :])
    s_loc = jnp.einsum('bnqhgd,bnkhd->bnhgqk', qb, kw).astype(jnp.float32) * scale
    s_loc = jnp.where(valid[None, :, None, None], s_loc, NEG_INF)
    s_ctx = jnp.einsum('bnqhgd,bkhd->bnhgqk', qb, k_ctx).astype(jnp.float32) * scale
    p = softmax_with_sink(jnp.concatenate([s_loc, s_ctx], axis=-1),
                          sink.astype(jnp.float32).reshape(1, 1, hk, g, 1, 1))
    p_loc = p[..., :3 * Q_BLOCK].astype(v.dtype)
    p_ctx = p[..., 3 * Q_BLOCK:].astype(v.dtype)
    o = (jnp.einsum('bnhgqk,bnkhd->bnqhgd', p_loc, vw)
         + jnp.einsum('bnhgqk,bkhd->bnqhgd', p_ctx, v_ctx))
    return o.reshape(bsz, n, hk, g, v.shape[-1])


def centred_dwconv(x, w):
    pad = C_CONV // 2
    return lax.conv_general_dilated(x, w[:, None, :], window_strides=(1,), padding=[(pad, pad)],
                                    dimension_numbers=('NWC', 'WIO', 'NWC'),
                                    feature_group_count=x.shape[-1])


def gated_delta_chunked(q, k, v, g, beta, s0):
    f32 = jnp.float32
    bsz, n, h, _ = q.shape
    dv = v.shape[-1]
    nc = n // C_CHUNK

    def to_chunks(t):
        t = jnp.moveaxis(t.astype(f32), 2, 1)
        return t.reshape(bsz, h, nc, C_CHUNK, *t.shape[3:])

    qc, kc, vc, gl, bc = (to_chunks(t) for t in (q, k, v, g, beta))
    gc = jnp.cumsum(gl, axis=-1)
    ar = jnp.arange(C_CHUNK)
    incl = ar[:, None] >= ar[None, :]
    strict = ar[:, None] > ar[None, :]
    decay = jnp.where(incl, jnp.exp(jnp.where(incl, gc[..., :, None] - gc[..., None, :], 0.0)), 0.0)
    kb = kc * bc[..., None]
    a_mat = jnp.where(strict, jnp.einsum('bhnid,bhnjd->bhnij', kb, kc) * decay, 0.0)
    eye = jnp.eye(C_CHUNK, dtype=f32)
    t_mat = lax.linalg.triangular_solve(eye + a_mat, jnp.broadcast_to(eye, a_mat.shape),
                                        left_side=True, lower=True, unit_diagonal=True)
    u = t_mat @ (vc * bc[..., None])
    w = t_mat @ (kb * jnp.exp(gc)[..., None])
    qk = jnp.where(incl, jnp.einsum('bhnid,bhnjd->bhnij', qc, kc) * decay, 0.0)
    q_dec = qc * jnp.exp(gc)[..., None]
    k_dec = kc * jnp.exp(gc[..., -1:] - gc)[..., None]
    g_end = jnp.exp(gc[..., -1])
    xs = tuple(jnp.moveaxis(t, 2, 0) for t in (u, w, qk, q_dec, k_dec, g_end))

    def step(s, xi):
        u_i, w_i, qk_i, qd_i, kd_i, ge_i = xi
        v_new = u_i - jnp.einsum('bhcd,bhde->bhce', w_i, s)
        o_i = jnp.einsum('bhcd,bhde->bhce', qd_i, s) + jnp.einsum('bhcj,bhje->bhce', qk_i, v_new)
        s = s * ge_i[..., None, None] + jnp.einsum('bhcd,bhce->bhde', kd_i, v_new)
        return s, o_i

    s_fin, o = lax.scan(step, s0.astype(f32), xs)
    o = jnp.moveaxis(o, 0, 2).reshape(bsz, h, n, dv)
    return jnp.moveaxis(o, 1, 2).astype(v.dtype), s_fin.astype(v.dtype)


def bidir_gated_delta(q, k, v, g, beta, s0_f, s0_b):
    o_f, s_f = gated_delta_chunked(q, k, v, g[:, :, 0], beta[:, :, 0], s0_f)
    fl = lambda t: jnp.flip(t, axis=1)
    o_b, s_b = gated_delta_chunked(fl(q), fl(k), fl(v), fl(g[:, :, 1]), fl(beta[:, :, 1]), s0_b)
    return o_f + fl(o_b), jnp.stack([s_f, s_b], axis=1)


def project_in(h, w_in):
    offs = np.cumsum(IN_SPLITS)[:-1].tolist()
    return jnp.split(h @ w_in, offs, axis=-1)


def mla_query(b_cq, p):
    bsz, n = b_cq.shape[:2]
    q = (rmsnorm(b_cq, p['b_gq']) @ p['b_w_uq']).reshape(bsz, n, B_HEADS, B_NOPE + B_ROPE)
    return q[..., :B_NOPE], q[..., B_NOPE:]


def mla_keys_values(ckv, k_rope, p):
    bsz, n = ckv.shape[:2]
    kv = (ckv @ p['b_w_ukv']).reshape(bsz, n, B_HEADS, B_NOPE + B_V)
    k = jnp.concatenate([kv[..., :B_NOPE], jnp.broadcast_to(k_rope[:, :, None, :], (bsz, n, B_HEADS, B_ROPE))], axis=-1)
    return k, kv[..., B_NOPE:]


def delta_branch(c_qkv, c_a, c_b, c_z, p, s0):
    bsz, n = c_qkv.shape[:2]
    qkv = jax.nn.silu(centred_dwconv(c_qkv, p['c_conv_w']))
    q, k, v = jnp.split(qkv, [C_QK_W, 2 * C_QK_W], axis=-1)
    q = l2norm(q.reshape(bsz, n, C_HEADS, C_DK)) * (C_DK ** -0.5)
    k = l2norm(k.reshape(bsz, n, C_HEADS, C_DK))
    v = v.reshape(bsz, n, C_HEADS, C_DV)
    g = -jnp.exp(p['c_a_log'].astype(jnp.float32)) * jax.nn.softplus(
        c_a.astype(jnp.float32).reshape(bsz, n, 2, C_HEADS) + p['c_dt_bias'].astype(jnp.float32))
    beta = jax.nn.sigmoid(c_b.astype(jnp.float32).reshape(bsz, n, 2, C_HEADS))
    o, s_fin = bidir_gated_delta(q, k, v, g, beta, s0[:, 0], s0[:, 1])
    o = rmsnorm(o, p['c_onorm_g']) * jax.nn.silu(c_z.reshape(bsz, n, C_HEADS, C_DV))
    return o.reshape(bsz, n, C_V_W), s_fin


def merge_branches(oa, ob, oc, gates, p):
    bsz, n = oa.shape[:2]
    br = jnp.stack([oa, ob, oc], axis=2)
    y = jnp.einsum('bnjc,jcd->bnjd', br, p['w_branch'])
    gt = jax.nn.sigmoid(gates.reshape(bsz, n, N_BRANCH, D_MODEL))
    return jnp.einsum('bnjd,de->bne', gt * y, p['w_out'])


def mixer_context(h, p):
    bsz, n = h.shape[:2]
    a_q, a_k, a_v, b_cq, b_ckv, b_kr, c_qkv, c_a, c_b, c_z, gates = project_in(h, p['w_in'])
    ka = a_k.reshape(bsz, n, A_KV_HEADS, A_HEAD_DIM)
    va = a_v.reshape(bsz, n, A_KV_HEADS, A_HEAD_DIM)
    oa = blocked_attention(a_q.reshape(bsz, n, A_KV_HEADS, A_GROUP, A_HEAD_DIM), ka, va, A_SCALE,
                           p['a_sink'].reshape(A_KV_HEADS, A_GROUP))
    q_nope, q_rope = mla_query(b_cq, p)
    ckv = rmsnorm(b_ckv, p['b_gkv'])
    kb, vb = mla_keys_values(ckv, b_kr, p)
    ob = blocked_attention(jnp.concatenate([q_nope, q_rope], axis=-1)[:, :, :, None], kb, vb, MLA_SCALE)
    s0 = jnp.zeros((bsz, 2, C_HEADS, C_DK, C_DV), h.dtype)
    oc, s_fin = delta_branch(c_qkv, c_a, c_b, c_z, p, s0)
    y = merge_branches(oa.reshape(bsz, n, BRANCH_W), ob.reshape(bsz, n, BRANCH_W), oc, gates, p)
    return y, (ka, va, ckv, b_kr, s_fin)


def mixer_latent(h, p, ka_ctx, va_ctx, ckv_ctx, kr_ctx, s_ctx):
    bsz, n = h.shape[:2]
    a_q, a_k, a_v, b_cq, b_ckv, b_kr, c_qkv, c_a, c_b, c_z, gates = project_in(h, p['w_in'])
    qa = axial_rope(a_q.reshape(bsz, n, A_HEADS, A_HEAD_DIM)).reshape(bsz, n, A_KV_HEADS, A_GROUP, A_HEAD_DIM)
    ka = axial_rope(a_k.reshape(bsz, n, A_KV_HEADS, A_HEAD_DIM))
    va = a_v.reshape(bsz, n, A_KV_HEADS, A_HEAD_DIM)
    oa = window_attention(qa, ka, va, ka_ctx, va_ctx, p['a_sink'].reshape(A_KV_HEADS, A_GROUP), A_SCALE)
    q_nope, q_rope = mla_query(b_cq, p)
    qb = jnp.concatenate([q_nope, axial_rope(q_rope)], axis=-1)
    kr = axial_rope(b_kr[:, :, None, :])[:, :, 0]
    kb_lat, vb_lat = mla_keys_values(rmsnorm(b_ckv, p['b_gkv']), kr, p)
    kb_ctx, vb_ctx = mla_keys_values(ckv_ctx, kr_ctx, p)
    ob = blocked_attention(qb[:, :, :, None], jnp.concatenate([kb_lat, kb_ctx], axis=1),
                           jnp.concatenate([vb_lat, vb_ctx], axis=1), MLA_SCALE)
    oc, _ = delta_branch(c_qkv, c_a, c_b, c_z, p, s_ctx)
    y = merge_branches(oa.reshape(bsz, n, BRANCH_W), ob.reshape(bsz, n, BRANCH_W), oc, gates, p)
    return y, None


def adaln(cond, w, bias):
    return (jax.nn.silu(cond) @ w + bias)[:, None, :]


def residual_block(x, mod, p, mixer):
    sh1, sc1, g1, sh2, sc2, g2 = jnp.split(mod, 6, axis=-1)
    y, extra = mixer(rmsnorm(x, p['norm1_g']) * (1 + sc1) + sh1)
    x = x + g1 * y
    h = rmsnorm(x, p['norm2_g']) * (1 + sc2) + sh2
    x = x + g2 * (jnp.square(jax.nn.relu(h @ p['w_ff1'])) @ p['w_ff2'])
    return x, extra


def setup_inputs(seed: int = 0) -> dict:
    key = jax.random.key(seed)
    ks = jax.random.split(key, 32)
    nrm = lambda k, shape, s: jax.random.normal(k, shape, jnp.float32) * s
    dt = jnp.exp(jax.random.uniform(ks[20], (DEPTH, 2, C_HEADS), jnp.float32, math.log(1e-3), math.log(1e-1)))
    return {
        'x_prompt': nrm(ks[0], (BATCH, SEQ, D_MODEL), 1.0),
        'x_sample': nrm(ks[1], (DEC_BATCH, DEC_SEQ, D_MODEL), 1.0),
        'cache_a_k': nrm(ks[2], (DEC_BATCH, DEPTH, PAST_LEN, A_KV_HEADS, A_HEAD_DIM), 1.0),
        'cache_a_v': nrm(ks[3], (DEC_BATCH, DEPTH, PAST_LEN, A_KV_HEADS, A_HEAD_DIM), 1.0),
        'cache_b_ckv': nrm(ks[4], (DEC_BATCH, DEPTH, PAST_LEN, B_KV_LORA), 1.0),
        'cache_b_krope': nrm(ks[5], (DEC_BATCH, DEPTH, PAST_LEN, B_ROPE), 1.0),
        'state_c': nrm(ks[6], (DEC_BATCH, DEPTH, 2, C_HEADS, C_DK, C_DV), 0.05),
        'c': nrm(ks[7], (DEC_BATCH, D_MODEL), 1.0),
        'c_ctx': nrm(ks[8], (D_MODEL,), 1.0),
        'norm1_g': 1.0 + nrm(ks[9], (DEPTH, D_MODEL), 0.02),
        'norm2_g': 1.0 + nrm(ks[10], (DEPTH, D_MODEL), 0.02),
        'w_ada': nrm(ks[11], (DEPTH, D_MODEL, 6 * D_MODEL), 0.5 * D_MODEL ** -0.5),
        'b_ada': nrm(ks[12], (DEPTH, 6 * D_MODEL), 0.02),
        'w_in': nrm(ks[13], (DEPTH, D_MODEL, D_IN), D_MODEL ** -0.5),
        'a_sink': nrm(ks[14], (DEPTH, A_HEADS), 0.5),
        'b_gq': 1.0 + nrm(ks[15], (DEPTH, B_Q_LORA), 0.02),
        'b_w_uq': nrm(ks[16], (DEPTH, B_Q_LORA, B_HEADS * (B_NOPE + B_ROPE)), B_Q_LORA ** -0.5),
        'b_gkv': 1.0 + nrm(ks[17], (DEPTH, B_KV_LORA), 0.02),
        'b_w_ukv': nrm(ks[18], (DEPTH, B_KV_LORA, B_HEADS * (B_NOPE + B_V)), B_KV_LORA ** -0.5),
        'c_conv_w': nrm(ks[19], (DEPTH, C_CONV, C_QKV_W), C_CONV ** -0.5),
        'c_a_log': jnp.log(jax.random.uniform(ks[21], (DEPTH, 2, C_HEADS), jnp.float32, 1.0, 16.0)),
        'c_dt_bias': dt + jnp.log(-jnp.expm1(-dt)),
        'c_onorm_g': 1.0 + nrm(ks[22], (DEPTH, C_DV), 0.02),
        'w_branch': nrm(ks[23], (DEPTH, N_BRANCH, BRANCH_W, D_MODEL), BRANCH_W ** -0.5),
        'w_out': nrm(ks[24], (DEPTH, D_MODEL, D_MODEL), D_MODEL ** -0.5),
        'w_ff1': nrm(ks[25], (DEPTH, D_MODEL, D_FF), D_MODEL ** -0.5),
        'w_ff2': nrm(ks[26], (DEPTH, D_FF, D_MODEL), D_FF ** -0.5),
        'final_g': 1.0 + nrm(ks[27], (D_MODEL,), 0.02),
    }


def reference(x_prompt, x_sample, cache_a_k, cache_a_v, cache_b_ckv, cache_b_krope, state_c, c, c_ctx,
              norm1_g, norm2_g, w_ada, b_ada, w_in, a_sink, b_gq, b_w_uq, b_gkv, b_w_ukv, c_conv_w,
              c_a_log, c_dt_bias, c_onorm_g, w_branch, w_out, w_ff1, w_ff2, final_g):
    xp, xs = x_prompt, x_sample
    ak_l, av_l, ckv_l, kr_l, sc_l = [], [], [], [], []
    for l in range(DEPTH):
        p = {'norm1_g': norm1_g[l], 'norm2_g': norm2_g[l], 'w_in': w_in[l], 'a_sink': a_sink[l],
             'b_gq': b_gq[l], 'b_w_uq': b_w_uq[l], 'b_gkv': b_gkv[l], 'b_w_ukv': b_w_ukv[l],
             'c_conv_w': c_conv_w[l], 'c_a_log': c_a_log[l], 'c_dt_bias': c_dt_bias[l],
             'c_onorm_g': c_onorm_g[l], 'w_branch': w_branch[l], 'w_out': w_out[l],
             'w_ff1': w_ff1[l], 'w_ff2': w_ff2[l]}
        mod_ctx = adaln(c_ctx[None, :], w_ada[l], b_ada[l])
        mod_lat = adaln(c, w_ada[l], b_ada[l])
        xp, (ak, av, ckv, kr, sc) = residual_block(xp, mod_ctx, p, lambda h: mixer_context(h, p))
        xs, _ = residual_block(xs, mod_lat, p, lambda h: mixer_latent(
            h, p, cache_a_k[:, l], cache_a_v[:, l], cache_b_ckv[:, l], cache_b_krope[:, l], state_c[:, l]))
        ak_l.append(ak)
        av_l.append(av)
        ckv_l.append(ckv)
        kr_l.append(kr)
        sc_l.append(sc)
    y_prompt = rmsnorm(xp, final_g)
    y_sample = rmsnorm(xs, final_g)
    new_a_k = jnp.stack(ak_l, axis=1)
    new_a_v = jnp.stack(av_l, axis=1)
    new_b_ckv = jnp.stack(ckv_l, axis=1)
    new_b_krope = jnp.stack(kr_l, axis=1)
    new_c_state = jnp.stack(sc_l, axis=1)
    return (y_prompt, y_sample, new_a_k, new_a_v, new_b_ckv, new_b_krope, new_c_state)
```

```python
import bisect
from contextlib import ExitStack

import numpy as np
import concourse.bass as bass
import concourse.mybir as mybir
from concourse.bass_utils import run_bass_kernel_spmd

F32 = mybir.dt.float32
BF16 = mybir.dt.bfloat16
AF = mybir.ActivationFunctionType
ALU = mybir.AluOpType
AX = mybir.AxisListType

COMPUTE = ("pe", "act", "dve", "pool")
NDS = 48
NDS_SP = 32


def _region(ap):
    t = ap.tensor
    dims = [(int(s), int(c)) for s, c in ap.ap]
    off = int(ap.offset)
    if type(t).__name__.startswith("DRam"):
        ext = sum((c - 1) * abs(s) for s, c in dims)
        return (t.name, 0, 1, off, off + ext + 1)
    rows = 1
    for d in list(t.shape)[1:]:
        rows *= int(d)
    p0 = off // rows
    f0 = off % rows
    pc = dims[0][1] if dims[0][0] != 0 else 1
    ext = sum((c - 1) * abs(s) for s, c in dims[1:])
    f1 = f0 + ext + 1
    if type(t).__name__.startswith("PSum"):
        be = 1024 if t.dtype == BF16 else 512
        f0 = (f0 // be) * be
        f1 = ((f1 + be - 1) // be) * be
        p0, pc = 0, 128
    return (t.name, p0, p0 + pc, f0, f1)


def _overlap(a, b):
    return a[1] < b[2] and b[1] < a[2] and a[3] < b[4] and b[3] < a[4]


def _contains(a, b):
    return a[1] <= b[1] and b[2] <= a[2] and a[3] <= b[3] and b[4] <= a[4]


class Op:
    __slots__ = ("eng", "fn", "deps", "signal", "is_dma", "sem", "semval", "seq", "idx")

    def __init__(self, eng, fn, idx, is_dma=False):
        self.eng = eng
        self.fn = fn
        self.deps = []
        self.signal = False
        self.is_dma = is_dma
        self.sem = None
        self.semval = 0
        self.seq = None
        self.idx = idx


class Prog:
    def __init__(self, nc, stack):
        self.nc = nc
        self.engs = {"pe": nc.tensor, "act": nc.scalar, "dve": nc.vector, "pool": nc.gpsimd, "sp": nc.sync}
        self.esem = {e: stack.enter_context(nc.semaphore("es_" + e)) for e in COMPUTE}
        self.dsems = [stack.enter_context(nc.semaphore("ds%d" % i)) for i in range(NDS)]
        self.dcount = [0] * NDS
        self.dlast = [None] * NDS
        self.dnext = {"sp": 0, "pool": 0}
        self.pending = []
        self.state = {}
        self.sigcount = {e: 0 for e in COMPUTE}
        self.sigidx = {e: [] for e in COMPUTE}
        self.sigseq = {e: [] for e in COMPUTE}
        self.waited = {f: {e: 0 for e in COMPUTE} for f in self.engs}
        self.waited_dma = {f: {} for f in self.engs}
        self.last_op = {}
        self.open_dmas = []
        self.nops = 0
        self.n_inst = 0
        self.n_wait = 0

    def add(self, eng, fn, reads=(), writes=(), dma=False):
        op = Op(eng, fn, self.nops, dma)
        self.nops += 1
        deps = {}
        rregs = [_region(a) for a in reads]
        wregs = [_region(a) for a in writes]
        for r in rregs:
            is_ps = r[0].startswith("pb")
            for (reg, o, isw) in self.state.get(r[0], ()):
                if _overlap(reg, r) and (isw or (is_ps and o.eng != eng)):
                    deps[o.idx] = o
        for w in wregs:
            for (reg, o, isw) in self.state.get(w[0], ()):
                if _overlap(reg, w):
                    if (not dma) and (not o.is_dma) and o.eng == eng and eng == "pe":
                        continue
                    deps[o.idx] = o
        if dma:
            if eng == "sp":
                s = self.dnext["sp"]
                self.dnext["sp"] = (s + 1) % NDS_SP
            else:
                s = NDS_SP + self.dnext["pool"]
                self.dnext["pool"] = (self.dnext["pool"] + 1) % (NDS - NDS_SP)
            if self.dlast[s] is not None:
                deps[self.dlast[s].idx] = self.dlast[s]
            self.dlast[s] = op
            self.dcount[s] += 1
            op.sem = s
            op.semval = 16 * self.dcount[s]
            self.open_dmas.append(op)
        for o in deps.values():
            if not o.is_dma:
                o.signal = True
        op.deps = list(deps.values())
        for w in wregs:
            lst = self.state.setdefault(w[0], [])
            lst[:] = [rec for rec in lst if not _contains(w, rec[0])]
            lst.append((w, op, True))
        for r in rregs:
            lst = self.state.setdefault(r[0], [])
            if not dma:
                lst[:] = [rec for rec in lst
                          if not ((not rec[2]) and (not rec[1].is_dma) and rec[1].eng == eng and _contains(r, rec[0]))]
            lst.append((r, op, False))
        self.pending.append(op)
        if not dma:
            self.last_op[eng] = op
        return op

    def fence(self):
        lasts = dict(self.last_op)
        dmas = list(self.open_dmas)
        self.open_dmas = []
        for f in self.engs:
            op = Op(f, None, self.nops, False)
            self.nops += 1
            for e, o in lasts.items():
                o.signal = True
                op.deps.append(o)
            op.deps.extend(dmas)
            self.pending.append(op)
        self.state = {}
        self.flush()

    def _seq_of(self, dep):
        if dep.seq is not None:
            return dep.seq
        e = dep.eng
        i = bisect.bisect_left(self.sigidx[e], dep.idx)
        return self.sigseq[e][i]

    def flush(self):
        lastp = {}
        for op in self.pending:
            if (not op.is_dma) and op.fn is not None and op.eng in COMPUTE:
                lastp[op.eng] = op
        for op in lastp.values():
            op.signal = True
        for op in self.pending:
            f = op.eng
            eng = self.engs[f]
            for dep in op.deps:
                if dep.is_dma:
                    if self.waited_dma[f].get(dep.sem, 0) >= dep.semval:
                        continue
                    eng.wait_ge(self.dsems[dep.sem], dep.semval)
                    self.waited_dma[f][dep.sem] = dep.semval
                    self.n_wait += 1
                else:
                    e = dep.eng
                    seq = self._seq_of(dep)
                    if self.waited[f][e] >= seq:
                        continue
                    eng.wait_ge(self.esem[e], seq)
                    self.waited[f][e] = seq
                    self.n_wait += 1
            if op.fn is None:
                continue
            ins = op.fn(eng)
            self.n_inst += 1
            if op.is_dma:
                ins.then_inc(self.dsems[op.sem], 16)
            elif op.signal:
                self.sigcount[f] += 1
                op.seq = self.sigcount[f]
                ins.then_inc(self.esem[f], 1)
                self.sigidx[f].append(op.idx)
                self.sigseq[f].append(op.seq)
                self.waited[f][f] = max(self.waited[f][f], 0)
        self.pending = []

    def finish(self):
        self.fence()

    def mm(self, out, lhsT, rhs, start=True, stop=True):
        return self.add("pe", lambda e: e.matmul(out, lhsT, rhs, start=start, stop=stop),
                        reads=[lhsT, rhs], writes=[out])

    def tr(self, out, in_, ident):
        return self.add("pe", lambda e: e.transpose(out, in_, ident), reads=[in_, ident], writes=[out])

    def act(self, out, in_, func, bias=None, scale=1.0, accum_out=None, eng="act"):
        reads = [in_]
        kw = {}
        if bias is not None:
            kw["bias"] = bias
            if not isinstance(bias, (int, float)):
                reads.append(bias)
        if not isinstance(scale, (int, float)):
            reads.append(scale)
        kw["scale"] = scale
        writes = [out]
        if accum_out is not None:
            kw["accum_out"] = accum_out
            writes.append(accum_out)
        return self.add("act", lambda e: e.activation(out, in_, func, **kw), reads=reads, writes=writes)

    def tt(self, eng, out, in0, in1, op):
        return self.add(eng, lambda e: e.tensor_tensor(out, in0, in1, op), reads=[in0, in1], writes=[out])

    def ts(self, eng, out, in0, s1, s2=None, op0=ALU.mult, op1=None, accum_out=None):
        reads = [in0]
        for s in (s1, s2):
            if s is not None and not isinstance(s, (int, float)):
                reads.append(s)
        writes = [out]
        kw = {}
        if op1 is not None:
            kw["op1"] = op1
        if accum_out is not None:
            kw["accum_out"] = accum_out
            writes.append(accum_out)
        return self.add(eng, lambda e: e.tensor_scalar(out, in0, s1, s2, op0, **kw), reads=reads, writes=writes)

    def stt(self, out, in0, scalar, in1, op0, op1, eng="dve"):
        reads = [in0, in1]
        if not isinstance(scalar, (int, float)):
            reads.append(scalar)
        return self.add(eng, lambda e: e.scalar_tensor_tensor(out, in0, scalar, in1, op0, op1),
                        reads=reads, writes=[out])

    def copy(self, eng, out, in_):
        if eng == "act":
            return self.add("act", lambda e: e.copy(out, in_), reads=[in_], writes=[out])
        return self.add(eng, lambda e: e.tensor_copy(out, in_), reads=[in_], writes=[out])

    def memset(self, eng, out, val):
        return self.add(eng, lambda e: e.memset(out, val), reads=[], writes=[out])

    def recip(self, out, in_):
        return self.add("dve", lambda e: e.reciprocal(out, in_), reads=[in_], writes=[out])

    def reduce(self, out, in_, op, axis=AX.X, eng="dve"):
        return self.add(eng, lambda e: e.tensor_reduce(out, in_, axis, op), reads=[in_], writes=[out])

    def dma(self, out, in_, q="sp"):
        return self.add(q, lambda e: e.dma_start(out=out, in_=in_), reads=[in_], writes=[out], dma=True)


D = 1024
DEPTH = 2
D_IN = 6576
OFF_AQ, OFF_AK, OFF_AV = 0, 512, 640
OFF_BCQ, OFF_BCKV, OFF_BKR = 768, 1152, 1408
OFF_CQKV, OFF_CA, OFF_CB, OFF_CZ, OFF_GATES = 1440, 2976, 2984, 2992, 3504
EPS = 1e-6


class Ctx:
    pass


def build(NTS=16, NSP=2, depth=DEPTH, mixers=("B", "A", "C"), dbg=(), parts=("adaln", "ffn", "final")):
    NTP = NSP * 2
    NT = NTP + NTS
    NTOK = NT * 128
    nc = bass.Bass("TRN2", target_bir_lowering=False)
    K = Ctx()

    def din(name, shape):
        return nc.dram_tensor(name, list(shape), F32, kind="ExternalInput").ap()

    def dout(name, shape):
        return nc.dram_tensor(name, list(shape), F32, kind="ExternalOutput").ap()

    xin = din("xin", [NTOK, D])
    condT = din("condT", [128, 8, 2])
    w_ada = din("w_ada", [depth, D, 6 * D])
    b_adaT = din("b_adaT", [depth, 128, 48])
    b_ada = din("b_ada", [depth, 6 * D])
    norm1T = din("norm1T", [depth, 128, 8])
    norm2T = din("norm2T", [depth, 128, 8])
    final_g = din("final_g", [1, D])
    w_ff1 = din("w_ff1", [depth, D, 4 * D])
    w_ff2 = din("w_ff2", [depth, 4 * D, D])
    cst_ident = din("cst_ident", [128, 128])
    w_branch = din("w_branch", [depth, 3, 512, D])
    w_out = din("w_out", [depth, D, D])
    w_in = din("w_in", [depth, D, D_IN])
    w_in_sw = din("w_in_sw", [depth, D, 672])
    b_w_uq_r = din("b_w_uq_r", [depth, 384, 1024])
    b_w_uq_rs = din("b_w_uq_rs", [depth, 384, 1024])
    b_gqT = din("b_gqT", [depth, 128, 3])
    b_gkv = din("b_gkv", [depth, 256])
    b_w_ukv = din("b_w_ukv", [depth, 256, 1024])
    a_sink = din("a_sink", [depth, 8])
    ropeQB_C = din("ropeQB_C", [128, NTOK])
    ropeQB_S = din("ropeQB_S", [128, NTOK])
    ropeKB_C = din("ropeKB_C", [NTOK, 32])
    ropeKB_S = din("ropeKB_S", [NTOK, 32])
    ropeA_C = din("ropeA_C", [64, NTOK])
    ropeA_S = din("ropeA_S", [64, NTOK])
    cak = din("cak", [depth, 256, 128])
    cav = din("cav", [depth, 256, 128])
    cckv = din("cckv", [depth, 256, 256])
    ckr = din("ckr", [depth, 256, 32])
    cstate = din("cstate", [depth, 2, 4, 128, 128])
    cst_masks = din("cst_masks", [128, 8, 128])
    cst_tri = din("cst_tri", [128, 4, 128])
    cst_trc = din("cst_trc", [128, 2, 2, 128])
    cst_lvN = din("cst_lvN", [128, 7, 2, 128])
    cst_lvA = din("cst_lvA", [128, 7, 2, 128])
    c_convT = din("c_convT", [depth, 128, 12, 3])
    c_a_log = din("c_a_log", [depth, 8])
    c_dt_bias = din("c_dt_bias", [depth, 8])
    c_onorm_g = din("c_onorm_g", [depth, 128])
    nak_out = dout("nak", [NSP, depth, 256, 128])
    nav_out = dout("nav", [NSP, depth, 256, 128])
    nckv_out = dout("nckv", [NSP, depth, 256, 256])
    nkr_out = dout("nkr", [NSP, depth, 256, 32])
    ncs_out = dout("ncs", [NSP, depth, 2, 4, 128, 128])
    y_out = dout("y", [NTOK, D])
    dbg_out = {}
    for name, shape in dbg:
        dbg_out[name] = dout(name, shape)

    with ExitStack() as st:
        P = Prog(nc, st)

        K.uid = 0

        K.phase = "init"
        K.minrem = {}

        def sb(name, shape, dt=F32, stack=st):
            K.uid += 1
            t_ = stack.enter_context(nc.sbuf_tensor("%s_%d" % (name, K.uid), list(shape), dt))
            K.minrem[K.phase] = min(K.minrem.get(K.phase, 1 << 30), nc.sbuf_bytes_remaining)
            return t_

        def psb(name, shape=(128, 512), dt=F32, stack=st):
            K.uid += 1
            return stack.enter_context(nc.psum_tensor("%s_%d" % (name, K.uid), list(shape), dt))

        x = sb("x", [128, NT, D])
        pb = [psb("pb%d" % i) for i in range(7)]
        pbb = psb("pbb", (128, 1024), BF16)
        ident_f = sb("ident_f", [128, 128])
        ident_b = sb("ident_b", [128, 128], BF16)
        condT_sb = sb("condT_sb", [128, 8, 2])
        scT = sb("scT", [128, 8, 2], BF16)
        modT = sb("modT", [128, 48, 2])
        affA = sb("affA", [128, 2, 8, 2])
        nrmT = sb("nrmT", [128, 2, 8])
        ss = sb("ss", [128, NT])
        rstd = sb("rstd", [128, NT])
        junkb = [sb("junk%d" % i, [128, D], BF16) for i in range(2)] * 2
        K.nj = 0
        xnb = [sb("xnb%d" % i, [128, D]) for i in range(2)]

        P.dma(ident_f[:], cst_ident[:, :])
        P.dma(condT_sb[:], condT[:, :, :])
        for t in range(NT):
            P.dma(x[:, t, :], xin[t * 128:(t + 1) * 128, :])
        P.copy("dve", ident_b[:], ident_f[:])
        ones_b = sb("ones_b", [128, 128], BF16)
        ones_f = sb("ones_f", [128, 128])
        P.memset("dve", ones_b[:], 1.0)
        P.memset("dve", ones_f[:], 1.0)
        P.act(scT[:], condT_sb[:], AF.Silu)

        def cond_of(t):
            return 0 if t < NTP else 1

        def calc_rstd(tiles):
            for t in tiles:
                K.nj += 1
                P.act(junkb[K.nj % 2][:], x[:, t, :], AF.Square, accum_out=ss[:, t:t + 1])
            t0, t1 = tiles[0], tiles[-1] + 1
            P.ts("dve", rstd[:, t0:t1], ss[:, t0:t1], 1.0 / D, EPS, op0=ALU.mult, op1=ALU.add)
            P.act(rstd[:, t0:t1], rstd[:, t0:t1], AF.Sqrt)
            P.recip(rstd[:, t0:t1], rstd[:, t0:t1])

        K.ntr = 0
        K.ew = "pool"

        def build_hT(dst, tiles, which):
            for ti, t in enumerate(tiles):
                cd = cond_of(t)
                xn = xnb[K.ntr % 2]
                K.ntr += 1
                P.ts("dve", xn[:], x[:, t, :], rstd[:, t:t + 1], None, op0=ALU.mult)
                for g in range(2):
                    pp = pb[5 + g]
                    for c in range(4):
                        P.tr(pp[:, c * 128:(c + 1) * 128], xn[:, (g * 4 + c) * 128:(g * 4 + c + 1) * 128], ident_f[:])
                    for c in range(4):
                        ch = g * 4 + c
                        a_ap = affA[:, which, ch, cd:cd + 1]
                        b_ap = modT[:, (3 * which) * 8 + ch, cd:cd + 1]
                        o_ap = dst[:, ch, ti * 128:(ti + 1) * 128]
                        if g % 2 == 0:
                            P.act(o_ap, pp[:, c * 128:(c + 1) * 128], AF.Identity, bias=b_ap, scale=a_ap)
                        else:
                            P.ts("dve", o_ap, pp[:, c * 128:(c + 1) * 128], a_ap, b_ap, op0=ALU.mult, op1=ALU.add)

        def make_G(dst, which, cd, ph):
            dg = [sb("dgG%d" % i, [128, 128], F32, ph) for i in range(2)]
            j = 2 + 3 * which
            for c in range(8):
                P.ts("pool", dg[c % 2][:], ident_f[:], modT[:, j * 8 + c, cd:cd + 1], None, op0=ALU.mult)
                pg = pb[c % 2]
                P.mm(pg[:, 0:128], ones_f[:], dg[c % 2][:])
                P.copy("act", dst[:, c * 128:(c + 1) * 128], pg[:, 0:128])

        def adaln(l):
            with ExitStack() as ph:
                wb = [sb("wada%d" % i, [128, 8, D], BF16, ph) for i in range(2)]
                badaT = sb("badaT", [128, 48], F32, ph)
                P.dma(badaT[:], b_adaT[l, :, :])
                P.dma(nrmT[:, 0, :], norm1T[l, :, :])
                P.dma(nrmT[:, 1, :], norm2T[l, :, :])
                for j in range(6):
                    w = wb[j % 2]
                    P.dma(w[:], w_ada[l, :, j * D:(j + 1) * D].rearrange("(c p) n -> p c n", p=128), q="pool")
                    pm = pb[j % 2]
                    for c in range(8):
                        for kc in range(8):
                            P.mm(pm[:, c * 2:c * 2 + 2], w[:, kc, c * 128:(c + 1) * 128], scT[:, kc, :],
                                 start=(kc == 0), stop=(kc == 7))
                    P.tt("dve", modT[:, j * 8:(j + 1) * 8, :], pm[:, 0:16].rearrange("p (c k) -> p c k", k=2),
                         badaT[:, j * 8:(j + 1) * 8].unsqueeze(2).broadcast_to([128, 8, 2]), ALU.add)
                for which in range(2):
                    j = 1 + 3 * which
                    P.ts("dve", affA[:, which, :, :], modT[:, j * 8:(j + 1) * 8, :], 1.0, None, op0=ALU.add)
                    P.tt("dve", affA[:, which, :, :], affA[:, which, :, :],
                         nrmT[:, which, :].unsqueeze(2).broadcast_to([128, 8, 2]), ALU.mult)
                P.fence()

        def ffn(l):
            with ExitStack() as ph:
                h2T = sb("h2T", [128, 8, NTOK], BF16, ph)
                w1b = [sb("w1b%d" % i, [128, 8, 512], BF16, ph) for i in range(2)]
                w2b = [sb("w2b%d" % i, [128, 4, D], BF16, ph) for i in range(2)]
                aTb = [sb("aTb%d" % i, [128, 4, 256], BF16, ph) for i in range(2)]
                rfb = [sb("rfb%d" % i, [128, 256], F32, ph) for i in range(2)]
                tmpb = [sb("tmpb%d" % i, [128, 512], F32, ph) for i in range(2)]
                G2 = sb("G2", [128, 2, D], F32, ph)
                for cd_ in range(2):
                    make_G(G2[:, cd_, :], 1, cd_, ph)

                def load_w(g):
                    P.dma(w1b[g % 2][:], w_ff1[l, :, g * 512:(g + 1) * 512].rearrange("(c p) n -> p c n", p=128), q="pool")
                    P.dma(w2b[g % 2][:], w_ff2[l, g * 512:(g + 1) * 512, :].rearrange("(c p) n -> p c n", p=128), q="pool")

                load_w(0)
                calc_rstd(list(range(NT)))
                build_hT(h2T, list(range(NT)), 1)
                K.cnt = 0
                NG = 8
                items = [(g, blk) for g in range(NG) for blk in range(NT // 2)]

                def ff1(g, blk):
                    w1 = w1b[g % 2]
                    aT = aTb[blk % 2]
                    for fc in range(4):
                        pa = pb[fc % 2]
                        for kc in range(8):
                            P.mm(pa[:, 0:256], w1[:, kc, fc * 128:(fc + 1) * 128],
                                 h2T[:, kc, blk * 256:(blk + 1) * 256], start=(kc == 0), stop=(kc == 7))
                        rf = rfb[fc % 2]
                        P.act(rf[:], pa[:, 0:256], AF.Relu)
                        P.tt("dve" if fc % 2 == 0 else "pool", aT[:, fc, :], rf[:], rf[:], ALU.mult)

                def ff2(g, blk):
                    w2 = w2b[g % 2]
                    aT = aTb[blk % 2]
                    for tt_ in range(2):
                        t = blk * 2 + tt_
                        cd = cond_of(t)
                        for half in range(2):
                            py = pb[2 + tt_ * 2 + half]
                            for fc in range(4):
                                P.mm(py[:], aT[:, fc, tt_ * 128:(tt_ + 1) * 128],
                                     w2[:, fc, half * 512:(half + 1) * 512], start=(fc == 0), stop=(fc == 3))
                            if cd == 0:
                                tmp = tmpb[K.cnt % 2]
                                K.cnt += 1
                                P.tt("dve", tmp[:], py[:], G2[:, cd, half * 512:(half + 1) * 512], ALU.mult)
                                P.tt(K.ew, x[:, t, half * 512:(half + 1) * 512],
                                     x[:, t, half * 512:(half + 1) * 512], tmp[:], ALU.add)
                            else:
                                P.tt("dve", x[:, t, half * 512:(half + 1) * 512],
                                     x[:, t, half * 512:(half + 1) * 512], py[:], ALU.add)
                    if blk == NTP // 2 - 1:
                        P.tt("pool", w2[:], w2[:], G2[:, 1, :].unsqueeze(1).broadcast_to([128, 4, D]), ALU.mult)

                ff1(*items[0])
                for n, (g, blk) in enumerate(items):
                    if blk == 0 and g + 1 < NG:
                        load_w(g + 1)
                    if n + 1 < len(items):
                        ff1(*items[n + 1])
                    ff2(g, blk)
                P.fence()

        def final_norm():
            with ExitStack() as ph:
                gf = sb("gf", [128, D], F32, ph)
                ob = [sb("ob%d" % i, [128, D], F32, ph) for i in range(2)]
                P.dma(gf[:], final_g.partition_broadcast(128) if False else final_g[0:1, :].broadcast_to([128, D]))
                calc_rstd(list(range(NT)))
                for t in range(NT):
                    o = ob[t % 2]
                    P.stt(o[:], x[:, t, :], rstd[:, t:t + 1], gf[:], ALU.mult, ALU.mult)
                    P.dma(y_out[t * 128:(t + 1) * 128, :], o[:])
                P.fence()


        units = [
            dict(name="P", tiles=list(range(0, NTP)), cond=0, prompt=True,
                 seqs=[dict(t0=2 * i, nt=2, ctx=False, idx=i) for i in range(NSP)]),
            dict(name="S", tiles=list(range(NTP, NT)), cond=1, prompt=False,
                 seqs=[dict(t0=NTP, nt=NTS, ctx=True, idx=0)]),
        ]
        K.rr = 0

        def blocks_of(unit, bs=512):
            out = []
            for sq in unit["seqs"]:
                n = sq["nt"] * 128
                o = 0
                while o < n:
                    b = min(bs, n - o)
                    out.append((sq, sq["t0"] * 128 + o, b))
                    o += b
            return out

        def attend(KT, V1, QT, nq, negm, key_tiles, obanks, sbanks, PTb, scale, tail=None):
            nk = len(key_tiles)
            slots = []

            def scores(i):
                kt, mask = key_tiles[i]
                sp_ = sbanks[K.rr % len(sbanks)]
                pt_ = PTb[K.rr % len(PTb)]
                K.rr += 1
                P.mm(sp_[:, 0:nq], KT(kt), QT, start=True, stop=(mask is None and negm is None))
                if mask is not None:
                    P.mm(sp_[:, 0:nq], ident_b[:], mask, start=False, stop=(negm is None))
                if negm is not None:
                    P.mm(sp_[:, 0:nq], ones_b[0:1, 0:128], negm, start=False, stop=True)
                slots.append((sp_, pt_))

            scores(0)
            for i, (kt, mask) in enumerate(key_tiles):
                if i + 1 < nk:
                    scores(i + 1)
                sp_, pt_ = slots[i]
                P.act(pt_[:, 0:nq], sp_[:, 0:nq], AF.Exp, scale=scale)
                for qi, ob in enumerate(obanks):
                    last = (i == nk - 1) and tail is None
                    P.mm(ob, pt_[:, qi * 128:(qi + 1) * 128], V1(kt), start=(i == 0), stop=last)
            if tail is not None:
                tail()

        def mixer_B(l, unit, obT):
            with ExitStack() as ph:
                tiles = unit["tiles"]
                ntok_u = len(tiles) * 128
                tok_base = tiles[0] * 128
                has_ctx = any(sq["ctx"] for sq in unit["seqs"])
                maxkeys = max(sq["nt"] * 128 + (256 if sq["ctx"] else 0) for sq in unit["seqs"])
                maxnt = max(sq["nt"] for sq in unit["seqs"])
                nkeys_u = ntok_u + (256 if has_ctx else 0)
                cqnT = sb("cqnT", [128, 3, ntok_u], BF16, ph)
                ckvT = sb("ckvT", [128, 2, nkeys_u], BF16, ph)
                krT = sb("krT", [32, nkeys_u], BF16, ph)
                scale = 96.0 ** -0.5
                b1 = ExitStack()
                wB = sb("wB", [128, 8, 704], BF16, b1)
                gkvb = sb("gkvb", [128, 256], F32, b1)
                P.dma(wB[:, :, 0:672], w_in[l, :, OFF_BCQ:OFF_BCQ + 672].rearrange("(c p) n -> p c n", p=128), q="pool")
                P.dma(wB[:, :, 672:704], w_in_sw[l, :, 640:672].rearrange("(c p) n -> p c n", p=128), q="pool")
                P.dma(gkvb[:], b_gkv[l:l + 1, :].broadcast_to([128, 256]))
                hTb = [sb("hTbB%d" % i, [128, 8, 256], BF16, b1) for i in range(2)]
                ssb = [sb("ssB%d" % i, [128, 8], F32, b1) for i in range(2)]
                jk = [sb("jkB%d" % i, [128, 384], F32, b1) for i in range(2)]
                cqn_b = [sb("cqnb%d" % i, [128, 384], BF16, b1) for i in range(2)]
                ckvn_f = [sb("ckvnf%d" % i, [128, 256], F32, b1) for i in range(2)]
                ckvn_b = [sb("ckvnb%d" % i, [128, 256], BF16, b1) for i in range(2)]
                kr_f = [sb("krf%d" % i, [128, 64], F32, b1) for i in range(2)]
                kr_t = [sb("krt%d" % i, [128, 64], F32, b1) for i in range(2)]
                kr_b = [sb("krb%d" % i, [128, 32], BF16, b1) for i in range(2)]
                rKC = [sb("rKC%d" % i, [128, 32], F32, b1) for i in range(2)]
                rKS = [sb("rKS%d" % i, [128, 32], F32, b1) for i in range(2)]

                n_t = 0
                for bi, (sq, tok0, nb) in enumerate(blocks_of(unit, 256)):
                    hT = hTb[bi % 2]
                    btiles = [tok0 // 128 + i for i in range(nb // 128)]
                    build_hT(hT, btiles, 0)
                    for ti, t in enumerate(btiles):
                        r = n_t % 2
                        n_t += 1
                        sv = ssb[r]
                        p1, p2 = pb[0], pb[1]
                        for kc in range(8):
                            P.mm(p1[:, 0:384], hT[:, kc, ti * 128:(ti + 1) * 128], wB[:, kc, 0:384], start=(kc == 0), stop=(kc == 7))
                        for kc in range(8):
                            P.mm(p2[:, 0:320], hT[:, kc, ti * 128:(ti + 1) * 128], wB[:, kc, 384:704], start=(kc == 0), stop=(kc == 7))
                        P.act(jk[r][:, 0:384], p1[:, 0:384], AF.Square, accum_out=sv[:, 0:1])
                        P.ts("dve", sv[:, 1:2], sv[:, 0:1], 1.0 / 384, EPS, op0=ALU.mult, op1=ALU.add)
                        P.act(sv[:, 2:3], sv[:, 1:2], AF.Sqrt)
                        P.recip(sv[:, 3:4], sv[:, 2:3])
                        P.act(cqn_b[r][:], p1[:, 0:384], AF.Copy, scale=sv[:, 3:4])
                        P.act(jk[r][:, 0:256], p2[:, 0:256], AF.Square, accum_out=sv[:, 4:5])
                        P.ts("dve", sv[:, 5:6], sv[:, 4:5], 1.0 / 256, EPS, op0=ALU.mult, op1=ALU.add)
                        P.act(sv[:, 6:7], sv[:, 5:6], AF.Sqrt)
                        P.recip(sv[:, 7:8], sv[:, 6:7])
                        P.act(kr_f[r][:], p2[:, 256:320], AF.Copy)
                        P.act(ckvn_f[r][:], p2[:, 0:256], AF.Copy, scale=sv[:, 7:8])
                        P.tt("dve", ckvn_f[r][:], ckvn_f[r][:], gkvb[:], ALU.mult)
                        P.copy("dve", ckvn_b[r][:], ckvn_f[r][:])
                        lt = t - sq["t0"]
                        if unit["prompt"]:
                            P.dma(nckv_out[sq["idx"], l, lt * 128:(lt + 1) * 128, :], ckvn_f[r][:])
                            P.dma(nkr_out[sq["idx"], l, lt * 128:(lt + 1) * 128, :], kr_f[r][:, 0:32])
                        P.dma(rKC[r][:], ropeKB_C[t * 128:(t + 1) * 128, :])
                        P.dma(rKS[r][:], ropeKB_S[t * 128:(t + 1) * 128, :])
                        P.tt("dve", kr_t[r][:, 0:32], kr_f[r][:, 0:32], rKC[r][:], ALU.mult)
                        P.tt("dve", kr_t[r][:, 32:64], kr_f[r][:, 32:64], rKS[r][:], ALU.mult)
                        P.tt("dve", kr_b[r][:], kr_t[r][:, 0:32], kr_t[r][:, 32:64], ALU.add)
                        for c in range(3):
                            P.tr(pbb[:, c * 128:(c + 1) * 128], cqn_b[r][:, c * 128:(c + 1) * 128], ident_b[:])
                        for c in range(2):
                            P.tr(pbb[:, (3 + c) * 128:(4 + c) * 128], ckvn_b[r][:, c * 128:(c + 1) * 128], ident_b[:])
                        P.tr(pbb[0:32, 640:768], kr_b[r][:], ident_b[:])
                        ut = t - tiles[0]
                        P.copy("dve", cqnT[:, :, ut * 128:(ut + 1) * 128], pbb[:, 0:384].rearrange("p (c n) -> p c n", c=3))
                        P.copy("dve", ckvT[:, :, ut * 128:(ut + 1) * 128], pbb[:, 384:640].rearrange("p (c n) -> p c n", c=2))
                        P.copy("dve", krT[:, ut * 128:(ut + 1) * 128], pbb[0:32, 640:768])

                P.fence()
                b1.close()
                wuq = sb("wuq", [128, 3, 1024], BF16, ph)
                wuqs = sb("wuqs", [128, 3, 1024], BF16, ph)
                wukA = sb("wukA", [128, 2, 8, 128], BF16, ph)
                wukV = sb("wukV", [128, 2, 512], BF16, ph)
                E32 = sb("E32", [32, 128], BF16, ph)
                gqT = sb("gqT", [128, 3], F32, ph)
                P.dma(wuq[:], b_w_uq_r[l].rearrange("(c p) n -> p c n", p=128), q="pool")
                P.dma(wuqs[:], b_w_uq_rs[l].rearrange("(c p) n -> p c n", p=128), q="pool")
                P.dma(gqT[:], b_gqT[l, :, :])
                for c in range(3):
                    P.ts("pool", wuq[:, c, :], wuq[:, c, :], gqT[:, c:c + 1], None, op0=ALU.mult)
                    P.ts("pool", wuqs[:, c, :], wuqs[:, c, :], gqT[:, c:c + 1], None, op0=ALU.mult)
                P.memset("pool", wukA[:], 0.0)
                P.memset("pool", E32[:], 0.0)
                P.copy("pool", E32[:, 32:64], ident_b[0:32, 0:32])
                with ExitStack() as tw:
                    wukv = sb("wukv", [128, 2, 1024], BF16, tw)
                    P.dma(wukv[:], b_w_ukv[l].rearrange("(c p) n -> p c n", p=128), q="pool")
                    wv = wukv[:].rearrange("p c (h e) -> p c h e", e=128)
                    for c in range(2):
                        P.copy("pool", wukA[:, c, :, 64:128], wv[:, c, :, 0:64])
                        P.copy("pool", wukV[:, c, :].rearrange("p (h e) -> p h e", e=64), wv[:, c, :, 64:128])
                    P.fence()
                KTh = sb("KTh", [128, maxkeys], BF16, ph)
                ksq = sb("ksq", [128, 512], BF16, ph)
                V1 = sb("V1", [128, maxkeys // 128, 2, 65], BF16, ph)
                P.memset("pool", V1[:, :, :, 64:65], 1.0)
                rQC = [sb("rQC%d" % i, [128, 512], F32, ph) for i in range(1)] * 2
                rQS = [sb("rQS%d" % i, [128, 512], F32, ph) for i in range(1)] * 2
                qt1 = [sb("qt1_%d" % i, [128, 512], F32, ph) for i in range(1)] * 2
                qt2 = [sb("qt2_%d" % i, [128, 512], F32, ph) for i in range(1)] * 2
                QT = [sb("QT%d" % i, [128, 512], BF16, ph) for i in range(2)]
                qsq = [sb("qsq%d" % i, [128, 512], BF16, ph) for i in range(1)] * 2
                sqr = [sb("sqr%d" % i, [1, 512], F32, ph) for i in range(1)] * 2
                kmx = sb("kmx", [1, 8], F32, ph)
                kmax = sb("kmax", [1, 1], F32, ph)
                PTb = [sb("PTb%d" % i, [128, 512], BF16, ph) for i in range(2)]
                otok = sb("otok", [128, maxnt, 128], BF16, ph)
                rden = [sb("rden%d" % i, [128, 1], F32, ph) for i in range(4)]
                ctx_b = sb("ctx_b", [128, 2, 288], BF16, ph)

                for sq in unit["seqs"]:
                    nk_own = sq["nt"] * 128
                    nkeys = nk_own + (256 if sq["ctx"] else 0)
                    nkt = nkeys // 128
                    kb = sq["t0"] * 128 - tok_base
                    if sq["ctx"]:
                        for c2 in range(2):
                            P.dma(ctx_b[:, c2, 0:256], cckv[l, c2 * 128:(c2 + 1) * 128, :], q="pool")
                            P.dma(ctx_b[:, c2, 256:288], ckr[l, c2 * 128:(c2 + 1) * 128, :], q="pool")
                        for c2 in range(2):
                            for c in range(2):
                                P.tr(pbb[:, c * 128:(c + 1) * 128], ctx_b[:, c2, c * 128:(c + 1) * 128], ident_b[:])
                            P.tr(pbb[0:32, 256:384], ctx_b[:, c2, 256:288], ident_b[:])
                            k0 = kb + nk_own + c2 * 128
                            P.copy("dve", ckvT[:, :, k0:k0 + 128], pbb[:, 0:256].rearrange("p (c n) -> p c n", c=2))
                            P.copy("dve", krT[:, k0:k0 + 128], pbb[0:32, 256:384])
                    qblocks = [(o, min(512, nk_own - o)) for o in range(0, nk_own, 512)]
                    for h in range(8):
                        if h % 2 == 0:
                            for kt in range(nkt):
                                pv = pb[kt % 2]
                                for c in range(2):
                                    P.mm(pv[:, 0:128], ckvT[:, c, kb + kt * 128:kb + (kt + 1) * 128], wukV[:, c, h * 64:(h + 2) * 64], start=(c == 0), stop=(c == 1))
                                P.copy("act", V1[:, kt, :, 0:64], pv[:, 0:128].rearrange("p (h e) -> p h e", e=64))
                        for k0 in range(0, nkeys, 512):
                            kn = min(512, nkeys - k0)
                            pk = pb[4 + (k0 // 512) % 2]
                            for c in range(2):
                                P.mm(pk[:, 0:kn], wukA[:, c, h, :], ckvT[:, c, kb + k0:kb + k0 + kn], start=(c == 0), stop=False)
                            P.mm(pk[:, 0:kn], E32[:], krT[:, kb + k0:kb + k0 + kn], start=False, stop=True)
                            P.copy("act", KTh[:, k0:k0 + kn], pk[:, 0:kn])
                            P.tt("pool", ksq[:, 0:kn], KTh[:, k0:k0 + kn], KTh[:, k0:k0 + kn], ALU.mult)
                            P.memset("dve", KTh[0:1, k0:k0 + kn], 1.0)
                            pn = pb[6]
                            P.mm(pn[0:1, 0:kn], ones_b[:, 0:1], ksq[:, 0:kn])
                            P.reduce(kmx[:, k0 // 512:k0 // 512 + 1], pn[0:1, 0:kn], ALU.max)
                        nkb = (nkeys + 511) // 512
                        P.reduce(kmax[:], kmx[:, 0:nkb], ALU.max)
                        P.act(kmax[:], kmax[:], AF.Sqrt)
                        P.ts("dve", kmax[:], kmax[:], -1.0, None, op0=ALU.mult)
                        def prepQ(qi, h=h):
                            qo, nq = qblocks[qi]
                            r = (h * len(qblocks) + qi) % 2
                            g0 = sq["t0"] * 128 + qo
                            u0 = g0 - tok_base
                            P.dma(rQC[r][:, 0:nq], ropeQB_C[:, g0:g0 + nq])
                            P.dma(rQS[r][:, 0:nq], ropeQB_S[:, g0:g0 + nq])
                            pq = pb[6]
                            for c in range(3):
                                P.mm(pq[:, 0:nq], wuq[:, c, h * 128:(h + 1) * 128], cqnT[:, c, u0:u0 + nq], start=(c == 0), stop=(c == 2))
                            P.tt("dve", qt1[r][:, 0:nq], pq[:, 0:nq], rQC[r][:, 0:nq], ALU.mult)
                            for c in range(3):
                                P.mm(pq[:, 0:nq], wuqs[:, c, h * 128:(h + 1) * 128], cqnT[:, c, u0:u0 + nq], start=(c == 0), stop=(c == 2))
                            P.tt("dve", qt2[r][:, 0:nq], pq[:, 0:nq], rQS[r][:, 0:nq], ALU.mult)
                            P.tt("pool", QT[r][:, 0:nq], qt1[r][:, 0:nq], qt2[r][:, 0:nq], ALU.add)
                            P.tt("pool", qsq[r][:, 0:nq], QT[r][:, 0:nq], QT[r][:, 0:nq], ALU.mult)
                            P.mm(pq[0:1, 0:nq], ones_b[:, 0:1], qsq[r][:, 0:nq])
                            P.act(sqr[r][:, 0:nq], pq[0:1, 0:nq], AF.Sqrt)
                            P.ts("dve", QT[r][0:1, 0:nq], sqr[r][:, 0:nq], kmax[0:1, 0:1], None, op0=ALU.mult)

                        prepQ(0)
                        for qi, (qo, nq) in enumerate(qblocks):
                            r = (h * len(qblocks) + qi) % 2
                            if qi + 1 < len(qblocks):
                                prepQ(qi + 1)
                            nqt = nq // 128
                            obanks = [pb[qq][:, 0:65] for qq in range(nqt)]
                            attend(lambda kt: KTh[:, kt * 128:(kt + 1) * 128], lambda kt: V1[:, kt, h % 2, :],
                                   QT[r][:, 0:nq], nq, None, [(kt, None) for kt in range(nkt)],
                                   obanks, [pb[4], pb[5]], PTb, scale)
                            for qq in range(nqt):
                                lt = qo // 128 + qq
                                P.recip(rden[qq][:], pb[qq][:, 64:65])
                                P.ts("dve", otok[:, lt, (h % 2) * 64:(h % 2 + 1) * 64], pb[qq][:, 0:64],
                                     rden[qq][:, 0:1], None, op0=ALU.mult)
                        if h % 2 == 1:
                            for lt0 in range(0, sq["nt"], 8):
                                n8 = min(8, sq["nt"] - lt0)
                                for j in range(n8):
                                    P.tr(pbb[:, j * 128:(j + 1) * 128], otok[:, lt0 + j, :], ident_b[:])
                                uo = kb + lt0 * 128
                                P.copy("act", obT[:, h // 2, uo:uo + n8 * 128], pbb[:, 0:n8 * 128])
                P.fence()

        def mixer_A(l, unit, oaT):
            with ExitStack() as ph:
                tiles = unit["tiles"]
                ntok_u = len(tiles) * 128
                tok_base = tiles[0] * 128
                has_ctx = any(sq["ctx"] for sq in unit["seqs"])
                nkeys_u = ntok_u + (256 if has_ctx else 0)
                scale = 64.0 ** -0.5
                KTa = sb("KTa", [64, 2, nkeys_u], BF16, ph)
                Va = sb("Va", [128, nkeys_u // 128, 2, 65], BF16, ph)
                P.memset("pool", Va[:, :, :, 64:65], 1.0)
                rC = sb("rAC", [64, 256], F32, ph)
                rS = sb("rAS", [64, 256], F32, ph)
                t1 = sb("tA1", [64, 256], F32, ph)
                t2 = sb("tA2", [64, 256], F32, ph)
                hTb = [sb("hTbA0", [128, 8, 256], BF16, ph)]

                with ExitStack() as a1:
                    hTb.append(sb("hTbA1", [128, 8, 256], BF16, a1))
                    wAk = sb("wAk", [128, 8, 384], BF16, a1)
                    P.dma(wAk[:, :, 0:256], w_in[l, :, OFF_AK:OFF_AK + 256].rearrange("(c p) n -> p c n", p=128), q="pool")
                    P.dma(wAk[:, :, 256:384], w_in_sw[l, :, 512:640].rearrange("(c p) n -> p c n", p=128), q="pool")
                    kvf = [sb("kvf%d" % i, [128, 256], F32, a1) for i in range(2)]
                    for bi, (sq, tok0, nb) in enumerate(blocks_of(unit, 256)):
                        hT = hTb[bi % 2]
                        btiles = [tok0 // 128 + i for i in range(nb // 128)]
                        build_hT(hT, btiles, 0)
                        u0 = tok0 - tok_base
                        P.dma(rC[:, 0:nb], ropeA_C[:, tok0:tok0 + nb])
                        P.dma(rS[:, 0:nb], ropeA_S[:, tok0:tok0 + nb])
                        for g in range(2):
                            p1, p2 = pb[(2 * g) % 4], pb[(2 * g + 1) % 4]
                            for kc in range(8):
                                P.mm(p1[0:64, 0:nb], wAk[:, kc, g * 64:(g + 1) * 64], hT[:, kc, 0:nb], start=(kc == 0), stop=(kc == 7))
                            P.tt("dve", t1[:, 0:nb], p1[0:64, 0:nb], rC[:, 0:nb], ALU.mult)
                            for kc in range(8):
                                P.mm(p2[0:64, 0:nb], wAk[:, kc, 256 + g * 64:256 + (g + 1) * 64], hT[:, kc, 0:nb], start=(kc == 0), stop=(kc == 7))
                            P.tt("dve", t2[:, 0:nb], p2[0:64, 0:nb], rS[:, 0:nb], ALU.mult)
                            P.tt("pool", KTa[:, g, u0:u0 + nb], t1[:, 0:nb], t2[:, 0:nb], ALU.add)
                        for ti, t in enumerate(btiles):
                            pv = pb[4 + ti % 2]
                            for kc in range(8):
                                P.mm(pv[:, 0:256], hT[:, kc, ti * 128:(ti + 1) * 128], wAk[:, kc, 0:256], start=(kc == 0), stop=(kc == 7))
                            ut = t - tiles[0]
                            P.copy("act", Va[:, ut, :, 0:64], pv[:, 128:256].rearrange("p (g e) -> p g e", e=64))
                            if unit["prompt"]:
                                lt = t - sq["t0"]
                                kv = kvf[ti % 2]
                                P.copy("act", kv[:], pv[:, 0:256])
                                P.dma(nak_out[sq["idx"], l, lt * 128:(lt + 1) * 128, :], kv[:, 0:128])
                                P.dma(nav_out[sq["idx"], l, lt * 128:(lt + 1) * 128, :], kv[:, 128:256])
                    if has_ctx:
                        ctxkb = sb("ctxkb", [128, 2, 128], BF16, a1)
                        ctxvb = sb("ctxvb", [128, 2, 128], BF16, a1)
                        for c2 in range(2):
                            P.dma(ctxkb[:, c2, :], cak[l, c2 * 128:(c2 + 1) * 128, :], q="pool")
                            P.dma(ctxvb[:, c2, :], cav[l, c2 * 128:(c2 + 1) * 128, :], q="pool")
                        for c2 in range(2):
                            for g in range(2):
                                P.tr(pbb[0:64, (c2 * 2 + g) * 128:(c2 * 2 + g + 1) * 128], ctxkb[:, c2, g * 64:(g + 1) * 64], ident_b[:])
                            P.copy("dve", Va[:, ntok_u // 128 + c2, :, 0:64], ctxvb[:, c2, :].rearrange("p (g e) -> p g e", e=64))
                        for c2 in range(2):
                            for g in range(2):
                                P.copy("act", KTa[:, g, ntok_u + c2 * 128:ntok_u + (c2 + 1) * 128],
                                       pbb[0:64, (c2 * 2 + g) * 128:(c2 * 2 + g + 1) * 128])
                    P.fence()

                wAq = sb("wAq", [128, 8, 512], BF16, ph)
                wAqs = sb("wAqs", [128, 8, 512], BF16, ph)
                P.dma(wAq[:], w_in[l, :, 0:512].rearrange("(c p) n -> p c n", p=128), q="pool")
                P.dma(wAqs[:], w_in_sw[l, :, 0:512].rearrange("(c p) n -> p c n", p=128), q="pool")
                maskb = sb("maskb", [128, 8, 128], BF16, ph)
                P.dma(maskb[:], cst_masks[:, :, :], q="pool")
                sinkf = sb("sinkf", [1, 8], F32, ph)
                vsink = sb("vsink", [1, 65], BF16, ph)
                P.dma(sinkf[:], a_sink[l:l + 1, :])
                P.ts("dve", sinkf[:], sinkf[:], 1.0 / scale, None, op0=ALU.mult)
                P.memset("dve", vsink[:], 0.0)
                P.memset("dve", vsink[:, 64:65], 1.0)
                QTb = [sb("QTb%d" % i, [64, 8, 256], BF16, ph) for i in range(2)]
                ksqa = sb("ksqa", [64, 512], BF16, ph)
                qsqa = sb("qsqa", [64, 4, 128], BF16, ph)
                kmx = sb("kmxA", [1, 2, 8], F32, ph)
                kmax = sb("kmaxA", [1, 2], F32, ph)
                sqr = sb("sqrA", [1, 512], F32, ph)
                negf = sqr
                negm = [sb("negmA%d" % i, [1, 512], BF16, ph) for i in range(2)]
                srow = sb("srowA", [1, 512], F32, ph)
                psink = [sb("psinkA%d" % i, [1, 512], BF16, ph) for i in range(2)]
                PTb = [sb("PTbA%d" % i, [128, 512], BF16, ph) for i in range(2)]
                otok = [sb("otokA%d" % i, [128, 512], BF16, ph) for i in range(2)]
                rden = [sb("rdenA%d" % i, [128, 1], F32, ph) for i in range(4)]
                nq_ = 0
                nblk = 0
                for sq in unit["seqs"]:
                    kb = sq["t0"] * 128 - tok_base
                    nk_own = sq["nt"] * 128
                    kranges = [(kb, nk_own)] + ([(ntok_u, 256)] if sq["ctx"] else [])
                    for g in range(2):
                        nb_ = 0
                        for (k0, kn_all) in kranges:
                            for o in range(0, kn_all, 512):
                                kn = min(512, kn_all - o)
                                P.tt("pool", ksqa[:, 0:kn], KTa[:, g, k0 + o:k0 + o + kn], KTa[:, g, k0 + o:k0 + o + kn], ALU.mult)
                                P.mm(pb[6][0:1, 0:kn], ones_b[0:64, 0:1], ksqa[:, 0:kn])
                                P.reduce(kmx[:, g, nb_:nb_ + 1], pb[6][0:1, 0:kn], ALU.max)
                                nb_ += 1
                        P.reduce(kmax[:, g:g + 1], kmx[:, g, 0:nb_], ALU.max)
                    P.act(kmax[:], kmax[:], AF.Sqrt)
                    P.ts("dve", kmax[:], kmax[:], -1.0, None, op0=ALU.mult)
                    blocks = [(q0, min(2, sq["nt"] - q0)) for q0 in range(0, sq["nt"], 2)]

                    def qproj(b):
                        q0, nbt = blocks[b]
                        nb = nbt * 128
                        tok0 = (sq["t0"] + q0) * 128
                        hT = hTb[0]
                        QT = QTb[b % 2]
                        build_hT(hT, [sq["t0"] + q0 + i for i in range(nbt)], 0)
                        P.dma(rC[:, 0:nb], ropeA_C[:, tok0:tok0 + nb])
                        P.dma(rS[:, 0:nb], ropeA_S[:, tok0:tok0 + nb])
                        for h in range(8):
                            p1 = pb[6]
                            for kc in range(8):
                                P.mm(p1[0:64, 0:nb], wAq[:, kc, h * 64:(h + 1) * 64], hT[:, kc, 0:nb], start=(kc == 0), stop=(kc == 7))
                            for kc in range(8):
                                P.mm(p1[0:64, 256:256 + nb], wAqs[:, kc, h * 64:(h + 1) * 64], hT[:, kc, 0:nb], start=(kc == 0), stop=(kc == 7))
                            P.tt("dve", t1[:, 0:nb], p1[0:64, 0:nb], rC[:, 0:nb], ALU.mult)
                            P.tt("dve", t2[:, 0:nb], p1[0:64, 256:256 + nb], rS[:, 0:nb], ALU.mult)
                            P.tt("pool", QT[:, h, 0:nb], t1[:, 0:nb], t2[:, 0:nb], ALU.add)

                    items = [(b, qi, g) for b in range(len(blocks)) for qi in range(blocks[b][1]) for g in range(2)]

                    def prep_item(n):
                        b, qi, g = items[n]
                        r = n % 2
                        QTg = QTb[b % 2][:, 4 * g:4 * g + 4, qi * 128:(qi + 1) * 128]
                        P.tt("pool", qsqa[:], QTg, QTg, ALU.mult)
                        P.mm(pb[6][0:1, 0:512], ones_b[0:64, 0:1], qsqa[:])
                        P.act(sqr[:], pb[6][0:1, 0:512], AF.Sqrt)
                        P.ts("dve", negf[:], sqr[:], kmax[0:1, g:g + 1], None, op0=ALU.mult)
                        P.copy("dve", negm[r][:], negf[:])
                        for hh in range(4):
                            P.ts("dve", srow[:, hh * 128:(hh + 1) * 128], negf[:, hh * 128:(hh + 1) * 128],
                                 sinkf[0:1, 4 * g + hh:4 * g + hh + 1], None, op0=ALU.add)
                        P.act(psink[r][:], srow[:], AF.Exp, scale=scale)

                    qproj(0)
                    prep_item(0)
                    for n, (b, qi, g) in enumerate(items):
                        r = n % 2
                        q0 = blocks[b][0]
                        qt = q0 + qi
                        uq = kb + qt * 128
                        ot = otok[qt % 2]
                        if qi == 0 and g == 0 and b + 1 < len(blocks):
                            qproj(b + 1)
                        if n + 1 < len(items):
                            prep_item(n + 1)
                        QTg = QTb[b % 2][:, 4 * g:4 * g + 4, qi * 128:(qi + 1) * 128]
                        if sq["ctx"]:
                            kts = []
                            if qt > 0:
                                kts.append((kb // 128 + qt - 1, maskb[:, 0:4, :]))
                            kts.append((kb // 128 + qt, None))
                            if qt + 1 < sq["nt"]:
                                kts.append((kb // 128 + qt + 1, maskb[:, 4:8, :]))
                            kts += [(ntok_u // 128, None), (ntok_u // 128 + 1, None)]
                        else:
                            kts = [(kb // 128 + j, None) for j in range(sq["nt"])]
                        obanks = [pb[hh][:, 0:65] for hh in range(4)]

                        def tail(r=r):
                            for hh in range(4):
                                P.mm(pb[hh][:, 0:65], psink[r][0:1, hh * 128:(hh + 1) * 128], vsink[:], start=False, stop=True)

                        attend(lambda kt: KTa[:, g, kt * 128:(kt + 1) * 128], lambda kt: Va[:, kt, g, :],
                               QTg, 512, negm[r][:], kts, obanks, [pb[4], pb[5]], PTb, scale, tail=tail)
                        for hh in range(4):
                            h = 4 * g + hh
                            P.recip(rden[hh][:], pb[hh][:, 64:65])
                            P.ts("dve", ot[:, h * 64:(h + 1) * 64], pb[hh][:, 0:64], rden[hh][:, 0:1], None, op0=ALU.mult)
                        if g == 1:
                            for c in range(4):
                                P.tr(pbb[:, c * 128:(c + 1) * 128], ot[:, c * 128:(c + 1) * 128], ident_b[:])
                            P.copy("act", oaT[:, :, uq:uq + 128], pbb[:, 0:512].rearrange("p (c n) -> p c n", c=4))
                P.fence()

        def mixer_C(l, unit, ocT):
            with ExitStack() as ph:
                tiles = unit["tiles"]
                ntok_u = len(tiles) * 128
                nt_u = len(tiles)
                tok_base = tiles[0] * 128
                trc = sb("trc", [128, 2, 2, 128], BF16, ph)
                trif = sb("trif", [128, 2, 128], F32, ph)
                lvN = sb("lvN", [128, 7, 2, 128], BF16, ph)
                lvA = sb("lvA", [128, 7, 2, 128], BF16, ph)
                convw = sb("convw", [128, 12, 3], F32, ph)
                alog = sb("alog", [128, 8], F32, ph)
                dtb = sb("dtb", [128, 8], F32, ph)
                ong = sb("ong", [128, 128], F32, ph)
                wab = sb("wab", [128, 8, 16], BF16, ph)
                gb = sb("gb", [128, nt_u, 16], F32, ph)
                P.dma(trc[:], cst_trc[:, :, :, :], q="pool")
                P.dma(lvN[:], cst_lvN[:, :, :, :], q="pool")
                P.dma(lvA[:], cst_lvA[:, :, :, :], q="pool")
                P.dma(trif[:, 0, :], cst_tri[:, 1, :])
                P.dma(trif[:, 1, :], cst_tri[:, 3, :])
                P.dma(convw[:], c_convT[l, :, :, :])
                P.dma(alog[:], c_a_log[l:l + 1, :].broadcast_to([128, 8]))
                P.dma(dtb[:], c_dt_bias[l:l + 1, :].broadcast_to([128, 8]))
                P.dma(ong[:], c_onorm_g[l:l + 1, :].broadcast_to([128, 128]))
                P.dma(wab[:], w_in[l, :, OFF_CA:OFF_CA + 16].rearrange("(c p) n -> p c n", p=128), q="pool")
                P.act(alog[:], alog[:], AF.Exp)
                P.ts("dve", alog[:], alog[:], -1.0, None, op0=ALU.mult)
                sm = [sb("smC%d" % i, [128, 8], F32, ph) for i in range(6)]

                for h in range(4):
                    with ExitStack() as hs:
                        qkvT = sb("qkvT", [128, 3, ntok_u], BF16, hs)
                        zs = sb("zs", [128, nt_u, 128], F32, hs)
                        oacc = sb("oacc", [128, nt_u, 128], F32, hs)
                        P.memset("pool", oacc[:], 0.0)
                        with ExitStack() as ps_:
                            wC = sb("wC", [128, 8, 512], BF16, ps_)
                            for j in range(3):
                                c0 = OFF_CQKV + j * 512 + h * 128
                                P.dma(wC[:, :, j * 128:(j + 1) * 128], w_in[l, :, c0:c0 + 128].rearrange("(c p) n -> p c n", p=128), q="pool")
                            P.dma(wC[:, :, 384:512], w_in[l, :, OFF_CZ + h * 128:OFF_CZ + (h + 1) * 128].rearrange("(c p) n -> p c n", p=128), q="pool")
                            hTb = [sb("hTbC%d" % i, [128, 8, 512], BF16, ps_) for i in range(2)]
                            rawT = sb("rawT", [128, 3, ntok_u], F32, ps_)
                            cv = [sb("cvC%d" % i, [128, 512], F32, ps_) for i in range(2)]
                            zf = [sb("zfC%d" % i, [128, 128], F32, ps_) for i in range(2)]
                            for bi, (sq, tok0, nb) in enumerate(blocks_of(unit)):
                                hT = hTb[bi % 2]
                                btiles = [tok0 // 128 + i for i in range(nb // 128)]
                                build_hT(hT, btiles, 0)
                                u0 = tok0 - tok_base
                                for j in range(3):
                                    pj = pb[j % 2]
                                    for kc in range(8):
                                        P.mm(pj[:, 0:nb], wC[:, kc, j * 128:(j + 1) * 128], hT[:, kc, 0:nb], start=(kc == 0), stop=(kc == 7))
                                    P.copy("act", rawT[:, j, u0:u0 + nb], pj[:, 0:nb])
                                for ti, t in enumerate(btiles):
                                    ut = t - tiles[0]
                                    pz = pb[2 + ti % 2]
                                    for kc in range(8):
                                        P.mm(pz[:, 0:128], hT[:, kc, ti * 128:(ti + 1) * 128], wC[:, kc, 384:512], start=(kc == 0), stop=(kc == 7))
                                    P.act(zf[ti % 2][:], pz[:, 0:128], AF.Silu)
                                    P.tt("pool", zs[:, ut, :], zf[ti % 2][:], ong[:], ALU.mult)
                                    if h == 0:
                                        pa_ = pb[4]
                                        for kc in range(8):
                                            P.mm(pa_[:, 0:16], hT[:, kc, ti * 128:(ti + 1) * 128], wab[:, kc, :], start=(kc == 0), stop=(kc == 7))
                                        P.tt("dve", sm[0][:], pa_[:, 0:8], dtb[:], ALU.add)
                                        P.act(sm[1][:], sm[0][:], AF.Abs)
                                        P.act(sm[2][:], sm[1][:], AF.Exp, scale=-1.0)
                                        P.act(sm[3][:], sm[2][:], AF.Ln, bias=1.0)
                                        P.ts("dve", sm[4][:], sm[0][:], 0.0, None, op0=ALU.max)
                                        P.tt("dve", sm[5][:], sm[4][:], sm[3][:], ALU.add)
                                        P.tt("dve", gb[:, ut, 0:8], sm[5][:], alog[:], ALU.mult)
                                        P.act(gb[:, ut, 8:16], pa_[:, 8:16], AF.Sigmoid)
                            for sq in unit["seqs"]:
                                s0 = sq["t0"] * 128 - tok_base
                                n = sq["nt"] * 128
                                for j in range(3):
                                    ci = j * 4 + h
                                    for o in range(0, n, 512):
                                        nb = min(512, n - o)
                                        a0 = s0 + o
                                        c_ = cv[(o // 512) % 2]
                                        P.act(c_[:, 0:nb], rawT[:, j, a0:a0 + nb], AF.Copy, scale=convw[:, ci, 1:2])
                                        lo = 1 if o == 0 else 0
                                        P.stt(c_[:, lo:nb], rawT[:, j, a0 + lo - 1:a0 + nb - 1], convw[:, ci, 0:1], c_[:, lo:nb], ALU.mult, ALU.add)
                                        hi = nb - 1 if o + nb == n else nb
                                        P.stt(c_[:, 0:hi], rawT[:, j, a0 + 1:a0 + hi + 1], convw[:, ci, 2:3], c_[:, 0:hi], ALU.mult, ALU.add)
                                        P.act(qkvT[:, j, a0:a0 + nb], c_[:, 0:nb], AF.Silu)
                            P.fence()

                        gc = sb("gc", [128, 2, nt_u], F32, hs)
                        egc = sb("egc", [128, 2, nt_u], F32, hs)
                        gl = sb("gl", [128, 2, nt_u], F32, hs)
                        edl = sb("edl", [128, 2, nt_u], F32, hs)
                        ge = sb("ge", [128, 2, nt_u], F32, hs)
                        gsel = sb("gsel", [128, 2, nt_u], F32, hs)
                        for d in range(2):
                            P.copy("dve", gsel[:, d, :], gb[:, :, d * 4 + h])
                        for d in range(2):
                            P.mm(pb[6][:, d * nt_u:(d + 1) * nt_u], trif[:, d, :], gsel[:, d, :])
                        P.copy("dve", gc[:], pb[6][:, 0:2 * nt_u].rearrange("p (d n) -> p d n", d=2))
                        P.mm(pb[5][:, 0:2 * nt_u], ones_f[:], gsel[:].rearrange("p d n -> p (d n)"))
                        P.copy("dve", gl[:], pb[5][:, 0:2 * nt_u].rearrange("p (d n) -> p d n", d=2))
                        P.act(egc[:], gc[:], AF.Exp)
                        P.act(ge[:], gl[:], AF.Exp)
                        P.tt("dve", edl[:], gl[:], gc[:], ALU.subtract)
                        P.act(edl[:], edl[:], AF.Exp)

                        Sst = sb("Sst", [128, 2, 128], F32, hs)
                        Sbf = sb("Sbf", [128, 2, 128], BF16, hs)
                        ident2 = ident_f[:].unsqueeze(1).broadcast_to([128, 2, 128])

                        def slotbufs(s_):
                            W = Ctx()
                            f32 = lambda n, shp=(128, 2, 128): sb("%s_s%d" % (n, s_), list(shp), F32, hs)
                            b16 = lambda n, shp=(128, 2, 128): sb("%s_s%d" % (n, s_), list(shp), BF16, hs)
                            W.ssq = f32("ssq", (128, 2, 2)); W.rkq = f32("rkq", (128, 2, 2))
                            W.qkv = b16("qkvtok", (128, 2, 3, 128)); W.kn = b16("kn"); W.qn = b16("qn")
                            W.vke = b16("vke", (128, 2, 256)); W.kqT = b16("kqT", (128, 2, 2, 128))
                            W.diag = f32("diag"); W.DT = f32("DT"); W.tmp1 = f32("tmp1"); W.AT = f32("AT"); W.tmp2 = f32("tmp2")
                            W.qkmT = b16("qkmT"); W.A = f32("A"); W.R = f32("R"); W.RT = f32("RT")
                            W.BkT = [f32("BkT0"), f32("BkT1")]; W.Y = f32("Y"); W.Rb = b16("Rb"); W.u = f32("u")
                            W.w = b16("w"); W.wT = b16("wT"); W.kd = b16("kd"); W.qd = b16("qd"); W.qdT = b16("qdT")
                            W.vnew = b16("vnew"); W.t3 = f32("t3"); W.t4 = f32("t4")
                            return W
                        SB_ = [slotbufs(0), slotbufs(1)]

                        def run_slot(sq, s_):
                            W = SB_[s_]
                            kb = sq["t0"] * 128 - tok_base
                            n = sq["nt"]
                            bA, bB, bC = pb[3 * s_], pb[3 * s_ + 1], pb[3 * s_ + 2]
                            tb0 = s_ * 4 * 128
                            tb = lambda j, cnt=1: pbb[:, tb0 + j * 128:tb0 + (j + cnt) * 128]
                            v3 = lambda ap: ap.rearrange("p (d n) -> p d n", d=2)
                            if s_ == 1:
                                for _ in range(3):
                                    yield
                            for st in range(s_, n, 2):
                                cs = [st, n - 1 - st]
                                uts = [kb // 128 + c for c in cs]
                                gcc = [gc[:, d, uts[d]:uts[d] + 1] for d in range(2)]
                                beta = [gb[:, uts[d], 8 + d * 4 + h:8 + d * 4 + h + 1] for d in range(2)]
                                for d in range(2):
                                    sl = slice(kb + cs[d] * 128, kb + (cs[d] + 1) * 128)
                                    for j in range(3):
                                        P.tr(tb(j), qkvT[:, j, sl], ident_b[:])
                                    P.copy("act", W.qkv[:, d, :, :], tb(0, 3).rearrange("p (j n) -> p j n", j=3))
                                yield
                                for d in range(2):
                                    P.act(W.t3[:, d, :], W.qkv[:, d, 0, :], AF.Square, accum_out=W.ssq[:, d, 0:1])
                                    P.act(W.t4[:, d, :], W.qkv[:, d, 1, :], AF.Square, accum_out=W.ssq[:, d, 1:2])
                                yield
                                P.ts("dve", W.rkq[:], W.ssq[:], EPS, None, op0=ALU.add)
                                P.act(W.rkq[:], W.rkq[:], AF.Sqrt)
                                P.recip(W.rkq[:], W.rkq[:])
                                yield
                                for d in range(2):
                                    P.act(W.kn[:, d, :], W.qkv[:, d, 1, :], AF.Copy, scale=W.rkq[:, d, 1:2])
                                    P.ts("dve", W.qn[:, d, :], W.qkv[:, d, 0, :], W.rkq[:, d, 0:1], 128.0 ** -0.5, op0=ALU.mult, op1=ALU.mult)
                                    P.copy("pool", W.vke[:, d, 0:128], W.qkv[:, d, 2, :])
                                yield
                                for d in range(2):
                                    P.tr(tb(0), W.kn[:, d, :], ident_b[:])
                                    P.tr(tb(1), W.qn[:, d, :], ident_b[:])
                                    P.copy("act", W.kqT[:, d, :, :], tb(0, 2).rearrange("p (c n) -> p c n", c=2))
                                    P.act(W.vke[:, d, 128:256], W.kn[:, d, :], AF.Copy, scale=egc[:, d, uts[d]:uts[d] + 1])
                                    P.ts("dve", W.kd[:, d, :], W.kn[:, d, :], edl[:, d, uts[d]:uts[d] + 1], None, op0=ALU.mult)
                                    P.act(W.qd[:, d, :], W.qn[:, d, :], AF.Copy, scale=egc[:, d, uts[d]:uts[d] + 1])
                                    P.ts("dve", W.diag[:, d, :], ident_f[:], gcc[d], None, op0=ALU.mult)
                                yield
                                for d in range(2):
                                    P.mm(bA[:, d * 128:(d + 1) * 128], W.kqT[:, d, 0, :], W.kqT[:, d, 0, :])
                                    P.mm(bA[:, 256 + d * 128:256 + (d + 1) * 128], W.kqT[:, d, 0, :], W.kqT[:, d, 1, :])
                                    P.mm(bB[:, d * 128:(d + 1) * 128], ones_f[:], W.diag[:, d, :])
                                    P.tr(tb(2 + d), W.qd[:, d, :], ident_b[:])
                                for d in range(2):
                                    P.ts("dve", W.DT[:, d, :], bB[:, d * 128:(d + 1) * 128], gcc[d], 0.0, op0=ALU.subtract, op1=ALU.min)
                                P.act(W.DT[:], W.DT[:], AF.Exp)
                                P.copy("act", W.qdT[:], v3(tb(2, 2)))
                                yield
                                P.tt("dve", W.tmp1[:], v3(bA[:, 0:256]), W.DT[:], ALU.mult)
                                P.tt("dve", W.tmp2[:], v3(bA[:, 256:512]), W.DT[:], ALU.mult)
                                for d in range(2):
                                    P.stt(W.AT[:, d, :], W.tmp1[:, d, :], beta[d], trc[:, 0, d, :], ALU.mult, ALU.mult)
                                P.tt("pool", W.qkmT[:], W.tmp2[:], trc[:, 1, :, :], ALU.mult)
                                yield
                                for d in range(2):
                                    P.tr(bB[:, 256 + d * 128:256 + (d + 1) * 128], W.AT[:, d, :], ident_f[:])
                                P.copy("act", W.A[:], v3(bB[:, 256:512]))
                                P.tt("dve", W.t3[:], W.AT[:], lvN[:, 0, :, :], ALU.mult)
                                P.tt("dve", W.R[:], ident2, W.t3[:], ALU.subtract)
                                yield
                                P.tt("pool", W.t4[:], W.A[:], lvA[:, 0, :, :], ALU.mult)
                                P.tt("pool", W.RT[:], ident2, W.t4[:], ALU.subtract)
                                P.tt("pool", W.BkT[1][:], W.A[:], lvA[:, 1, :, :], ALU.mult)
                                yield
                                for k in range(1, 7):
                                    Bk = W.BkT[k % 2]
                                    for d in range(2):
                                        P.mm(bC[:, d * 128:(d + 1) * 128], Bk[:, d, :], W.R[:, d, :])
                                    if k < 6:
                                        P.tt("pool", W.BkT[(k + 1) % 2][:], W.A[:], lvA[:, k + 1, :, :], ALU.mult)
                                    P.copy("act", W.Y[:], v3(bC[:, 0:256]))
                                    yield
                                    for d in range(2):
                                        P.mm(bB[:, d * 128:(d + 1) * 128], W.RT[:, d, :], W.Y[:, d, :])
                                        P.mm(bB[:, 256 + d * 128:256 + (d + 1) * 128], W.Y[:, d, :], W.RT[:, d, :])
                                    yield
                                    P.tt("dve", W.R[:], W.R[:], v3(bB[:, 0:256]), ALU.subtract)
                                    P.tt("dve", W.RT[:], W.RT[:], v3(bB[:, 256:512]), ALU.subtract)
                                    yield
                                P.copy("act", W.Rb[:], W.R[:])
                                yield
                                for d in range(2):
                                    P.mm(bA[:, d * 256:(d + 1) * 256], W.Rb[:, d, :], W.vke[:, d, :])
                                for d in range(2):
                                    P.act(W.u[:, d, :], bA[:, d * 256:d * 256 + 128], AF.Copy, scale=beta[d])
                                    P.act(W.w[:, d, :], bA[:, d * 256 + 128:(d + 1) * 256], AF.Copy, scale=beta[d])
                                yield
                                for d in range(2):
                                    P.tr(tb(d), W.w[:, d, :], ident_b[:])
                                P.copy("act", W.wT[:], v3(tb(0, 2)))
                                yield
                                for d in range(2):
                                    P.mm(bC[:, d * 128:(d + 1) * 128], W.wT[:, d, :], Sbf[:, d, :])
                                P.tt("dve", W.vnew[:], W.u[:], v3(bC[:, 0:256]), ALU.subtract)
                                yield
                                for d in range(2):
                                    P.mm(bC[:, 256 + d * 128:256 + (d + 1) * 128], W.qdT[:, d, :], Sbf[:, d, :], start=True, stop=False)
                                    P.mm(bC[:, 256 + d * 128:256 + (d + 1) * 128], W.qkmT[:, d, :], W.vnew[:, d, :], start=False, stop=True)
                                for d in range(2):
                                    P.tt("dve", oacc[:, uts[d], :], oacc[:, uts[d], :], bC[:, 256 + d * 128:256 + (d + 1) * 128], ALU.add)
                                yield
                                for d in range(2):
                                    P.mm(bA[:, d * 128:(d + 1) * 128], W.kd[:, d, :], W.vnew[:, d, :])
                                for d in range(2):
                                    P.stt(Sst[:, d, :], Sst[:, d, :], ge[:, d, uts[d]:uts[d] + 1], bA[:, d * 128:(d + 1) * 128], ALU.mult, ALU.add)
                                P.copy("act", Sbf[:], Sst[:])
                                yield

                        for sq in unit["seqs"]:
                            for d in range(2):
                                if sq["ctx"]:
                                    P.dma(Sst[:, d, :], cstate[l, d, h, :, :])
                                else:
                                    P.memset("pool", Sst[:, d, :], 0.0)
                            P.copy("act", Sbf[:], Sst[:])
                            alive = [run_slot(sq, 0), run_slot(sq, 1)]
                            while alive:
                                for g_ in list(alive):
                                    try:
                                        next(g_)
                                    except StopIteration:
                                        alive.remove(g_)
                            if unit["prompt"]:
                                for d in range(2):
                                    P.dma(ncs_out[sq["idx"], l, d, h, :, :], Sst[:, d, :])
                        on_ss = sb("on_ss", [128, nt_u], F32, hs)
                        on_r = sb("on_r", [128, nt_u], F32, hs)
                        ojk = [sb("ojk%d" % i, [128, 128], F32, hs) for i in range(2)]
                        ob16 = sb("ob16", [128, nt_u, 128], BF16, hs)
                        for ut in range(nt_u):
                            P.act(ojk[ut % 2][:], oacc[:, ut, :], AF.Square, accum_out=on_ss[:, ut:ut + 1])
                        P.ts("dve", on_r[:], on_ss[:], 1.0 / 128, EPS, op0=ALU.mult, op1=ALU.add)
                        P.act(on_r[:], on_r[:], AF.Sqrt)
                        P.recip(on_r[:], on_r[:])
                        for ut in range(nt_u):
                            P.stt(ob16[:, ut, :], oacc[:, ut, :], on_r[:, ut:ut + 1], zs[:, ut, :], ALU.mult, ALU.mult)
                        for u0 in range(0, nt_u, 8):
                            n8 = min(8, nt_u - u0)
                            for j in range(n8):
                                P.tr(pbb[:, j * 128:(j + 1) * 128], ob16[:, u0 + j, :], ident_b[:])
                            P.copy("act", ocT[:, h, u0 * 128:(u0 + n8) * 128], pbb[:, 0:n8 * 128])
                        P.fence()
                P.fence()

        def merge(l, unit, oT):
            tiles = unit["tiles"]
            tok_base = tiles[0] * 128
            cd = unit["cond"]
            for sub0 in range(0, len(tiles), 8):
                stiles = tiles[sub0:sub0 + 8]
                with ExitStack() as ph:
                    nsub = len(stiles) * 128
                    mT = sb("mT", [128, 8, nsub], BF16, ph)
                    with ExitStack() as p2:
                        hTm = sb("hTm", [128, 8, 256], BF16, p2)
                        gt = [sb("gtM%d" % i, [128, 256], F32, p2) for i in range(2)]
                        tm = [sb("tmM%d" % i, [128, 256], F32, p2) for i in range(2)]
                        wg = sb("wg", [128, 8, D], BF16, p2)
                        wbr = sb("wbr", [128, 4, D], BF16, p2)
                        for j in range(3):
                            P.dma(wg[:], w_in[l, :, OFF_GATES + j * D:OFF_GATES + (j + 1) * D].rearrange("(c p) n -> p c n", p=128), q="pool")
                            P.dma(wbr[:], w_branch[l, j].rearrange("(c p) n -> p c n", p=128), q="pool")
                            for b0 in range(0, len(stiles), 2):
                                bt = stiles[b0:b0 + 2]
                                nb = len(bt) * 128
                                build_hT(hTm, bt, 0)
                                u0 = bt[0] * 128 - tok_base
                                m0 = b0 * 128
                                for fc in range(8):
                                    pg, py = pb[fc % 2], pb[2 + fc % 2]
                                    for kc in range(8):
                                        P.mm(pg[:, 0:nb], wg[:, kc, fc * 128:(fc + 1) * 128], hTm[:, kc, 0:nb], start=(kc == 0), stop=(kc == 7))
                                    P.act(gt[fc % 2][:, 0:nb], pg[:, 0:nb], AF.Sigmoid)
                                    for c4 in range(4):
                                        P.mm(py[:, 0:nb], wbr[:, c4, fc * 128:(fc + 1) * 128], oT[j][:, c4, u0:u0 + nb], start=(c4 == 0), stop=(c4 == 3))
                                    if j == 0:
                                        P.tt("dve", mT[:, fc, m0:m0 + nb], py[:, 0:nb], gt[fc % 2][:, 0:nb], ALU.mult)
                                    else:
                                        P.tt("dve", tm[fc % 2][:, 0:nb], py[:, 0:nb], gt[fc % 2][:, 0:nb], ALU.mult)
                                        P.tt("pool", mT[:, fc, m0:m0 + nb], mT[:, fc, m0:m0 + nb], tm[fc % 2][:, 0:nb], ALU.add)
                        P.fence()
                    with ExitStack() as p3:
                        wo = sb("wo", [128, 8, D], BF16, p3)
                        G1 = sb("G1", [128, D], F32, p3)
                        tmo = [sb("tmo%d" % i, [128, 512], F32, p3) for i in range(2)]
                        P.dma(wo[:], w_out[l].rearrange("(c p) n -> p c n", p=128), q="pool")
                        make_G(G1[:], 0, cd, p3)
                        n_ = 0
                        for ti, t in enumerate(stiles):
                            for half in range(2):
                                py = pb[2 + n_ % 4]
                                for kc in range(8):
                                    P.mm(py[:], mT[:, kc, ti * 128:(ti + 1) * 128], wo[:, kc, half * 512:(half + 1) * 512], start=(kc == 0), stop=(kc == 7))
                                tmp = tmo[n_ % 2]
                                n_ += 1
                                P.tt("dve", tmp[:], py[:], G1[:, half * 512:(half + 1) * 512], ALU.mult)
                                P.tt("pool", x[:, t, half * 512:(half + 1) * 512], x[:, t, half * 512:(half + 1) * 512], tmp[:], ALU.add)
                        P.fence()

        K.P, K.nc, K.x, K.pb, K.sb, K.psb = P, nc, x, pb, sb, psb
        def dump_bf16(name, src, n):
            with ExitStack() as ph:
                stg = sb("stg", [128, src.shape[1], n], F32, ph)
                P.copy("dve", stg[:], src)
                P.dma(dbg_out[name][:, :, :], stg[:])
                P.fence()

        for l in range(depth):
            if "adaln" in parts:
                adaln(l)
            calc_rstd(list(range(NT)))
            for unit in units:
                with ExitStack() as us:
                    ntu = len(unit["tiles"]) * 128
                    oT = {}
                    for m in ("C", "B", "A"):
                        oT[m] = sb("o%sT" % m.lower(), [128, 4, ntu], BF16, us)
                        if m in mixers:
                            K.phase = "mix%s_%s" % (m, unit["name"])
                            {"A": mixer_A, "B": mixer_B, "C": mixer_C}[m](l, unit, oT[m])
                            K.phase = "other"
                        else:
                            P.memset("pool", oT[m][:], 0.0)
                        nm = "o%sT_%s%d" % (m.lower(), unit["name"], l)
                        if nm in dbg_out:
                            dump_bf16(nm, oT[m][:], ntu)
                    if "merge" in parts:
                        K.phase = "merge_%s" % unit["name"]
                        merge(l, unit, [oT["A"], oT["B"], oT["C"]])
                        K.phase = "other"
                    P.fence()
            if "ffn" in parts:
                ffn(l)
        if "final" in parts:
            final_norm()
        P.finish()
        K.stats = (P.n_inst, P.n_wait)
    return nc, K


def _rope_feat(R, pos_row, pos_col):
    q = R // 4
    inv = np.power(10000.0, -np.arange(q, dtype=np.float32) / q).astype(np.float32)
    n = len(pos_row)
    C = np.zeros((R, n), np.float32)
    S = np.zeros((R, n), np.float32)
    for r in range(R):
        blk, i = r // q, r % q
        pos = pos_row if blk < 2 else pos_col
        ang = pos.astype(np.float32) * inv[i]
        C[r] = np.cos(ang)
        S[r] = (-1.0 if blk % 2 == 0 else 1.0) * np.sin(ang)
    return C, S


def _swap_perm(R):
    q = R // 4
    return np.array([r + q if (r // q) % 2 == 0 else r - q for r in range(R)])


def make_shared(inp, NTS, NSP):
    f = lambda a: np.ascontiguousarray(np.asarray(a, dtype=np.float32))
    depth = inp["w_in"].shape[0]
    NTP = NSP * 2
    NTOK = (NTP + NTS) * 128
    sh = {}
    for k in ("w_ada", "b_ada", "w_ff1", "w_ff2", "w_in", "b_gkv", "b_w_ukv", "a_sink",
              "w_branch", "w_out", "c_onorm_g"):
        if k in inp:
            sh[k] = f(inp[k])
    sh["b_adaT"] = f(np.asarray(inp["b_ada"]).reshape(depth, 48, 128).transpose(0, 2, 1))
    sh["norm1T"] = f(np.asarray(inp["norm1_g"]).reshape(depth, 8, 128).transpose(0, 2, 1))
    sh["norm2T"] = f(np.asarray(inp["norm2_g"]).reshape(depth, 8, 128).transpose(0, 2, 1))
    sh["final_g"] = f(np.asarray(inp["final_g"]).reshape(1, D))
    sh["b_gqT"] = f(np.asarray(inp["b_gq"]).reshape(depth, 3, 128).transpose(0, 2, 1))
    w_in = np.asarray(inp["w_in"])
    p64, p32 = _swap_perm(64), _swap_perm(32)
    cols = []
    for h in range(8):
        cols += list(OFF_AQ + h * 64 + p64)
    for h in range(2):
        cols += list(OFF_AK + h * 64 + p64)
    cols += list(OFF_BKR + p32)
    sh["w_in_sw"] = f(w_in[:, :, np.array(cols)])
    wuq_ = np.asarray(inp["b_w_uq"], dtype=np.float32).reshape(depth, 384, 8, 96)
    zpad = np.zeros((depth, 384, 8, 32), np.float32)
    sh["b_w_uq_r"] = f(np.concatenate([zpad, wuq_[..., 64:96], wuq_[..., 0:64]], -1).reshape(depth, 384, 1024))
    sh["b_w_uq_rs"] = f(np.concatenate([zpad, wuq_[..., 64 + p32], wuq_[..., 0:64]], -1).reshape(depth, 384, 1024))
    ts = np.arange(NTS * 128)
    row, col = ts // 64, ts % 64
    CA, SA = _rope_feat(64, row, col)
    CB, SB = _rope_feat(32, row, col)
    npad = NTP * 128
    padC = lambda C: np.concatenate([np.ones((C.shape[0], npad), np.float32), C], axis=1)
    padS = lambda S: np.concatenate([np.zeros((S.shape[0], npad), np.float32), S], axis=1)
    sh["ropeA_C"], sh["ropeA_S"] = f(padC(CA)), f(padS(SA))
    sh["ropeQB_C"] = f(np.concatenate([np.ones((32, NTOK), np.float32), padC(CB), np.ones((64, NTOK), np.float32)], axis=0))
    sh["ropeQB_S"] = f(np.concatenate([np.zeros((32, NTOK), np.float32), padS(SB), np.zeros((64, NTOK), np.float32)], axis=0))
    sh["ropeKB_C"], sh["ropeKB_S"] = f(padC(CB).T), f(padS(SB).T)
    sh["cst_ident"] = np.eye(128, dtype=np.float32)
    jj = np.arange(128)[:, None]
    ii = np.arange(128)[None, :]
    sh["cst_tri"] = f(np.stack([(jj < ii), (jj <= ii), (jj > ii), (jj >= ii)], axis=1))
    lv = []
    for k in range(1, 8):
        lv.append(((jj >> k) == (ii >> k)) & ((jj >> (k - 1)) != (ii >> (k - 1))) & (jj < ii))
    tri4 = [(jj < ii), (jj <= ii), (jj > ii), (jj >= ii)]
    sh["cst_trc"] = f(np.stack([np.stack([tri4[0], tri4[2]], 1), np.stack([tri4[1], tri4[3]], 1)], 1))
    sh["cst_lvN"] = f(np.stack([np.stack([m, m.T], 1) for m in lv], 1))
    sh["cst_lvA"] = f(np.stack([np.stack([m.T, m], 1) for m in lv], 1))
    sh["c_convT"] = f(np.asarray(inp["c_conv_w"]).reshape(depth, 3, 12, 128).transpose(0, 3, 2, 1))
    sh["c_a_log"] = f(np.asarray(inp["c_a_log"]).reshape(depth, 8))
    sh["c_dt_bias"] = f(np.asarray(inp["c_dt_bias"]).reshape(depth, 8))
    sel = np.zeros((2, 2, 128), np.float32)
    sel[0, 0] = 1
    sel[1, 1] = 1
    sh["cst_sel"] = sel
    b_ = np.arange(128)[:, None]
    a_ = np.arange(128)[None, :]
    mprev = np.where(a_ > b_, -30000.0, 0.0).astype(np.float32)
    mnext = np.where(b_ > a_, -30000.0, 0.0).astype(np.float32)
    sh["cst_masks"] = f(np.stack([mprev] * 4 + [mnext] * 4, axis=1))
    return sh


def make_core(inp, prompt_ids, sample_id, NTS):
    f = lambda a: np.ascontiguousarray(np.asarray(a, dtype=np.float32))
    xp = np.asarray(inp["x_prompt"])[prompt_ids].reshape(-1, D)
    xs = np.asarray(inp["x_sample"])[sample_id].reshape(-1, D)
    cond = np.stack([np.asarray(inp["c_ctx"]), np.asarray(inp["c"])[sample_id]], 0)
    c = {"xin": f(np.concatenate([xp, xs], 0)),
         "condT": f(cond.reshape(2, 8, 128).transpose(2, 1, 0)),
         "cak": f(np.asarray(inp["cache_a_k"])[sample_id].reshape(-1, 256, 128)),
         "cav": f(np.asarray(inp["cache_a_v"])[sample_id].reshape(-1, 256, 128)),
         "cckv": f(np.asarray(inp["cache_b_ckv"])[sample_id]),
         "ckr": f(np.asarray(inp["cache_b_krope"])[sample_id]),
         "cstate": f(np.asarray(inp["state_c"])[sample_id])}
    return c


_NTS, _NSP = 16, 2
_CACHE = {}


def kernel(**inputs):
    if "nc" not in _CACHE:
        _CACHE["nc"] = build(NTS=_NTS, NSP=_NSP, depth=DEPTH, parts=("adaln", "merge", "ffn", "final"))
    nc, K = _CACHE["nc"]
    names = [a.memorylocations[0].name for a in nc.m.functions[0].allocations
             if isinstance(a, mybir.MemoryLocationSet) and a.kind == "ExternalInput"]
    shared = make_shared(inputs, _NTS, _NSP)
    in_maps = []
    for core in range(8):
        d = dict(shared)
        d.update(make_core(inputs, [2 * core, 2 * core + 1], core // 2, _NTS))
        in_maps.append({k: d[k] for k in names if k in d})
    res = run_bass_kernel_spmd(nc, in_maps, core_ids=list(range(8))).results
    npt = _NSP * 256
    y_prompt = np.concatenate([res[c]["y"][:npt].reshape(_NSP, 256, D) for c in range(8)], 0)
    y_sample = np.stack([res[2 * b]["y"][npt:] for b in range(4)], 0)
    cat = lambda k: np.concatenate([res[c][k] for c in range(8)], 0)
    new_a_k = cat("nak").reshape(16, DEPTH, 256, 2, 64)
    new_a_v = cat("nav").reshape(16, DEPTH, 256, 2, 64)
    new_b_ckv = cat("nckv")
    new_b_krope = cat("nkr")
    new_c_state = cat("ncs")
    f = lambda a: np.ascontiguousarray(a, dtype=np.float32)
    return (f(y_prompt), f(y_sample), f(new_a_k), f(new_a_v), f(new_b_ckv), f(new_b_krope), f(new_c_state))
```

```python
import bisect
from contextlib import ExitStack

import numpy as np
import concourse.bass as bass
import concourse.mybir as mybir
from concourse.bass_utils import run_bass_kernel_spmd

F32 = mybir.dt.float32
BF16 = mybir.dt.bfloat16
AF = mybir.ActivationFunctionType
ALU = mybir.AluOpType
AX = mybir.AxisListType

COMPUTE = ("pe", "act", "dve", "pool")
NDS = 48
NDS_SP = 32


def _region(ap):
    t = ap.tensor
    dims = [(int(s), int(c)) for s, c in ap.ap]
    off = int(ap.offset)
    if type(t).__name__.startswith("DRam"):
        ext = sum((c - 1) * abs(s) for s, c in dims)
        return (t.name, 0, 1, off, off + ext + 1)
    rows = 1
    for d in list(t.shape)[1:]:
        rows *= int(d)
    p0 = off // rows
    f0 = off % rows
    pc = dims[0][1] if dims[0][0] != 0 else 1
    ext = sum((c - 1) * abs(s) for s, c in dims[1:])
    f1 = f0 + ext + 1
    if type(t).__name__.startswith("PSum"):
        be = 1024 if t.dtype == BF16 else 512
        f0 = (f0 // be) * be
        f1 = ((f1 + be - 1) // be) * be
        p0, pc = 0, 128
    return (t.name, p0, p0 + pc, f0, f1)


def _overlap(a, b):
    return a[1] < b[2] and b[1] < a[2] and a[3] < b[4] and b[3] < a[4]


def _contains(a, b):
    return a[1] <= b[1] and b[2] <= a[2] and a[3] <= b[3] and b[4] <= a[4]


class Op:
    __slots__ = ("eng", "fn", "deps", "signal", "is_dma", "sem", "semval", "seq", "idx")

    def __init__(self, eng, fn, idx, is_dma=False):
        self.eng = eng
        self.fn = fn
        self.deps = []
        self.signal = False
        self.is_dma = is_dma
        self.sem = None
        self.semval = 0
        self.seq = None
        self.idx = idx


class Prog:
    def __init__(self, nc, stack):
        self.nc = nc
        self.engs = {"pe": nc.tensor, "act": nc.scalar, "dve": nc.vector, "pool": nc.gpsimd, "sp": nc.sync}
        self.esem = {e: stack.enter_context(nc.semaphore("es_" + e)) for e in COMPUTE}
        self.dsems = [stack.enter_context(nc.semaphore("ds%d" % i)) for i in range(NDS)]
        self.dcount = [0] * NDS
        self.dlast = [None] * NDS
        self.dnext = {"sp": 0, "pool": 0}
        self.pending = []
        self.state = {}
        self.sigcount = {e: 0 for e in COMPUTE}
        self.sigidx = {e: [] for e in COMPUTE}
        self.sigseq = {e: [] for e in COMPUTE}
        self.waited = {f: {e: 0 for e in COMPUTE} for f in self.engs}
        self.waited_dma = {f: {} for f in self.engs}
        self.last_op = {}
        self.open_dmas = []
        self.nops = 0
        self.n_inst = 0
        self.n_wait = 0

    def add(self, eng, fn, reads=(), writes=(), dma=False):
        op = Op(eng, fn, self.nops, dma)
        self.nops += 1
        deps = {}
        rregs = [_region(a) for a in reads]
        wregs = [_region(a) for a in writes]
        for r in rregs:
            is_ps = r[0].startswith("pb")
            for (reg, o, isw) in self.state.get(r[0], ()):
                if _overlap(reg, r) and (isw or (is_ps and o.eng != eng)):
                    deps[o.idx] = o
        for w in wregs:
            for (reg, o, isw) in self.state.get(w[0], ()):
                if _overlap(reg, w):
                    if (not dma) and (not o.is_dma) and o.eng == eng and eng == "pe":
                        continue
                    deps[o.idx] = o
        if dma:
            if eng == "sp":
                s = self.dnext["sp"]
                self.dnext["sp"] = (s + 1) % NDS_SP
            else:
                s = NDS_SP + self.dnext["pool"]
                self.dnext["pool"] = (self.dnext["pool"] + 1) % (NDS - NDS_SP)
            if self.dlast[s] is not None:
                deps[self.dlast[s].idx] = self.dlast[s]
            self.dlast[s] = op
            self.dcount[s] += 1
            op.sem = s
            op.semval = 16 * self.dcount[s]
            self.open_dmas.append(op)
        for o in deps.values():
            if not o.is_dma:
                o.signal = True
        op.deps = list(deps.values())
        for w in wregs:
            lst = self.state.setdefault(w[0], [])
            lst[:] = [rec for rec in lst if not _contains(w, rec[0])]
            lst.append((w, op, True))
        for r in rregs:
            lst = self.state.setdefault(r[0], [])
            if not dma:
                lst[:] = [rec for rec in lst
                          if not ((not rec[2]) and (not rec[1].is_dma) and rec[1].eng == eng and _contains(r, rec[0]))]
            lst.append((r, op, False))
        self.pending.append(op)
        if not dma:
            self.last_op[eng] = op
        return op

    def fence(self):
        lasts = dict(self.last_op)
        dmas = list(self.open_dmas)
        self.open_dmas = []
        for f in self.engs:
            op = Op(f, None, self.nops, False)
            self.nops += 1
            for e, o in lasts.items():
                o.signal = True
                op.deps.append(o)
            op.deps.extend(dmas)
            self.pending.append(op)
        self.state = {}
        self.flush()

    def _seq_of(self, dep):
        if dep.seq is not None:
            return dep.seq
        e = dep.eng
        i = bisect.bisect_left(self.sigidx[e], dep.idx)
        return self.sigseq[e][i]

    def flush(self):
        lastp = {}
        for op in self.pending:
            if (not op.is_dma) and op.fn is not None and op.eng in COMPUTE:
                lastp[op.eng] = op
        for op in lastp.values():
            op.signal = True
        for op in self.pending:
            f = op.eng
            eng = self.engs[f]
            waits = []
            for dep in op.deps:
                if dep.is_dma:
                    if self.waited_dma[f].get(dep.sem, 0) >= dep.semval:
                        continue
                    waits.append((self.dsems[dep.sem], dep.semval))
                    self.waited_dma[f][dep.sem] = dep.semval
                else:
                    e = dep.eng
                    seq = self._seq_of(dep)
                    if self.waited[f][e] >= seq:
                        continue
                    waits.append((self.esem[e], seq))
                    self.waited[f][e] = seq
            self.n_wait += len(waits)
            embed = None
            if op.fn is not None and (not op.is_dma) and waits:
                embed = waits.pop()
            for (sem_, val_) in waits:
                eng.wait_ge(sem_, val_)
            if op.fn is None:
                continue
            ins = op.fn(eng)
            if embed is not None:
                ins._wait_ge(embed[0], embed[1])
            self.n_inst += 1
            if op.is_dma:
                ins.then_inc(self.dsems[op.sem], 16)
            elif op.signal:
                self.sigcount[f] += 1
                op.seq = self.sigcount[f]
                ins.then_inc(self.esem[f], 1)
                self.sigidx[f].append(op.idx)
                self.sigseq[f].append(op.seq)
                self.waited[f][f] = max(self.waited[f][f], 0)
        self.pending = []

    def finish(self):
        self.fence()

    def mm(self, out, lhsT, rhs, start=True, stop=True):
        return self.add("pe", lambda e: e.matmul(out, lhsT, rhs, start=start, stop=stop),
                        reads=[lhsT, rhs], writes=[out])

    def tr(self, out, in_, ident):
        return self.add("pe", lambda e: e.transpose(out, in_, ident), reads=[in_, ident], writes=[out])

    def act(self, out, in_, func, bias=None, scale=1.0, accum_out=None, eng="act"):
        reads = [in_]
        kw = {}
        if bias is not None:
            kw["bias"] = bias
            if not isinstance(bias, (int, float)):
                reads.append(bias)
        if not isinstance(scale, (int, float)):
            reads.append(scale)
        kw["scale"] = scale
        writes = [out]
        if accum_out is not None:
            kw["accum_out"] = accum_out
            writes.append(accum_out)
        return self.add("act", lambda e: e.activation(out, in_, func, **kw), reads=reads, writes=writes)

    def tt(self, eng, out, in0, in1, op):
        return self.add(eng, lambda e: e.tensor_tensor(out, in0, in1, op), reads=[in0, in1], writes=[out])

    def ts(self, eng, out, in0, s1, s2=None, op0=ALU.mult, op1=None, accum_out=None):
        reads = [in0]
        for s in (s1, s2):
            if s is not None and not isinstance(s, (int, float)):
                reads.append(s)
        writes = [out]
        kw = {}
        if op1 is not None:
            kw["op1"] = op1
        if accum_out is not None:
            kw["accum_out"] = accum_out
            writes.append(accum_out)
        return self.add(eng, lambda e: e.tensor_scalar(out, in0, s1, s2, op0, **kw), reads=reads, writes=writes)

    def stt(self, out, in0, scalar, in1, op0, op1, eng="dve"):
        reads = [in0, in1]
        if not isinstance(scalar, (int, float)):
            reads.append(scalar)
        return self.add(eng, lambda e: e.scalar_tensor_tensor(out, in0, scalar, in1, op0, op1),
                        reads=reads, writes=[out])

    def copy(self, eng, out, in_):
        if eng == "act":
            return self.add("act", lambda e: e.copy(out, in_), reads=[in_], writes=[out])
        return self.add(eng, lambda e: e.tensor_copy(out, in_), reads=[in_], writes=[out])

    def memset(self, eng, out, val):
        return self.add(eng, lambda e: e.memset(out, val), reads=[], writes=[out])

    def recip(self, out, in_):
        return self.add("dve", lambda e: e.reciprocal(out, in_), reads=[in_], writes=[out])

    def reduce(self, out, in_, op, axis=AX.X, eng="dve"):
        return self.add(eng, lambda e: e.tensor_reduce(out, in_, axis, op), reads=[in_], writes=[out])

    def dma(self, out, in_, q="sp"):
        return self.add(q, lambda e: e.dma_start(out=out, in_=in_), reads=[in_], writes=[out], dma=True)


D = 1024
DEPTH = 2
D_IN = 6576
OFF_AQ, OFF_AK, OFF_AV = 0, 512, 640
OFF_BCQ, OFF_BCKV, OFF_BKR = 768, 1152, 1408
OFF_CQKV, OFF_CA, OFF_CB, OFF_CZ, OFF_GATES = 1440, 2976, 2984, 2992, 3504
EPS = 1e-6


class Ctx:
    pass


def build(NTS=16, NSP=2, depth=DEPTH, mixers=("B", "A", "C"), dbg=(), parts=("adaln", "ffn", "final")):
    NTP = NSP * 2
    NT = NTP + NTS
    NTOK = NT * 128
    nc = bass.Bass("TRN2", target_bir_lowering=False)
    K = Ctx()

    def din(name, shape):
        return nc.dram_tensor(name, list(shape), F32, kind="ExternalInput").ap()

    def dout(name, shape):
        return nc.dram_tensor(name, list(shape), F32, kind="ExternalOutput").ap()

    xin = din("xin", [NTOK, D])
    condT = din("condT", [128, 8, 2])
    w_ada = din("w_ada", [depth, D, 6 * D])
    b_adaT = din("b_adaT", [depth, 128, 48])
    b_ada = din("b_ada", [depth, 6 * D])
    norm1T = din("norm1T", [depth, 128, 8])
    norm2T = din("norm2T", [depth, 128, 8])
    final_g = din("final_g", [1, D])
    w_ff1 = din("w_ff1", [depth, D, 4 * D])
    w_ff2 = din("w_ff2", [depth, 4 * D, D])
    cst_ident = din("cst_ident", [128, 128])
    w_branch = din("w_branch", [depth, 3, 512, D])
    w_out = din("w_out", [depth, D, D])
    w_in = din("w_in", [depth, D, D_IN])
    w_in_sw = din("w_in_sw", [depth, D, 672])
    b_w_uq_r = din("b_w_uq_r", [depth, 384, 1024])
    b_w_uq_rs = din("b_w_uq_rs", [depth, 384, 1024])
    b_gqT = din("b_gqT", [depth, 128, 3])
    b_gkv = din("b_gkv", [depth, 256])
    b_w_ukv = din("b_w_ukv", [depth, 256, 1024])
    a_sink = din("a_sink", [depth, 8])
    ropeQB_C = din("ropeQB_C", [128, NTOK])
    ropeQB_S = din("ropeQB_S", [128, NTOK])
    ropeKB_C = din("ropeKB_C", [NTOK, 32])
    ropeKB_S = din("ropeKB_S", [NTOK, 32])
    ropeA_C = din("ropeA_C", [64, NTOK])
    ropeA_S = din("ropeA_S", [64, NTOK])
    cak = din("cak", [depth, 256, 128])
    cav = din("cav", [depth, 256, 128])
    cckv = din("cckv", [depth, 256, 256])
    ckr = din("ckr", [depth, 256, 32])
    cstate = din("cstate", [depth, 2, 4, 128, 128])
    cst_masks = din("cst_masks", [128, 8, 128])
    cst_tri = din("cst_tri", [128, 4, 128])
    cst_trc = din("cst_trc", [128, 2, 2, 128])
    cst_lvN = din("cst_lvN", [128, 7, 2, 128])
    cst_lvA = din("cst_lvA", [128, 7, 2, 128])
    c_convT = din("c_convT", [depth, 128, 12, 3])
    c_a_log = din("c_a_log", [depth, 8])
    c_dt_bias = din("c_dt_bias", [depth, 8])
    c_onorm_g = din("c_onorm_g", [depth, 128])
    nak_out = dout("nak", [NSP, depth, 256, 128])
    nav_out = dout("nav", [NSP, depth, 256, 128])
    nckv_out = dout("nckv", [NSP, depth, 256, 256])
    nkr_out = dout("nkr", [NSP, depth, 256, 32])
    ncs_out = dout("ncs", [NSP, depth, 2, 4, 128, 128])
    y_out = dout("y", [NTOK, D])
    dbg_out = {}
    for name, shape in dbg:
        dbg_out[name] = dout(name, shape)

    with ExitStack() as st:
        P = Prog(nc, st)

        K.uid = 0

        K.phase = "init"
        K.minrem = {}

        def sb(name, shape, dt=F32, stack=st):
            K.uid += 1
            t_ = stack.enter_context(nc.sbuf_tensor("%s_%d" % (name, K.uid), list(shape), dt))
            K.minrem[K.phase] = min(K.minrem.get(K.phase, 1 << 30), nc.sbuf_bytes_remaining)
            return t_

        def psb(name, shape=(128, 512), dt=F32, stack=st):
            K.uid += 1
            return stack.enter_context(nc.psum_tensor("%s_%d" % (name, K.uid), list(shape), dt))

        x = sb("x", [128, NT, D])
        pb = [psb("pb%d" % i) for i in range(7)]
        pbb = psb("pbb", (128, 1024), BF16)
        ident_f = sb("ident_f", [128, 128])
        ident_b = sb("ident_b", [128, 128], BF16)
        condT_sb = sb("condT_sb", [128, 8, 2])
        scT = sb("scT", [128, 8, 2], BF16)
        modT = sb("modT", [128, 48, 2])
        affA = sb("affA", [128, 2, 8, 2])
        nrmT = sb("nrmT", [128, 2, 8])
        ss = sb("ss", [128, NT])
        rstd = sb("rstd", [128, NT])
        junkb = [sb("junk%d" % i, [128, D], BF16) for i in range(2)] * 2
        K.nj = 0
        xnb = [sb("xnb%d" % i, [128, D]) for i in range(2)]

        P.dma(ident_f[:], cst_ident[:, :])
        P.dma(condT_sb[:], condT[:, :, :])
        for t in range(NT):
            P.dma(x[:, t, :], xin[t * 128:(t + 1) * 128, :])
        P.copy("dve", ident_b[:], ident_f[:])
        ones_b = sb("ones_b", [128, 128], BF16)
        ones_f = sb("ones_f", [128, 128])
        P.memset("dve", ones_b[:], 1.0)
        P.memset("dve", ones_f[:], 1.0)
        P.act(scT[:], condT_sb[:], AF.Silu)

        def cond_of(t):
            return 0 if t < NTP else 1

        def calc_rstd(tiles):
            for t in tiles:
                K.nj += 1
                P.act(junkb[K.nj % 2][:], x[:, t, :], AF.Square, accum_out=ss[:, t:t + 1])
            t0, t1 = tiles[0], tiles[-1] + 1
            P.ts("dve", rstd[:, t0:t1], ss[:, t0:t1], 1.0 / D, EPS, op0=ALU.mult, op1=ALU.add)
            P.act(rstd[:, t0:t1], rstd[:, t0:t1], AF.Sqrt)
            P.recip(rstd[:, t0:t1], rstd[:, t0:t1])

        K.ntr = 0
        K.ew = "pool"

        def build_hT(dst, tiles, which):
            for ti, t in enumerate(tiles):
                cd = cond_of(t)
                xn = xnb[K.ntr % 2]
                K.ntr += 1
                P.ts("dve", xn[:], x[:, t, :], rstd[:, t:t + 1], None, op0=ALU.mult)
                for g in range(2):
                    pp = pb[5 + g]
                    for c in range(4):
                        P.tr(pp[:, c * 128:(c + 1) * 128], xn[:, (g * 4 + c) * 128:(g * 4 + c + 1) * 128], ident_f[:])
                    for c in range(4):
                        ch = g * 4 + c
                        a_ap = affA[:, which, ch, cd:cd + 1]
                        b_ap = modT[:, (3 * which) * 8 + ch, cd:cd + 1]
                        o_ap = dst[:, ch, ti * 128:(ti + 1) * 128]
                        if g % 2 == 0:
                            P.act(o_ap, pp[:, c * 128:(c + 1) * 128], AF.Identity, bias=b_ap, scale=a_ap)
                        else:
                            P.ts("dve", o_ap, pp[:, c * 128:(c + 1) * 128], a_ap, b_ap, op0=ALU.mult, op1=ALU.add)

        def make_G(dst, which, cd, ph):
            dg = [sb("dgG%d" % i, [128, 128], F32, ph) for i in range(2)]
            j = 2 + 3 * which
            for c in range(8):
                P.ts("pool", dg[c % 2][:], ident_f[:], modT[:, j * 8 + c, cd:cd + 1], None, op0=ALU.mult)
                pg = pb[c % 2]
                P.mm(pg[:, 0:128], ones_f[:], dg[c % 2][:])
                P.copy("act", dst[:, c * 128:(c + 1) * 128], pg[:, 0:128])

        def adaln(l):
            with ExitStack() as ph:
                wb = [sb("wada%d" % i, [128, 8, D], BF16, ph) for i in range(2)]
                badaT = sb("badaT", [128, 48], F32, ph)
                P.dma(badaT[:], b_adaT[l, :, :])
                P.dma(nrmT[:, 0, :], norm1T[l, :, :])
                P.dma(nrmT[:, 1, :], norm2T[l, :, :])
                for j in range(6):
                    w = wb[j % 2]
                    P.dma(w[:], w_ada[l, :, j * D:(j + 1) * D].rearrange("(c p) n -> p c n", p=128), q="pool")
                    pm = pb[j % 2]
                    for c in range(8):
                        for kc in range(8):
                            P.mm(pm[:, c * 2:c * 2 + 2], w[:, kc, c * 128:(c + 1) * 128], scT[:, kc, :],
                                 start=(kc == 0), stop=(kc == 7))
                    P.tt("dve", modT[:, j * 8:(j + 1) * 8, :], pm[:, 0:16].rearrange("p (c k) -> p c k", k=2),
                         badaT[:, j * 8:(j + 1) * 8].unsqueeze(2).broadcast_to([128, 8, 2]), ALU.add)
                for which in range(2):
                    j = 1 + 3 * which
                    P.ts("dve", affA[:, which, :, :], modT[:, j * 8:(j + 1) * 8, :], 1.0, None, op0=ALU.add)
                    P.tt("dve", affA[:, which, :, :], affA[:, which, :, :],
                         nrmT[:, which, :].unsqueeze(2).broadcast_to([128, 8, 2]), ALU.mult)
                P.fence()

        def ffn(l):
            with ExitStack() as ph:
                h2T = sb("h2T", [128, 8, NTOK], BF16, ph)
                w1b = [sb("w1b%d" % i, [128, 8, 512], BF16, ph) for i in range(2)]
                w2b = [sb("w2b%d" % i, [128, 4, D], BF16, ph) for i in range(2)]
                aTb = [sb("aTb%d" % i, [128, 4, 256], BF16, ph) for i in range(2)]
                rfb = [sb("rfb%d" % i, [128, 256], F32, ph) for i in range(2)]
                tmpb = [sb("tmpb%d" % i, [128, 512], F32, ph) for i in range(2)]
                G2 = sb("G2", [128, 2, D], F32, ph)
                for cd_ in range(2):
                    make_G(G2[:, cd_, :], 1, cd_, ph)

                def load_w(g):
                    P.dma(w1b[g % 2][:], w_ff1[l, :, g * 512:(g + 1) * 512].rearrange("(c p) n -> p c n", p=128), q="pool")
                    P.dma(w2b[g % 2][:], w_ff2[l, g * 512:(g + 1) * 512, :].rearrange("(c p) n -> p c n", p=128), q="pool")

                load_w(0)
                calc_rstd(list(range(NT)))
                build_hT(h2T, list(range(NT)), 1)
                K.cnt = 0
                NG = 8
                items = [(g, blk) for g in range(NG) for blk in range(NT // 2)]

                def ff1(g, blk):
                    w1 = w1b[g % 2]
                    aT = aTb[blk % 2]
                    for fc in range(4):
                        pa = pb[fc % 2]
                        for kc in range(8):
                            P.mm(pa[:, 0:256], w1[:, kc, fc * 128:(fc + 1) * 128],
                                 h2T[:, kc, blk * 256:(blk + 1) * 256], start=(kc == 0), stop=(kc == 7))
                        rf = rfb[fc % 2]
                        P.act(rf[:], pa[:, 0:256], AF.Relu)
                        P.tt("dve" if fc % 2 == 0 else "pool", aT[:, fc, :], rf[:], rf[:], ALU.mult)

                def ff2(g, blk):
                    w2 = w2b[g % 2]
                    aT = aTb[blk % 2]
                    for tt_ in range(2):
                        t = blk * 2 + tt_
                        cd = cond_of(t)
                        for half in range(2):
                            py = pb[2 + tt_ * 2 + half]
                            for fc in range(4):
                                P.mm(py[:], aT[:, fc, tt_ * 128:(tt_ + 1) * 128],
                                     w2[:, fc, half * 512:(half + 1) * 512], start=(fc == 0), stop=(fc == 3))
                            if cd == 0:
                                tmp = tmpb[K.cnt % 2]
                                K.cnt += 1
                                P.tt("dve", tmp[:], py[:], G2[:, cd, half * 512:(half + 1) * 512], ALU.mult)
                                P.tt(K.ew, x[:, t, half * 512:(half + 1) * 512],
                                     x[:, t, half * 512:(half + 1) * 512], tmp[:], ALU.add)
                            else:
                                P.tt("dve", x[:, t, half * 512:(half + 1) * 512],
                                     x[:, t, half * 512:(half + 1) * 512], py[:], ALU.add)
                    if blk == NTP // 2 - 1:
                        P.tt("pool", w2[:], w2[:], G2[:, 1, :].unsqueeze(1).broadcast_to([128, 4, D]), ALU.mult)

                ff1(*items[0])
                for n, (g, blk) in enumerate(items):
                    if blk == 0 and g + 1 < NG:
                        load_w(g + 1)
                    if n + 1 < len(items):
                        ff1(*items[n + 1])
                    ff2(g, blk)
                P.fence()

        def final_norm():
            with ExitStack() as ph:
                gf = sb("gf", [128, D], F32, ph)
                ob = [sb("ob%d" % i, [128, D], F32, ph) for i in range(2)]
                P.dma(gf[:], final_g.partition_broadcast(128) if False else final_g[0:1, :].broadcast_to([128, D]))
                calc_rstd(list(range(NT)))
                for t in range(NT):
                    o = ob[t % 2]
                    P.stt(o[:], x[:, t, :], rstd[:, t:t + 1], gf[:], ALU.mult, ALU.mult)
                    P.dma(y_out[t * 128:(t + 1) * 128, :], o[:])
                P.fence()


        units = [
            dict(name="P", tiles=list(range(0, NTP)), cond=0, prompt=True,
                 seqs=[dict(t0=2 * i, nt=2, ctx=False, idx=i) for i in range(NSP)]),
            dict(name="S", tiles=list(range(NTP, NT)), cond=1, prompt=False,
                 seqs=[dict(t0=NTP, nt=NTS, ctx=True, idx=0)]),
        ]
        K.rr = 0

        def blocks_of(unit, bs=512):
            out = []
            for sq in unit["seqs"]:
                n = sq["nt"] * 128
                o = 0
                while o < n:
                    b = min(bs, n - o)
                    out.append((sq, sq["t0"] * 128 + o, b))
                    o += b
            return out

        def attend(KT, V1, QT, nq, negm, key_tiles, obanks, sbanks, PTb, scale, tail=None):
            nk = len(key_tiles)
            slots = []

            def scores(i):
                kt, mask = key_tiles[i]
                sp_ = sbanks[K.rr % len(sbanks)]
                pt_ = PTb[K.rr % len(PTb)]
                K.rr += 1
                P.mm(sp_[:, 0:nq], KT(kt), QT, start=True, stop=(mask is None and negm is None))
                if mask is not None:
                    P.mm(sp_[:, 0:nq], ident_b[:], mask, start=False, stop=(negm is None))
                if negm is not None:
                    P.mm(sp_[:, 0:nq], ones_b[0:1, 0:128], negm, start=False, stop=True)
                slots.append((sp_, pt_))

            scores(0)
            for i, (kt, mask) in enumerate(key_tiles):
                if i + 1 < nk:
                    scores(i + 1)
                sp_, pt_ = slots[i]
                P.act(pt_[:, 0:nq], sp_[:, 0:nq], AF.Exp, scale=scale)
                for qi, ob in enumerate(obanks):
                    last = (i == nk - 1) and tail is None
                    P.mm(ob, pt_[:, qi * 128:(qi + 1) * 128], V1(kt), start=(i == 0), stop=last)
            if tail is not None:
                tail()

        def mixer_B(l, unit, obT):
            with ExitStack() as ph:
                tiles = unit["tiles"]
                ntok_u = len(tiles) * 128
                tok_base = tiles[0] * 128
                has_ctx = any(sq["ctx"] for sq in unit["seqs"])
                maxkeys = max(sq["nt"] * 128 + (256 if sq["ctx"] else 0) for sq in unit["seqs"])
                maxnt = max(sq["nt"] for sq in unit["seqs"])
                nkeys_u = ntok_u + (256 if has_ctx else 0)
                cqnT = sb("cqnT", [128, 3, ntok_u], BF16, ph)
                ckvT = sb("ckvT", [128, 2, nkeys_u], BF16, ph)
                krT = sb("krT", [32, nkeys_u], BF16, ph)
                scale = 96.0 ** -0.5
                b1 = ExitStack()
                wB = sb("wB", [128, 8, 704], BF16, b1)
                gkvb = sb("gkvb", [128, 256], F32, b1)
                P.dma(wB[:, :, 0:672], w_in[l, :, OFF_BCQ:OFF_BCQ + 672].rearrange("(c p) n -> p c n", p=128), q="pool")
                P.dma(wB[:, :, 672:704], w_in_sw[l, :, 640:672].rearrange("(c p) n -> p c n", p=128), q="pool")
                P.dma(gkvb[:], b_gkv[l:l + 1, :].broadcast_to([128, 256]))
                hTb = [sb("hTbB%d" % i, [128, 8, 256], BF16, b1) for i in range(2)]
                ssb = [sb("ssB%d" % i, [128, 8], F32, b1) for i in range(2)]
                jk = [sb("jkB%d" % i, [128, 384], F32, b1) for i in range(2)]
                cqn_b = [sb("cqnb%d" % i, [128, 384], BF16, b1) for i in range(2)]
                ckvn_f = [sb("ckvnf%d" % i, [128, 256], F32, b1) for i in range(2)]
                ckvn_b = [sb("ckvnb%d" % i, [128, 256], BF16, b1) for i in range(2)]
                kr_f = [sb("krf%d" % i, [128, 64], F32, b1) for i in range(2)]
                kr_t = [sb("krt%d" % i, [128, 64], F32, b1) for i in range(2)]
                kr_b = [sb("krb%d" % i, [128, 32], BF16, b1) for i in range(2)]
                rKC = [sb("rKC%d" % i, [128, 32], F32, b1) for i in range(2)]
                rKS = [sb("rKS%d" % i, [128, 32], F32, b1) for i in range(2)]

                n_t = 0
                for bi, (sq, tok0, nb) in enumerate(blocks_of(unit, 256)):
                    hT = hTb[bi % 2]
                    btiles = [tok0 // 128 + i for i in range(nb // 128)]
                    build_hT(hT, btiles, 0)
                    for ti, t in enumerate(btiles):
                        r = n_t % 2
                        n_t += 1
                        sv = ssb[r]
                        p1, p2 = pb[0], pb[1]
                        for kc in range(8):
                            P.mm(p1[:, 0:384], hT[:, kc, ti * 128:(ti + 1) * 128], wB[:, kc, 0:384], start=(kc == 0), stop=(kc == 7))
                        for kc in range(8):
                            P.mm(p2[:, 0:320], hT[:, kc, ti * 128:(ti + 1) * 128], wB[:, kc, 384:704], start=(kc == 0), stop=(kc == 7))
                        P.act(jk[r][:, 0:384], p1[:, 0:384], AF.Square, accum_out=sv[:, 0:1])
                        P.ts("dve", sv[:, 1:2], sv[:, 0:1], 1.0 / 384, EPS, op0=ALU.mult, op1=ALU.add)
                        P.act(sv[:, 2:3], sv[:, 1:2], AF.Sqrt)
                        P.recip(sv[:, 3:4], sv[:, 2:3])
                        P.act(cqn_b[r][:], p1[:, 0:384], AF.Copy, scale=sv[:, 3:4])
                        P.act(jk[r][:, 0:256], p2[:, 0:256], AF.Square, accum_out=sv[:, 4:5])
                        P.ts("dve", sv[:, 5:6], sv[:, 4:5], 1.0 / 256, EPS, op0=ALU.mult, op1=ALU.add)
                        P.act(sv[:, 6:7], sv[:, 5:6], AF.Sqrt)
                        P.recip(sv[:, 7:8], sv[:, 6:7])
                        P.act(kr_f[r][:], p2[:, 256:320], AF.Copy)
                        P.act(ckvn_f[r][:], p2[:, 0:256], AF.Copy, scale=sv[:, 7:8])
                        P.tt("dve", ckvn_f[r][:], ckvn_f[r][:], gkvb[:], ALU.mult)
                        P.copy("dve", ckvn_b[r][:], ckvn_f[r][:])
                        lt = t - sq["t0"]
                        if unit["prompt"]:
                            P.dma(nckv_out[sq["idx"], l, lt * 128:(lt + 1) * 128, :], ckvn_f[r][:])
                            P.dma(nkr_out[sq["idx"], l, lt * 128:(lt + 1) * 128, :], kr_f[r][:, 0:32])
                        P.dma(rKC[r][:], ropeKB_C[t * 128:(t + 1) * 128, :])
                        P.dma(rKS[r][:], ropeKB_S[t * 128:(t + 1) * 128, :])
                        P.tt("dve", kr_t[r][:, 0:32], kr_f[r][:, 0:32], rKC[r][:], ALU.mult)
                        P.tt("dve", kr_t[r][:, 32:64], kr_f[r][:, 32:64], rKS[r][:], ALU.mult)
                        P.tt("dve", kr_b[r][:], kr_t[r][:, 0:32], kr_t[r][:, 32:64], ALU.add)
                        for c in range(3):
                            P.tr(pbb[:, c * 128:(c + 1) * 128], cqn_b[r][:, c * 128:(c + 1) * 128], ident_b[:])
                        for c in range(2):
                            P.tr(pbb[:, (3 + c) * 128:(4 + c) * 128], ckvn_b[r][:, c * 128:(c + 1) * 128], ident_b[:])
                        P.tr(pbb[0:32, 640:768], kr_b[r][:], ident_b[:])
                        ut = t - tiles[0]
                        P.copy("dve", cqnT[:, :, ut * 128:(ut + 1) * 128], pbb[:, 0:384].rearrange("p (c n) -> p c n", c=3))
                        P.copy("dve", ckvT[:, :, ut * 128:(ut + 1) * 128], pbb[:, 384:640].rearrange("p (c n) -> p c n", c=2))
                        P.copy("dve", krT[:, ut * 128:(ut + 1) * 128], pbb[0:32, 640:768])

                P.fence()
                b1.close()
                wuq = sb("wuq", [128, 3, 1024], BF16, ph)
                wuqs = sb("wuqs", [128, 3, 1024], BF16, ph)
                wukA = sb("wukA", [128, 2, 8, 128], BF16, ph)
                wukV = sb("wukV", [128, 2, 512], BF16, ph)
                E32 = sb("E32", [32, 128], BF16, ph)
                gqT = sb("gqT", [128, 3], F32, ph)
                P.dma(wuq[:], b_w_uq_r[l].rearrange("(c p) n -> p c n", p=128), q="pool")
                P.dma(wuqs[:], b_w_uq_rs[l].rearrange("(c p) n -> p c n", p=128), q="pool")
                P.dma(gqT[:], b_gqT[l, :, :])
                for c in range(3):
                    P.ts("pool", wuq[:, c, :], wuq[:, c, :], gqT[:, c:c + 1], None, op0=ALU.mult)
                    P.ts("pool", wuqs[:, c, :], wuqs[:, c, :], gqT[:, c:c + 1], None, op0=ALU.mult)
                P.memset("pool", wukA[:], 0.0)
                P.memset("pool", E32[:], 0.0)
                P.copy("pool", E32[:, 32:64], ident_b[0:32, 0:32])
                with ExitStack() as tw:
                    wukv = sb("wukv", [128, 2, 1024], BF16, tw)
                    P.dma(wukv[:], b_w_ukv[l].rearrange("(c p) n -> p c n", p=128), q="pool")
                    wv = wukv[:].rearrange("p c (h e) -> p c h e", e=128)
                    for c in range(2):
                        P.copy("pool", wukA[:, c, :, 64:128], wv[:, c, :, 0:64])
                        P.copy("pool", wukV[:, c, :].rearrange("p (h e) -> p h e", e=64), wv[:, c, :, 64:128])
                    P.fence()
                KTh = sb("KTh", [128, maxkeys], BF16, ph)
                ksq = sb("ksq", [128, 512], BF16, ph)
                V1 = sb("V1", [128, maxkeys // 128, 2, 65], BF16, ph)
                P.memset("pool", V1[:, :, :, 64:65], 1.0)
                rQC = [sb("rQC%d" % i, [128, 512], F32, ph) for i in range(1)] * 2
                rQS = [sb("rQS%d" % i, [128, 512], F32, ph) for i in range(1)] * 2
                qt1 = [sb("qt1_%d" % i, [128, 512], F32, ph) for i in range(1)] * 2
                qt2 = [sb("qt2_%d" % i, [128, 512], F32, ph) for i in range(1)] * 2
                QT = [sb("QT%d" % i, [128, 512], BF16, ph) for i in range(2)]
                qsq = [sb("qsq%d" % i, [128, 512], BF16, ph) for i in range(1)] * 2
                sqr = [sb("sqr%d" % i, [1, 512], F32, ph) for i in range(1)] * 2
                kmx = sb("kmx", [1, 8], F32, ph)
                kmax = sb("kmax", [1, 1], F32, ph)
                PTb = [sb("PTb%d" % i, [128, 512], BF16, ph) for i in range(2)]
                otok = sb("otok", [128, maxnt, 128], BF16, ph)
                rden = [sb("rden%d" % i, [128, 1], F32, ph) for i in range(4)]
                ctx_b = sb("ctx_b", [128, 2, 288], BF16, ph)

                for sq in unit["seqs"]:
                    nk_own = sq["nt"] * 128
                    nkeys = nk_own + (256 if sq["ctx"] else 0)
                    nkt = nkeys // 128
                    kb = sq["t0"] * 128 - tok_base
                    if sq["ctx"]:
                        for c2 in range(2):
                            P.dma(ctx_b[:, c2, 0:256], cckv[l, c2 * 128:(c2 + 1) * 128, :], q="pool")
                            P.dma(ctx_b[:, c2, 256:288], ckr[l, c2 * 128:(c2 + 1) * 128, :], q="pool")
                        for c2 in range(2):
                            for c in range(2):
                                P.tr(pbb[:, c * 128:(c + 1) * 128], ctx_b[:, c2, c * 128:(c + 1) * 128], ident_b[:])
                            P.tr(pbb[0:32, 256:384], ctx_b[:, c2, 256:288], ident_b[:])
                            k0 = kb + nk_own + c2 * 128
                            P.copy("dve", ckvT[:, :, k0:k0 + 128], pbb[:, 0:256].rearrange("p (c n) -> p c n", c=2))
                            P.copy("dve", krT[:, k0:k0 + 128], pbb[0:32, 256:384])
                    qblocks = [(o, min(512, nk_own - o)) for o in range(0, nk_own, 512)]
                    for h in range(8):
                        if h % 2 == 0:
                            for kt in range(nkt):
                                pv = pb[kt % 2]
                                for c in range(2):
                                    P.mm(pv[:, 0:128], ckvT[:, c, kb + kt * 128:kb + (kt + 1) * 128], wukV[:, c, h * 64:(h + 2) * 64], start=(c == 0), stop=(c == 1))
                                P.copy("act", V1[:, kt, :, 0:64], pv[:, 0:128].rearrange("p (h e) -> p h e", e=64))
                        for k0 in range(0, nkeys, 512):
                            kn = min(512, nkeys - k0)
                            pk = pb[4 + (k0 // 512) % 2]
                            for c in range(2):
                                P.mm(pk[:, 0:kn], wukA[:, c, h, :], ckvT[:, c, kb + k0:kb + k0 + kn], start=(c == 0), stop=False)
                            P.mm(pk[:, 0:kn], E32[:], krT[:, kb + k0:kb + k0 + kn], start=False, stop=True)
                            P.copy("act", KTh[:, k0:k0 + kn], pk[:, 0:kn])
                            P.tt("pool", ksq[:, 0:kn], KTh[:, k0:k0 + kn], KTh[:, k0:k0 + kn], ALU.mult)
                            P.memset("dve", KTh[0:1, k0:k0 + kn], 1.0)
                            pn = pb[6]
                            P.mm(pn[0:1, 0:kn], ones_b[:, 0:1], ksq[:, 0:kn])
                            P.reduce(kmx[:, k0 // 512:k0 // 512 + 1], pn[0:1, 0:kn], ALU.max)
                        nkb = (nkeys + 511) // 512
                        P.reduce(kmax[:], kmx[:, 0:nkb], ALU.max)
                        P.act(kmax[:], kmax[:], AF.Sqrt)
                        P.ts("dve", kmax[:], kmax[:], -1.0, None, op0=ALU.mult)
                        def prepQ(qi, h=h):
                            qo, nq = qblocks[qi]
                            r = (h * len(qblocks) + qi) % 2
                            g0 = sq["t0"] * 128 + qo
                            u0 = g0 - tok_base
                            P.dma(rQC[r][:, 0:nq], ropeQB_C[:, g0:g0 + nq])
                            P.dma(rQS[r][:, 0:nq], ropeQB_S[:, g0:g0 + nq])
                            pq = pb[6]
                            for c in range(3):
                                P.mm(pq[:, 0:nq], wuq[:, c, h * 128:(h + 1) * 128], cqnT[:, c, u0:u0 + nq], start=(c == 0), stop=(c == 2))
                            P.tt("dve", qt1[r][:, 0:nq], pq[:, 0:nq], rQC[r][:, 0:nq], ALU.mult)
                            for c in range(3):
                                P.mm(pq[:, 0:nq], wuqs[:, c, h * 128:(h + 1) * 128], cqnT[:, c, u0:u0 + nq], start=(c == 0), stop=(c == 2))
                            P.tt("dve", qt2[r][:, 0:nq], pq[:, 0:nq], rQS[r][:, 0:nq], ALU.mult)
                            P.tt("pool", QT[r][:, 0:nq], qt1[r][:, 0:nq], qt2[r][:, 0:nq], ALU.add)
                            P.tt("pool", qsq[r][:, 0:nq], QT[r][:, 0:nq], QT[r][:, 0:nq], ALU.mult)
                            P.mm(pq[0:1, 0:nq], ones_b[:, 0:1], qsq[r][:, 0:nq])
                            P.act(sqr[r][:, 0:nq], pq[0:1, 0:nq], AF.Sqrt)
                            P.ts("dve", QT[r][0:1, 0:nq], sqr[r][:, 0:nq], kmax[0:1, 0:1], None, op0=ALU.mult)

                        prepQ(0)
                        for qi, (qo, nq) in enumerate(qblocks):
                            r = (h * len(qblocks) + qi) % 2
                            if qi + 1 < len(qblocks):
                                prepQ(qi + 1)
                            nqt = nq // 128
                            obanks = [pb[qq][:, 0:65] for qq in range(nqt)]
                            attend(lambda kt: KTh[:, kt * 128:(kt + 1) * 128], lambda kt: V1[:, kt, h % 2, :],
                                   QT[r][:, 0:nq], nq, None, [(kt, None) for kt in range(nkt)],
                                   obanks, [pb[4], pb[5]], PTb, scale)
                            for qq in range(nqt):
                                lt = qo // 128 + qq
                                P.recip(rden[qq][:], pb[qq][:, 64:65])
                                P.ts("dve", otok[:, lt, (h % 2) * 64:(h % 2 + 1) * 64], pb[qq][:, 0:64],
                                     rden[qq][:, 0:1], None, op0=ALU.mult)
                        if h % 2 == 1:
                            for lt0 in range(0, sq["nt"], 8):
                                n8 = min(8, sq["nt"] - lt0)
                                for j in range(n8):
                                    P.tr(pbb[:, j * 128:(j + 1) * 128], otok[:, lt0 + j, :], ident_b[:])
                                uo = kb + lt0 * 128
                                P.copy("act", obT[:, h // 2, uo:uo + n8 * 128], pbb[:, 0:n8 * 128])
                P.fence()

        def mixer_A(l, unit, oaT):
            with ExitStack() as ph:
                tiles = unit["tiles"]
                ntok_u = len(tiles) * 128
                tok_base = tiles[0] * 128
                has_ctx = any(sq["ctx"] for sq in unit["seqs"])
                nkeys_u = ntok_u + (256 if has_ctx else 0)
                scale = 64.0 ** -0.5
                KTa = sb("KTa", [64, 2, nkeys_u], BF16, ph)
                Va = sb("Va", [128, nkeys_u // 128, 2, 65], BF16, ph)
                P.memset("pool", Va[:, :, :, 64:65], 1.0)
                rC = sb("rAC", [64, 256], F32, ph)
                rS = sb("rAS", [64, 256], F32, ph)
                t1 = sb("tA1", [64, 256], F32, ph)
                t2 = sb("tA2", [64, 256], F32, ph)
                hTb = [sb("hTbA0", [128, 8, 256], BF16, ph)]

                with ExitStack() as a1:
                    hTb.append(sb("hTbA1", [128, 8, 256], BF16, a1))
                    wAk = sb("wAk", [128, 8, 384], BF16, a1)
                    P.dma(wAk[:, :, 0:256], w_in[l, :, OFF_AK:OFF_AK + 256].rearrange("(c p) n -> p c n", p=128), q="pool")
                    P.dma(wAk[:, :, 256:384], w_in_sw[l, :, 512:640].rearrange("(c p) n -> p c n", p=128), q="pool")
                    kvf = [sb("kvf%d" % i, [128, 256], F32, a1) for i in range(2)]
                    for bi, (sq, tok0, nb) in enumerate(blocks_of(unit, 256)):
                        hT = hTb[bi % 2]
                        btiles = [tok0 // 128 + i for i in range(nb // 128)]
                        build_hT(hT, btiles, 0)
                        u0 = tok0 - tok_base
                        P.dma(rC[:, 0:nb], ropeA_C[:, tok0:tok0 + nb])
                        P.dma(rS[:, 0:nb], ropeA_S[:, tok0:tok0 + nb])
                        for g in range(2):
                            p1, p2 = pb[(2 * g) % 4], pb[(2 * g + 1) % 4]
                            for kc in range(8):
                                P.mm(p1[0:64, 0:nb], wAk[:, kc, g * 64:(g + 1) * 64], hT[:, kc, 0:nb], start=(kc == 0), stop=(kc == 7))
                            P.tt("dve", t1[:, 0:nb], p1[0:64, 0:nb], rC[:, 0:nb], ALU.mult)
                            for kc in range(8):
                                P.mm(p2[0:64, 0:nb], wAk[:, kc, 256 + g * 64:256 + (g + 1) * 64], hT[:, kc, 0:nb], start=(kc == 0), stop=(kc == 7))
                            P.tt("dve", t2[:, 0:nb], p2[0:64, 0:nb], rS[:, 0:nb], ALU.mult)
                            P.tt("pool", KTa[:, g, u0:u0 + nb], t1[:, 0:nb], t2[:, 0:nb], ALU.add)
                        for ti, t in enumerate(btiles):
                            pv = pb[4 + ti % 2]
                            for kc in range(8):
                                P.mm(pv[:, 0:256], hT[:, kc, ti * 128:(ti + 1) * 128], wAk[:, kc, 0:256], start=(kc == 0), stop=(kc == 7))
                            ut = t - tiles[0]
                            P.copy("act", Va[:, ut, :, 0:64], pv[:, 128:256].rearrange("p (g e) -> p g e", e=64))
                            if unit["prompt"]:
                                lt = t - sq["t0"]
                                kv = kvf[ti % 2]
                                P.copy("act", kv[:], pv[:, 0:256])
                                P.dma(nak_out[sq["idx"], l, lt * 128:(lt + 1) * 128, :], kv[:, 0:128])
                                P.dma(nav_out[sq["idx"], l, lt * 128:(lt + 1) * 128, :], kv[:, 128:256])
                    if has_ctx:
                        ctxkb = sb("ctxkb", [128, 2, 128], BF16, a1)
                        ctxvb = sb("ctxvb", [128, 2, 128], BF16, a1)
                        for c2 in range(2):
                            P.dma(ctxkb[:, c2, :], cak[l, c2 * 128:(c2 + 1) * 128, :], q="pool")
                            P.dma(ctxvb[:, c2, :], cav[l, c2 * 128:(c2 + 1) * 128, :], q="pool")
                        for c2 in range(2):
                            for g in range(2):
                                P.tr(pbb[0:64, (c2 * 2 + g) * 128:(c2 * 2 + g + 1) * 128], ctxkb[:, c2, g * 64:(g + 1) * 64], ident_b[:])
                            P.copy("dve", Va[:, ntok_u // 128 + c2, :, 0:64], ctxvb[:, c2, :].rearrange("p (g e) -> p g e", e=64))
                        for c2 in range(2):
                            for g in range(2):
                                P.copy("act", KTa[:, g, ntok_u + c2 * 128:ntok_u + (c2 + 1) * 128],
                                       pbb[0:64, (c2 * 2 + g) * 128:(c2 * 2 + g + 1) * 128])
                    P.fence()

                wAq = sb("wAq", [128, 8, 512], BF16, ph)
                wAqs = sb("wAqs", [128, 8, 512], BF16, ph)
                P.dma(wAq[:], w_in[l, :, 0:512].rearrange("(c p) n -> p c n", p=128), q="pool")
                P.dma(wAqs[:], w_in_sw[l, :, 0:512].rearrange("(c p) n -> p c n", p=128), q="pool")
                maskb = sb("maskb", [128, 8, 128], BF16, ph)
                P.dma(maskb[:], cst_masks[:, :, :], q="pool")
                sinkf = sb("sinkf", [1, 8], F32, ph)
                vsink = sb("vsink", [1, 65], BF16, ph)
                P.dma(sinkf[:], a_sink[l:l + 1, :])
                P.ts("dve", sinkf[:], sinkf[:], 1.0 / scale, None, op0=ALU.mult)
                P.memset("dve", vsink[:], 0.0)
                P.memset("dve", vsink[:, 64:65], 1.0)
                QTb = [sb("QTb%d" % i, [64, 8, 256], BF16, ph) for i in range(2)]
                ksqa = sb("ksqa", [64, 512], BF16, ph)
                qsqa = sb("qsqa", [64, 4, 128], BF16, ph)
                kmx = sb("kmxA", [1, 2, 8], F32, ph)
                kmax = sb("kmaxA", [1, 2], F32, ph)
                sqr = sb("sqrA", [1, 512], F32, ph)
                negf = sqr
                negm = [sb("negmA%d" % i, [1, 512], BF16, ph) for i in range(2)]
                srow = sb("srowA", [1, 512], F32, ph)
                psink = [sb("psinkA%d" % i, [1, 512], BF16, ph) for i in range(2)]
                PTb = [sb("PTbA%d" % i, [128, 512], BF16, ph) for i in range(2)]
                otok = [sb("otokA%d" % i, [128, 512], BF16, ph) for i in range(2)]
                rden = [sb("rdenA%d" % i, [128, 1], F32, ph) for i in range(4)]
                nq_ = 0
                nblk = 0
                for sq in unit["seqs"]:
                    kb = sq["t0"] * 128 - tok_base
                    nk_own = sq["nt"] * 128
                    kranges = [(kb, nk_own)] + ([(ntok_u, 256)] if sq["ctx"] else [])
                    for g in range(2):
                        nb_ = 0
                        for (k0, kn_all) in kranges:
                            for o in range(0, kn_all, 512):
                                kn = min(512, kn_all - o)
                                P.tt("pool", ksqa[:, 0:kn], KTa[:, g, k0 + o:k0 + o + kn], KTa[:, g, k0 + o:k0 + o + kn], ALU.mult)
                                P.mm(pb[6][0:1, 0:kn], ones_b[0:64, 0:1], ksqa[:, 0:kn])
                                P.reduce(kmx[:, g, nb_:nb_ + 1], pb[6][0:1, 0:kn], ALU.max)
                                nb_ += 1
                        P.reduce(kmax[:, g:g + 1], kmx[:, g, 0:nb_], ALU.max)
                    P.act(kmax[:], kmax[:], AF.Sqrt)
                    P.ts("dve", kmax[:], kmax[:], -1.0, None, op0=ALU.mult)
                    blocks = [(q0, min(2, sq["nt"] - q0)) for q0 in range(0, sq["nt"], 2)]

                    def qproj(b):
                        q0, nbt = blocks[b]
                        nb = nbt * 128
                        tok0 = (sq["t0"] + q0) * 128
                        hT = hTb[0]
                        QT = QTb[b % 2]
                        build_hT(hT, [sq["t0"] + q0 + i for i in range(nbt)], 0)
                        P.dma(rC[:, 0:nb], ropeA_C[:, tok0:tok0 + nb])
                        P.dma(rS[:, 0:nb], ropeA_S[:, tok0:tok0 + nb])
                        for h in range(8):
                            p1 = pb[6]
                            for kc in range(8):
                                P.mm(p1[0:64, 0:nb], wAq[:, kc, h * 64:(h + 1) * 64], hT[:, kc, 0:nb], start=(kc == 0), stop=(kc == 7))
                            for kc in range(8):
                                P.mm(p1[0:64, 256:256 + nb], wAqs[:, kc, h * 64:(h + 1) * 64], hT[:, kc, 0:nb], start=(kc == 0), stop=(kc == 7))
                            P.tt("dve", t1[:, 0:nb], p1[0:64, 0:nb], rC[:, 0:nb], ALU.mult)
                            P.tt("dve", t2[:, 0:nb], p1[0:64, 256:256 + nb], rS[:, 0:nb], ALU.mult)
                            P.tt("pool", QT[:, h, 0:nb], t1[:, 0:nb], t2[:, 0:nb], ALU.add)

                    items = [(b, qi, g) for b in range(len(blocks)) for qi in range(blocks[b][1]) for g in range(2)]

                    def prep_item(n):
                        b, qi, g = items[n]
                        r = n % 2
                        QTg = QTb[b % 2][:, 4 * g:4 * g + 4, qi * 128:(qi + 1) * 128]
                        P.tt("pool", qsqa[:], QTg, QTg, ALU.mult)
                        P.mm(pb[6][0:1, 0:512], ones_b[0:64, 0:1], qsqa[:])
                        P.act(sqr[:], pb[6][0:1, 0:512], AF.Sqrt)
                        P.ts("dve", negf[:], sqr[:], kmax[0:1, g:g + 1], None, op0=ALU.mult)
                        P.copy("dve", negm[r][:], negf[:])
                        for hh in range(4):
                            P.ts("dve", srow[:, hh * 128:(hh + 1) * 128], negf[:, hh * 128:(hh + 1) * 128],
                                 sinkf[0:1, 4 * g + hh:4 * g + hh + 1], None, op0=ALU.add)
                        P.act(psink[r][:], srow[:], AF.Exp, scale=scale)

                    qproj(0)
                    prep_item(0)
                    for n, (b, qi, g) in enumerate(items):
                        r = n % 2
                        q0 = blocks[b][0]
                        qt = q0 + qi
                        uq = kb + qt * 128
                        ot = otok[qt % 2]
                        if qi == 0 and g == 0 and b + 1 < len(blocks):
                            qproj(b + 1)
                        if n + 1 < len(items):
                            prep_item(n + 1)
                        QTg = QTb[b % 2][:, 4 * g:4 * g + 4, qi * 128:(qi + 1) * 128]
                        if sq["ctx"]:
                            kts = []
                            if qt > 0:
                                kts.append((kb // 128 + qt - 1, maskb[:, 0:4, :]))
                            kts.append((kb // 128 + qt, None))
                            if qt + 1 < sq["nt"]:
                                kts.append((kb // 128 + qt + 1, maskb[:, 4:8, :]))
                            kts += [(ntok_u // 128, None), (ntok_u // 128 + 1, None)]
                        else:
                            kts = [(kb // 128 + j, None) for j in range(sq["nt"])]
                        obanks = [pb[hh][:, 0:65] for hh in range(4)]

                        def tail(r=r):
                            for hh in range(4):
                                P.mm(pb[hh][:, 0:65], psink[r][0:1, hh * 128:(hh + 1) * 128], vsink[:], start=False, stop=True)

                        attend(lambda kt: KTa[:, g, kt * 128:(kt + 1) * 128], lambda kt: Va[:, kt, g, :],
                               QTg, 512, negm[r][:], kts, obanks, [pb[4], pb[5]], PTb, scale, tail=tail)
                        for hh in range(4):
                            h = 4 * g + hh
                            P.recip(rden[hh][:], pb[hh][:, 64:65])
                            P.ts("dve", ot[:, h * 64:(h + 1) * 64], pb[hh][:, 0:64], rden[hh][:, 0:1], None, op0=ALU.mult)
                        if g == 1:
                            for c in range(4):
                                P.tr(pbb[:, c * 128:(c + 1) * 128], ot[:, c * 128:(c + 1) * 128], ident_b[:])
                            P.copy("act", oaT[:, :, uq:uq + 128], pbb[:, 0:512].rearrange("p (c n) -> p c n", c=4))
                P.fence()

        def mixer_C(l, unit, ocT):
            with ExitStack() as ph:
                tiles = unit["tiles"]
                ntok_u = len(tiles) * 128
                nt_u = len(tiles)
                tok_base = tiles[0] * 128
                trc = sb("trc", [128, 2, 2, 128], BF16, ph)
                trif = sb("trif", [128, 2, 128], F32, ph)
                lvN = sb("lvN", [128, 7, 2, 128], BF16, ph)
                lvA = sb("lvA", [128, 7, 2, 128], BF16, ph)
                convw = sb("convw", [128, 12, 3], F32, ph)
                alog = sb("alog", [128, 8], F32, ph)
                dtb = sb("dtb", [128, 8], F32, ph)
                ong = sb("ong", [128, 128], F32, ph)
                wab = sb("wab", [128, 8, 16], BF16, ph)
                gb = sb("gb", [128, nt_u, 16], F32, ph)
                P.dma(trc[:], cst_trc[:, :, :, :], q="pool")
                P.dma(lvN[:], cst_lvN[:, :, :, :], q="pool")
                P.dma(lvA[:], cst_lvA[:, :, :, :], q="pool")
                P.dma(trif[:, 0, :], cst_tri[:, 1, :])
                P.dma(trif[:, 1, :], cst_tri[:, 3, :])
                P.dma(convw[:], c_convT[l, :, :, :])
                P.dma(alog[:], c_a_log[l:l + 1, :].broadcast_to([128, 8]))
                P.dma(dtb[:], c_dt_bias[l:l + 1, :].broadcast_to([128, 8]))
                P.dma(ong[:], c_onorm_g[l:l + 1, :].broadcast_to([128, 128]))
                P.dma(wab[:], w_in[l, :, OFF_CA:OFF_CA + 16].rearrange("(c p) n -> p c n", p=128), q="pool")
                P.act(alog[:], alog[:], AF.Exp)
                P.ts("dve", alog[:], alog[:], -1.0, None, op0=ALU.mult)
                sm = [sb("smC%d" % i, [128, 8], F32, ph) for i in range(6)]

                for h in range(4):
                    with ExitStack() as hs:
                        qkvT = sb("qkvT", [128, 3, ntok_u], BF16, hs)
                        zs = sb("zs", [128, nt_u, 128], F32, hs)
                        oacc = sb("oacc", [128, nt_u, 128], F32, hs)
                        P.memset("pool", oacc[:], 0.0)
                        with ExitStack() as ps_:
                            wC = sb("wC", [128, 8, 512], BF16, ps_)
                            for j in range(3):
                                c0 = OFF_CQKV + j * 512 + h * 128
                                P.dma(wC[:, :, j * 128:(j + 1) * 128], w_in[l, :, c0:c0 + 128].rearrange("(c p) n -> p c n", p=128), q="pool")
                            P.dma(wC[:, :, 384:512], w_in[l, :, OFF_CZ + h * 128:OFF_CZ + (h + 1) * 128].rearrange("(c p) n -> p c n", p=128), q="pool")
                            hTb = [sb("hTbC%d" % i, [128, 8, 512], BF16, ps_) for i in range(2)]
                            rawT = sb("rawT", [128, 3, ntok_u], F32, ps_)
                            cv = [sb("cvC%d" % i, [128, 512], F32, ps_) for i in range(2)]
                            zf = [sb("zfC%d" % i, [128, 128], F32, ps_) for i in range(2)]
                            for bi, (sq, tok0, nb) in enumerate(blocks_of(unit)):
                                hT = hTb[bi % 2]
                                btiles = [tok0 // 128 + i for i in range(nb // 128)]
                                build_hT(hT, btiles, 0)
                                u0 = tok0 - tok_base
                                for j in range(3):
                                    pj = pb[j % 2]
                                    for kc in range(8):
                                        P.mm(pj[:, 0:nb], wC[:, kc, j * 128:(j + 1) * 128], hT[:, kc, 0:nb], start=(kc == 0), stop=(kc == 7))
                                    P.copy("act", rawT[:, j, u0:u0 + nb], pj[:, 0:nb])
                                for ti, t in enumerate(btiles):
                                    ut = t - tiles[0]
                                    pz = pb[2 + ti % 2]
                                    for kc in range(8):
                                        P.mm(pz[:, 0:128], hT[:, kc, ti * 128:(ti + 1) * 128], wC[:, kc, 384:512], start=(kc == 0), stop=(kc == 7))
                                    P.act(zf[ti % 2][:], pz[:, 0:128], AF.Silu)
                                    P.tt("pool", zs[:, ut, :], zf[ti % 2][:], ong[:], ALU.mult)
                                    if h == 0:
                                        pa_ = pb[4]
                                        for kc in range(8):
                                            P.mm(pa_[:, 0:16], hT[:, kc, ti * 128:(ti + 1) * 128], wab[:, kc, :], start=(kc == 0), stop=(kc == 7))
                                        P.tt("dve", sm[0][:], pa_[:, 0:8], dtb[:], ALU.add)
                                        P.act(sm[1][:], sm[0][:], AF.Abs)
                                        P.act(sm[2][:], sm[1][:], AF.Exp, scale=-1.0)
                                        P.act(sm[3][:], sm[2][:], AF.Ln, bias=1.0)
                                        P.ts("dve", sm[4][:], sm[0][:], 0.0, None, op0=ALU.max)
                                        P.tt("dve", sm[5][:], sm[4][:], sm[3][:], ALU.add)
                                        P.tt("dve", gb[:, ut, 0:8], sm[5][:], alog[:], ALU.mult)
                                        P.act(gb[:, ut, 8:16], pa_[:, 8:16], AF.Sigmoid)
                            for sq in unit["seqs"]:
                                s0 = sq["t0"] * 128 - tok_base
                                n = sq["nt"] * 128
                                for j in range(3):
                                    ci = j * 4 + h
                                    for o in range(0, n, 512):
                                        nb = min(512, n - o)
                                        a0 = s0 + o
                                        c_ = cv[(o // 512) % 2]
                                        P.act(c_[:, 0:nb], rawT[:, j, a0:a0 + nb], AF.Copy, scale=convw[:, ci, 1:2])
                                        lo = 1 if o == 0 else 0
                                        P.stt(c_[:, lo:nb], rawT[:, j, a0 + lo - 1:a0 + nb - 1], convw[:, ci, 0:1], c_[:, lo:nb], ALU.mult, ALU.add)
                                        hi = nb - 1 if o + nb == n else nb
                                        P.stt(c_[:, 0:hi], rawT[:, j, a0 + 1:a0 + hi + 1], convw[:, ci, 2:3], c_[:, 0:hi], ALU.mult, ALU.add)
                                        P.act(qkvT[:, j, a0:a0 + nb], c_[:, 0:nb], AF.Silu)
                            P.fence()

                        gc = sb("gc", [128, 2, nt_u], F32, hs)
                        egc = sb("egc", [128, 2, nt_u], F32, hs)
                        gl = sb("gl", [128, 2, nt_u], F32, hs)
                        edl = sb("edl", [128, 2, nt_u], F32, hs)
                        ge = sb("ge", [128, 2, nt_u], F32, hs)
                        gsel = sb("gsel", [128, 2, nt_u], F32, hs)
                        for d in range(2):
                            P.copy("dve", gsel[:, d, :], gb[:, :, d * 4 + h])
                        for d in range(2):
                            P.mm(pb[6][:, d * nt_u:(d + 1) * nt_u], trif[:, d, :], gsel[:, d, :])
                        P.copy("dve", gc[:], pb[6][:, 0:2 * nt_u].rearrange("p (d n) -> p d n", d=2))
                        P.mm(pb[5][:, 0:2 * nt_u], ones_f[:], gsel[:].rearrange("p d n -> p (d n)"))
                        P.copy("dve", gl[:], pb[5][:, 0:2 * nt_u].rearrange("p (d n) -> p d n", d=2))
                        P.act(egc[:], gc[:], AF.Exp)
                        P.act(ge[:], gl[:], AF.Exp)
                        P.tt("dve", edl[:], gl[:], gc[:], ALU.subtract)
                        P.act(edl[:], edl[:], AF.Exp)

                        Sst = sb("Sst", [128, 2, 128], F32, hs)
                        Sbf = sb("Sbf", [128, 2, 128], BF16, hs)
                        ident2 = ident_f[:].unsqueeze(1).broadcast_to([128, 2, 128])

                        def slotbufs(s_):
                            W = Ctx()
                            f32 = lambda n, shp=(128, 2, 128): sb("%s_s%d" % (n, s_), list(shp), F32, hs)
                            b16 = lambda n, shp=(128, 2, 128): sb("%s_s%d" % (n, s_), list(shp), BF16, hs)
                            W.ssq = f32("ssq", (128, 2, 2)); W.rkq = f32("rkq", (128, 2, 2))
                            W.qkv = b16("qkvtok", (128, 2, 3, 128)); W.kn = b16("kn"); W.qn = b16("qn")
                            W.vke = b16("vke", (128, 2, 256)); W.kqT = b16("kqT", (128, 2, 2, 128))
                            W.diag = f32("diag"); W.DT = f32("DT"); W.tmp1 = f32("tmp1"); W.AT = f32("AT"); W.tmp2 = f32("tmp2")
                            W.qkmT = b16("qkmT"); W.A = f32("A"); W.R = f32("R"); W.RT = f32("RT")
                            W.BkT = [f32("BkT0"), f32("BkT1")]; W.Y = f32("Y"); W.Rb = b16("Rb"); W.u = f32("u")
                            W.w = b16("w"); W.wT = b16("wT"); W.kd = b16("kd"); W.qd = b16("qd"); W.qdT = b16("qdT")
                            W.vnew = b16("vnew"); W.t3 = f32("t3"); W.t4 = f32("t4")
                            return W
                        SB_ = [slotbufs(0), slotbufs(1)]

                        def run_slot(sq, s_):
                            W = SB_[s_]
                            kb = sq["t0"] * 128 - tok_base
                            n = sq["nt"]
                            bA, bB, bC = pb[3 * s_], pb[3 * s_ + 1], pb[3 * s_ + 2]
                            tb0 = s_ * 4 * 128
                            tb = lambda j, cnt=1: pbb[:, tb0 + j * 128:tb0 + (j + cnt) * 128]
                            v3 = lambda ap: ap.rearrange("p (d n) -> p d n", d=2)
                            if s_ == 1:
                                for _ in range(3):
                                    yield
                            for st in range(s_, n, 2):
                                cs = [st, n - 1 - st]
                                uts = [kb // 128 + c for c in cs]
                                gcc = [gc[:, d, uts[d]:uts[d] + 1] for d in range(2)]
                                beta = [gb[:, uts[d], 8 + d * 4 + h:8 + d * 4 + h + 1] for d in range(2)]
                                for d in range(2):
                                    sl = slice(kb + cs[d] * 128, kb + (cs[d] + 1) * 128)
                                    for j in range(3):
                                        P.tr(tb(j), qkvT[:, j, sl], ident_b[:])
                                    P.copy("act", W.qkv[:, d, :, :], tb(0, 3).rearrange("p (j n) -> p j n", j=3))
                                yield
                                for d in range(2):
                                    P.act(W.t3[:, d, :], W.qkv[:, d, 0, :], AF.Square, accum_out=W.ssq[:, d, 0:1])
                                    P.act(W.t4[:, d, :], W.qkv[:, d, 1, :], AF.Square, accum_out=W.ssq[:, d, 1:2])
                                yield
                                P.ts("dve", W.rkq[:], W.ssq[:], EPS, None, op0=ALU.add)
                                P.act(W.rkq[:], W.rkq[:], AF.Sqrt)
                                P.recip(W.rkq[:], W.rkq[:])
                                yield
                                for d in range(2):
                                    P.act(W.kn[:, d, :], W.qkv[:, d, 1, :], AF.Copy, scale=W.rkq[:, d, 1:2])
                                    P.ts("dve", W.qn[:, d, :], W.qkv[:, d, 0, :], W.rkq[:, d, 0:1], 128.0 ** -0.5, op0=ALU.mult, op1=ALU.mult)
                                    P.copy("pool", W.vke[:, d, 0:128], W.qkv[:, d, 2, :])
                                yield
                                for d in range(2):
                                    P.tr(tb(0), W.kn[:, d, :], ident_b[:])
                                    P.tr(tb(1), W.qn[:, d, :], ident_b[:])
                                    P.copy("act", W.kqT[:, d, :, :], tb(0, 2).rearrange("p (c n) -> p c n", c=2))
                                    P.act(W.vke[:, d, 128:256], W.kn[:, d, :], AF.Copy, scale=egc[:, d, uts[d]:uts[d] + 1])
                                    P.ts("dve", W.kd[:, d, :], W.kn[:, d, :], edl[:, d, uts[d]:uts[d] + 1], None, op0=ALU.mult)
                                    P.act(W.qd[:, d, :], W.qn[:, d, :], AF.Copy, scale=egc[:, d, uts[d]:uts[d] + 1])
                                    P.ts("dve", W.diag[:, d, :], ident_f[:], gcc[d], None, op0=ALU.mult)
                                yield
                                for d in range(2):
                                    P.mm(bA[:, d * 128:(d + 1) * 128], W.kqT[:, d, 0, :], W.kqT[:, d, 0, :])
                                    P.mm(bA[:, 256 + d * 128:256 + (d + 1) * 128], W.kqT[:, d, 0, :], W.kqT[:, d, 1, :])
                                    P.mm(bB[:, d * 128:(d + 1) * 128], ones_f[:], W.diag[:, d, :])
                                    P.tr(tb(2 + d), W.qd[:, d, :], ident_b[:])
                                for d in range(2):
                                    P.ts("dve", W.DT[:, d, :], bB[:, d * 128:(d + 1) * 128], gcc[d], 0.0, op0=ALU.subtract, op1=ALU.min)
                                P.act(W.DT[:], W.DT[:], AF.Exp)
                                P.copy("act", W.qdT[:], v3(tb(2, 2)))
                                yield
                                P.tt("dve", W.tmp1[:], v3(bA[:, 0:256]), W.DT[:], ALU.mult)
                                P.tt("dve", W.tmp2[:], v3(bA[:, 256:512]), W.DT[:], ALU.mult)
                                for d in range(2):
                                    P.stt(W.AT[:, d, :], W.tmp1[:, d, :], beta[d], trc[:, 0, d, :], ALU.mult, ALU.mult)
                                P.tt("pool", W.qkmT[:], W.tmp2[:], trc[:, 1, :, :], ALU.mult)
                                yield
                                for d in range(2):
                                    P.tr(bB[:, 256 + d * 128:256 + (d + 1) * 128], W.AT[:, d, :], ident_f[:])
                                P.copy("act", W.A[:], v3(bB[:, 256:512]))
                                P.tt("dve", W.t3[:], W.AT[:], lvN[:, 0, :, :], ALU.mult)
                                P.tt("dve", W.R[:], ident2, W.t3[:], ALU.subtract)
                                yield
                                P.tt("pool", W.t4[:], W.A[:], lvA[:, 0, :, :], ALU.mult)
                                P.tt("pool", W.RT[:], ident2, W.t4[:], ALU.subtract)
                                P.tt("pool", W.BkT[1][:], W.A[:], lvA[:, 1, :, :], ALU.mult)
                                yield
                                for k in range(1, 7):
                                    Bk = W.BkT[k % 2]
                                    for d in range(2):
                                        P.mm(bC[:, d * 128:(d + 1) * 128], Bk[:, d, :], W.R[:, d, :])
                                    if k < 6:
                                        P.tt("pool", W.BkT[(k + 1) % 2][:], W.A[:], lvA[:, k + 1, :, :], ALU.mult)
                                    P.copy("act", W.Y[:], v3(bC[:, 0:256]))
                                    yield
                                    for d in range(2):
                                        P.mm(bB[:, d * 128:(d + 1) * 128], W.RT[:, d, :], W.Y[:, d, :])
                                        P.mm(bB[:, 256 + d * 128:256 + (d + 1) * 128], W.Y[:, d, :], W.RT[:, d, :])
                                    yield
                                    P.tt("dve", W.R[:], W.R[:], v3(bB[:, 0:256]), ALU.subtract)
                                    P.tt("dve", W.RT[:], W.RT[:], v3(bB[:, 256:512]), ALU.subtract)
                                    yield
                                P.copy("act", W.Rb[:], W.R[:])
                                yield
                                for d in range(2):
                                    P.mm(bA[:, d * 256:(d + 1) * 256], W.Rb[:, d, :], W.vke[:, d, :])
                                for d in range(2):
                                    P.act(W.u[:, d, :], bA[:, d * 256:d * 256 + 128], AF.Copy, scale=beta[d])
                                    P.act(W.w[:, d, :], bA[:, d * 256 + 128:(d + 1) * 256], AF.Copy, scale=beta[d])
                                yield
                                for d in range(2):
                                    P.tr(tb(d), W.w[:, d, :], ident_b[:])
                                P.copy("act", W.wT[:], v3(tb(0, 2)))
                                yield
                                for d in range(2):
                                    P.mm(bC[:, d * 128:(d + 1) * 128], W.wT[:, d, :], Sbf[:, d, :])
                                P.tt("dve", W.vnew[:], W.u[:], v3(bC[:, 0:256]), ALU.subtract)
                                yield
                                for d in range(2):
                                    P.mm(bC[:, 256 + d * 128:256 + (d + 1) * 128], W.qdT[:, d, :], Sbf[:, d, :], start=True, stop=False)
                                    P.mm(bC[:, 256 + d * 128:256 + (d + 1) * 128], W.qkmT[:, d, :], W.vnew[:, d, :], start=False, stop=True)
                                for d in range(2):
                                    P.tt("dve", oacc[:, uts[d], :], oacc[:, uts[d], :], bC[:, 256 + d * 128:256 + (d + 1) * 128], ALU.add)
                                yield
                                for d in range(2):
                                    P.mm(bA[:, d * 128:(d + 1) * 128], W.kd[:, d, :], W.vnew[:, d, :])
                                for d in range(2):
                                    P.stt(Sst[:, d, :], Sst[:, d, :], ge[:, d, uts[d]:uts[d] + 1], bA[:, d * 128:(d + 1) * 128], ALU.mult, ALU.add)
                                P.copy("act", Sbf[:], Sst[:])
                                yield

                        for sq in unit["seqs"]:
                            for d in range(2):
                                if sq["ctx"]:
                                    P.dma(Sst[:, d, :], cstate[l, d, h, :, :])
                                else:
                                    P.memset("pool", Sst[:, d, :], 0.0)
                            P.copy("act", Sbf[:], Sst[:])
                            alive = [run_slot(sq, 0), run_slot(sq, 1)]
                            while alive:
                                for g_ in list(alive):
                                    try:
                                        next(g_)
                                    except StopIteration:
                                        alive.remove(g_)
                            if unit["prompt"]:
                                for d in range(2):
                                    P.dma(ncs_out[sq["idx"], l, d, h, :, :], Sst[:, d, :])
                        on_ss = sb("on_ss", [128, nt_u], F32, hs)
                        on_r = sb("on_r", [128, nt_u], F32, hs)
                        ojk = [sb("ojk%d" % i, [128, 128], F32, hs) for i in range(2)]
                        ob16 = sb("ob16", [128, nt_u, 128], BF16, hs)
                        for ut in range(nt_u):
                            P.act(ojk[ut % 2][:], oacc[:, ut, :], AF.Square, accum_out=on_ss[:, ut:ut + 1])
                        P.ts("dve", on_r[:], on_ss[:], 1.0 / 128, EPS, op0=ALU.mult, op1=ALU.add)
                        P.act(on_r[:], on_r[:], AF.Sqrt)
                        P.recip(on_r[:], on_r[:])
                        for ut in range(nt_u):
                            P.stt(ob16[:, ut, :], oacc[:, ut, :], on_r[:, ut:ut + 1], zs[:, ut, :], ALU.mult, ALU.mult)
                        for u0 in range(0, nt_u, 8):
                            n8 = min(8, nt_u - u0)
                            for j in range(n8):
                                P.tr(pbb[:, j * 128:(j + 1) * 128], ob16[:, u0 + j, :], ident_b[:])
                            P.copy("act", ocT[:, h, u0 * 128:(u0 + n8) * 128], pbb[:, 0:n8 * 128])
                        P.fence()
                P.fence()

        def merge(l, unit, oT):
            tiles = unit["tiles"]
            tok_base = tiles[0] * 128
            cd = unit["cond"]
            for sub0 in range(0, len(tiles), 8):
                stiles = tiles[sub0:sub0 + 8]
                with ExitStack() as ph:
                    nsub = len(stiles) * 128
                    mT = sb("mT", [128, 8, nsub], BF16, ph)
                    with ExitStack() as p2:
                        hTm = sb("hTm", [128, 8, 256], BF16, p2)
                        gt = [sb("gtM%d" % i, [128, 256], F32, p2) for i in range(2)]
                        tm = [sb("tmM%d" % i, [128, 256], F32, p2) for i in range(2)]
                        wg = sb("wg", [128, 8, D], BF16, p2)
                        wbr = sb("wbr", [128, 4, D], BF16, p2)
                        for j in range(3):
                            P.dma(wg[:], w_in[l, :, OFF_GATES + j * D:OFF_GATES + (j + 1) * D].rearrange("(c p) n -> p c n", p=128), q="pool")
                            P.dma(wbr[:], w_branch[l, j].rearrange("(c p) n -> p c n", p=128), q="pool")
                            for b0 in range(0, len(stiles), 2):
                                bt = stiles[b0:b0 + 2]
                                nb = len(bt) * 128
                                build_hT(hTm, bt, 0)
                                u0 = bt[0] * 128 - tok_base
                                m0 = b0 * 128
                                for fc in range(8):
                                    pg, py = pb[fc % 2], pb[2 + fc % 2]
                                    for kc in range(8):
                                        P.mm(pg[:, 0:nb], wg[:, kc, fc * 128:(fc + 1) * 128], hTm[:, kc, 0:nb], start=(kc == 0), stop=(kc == 7))
                                    P.act(gt[fc % 2][:, 0:nb], pg[:, 0:nb], AF.Sigmoid)
                                    for c4 in range(4):
                                        P.mm(py[:, 0:nb], wbr[:, c4, fc * 128:(fc + 1) * 128], oT[j][:, c4, u0:u0 + nb], start=(c4 == 0), stop=(c4 == 3))
                                    if j == 0:
                                        P.tt("dve", mT[:, fc, m0:m0 + nb], py[:, 0:nb], gt[fc % 2][:, 0:nb], ALU.mult)
                                    else:
                                        P.tt("dve", tm[fc % 2][:, 0:nb], py[:, 0:nb], gt[fc % 2][:, 0:nb], ALU.mult)
                                        P.tt("pool", mT[:, fc, m0:m0 + nb], mT[:, fc, m0:m0 + nb], tm[fc % 2][:, 0:nb], ALU.add)
                        P.fence()
                    with ExitStack() as p3:
                        wo = sb("wo", [128, 8, D], BF16, p3)
                        G1 = sb("G1", [128, D], F32, p3)
                        tmo = [sb("tmo%d" % i, [128, 512], F32, p3) for i in range(2)]
                        P.dma(wo[:], w_out[l].rearrange("(c p) n -> p c n", p=128), q="pool")
                        make_G(G1[:], 0, cd, p3)
                        n_ = 0
                        for ti, t in enumerate(stiles):
                            for half in range(2):
                                py = pb[2 + n_ % 4]
                                for kc in range(8):
                                    P.mm(py[:], mT[:, kc, ti * 128:(ti + 1) * 128], wo[:, kc, half * 512:(half + 1) * 512], start=(kc == 0), stop=(kc == 7))
                                tmp = tmo[n_ % 2]
                                n_ += 1
                                P.tt("dve", tmp[:], py[:], G1[:, half * 512:(half + 1) * 512], ALU.mult)
                                P.tt("pool", x[:, t, half * 512:(half + 1) * 512], x[:, t, half * 512:(half + 1) * 512], tmp[:], ALU.add)
                        P.fence()

        K.P, K.nc, K.x, K.pb, K.sb, K.psb = P, nc, x, pb, sb, psb
        def dump_bf16(name, src, n):
            with ExitStack() as ph:
                stg = sb("stg", [128, src.shape[1], n], F32, ph)
                P.copy("dve", stg[:], src)
                P.dma(dbg_out[name][:, :, :], stg[:])
                P.fence()

        for l in range(depth):
            if "adaln" in parts:
                adaln(l)
            calc_rstd(list(range(NT)))
            for unit in units:
                with ExitStack() as us:
                    ntu = len(unit["tiles"]) * 128
                    oT = {}
                    for m in ("C", "B", "A"):
                        oT[m] = sb("o%sT" % m.lower(), [128, 4, ntu], BF16, us)
                        if m in mixers:
                            K.phase = "mix%s_%s" % (m, unit["name"])
                            {"A": mixer_A, "B": mixer_B, "C": mixer_C}[m](l, unit, oT[m])
                            K.phase = "other"
                        else:
                            P.memset("pool", oT[m][:], 0.0)
                        nm = "o%sT_%s%d" % (m.lower(), unit["name"], l)
                        if nm in dbg_out:
                            dump_bf16(nm, oT[m][:], ntu)
                    if "merge" in parts:
                        K.phase = "merge_%s" % unit["name"]
                        merge(l, unit, [oT["A"], oT["B"], oT["C"]])
                        K.phase = "other"
                    P.fence()
            if "ffn" in parts:
                ffn(l)
        if "final" in parts:
            final_norm()
        P.finish()
        K.stats = (P.n_inst, P.n_wait)
    return nc, K


def _rope_feat(R, pos_row, pos_col):
    q = R // 4
    inv = np.power(10000.0, -np.arange(q, dtype=np.float32) / q).astype(np.float32)
    n = len(pos_row)
    C = np.zeros((R, n), np.float32)
    S = np.zeros((R, n), np.float32)
    for r in range(R):
        blk, i = r // q, r % q
        pos = pos_row if blk < 2 else pos_col
        ang = pos.astype(np.float32) * inv[i]
        C[r] = np.cos(ang)
        S[r] = (-1.0 if blk % 2 == 0 else 1.0) * np.sin(ang)
    return C, S


def _swap_perm(R):
    q = R // 4
    return np.array([r + q if (r // q) % 2 == 0 else r - q for r in range(R)])


def make_shared(inp, NTS, NSP):
    f = lambda a: np.ascontiguousarray(np.asarray(a, dtype=np.float32))
    depth = inp["w_in"].shape[0]
    NTP = NSP * 2
    NTOK = (NTP + NTS) * 128
    sh = {}
    for k in ("w_ada", "b_ada", "w_ff1", "w_ff2", "w_in", "b_gkv", "b_w_ukv", "a_sink",
              "w_branch", "w_out", "c_onorm_g"):
        if k in inp:
            sh[k] = f(inp[k])
    sh["b_adaT"] = f(np.asarray(inp["b_ada"]).reshape(depth, 48, 128).transpose(0, 2, 1))
    sh["norm1T"] = f(np.asarray(inp["norm1_g"]).reshape(depth, 8, 128).transpose(0, 2, 1))
    sh["norm2T"] = f(np.asarray(inp["norm2_g"]).reshape(depth, 8, 128).transpose(0, 2, 1))
    sh["final_g"] = f(np.asarray(inp["final_g"]).reshape(1, D))
    sh["b_gqT"] = f(np.asarray(inp["b_gq"]).reshape(depth, 3, 128).transpose(0, 2, 1))
    w_in = np.asarray(inp["w_in"])
    p64, p32 = _swap_perm(64), _swap_perm(32)
    cols = []
    for h in range(8):
        cols += list(OFF_AQ + h * 64 + p64)
    for h in range(2):
        cols += list(OFF_AK + h * 64 + p64)
    cols += list(OFF_BKR + p32)
    sh["w_in_sw"] = f(w_in[:, :, np.array(cols)])
    wuq_ = np.asarray(inp["b_w_uq"], dtype=np.float32).reshape(depth, 384, 8, 96)
    zpad = np.zeros((depth, 384, 8, 32), np.float32)
    sh["b_w_uq_r"] = f(np.concatenate([zpad, wuq_[..., 64:96], wuq_[..., 0:64]], -1).reshape(depth, 384, 1024))
    sh["b_w_uq_rs"] = f(np.concatenate([zpad, wuq_[..., 64 + p32], wuq_[..., 0:64]], -1).reshape(depth, 384, 1024))
    ts = np.arange(NTS * 128)
    row, col = ts // 64, ts % 64
    CA, SA = _rope_feat(64, row, col)
    CB, SB = _rope_feat(32, row, col)
    npad = NTP * 128
    padC = lambda C: np.concatenate([np.ones((C.shape[0], npad), np.float32), C], axis=1)
    padS = lambda S: np.concatenate([np.zeros((S.shape[0], npad), np.float32), S], axis=1)
    sh["ropeA_C"], sh["ropeA_S"] = f(padC(CA)), f(padS(SA))
    sh["ropeQB_C"] = f(np.concatenate([np.ones((32, NTOK), np.float32), padC(CB), np.ones((64, NTOK), np.float32)], axis=0))
    sh["ropeQB_S"] = f(np.concatenate([np.zeros((32, NTOK), np.float32), padS(SB), np.zeros((64, NTOK), np.float32)], axis=0))
    sh["ropeKB_C"], sh["ropeKB_S"] = f(padC(CB).T), f(padS(SB).T)
    sh["cst_ident"] = np.eye(128, dtype=np.float32)
    jj = np.arange(128)[:, None]
    ii = np.arange(128)[None, :]
    sh["cst_tri"] = f(np.stack([(jj < ii), (jj <= ii), (jj > ii), (jj >= ii)], axis=1))
    lv = []
    for k in range(1, 8):
        lv.append(((jj >> k) == (ii >> k)) & ((jj >> (k - 1)) != (ii >> (k - 1))) & (jj < ii))
    tri4 = [(jj < ii), (jj <= ii), (jj > ii), (jj >= ii)]
    sh["cst_trc"] = f(np.stack([np.stack([tri4[0], tri4[2]], 1), np.stack([tri4[1], tri4[3]], 1)], 1))
    sh["cst_lvN"] = f(np.stack([np.stack([m, m.T], 1) for m in lv], 1))
    sh["cst_lvA"] = f(np.stack([np.stack([m.T, m], 1) for m in lv], 1))
    sh["c_convT"] = f(np.asarray(inp["c_conv_w"]).reshape(depth, 3, 12, 128).transpose(0, 3, 2, 1))
    sh["c_a_log"] = f(np.asarray(inp["c_a_log"]).reshape(depth, 8))
    sh["c_dt_bias"] = f(np.asarray(inp["c_dt_bias"]).reshape(depth, 8))
    sel = np.zeros((2, 2, 128), np.float32)
    sel[0, 0] = 1
    sel[1, 1] = 1
    sh["cst_sel"] = sel
    b_ = np.arange(128)[:, None]
    a_ = np.arange(128)[None, :]
    mprev = np.where(a_ > b_, -30000.0, 0.0).astype(np.float32)
    mnext = np.where(b_ > a_, -30000.0, 0.0).astype(np.float32)
    sh["cst_masks"] = f(np.stack([mprev] * 4 + [mnext] * 4, axis=1))
    return sh


def make_core(inp, prompt_ids, sample_id, NTS):
    f = lambda a: np.ascontiguousarray(np.asarray(a, dtype=np.float32))
    xp = np.asarray(inp["x_prompt"])[prompt_ids].reshape(-1, D)
    xs = np.asarray(inp["x_sample"])[sample_id].reshape(-1, D)
    cond = np.stack([np.asarray(inp["c_ctx"]), np.asarray(inp["c"])[sample_id]], 0)
    c = {"xin": f(np.concatenate([xp, xs], 0)),
         "condT": f(cond.reshape(2, 8, 128).transpose(2, 1, 0)),
         "cak": f(np.asarray(inp["cache_a_k"])[sample_id].reshape(-1, 256, 128)),
         "cav": f(np.asarray(inp["cache_a_v"])[sample_id].reshape(-1, 256, 128)),
         "cckv": f(np.asarray(inp["cache_b_ckv"])[sample_id]),
         "ckr": f(np.asarray(inp["cache_b_krope"])[sample_id]),
         "cstate": f(np.asarray(inp["state_c"])[sample_id])}
    return c


_NTS, _NSP = 16, 2
_CACHE = {}


def kernel(**inputs):
    if "nc" not in _CACHE:
        _CACHE["nc"] = build(NTS=_NTS, NSP=_NSP, depth=DEPTH, parts=("adaln", "merge", "ffn", "final"))
    nc, K = _CACHE["nc"]
    names = [a.memorylocations[0].name for a in nc.m.functions[0].allocations
             if isinstance(a, mybir.MemoryLocationSet) and a.kind == "ExternalInput"]
    shared = make_shared(inputs, _NTS, _NSP)
    in_maps = []
    for core in range(8):
        d = dict(shared)
        d.update(make_core(inputs, [2 * core, 2 * core + 1], core // 2, _NTS))
        in_maps.append({k: d[k] for k in names if k in d})
    res = run_bass_kernel_spmd(nc, in_maps, core_ids=list(range(8))).results
    npt = _NSP * 256
    y_prompt = np.concatenate([res[c]["y"][:npt].reshape(_NSP, 256, D) for c in range(8)], 0)
    y_sample = np.stack([res[2 * b]["y"][npt:] for b in range(4)], 0)
    cat = lambda k: np.concatenate([res[c][k] for c in range(8)], 0)
    new_a_k = cat("nak").reshape(16, DEPTH, 256, 2, 64)
    new_a_v = cat("nav").reshape(16, DEPTH, 256, 2, 64)
    new_b_ckv = cat("nckv")
    new_b_krope = cat("nkr")
    new_c_state = cat("ncs")
    f = lambda a: np.ascontiguousarray(a, dtype=np.float32)
    return (f(y_prompt), f(y_sample), f(new_a_k), f(new_a_v), f(new_b_ckv), f(new_b_krope), f(new_c_state))
```

```python
import bisect
from contextlib import ExitStack

import numpy as np
import concourse.bass as bass
import concourse.mybir as mybir
from concourse.bass_utils import run_bass_kernel_spmd

F32 = mybir.dt.float32
BF16 = mybir.dt.bfloat16
AF = mybir.ActivationFunctionType
ALU = mybir.AluOpType
AX = mybir.AxisListType

COMPUTE = ("pe", "act", "dve", "pool")
NDS = 48
NDS_SP = 32


def _region(ap):
    t = ap.tensor
    dims = [(int(s), int(c)) for s, c in ap.ap]
    off = int(ap.offset)
    if type(t).__name__.startswith("DRam"):
        ext = sum((c - 1) * abs(s) for s, c in dims)
        return (t.name, 0, 1, off, off + ext + 1)
    rows = 1
    for d in list(t.shape)[1:]:
        rows *= int(d)
    p0 = off // rows
    f0 = off % rows
    pc = dims[0][1] if dims[0][0] != 0 else 1
    ext = sum((c - 1) * abs(s) for s, c in dims[1:])
    f1 = f0 + ext + 1
    if type(t).__name__.startswith("PSum"):
        be = 1024 if t.dtype == BF16 else 512
        f0 = (f0 // be) * be
        f1 = ((f1 + be - 1) // be) * be
        p0, pc = 0, 128
    return (t.name, p0, p0 + pc, f0, f1)


def _overlap(a, b):
    return a[1] < b[2] and b[1] < a[2] and a[3] < b[4] and b[3] < a[4]


def _contains(a, b):
    return a[1] <= b[1] and b[2] <= a[2] and a[3] <= b[3] and b[4] <= a[4]


class Op:
    __slots__ = ("eng", "fn", "deps", "signal", "is_dma", "sem", "semval", "seq", "idx")

    def __init__(self, eng, fn, idx, is_dma=False):
        self.eng = eng
        self.fn = fn
        self.deps = []
        self.signal = False
        self.is_dma = is_dma
        self.sem = None
        self.semval = 0
        self.seq = None
        self.idx = idx


class Prog:
    def __init__(self, nc, stack):
        self.nc = nc
        self.engs = {"pe": nc.tensor, "act": nc.scalar, "dve": nc.vector, "pool": nc.gpsimd, "sp": nc.sync}
        self.esem = {e: stack.enter_context(nc.semaphore("es_" + e)) for e in COMPUTE}
        self.dsems = [stack.enter_context(nc.semaphore("ds%d" % i)) for i in range(NDS)]
        self.dcount = [0] * NDS
        self.dlast = [None] * NDS
        self.dnext = {"sp": 0, "pool": 0}
        self.pending = []
        self.state = {}
        self.sigcount = {e: 0 for e in COMPUTE}
        self.sigidx = {e: [] for e in COMPUTE}
        self.sigseq = {e: [] for e in COMPUTE}
        self.waited = {f: {e: 0 for e in COMPUTE} for f in self.engs}
        self.waited_dma = {f: {} for f in self.engs}
        self.last_op = {}
        self.open_dmas = []
        self.nops = 0
        self.n_inst = 0
        self.n_wait = 0

    def add(self, eng, fn, reads=(), writes=(), dma=False):
        op = Op(eng, fn, self.nops, dma)
        self.nops += 1
        deps = {}
        rregs = [_region(a) for a in reads]
        wregs = [_region(a) for a in writes]
        for r in rregs:
            is_ps = r[0].startswith("pb")
            for (reg, o, isw) in self.state.get(r[0], ()):
                if _overlap(reg, r) and (isw or (is_ps and o.eng != eng)):
                    deps[o.idx] = o
        for w in wregs:
            for (reg, o, isw) in self.state.get(w[0], ()):
                if _overlap(reg, w):
                    if (not dma) and (not o.is_dma) and o.eng == eng and eng == "pe":
                        continue
                    deps[o.idx] = o
        if dma:
            if eng == "sp":
                s = self.dnext["sp"]
                self.dnext["sp"] = (s + 1) % NDS_SP
            else:
                s = NDS_SP + self.dnext["pool"]
                self.dnext["pool"] = (self.dnext["pool"] + 1) % (NDS - NDS_SP)
            if self.dlast[s] is not None:
                deps[self.dlast[s].idx] = self.dlast[s]
            self.dlast[s] = op
            self.dcount[s] += 1
            op.sem = s
            op.semval = 16 * self.dcount[s]
            self.open_dmas.append(op)
        for o in deps.values():
            if not o.is_dma:
                o.signal = True
        op.deps = list(deps.values())
        for w in wregs:
            lst = self.state.setdefault(w[0], [])
            lst[:] = [rec for rec in lst if not _contains(w, rec[0])]
            lst.append((w, op, True))
        for r in rregs:
            lst = self.state.setdefault(r[0], [])
            if not dma:
                lst[:] = [rec for rec in lst
                          if not ((not rec[2]) and (not rec[1].is_dma) and rec[1].eng == eng and _contains(r, rec[0]))]
            lst.append((r, op, False))
        self.pending.append(op)
        if not dma:
            self.last_op[eng] = op
        return op

    def fence(self):
        lasts = dict(self.last_op)
        dmas = list(self.open_dmas)
        self.open_dmas = []
        for f in self.engs:
            op = Op(f, None, self.nops, False)
            self.nops += 1
            for e, o in lasts.items():
                o.signal = True
                op.deps.append(o)
            op.deps.extend(dmas)
            self.pending.append(op)
        self.state = {}
        self.flush()

    def _seq_of(self, dep):
        if dep.seq is not None:
            return dep.seq
        e = dep.eng
        i = bisect.bisect_left(self.sigidx[e], dep.idx)
        return self.sigseq[e][i]

    def flush(self):
        lastp = {}
        for op in self.pending:
            if (not op.is_dma) and op.fn is not None and op.eng in COMPUTE:
                lastp[op.eng] = op
        for op in lastp.values():
            op.signal = True
        for op in self.pending:
            f = op.eng
            eng = self.engs[f]
            waits = []
            for dep in op.deps:
                if dep.is_dma:
                    if self.waited_dma[f].get(dep.sem, 0) >= dep.semval:
                        continue
                    waits.append((self.dsems[dep.sem], dep.semval))
                    self.waited_dma[f][dep.sem] = dep.semval
                else:
                    e = dep.eng
                    seq = self._seq_of(dep)
                    if self.waited[f][e] >= seq:
                        continue
                    waits.append((self.esem[e], seq))
                    self.waited[f][e] = seq
            self.n_wait += len(waits)
            embed = None
            if op.fn is not None and (not op.is_dma) and waits:
                embed = waits.pop()
            for (sem_, val_) in waits:
                eng.wait_ge(sem_, val_)
            if op.fn is None:
                continue
            ins = op.fn(eng)
            if embed is not None:
                ins._wait_ge(embed[0], embed[1])
            self.n_inst += 1
            if op.is_dma:
                ins.then_inc(self.dsems[op.sem], 16)
            elif op.signal:
                self.sigcount[f] += 1
                op.seq = self.sigcount[f]
                ins.then_inc(self.esem[f], 1)
                self.sigidx[f].append(op.idx)
                self.sigseq[f].append(op.seq)
                self.waited[f][f] = max(self.waited[f][f], 0)
        self.pending = []

    def finish(self):
        self.fence()

    def mm(self, out, lhsT, rhs, start=True, stop=True):
        return self.add("pe", lambda e: e.matmul(out, lhsT, rhs, start=start, stop=stop),
                        reads=[lhsT, rhs], writes=[out])

    def tr(self, out, in_, ident):
        return self.add("pe", lambda e: e.transpose(out, in_, ident), reads=[in_, ident], writes=[out])

    def act(self, out, in_, func, bias=None, scale=1.0, accum_out=None, eng="act"):
        reads = [in_]
        kw = {}
        if bias is not None:
            kw["bias"] = bias
            if not isinstance(bias, (int, float)):
                reads.append(bias)
        if not isinstance(scale, (int, float)):
            reads.append(scale)
        kw["scale"] = scale
        writes = [out]
        if accum_out is not None:
            kw["accum_out"] = accum_out
            writes.append(accum_out)
        return self.add("act", lambda e: e.activation(out, in_, func, **kw), reads=reads, writes=writes)

    def tt(self, eng, out, in0, in1, op):
        return self.add(eng, lambda e: e.tensor_tensor(out, in0, in1, op), reads=[in0, in1], writes=[out])

    def ts(self, eng, out, in0, s1, s2=None, op0=ALU.mult, op1=None, accum_out=None):
        reads = [in0]
        for s in (s1, s2):
            if s is not None and not isinstance(s, (int, float)):
                reads.append(s)
        writes = [out]
        kw = {}
        if op1 is not None:
            kw["op1"] = op1
        if accum_out is not None:
            kw["accum_out"] = accum_out
            writes.append(accum_out)
        return self.add(eng, lambda e: e.tensor_scalar(out, in0, s1, s2, op0, **kw), reads=reads, writes=writes)

    def stt(self, out, in0, scalar, in1, op0, op1, eng="dve"):
        reads = [in0, in1]
        if not isinstance(scalar, (int, float)):
            reads.append(scalar)
        return self.add(eng, lambda e: e.scalar_tensor_tensor(out, in0, scalar, in1, op0, op1),
                        reads=reads, writes=[out])

    def copy(self, eng, out, in_):
        if eng == "act":
            return self.add("act", lambda e: e.copy(out, in_), reads=[in_], writes=[out])
        return self.add(eng, lambda e: e.tensor_copy(out, in_), reads=[in_], writes=[out])

    def memset(self, eng, out, val):
        return self.add(eng, lambda e: e.memset(out, val), reads=[], writes=[out])

    def recip(self, out, in_):
        return self.add("dve", lambda e: e.reciprocal(out, in_), reads=[in_], writes=[out])

    def reduce(self, out, in_, op, axis=AX.X, eng="dve"):
        return self.add(eng, lambda e: e.tensor_reduce(out, in_, axis, op), reads=[in_], writes=[out])

    def dma(self, out, in_, q="sp"):
        return self.add(q, lambda e: e.dma_start(out=out, in_=in_), reads=[in_], writes=[out], dma=True)


D = 1024
DEPTH = 2
D_IN = 6576
OFF_AQ, OFF_AK, OFF_AV = 0, 512, 640
OFF_BCQ, OFF_BCKV, OFF_BKR = 768, 1152, 1408
OFF_CQKV, OFF_CA, OFF_CB, OFF_CZ, OFF_GATES = 1440, 2976, 2984, 2992, 3504
EPS = 1e-6


class Ctx:
    pass


def build(NTS=16, NSP=2, depth=DEPTH, mixers=("B", "A", "C"), dbg=(), parts=("adaln", "ffn", "final")):
    NTP = NSP * 2
    NT = NTP + NTS
    NTOK = NT * 128
    nc = bass.Bass("TRN2", target_bir_lowering=False)
    K = Ctx()

    def din(name, shape):
        return nc.dram_tensor(name, list(shape), F32, kind="ExternalInput").ap()

    def dout(name, shape):
        return nc.dram_tensor(name, list(shape), F32, kind="ExternalOutput").ap()

    xin = din("xin", [NTOK, D])
    condT = din("condT", [128, 8, 2])
    w_ada = din("w_ada", [depth, D, 6 * D])
    b_adaT = din("b_adaT", [depth, 128, 48])
    b_ada = din("b_ada", [depth, 6 * D])
    norm1T = din("norm1T", [depth, 128, 8])
    norm2T = din("norm2T", [depth, 128, 8])
    final_g = din("final_g", [1, D])
    w_ff1 = din("w_ff1", [depth, D, 4 * D])
    w_ff2 = din("w_ff2", [depth, 4 * D, D])
    cst_ident = din("cst_ident", [128, 128])
    w_branch = din("w_branch", [depth, 3, 512, D])
    w_out = din("w_out", [depth, D, D])
    w_in = din("w_in", [depth, D, D_IN])
    w_in_sw = din("w_in_sw", [depth, D, 672])
    b_w_uq_r = din("b_w_uq_r", [depth, 384, 1024])
    b_w_uq_rs = din("b_w_uq_rs", [depth, 384, 1024])
    b_gqT = din("b_gqT", [depth, 128, 3])
    b_gkv = din("b_gkv", [depth, 256])
    b_w_ukv = din("b_w_ukv", [depth, 256, 1024])
    a_sink = din("a_sink", [depth, 8])
    ropeQB_C = din("ropeQB_C", [128, NTOK])
    ropeQB_S = din("ropeQB_S", [128, NTOK])
    ropeKB_C = din("ropeKB_C", [NTOK, 32])
    ropeKB_S = din("ropeKB_S", [NTOK, 32])
    ropeA_C = din("ropeA_C", [64, NTOK])
    ropeA_S = din("ropeA_S", [64, NTOK])
    cak = din("cak", [depth, 256, 128])
    cav = din("cav", [depth, 256, 128])
    cckv = din("cckv", [depth, 256, 256])
    ckr = din("ckr", [depth, 256, 32])
    cstate = din("cstate", [depth, 2, 4, 128, 128])
    cst_masks = din("cst_masks", [128, 8, 128])
    cst_tri = din("cst_tri", [128, 4, 128])
    cst_trc = din("cst_trc", [128, 2, 2, 128])
    cst_lvN = din("cst_lvN", [128, 7, 2, 128])
    cst_lvA = din("cst_lvA", [128, 7, 2, 128])
    c_convT = din("c_convT", [depth, 128, 12, 3])
    c_a_log = din("c_a_log", [depth, 8])
    c_dt_bias = din("c_dt_bias", [depth, 8])
    c_onorm_g = din("c_onorm_g", [depth, 128])
    nak_out = dout("nak", [NSP, depth, 256, 128])
    nav_out = dout("nav", [NSP, depth, 256, 128])
    nckv_out = dout("nckv", [NSP, depth, 256, 256])
    nkr_out = dout("nkr", [NSP, depth, 256, 32])
    ncs_out = dout("ncs", [NSP, depth, 2, 4, 128, 128])
    y_out = dout("y", [NTOK, D])
    dbg_out = {}
    for name, shape in dbg:
        dbg_out[name] = dout(name, shape)

    with ExitStack() as st:
        P = Prog(nc, st)

        K.uid = 0

        K.phase = "init"
        K.minrem = {}

        def sb(name, shape, dt=F32, stack=st):
            K.uid += 1
            t_ = stack.enter_context(nc.sbuf_tensor("%s_%d" % (name, K.uid), list(shape), dt))
            K.minrem[K.phase] = min(K.minrem.get(K.phase, 1 << 30), nc.sbuf_bytes_remaining)
            return t_

        def psb(name, shape=(128, 512), dt=F32, stack=st):
            K.uid += 1
            return stack.enter_context(nc.psum_tensor("%s_%d" % (name, K.uid), list(shape), dt))

        x = sb("x", [128, NT, D])
        pb = [psb("pb%d" % i) for i in range(7)]
        pbb = psb("pbb", (128, 1024), BF16)
        ident_f = sb("ident_f", [128, 128])
        ident_b = sb("ident_b", [128, 128], BF16)
        condT_sb = sb("condT_sb", [128, 8, 2])
        scT = sb("scT", [128, 8, 2], BF16)
        modT = sb("modT", [128, 48, 2])
        affA = sb("affA", [128, 2, 8, 2])
        nrmT = sb("nrmT", [128, 2, 8])
        ss = sb("ss", [128, NT])
        rstd = sb("rstd", [128, NT])
        junkb = [sb("junk%d" % i, [128, D], BF16) for i in range(2)] * 2
        K.nj = 0
        xnb = [sb("xnb%d" % i, [128, D]) for i in range(2)]

        P.dma(ident_f[:], cst_ident[:, :])
        P.dma(condT_sb[:], condT[:, :, :])
        for t in range(NT):
            P.dma(x[:, t, :], xin[t * 128:(t + 1) * 128, :])
        P.copy("dve", ident_b[:], ident_f[:])
        ones_b = sb("ones_b", [128, 128], BF16)
        ones_f = sb("ones_f", [128, 128])
        P.memset("dve", ones_b[:], 1.0)
        P.memset("dve", ones_f[:], 1.0)
        P.act(scT[:], condT_sb[:], AF.Silu)

        def cond_of(t):
            return 0 if t < NTP else 1

        def calc_rstd(tiles):
            for t in tiles:
                K.nj += 1
                P.act(junkb[K.nj % 2][:], x[:, t, :], AF.Square, accum_out=ss[:, t:t + 1])
            t0, t1 = tiles[0], tiles[-1] + 1
            P.ts("dve", rstd[:, t0:t1], ss[:, t0:t1], 1.0 / D, EPS, op0=ALU.mult, op1=ALU.add)
            P.act(rstd[:, t0:t1], rstd[:, t0:t1], AF.Sqrt)
            P.recip(rstd[:, t0:t1], rstd[:, t0:t1])

        K.ntr = 0
        K.ew = "pool"

        def build_hT(dst, tiles, which):
            for ti, t in enumerate(tiles):
                cd = cond_of(t)
                xn = xnb[K.ntr % 2]
                K.ntr += 1
                P.ts("dve", xn[:], x[:, t, :], rstd[:, t:t + 1], None, op0=ALU.mult)
                for g in range(2):
                    pp = pb[5 + g]
                    for c in range(4):
                        P.tr(pp[:, c * 128:(c + 1) * 128], xn[:, (g * 4 + c) * 128:(g * 4 + c + 1) * 128], ident_f[:])
                    for c in range(4):
                        ch = g * 4 + c
                        a_ap = affA[:, which, ch, cd:cd + 1]
                        b_ap = modT[:, (3 * which) * 8 + ch, cd:cd + 1]
                        o_ap = dst[:, ch, ti * 128:(ti + 1) * 128]
                        if g % 2 == 0:
                            P.act(o_ap, pp[:, c * 128:(c + 1) * 128], AF.Identity, bias=b_ap, scale=a_ap)
                        else:
                            P.ts("dve", o_ap, pp[:, c * 128:(c + 1) * 128], a_ap, b_ap, op0=ALU.mult, op1=ALU.add)

        def make_G(dst, which, cd, ph):
            dg = [sb("dgG%d" % i, [128, 128], F32, ph) for i in range(2)]
            j = 2 + 3 * which
            for c in range(8):
                P.ts("pool", dg[c % 2][:], ident_f[:], modT[:, j * 8 + c, cd:cd + 1], None, op0=ALU.mult)
                pg = pb[c % 2]
                P.mm(pg[:, 0:128], ones_f[:], dg[c % 2][:])
                P.copy("act", dst[:, c * 128:(c + 1) * 128], pg[:, 0:128])

        def adaln(l):
            with ExitStack() as ph:
                wb = [sb("wada%d" % i, [128, 8, D], BF16, ph) for i in range(2)]
                badaT = sb("badaT", [128, 48], F32, ph)
                P.dma(badaT[:], b_adaT[l, :, :])
                P.dma(nrmT[:, 0, :], norm1T[l, :, :])
                P.dma(nrmT[:, 1, :], norm2T[l, :, :])
                for j in range(6):
                    w = wb[j % 2]
                    P.dma(w[:], w_ada[l, :, j * D:(j + 1) * D].rearrange("(c p) n -> p c n", p=128), q="pool")
                    pm = pb[j % 2]
                    for c in range(8):
                        for kc in range(8):
                            P.mm(pm[:, c * 2:c * 2 + 2], w[:, kc, c * 128:(c + 1) * 128], scT[:, kc, :],
                                 start=(kc == 0), stop=(kc == 7))
                    P.tt("dve", modT[:, j * 8:(j + 1) * 8, :], pm[:, 0:16].rearrange("p (c k) -> p c k", k=2),
                         badaT[:, j * 8:(j + 1) * 8].unsqueeze(2).broadcast_to([128, 8, 2]), ALU.add)
                for which in range(2):
                    j = 1 + 3 * which
                    P.ts("dve", affA[:, which, :, :], modT[:, j * 8:(j + 1) * 8, :], 1.0, None, op0=ALU.add)
                    P.tt("dve", affA[:, which, :, :], affA[:, which, :, :],
                         nrmT[:, which, :].unsqueeze(2).broadcast_to([128, 8, 2]), ALU.mult)
                P.fence()

        def ffn(l):
            with ExitStack() as ph:
                h2T = sb("h2T", [128, 8, NTOK], BF16, ph)
                w1b = [sb("w1b%d" % i, [128, 8, 512], BF16, ph) for i in range(2)]
                w2b = [sb("w2b%d" % i, [128, 4, D], BF16, ph) for i in range(2)]
                aTb = [sb("aTb%d" % i, [128, 4, 256], BF16, ph) for i in range(2)]
                rfb = [sb("rfb%d" % i, [128, 256], F32, ph) for i in range(2)]
                tmpb = [sb("tmpb%d" % i, [128, 512], F32, ph) for i in range(2)]
                G2 = sb("G2", [128, 2, D], F32, ph)
                for cd_ in range(2):
                    make_G(G2[:, cd_, :], 1, cd_, ph)

                def load_w(g):
                    P.dma(w1b[g % 2][:], w_ff1[l, :, g * 512:(g + 1) * 512].rearrange("(c p) n -> p c n", p=128), q="pool")
                    P.dma(w2b[g % 2][:], w_ff2[l, g * 512:(g + 1) * 512, :].rearrange("(c p) n -> p c n", p=128), q="pool")

                load_w(0)
                calc_rstd(list(range(NT)))
                build_hT(h2T, list(range(NT)), 1)
                K.cnt = 0
                NG = 8
                items = [(g, blk) for g in range(NG) for blk in range(NT // 2)]

                def ff1(g, blk):
                    w1 = w1b[g % 2]
                    aT = aTb[blk % 2]
                    for fc in range(4):
                        pa = pb[fc % 2]
                        for kc in range(8):
                            P.mm(pa[:, 0:256], w1[:, kc, fc * 128:(fc + 1) * 128],
                                 h2T[:, kc, blk * 256:(blk + 1) * 256], start=(kc == 0), stop=(kc == 7))
                        rf = rfb[fc % 2]
                        P.act(rf[:], pa[:, 0:256], AF.Relu)
                        P.tt("dve" if fc % 2 == 0 else "pool", aT[:, fc, :], rf[:], rf[:], ALU.mult)

                def ff2(g, blk):
                    w2 = w2b[g % 2]
                    aT = aTb[blk % 2]
                    for tt_ in range(2):
                        t = blk * 2 + tt_
                        cd = cond_of(t)
                        for half in range(2):
                            py = pb[2 + tt_ * 2 + half]
                            for fc in range(4):
                                P.mm(py[:], aT[:, fc, tt_ * 128:(tt_ + 1) * 128],
                                     w2[:, fc, half * 512:(half + 1) * 512], start=(fc == 0), stop=(fc == 3))
                            if cd == 0:
                                tmp = tmpb[K.cnt % 2]
                                K.cnt += 1
                                P.tt("dve", tmp[:], py[:], G2[:, cd, half * 512:(half + 1) * 512], ALU.mult)
                                P.tt(K.ew, x[:, t, half * 512:(half + 1) * 512],
                                     x[:, t, half * 512:(half + 1) * 512], tmp[:], ALU.add)
                            else:
                                P.tt("dve", x[:, t, half * 512:(half + 1) * 512],
                                     x[:, t, half * 512:(half + 1) * 512], py[:], ALU.add)
                    if blk == NTP // 2 - 1:
                        P.tt("pool", w2[:], w2[:], G2[:, 1, :].unsqueeze(1).broadcast_to([128, 4, D]), ALU.mult)

                ff1(*items[0])
                for n, (g, blk) in enumerate(items):
                    if blk == 0 and g + 1 < NG:
                        load_w(g + 1)
                    if n + 1 < len(items):
                        ff1(*items[n + 1])
                    ff2(g, blk)
                P.fence()

        def final_norm():
            with ExitStack() as ph:
                gf = sb("gf", [128, D], F32, ph)
                ob = [sb("ob%d" % i, [128, D], F32, ph) for i in range(2)]
                P.dma(gf[:], final_g.partition_broadcast(128) if False else final_g[0:1, :].broadcast_to([128, D]))
                calc_rstd(list(range(NT)))
                for t in range(NT):
                    o = ob[t % 2]
                    P.stt(o[:], x[:, t, :], rstd[:, t:t + 1], gf[:], ALU.mult, ALU.mult)
                    P.dma(y_out[t * 128:(t + 1) * 128, :], o[:])
                P.fence()


        units = [
            dict(name="P", tiles=list(range(0, NTP)), cond=0, prompt=True,
                 seqs=[dict(t0=2 * i, nt=2, ctx=False, idx=i) for i in range(NSP)]),
            dict(name="S", tiles=list(range(NTP, NT)), cond=1, prompt=False,
                 seqs=[dict(t0=NTP, nt=NTS, ctx=True, idx=0)]),
        ]
        K.rr = 0

        def blocks_of(unit, bs=512):
            out = []
            for sq in unit["seqs"]:
                n = sq["nt"] * 128
                o = 0
                while o < n:
                    b = min(bs, n - o)
                    out.append((sq, sq["t0"] * 128 + o, b))
                    o += b
            return out

        def attend(KT, V1, QT, nq, negm, key_tiles, obanks, sbanks, PTb, scale, tail=None):
            nk = len(key_tiles)
            slots = []

            def scores(i):
                kt, mask = key_tiles[i]
                sp_ = sbanks[K.rr % len(sbanks)]
                pt_ = PTb[K.rr % len(PTb)]
                K.rr += 1
                P.mm(sp_[:, 0:nq], KT(kt), QT, start=True, stop=(mask is None and negm is None))
                if mask is not None:
                    P.mm(sp_[:, 0:nq], ident_b[:], mask, start=False, stop=(negm is None))
                if negm is not None:
                    P.mm(sp_[:, 0:nq], ones_b[0:1, 0:128], negm, start=False, stop=True)
                slots.append((sp_, pt_))

            scores(0)
            for i, (kt, mask) in enumerate(key_tiles):
                if i + 1 < nk:
                    scores(i + 1)
                sp_, pt_ = slots[i]
                P.act(pt_[:, 0:nq], sp_[:, 0:nq], AF.Exp, scale=scale)
                for qi, ob in enumerate(obanks):
                    last = (i == nk - 1) and tail is None
                    P.mm(ob, pt_[:, qi * 128:(qi + 1) * 128], V1(kt), start=(i == 0), stop=last)
            if tail is not None:
                tail()

        def mixer_B(l, unit, obT):
            with ExitStack() as ph:
                tiles = unit["tiles"]
                ntok_u = len(tiles) * 128
                tok_base = tiles[0] * 128
                has_ctx = any(sq["ctx"] for sq in unit["seqs"])
                maxkeys = max(sq["nt"] * 128 + (256 if sq["ctx"] else 0) for sq in unit["seqs"])
                maxnt = max(sq["nt"] for sq in unit["seqs"])
                nkeys_u = ntok_u + (256 if has_ctx else 0)
                cqnT = sb("cqnT", [128, 3, ntok_u], BF16, ph)
                ckvT = sb("ckvT", [128, 2, nkeys_u], BF16, ph)
                krT = sb("krT", [32, nkeys_u], BF16, ph)
                scale = 96.0 ** -0.5
                b1 = ExitStack()
                wB = sb("wB", [128, 8, 704], BF16, b1)
                gkvb = sb("gkvb", [128, 256], F32, b1)
                P.dma(wB[:, :, 0:672], w_in[l, :, OFF_BCQ:OFF_BCQ + 672].rearrange("(c p) n -> p c n", p=128), q="pool")
                P.dma(wB[:, :, 672:704], w_in_sw[l, :, 640:672].rearrange("(c p) n -> p c n", p=128), q="pool")
                P.dma(gkvb[:], b_gkv[l:l + 1, :].broadcast_to([128, 256]))
                hTb = [sb("hTbB%d" % i, [128, 8, 256], BF16, b1) for i in range(2)]
                ssb = [sb("ssB%d" % i, [128, 8], F32, b1) for i in range(2)]
                jk = [sb("jkB%d" % i, [128, 384], F32, b1) for i in range(2)]
                cqn_b = [sb("cqnb%d" % i, [128, 384], BF16, b1) for i in range(2)]
                ckvn_f = [sb("ckvnf%d" % i, [128, 256], F32, b1) for i in range(2)]
                ckvn_b = [sb("ckvnb%d" % i, [128, 256], BF16, b1) for i in range(2)]
                kr_f = [sb("krf%d" % i, [128, 64], F32, b1) for i in range(2)]
                kr_t = [sb("krt%d" % i, [128, 64], F32, b1) for i in range(2)]
                kr_b = [sb("krb%d" % i, [128, 32], BF16, b1) for i in range(2)]
                rKC = [sb("rKC%d" % i, [128, 32], F32, b1) for i in range(2)]
                rKS = [sb("rKS%d" % i, [128, 32], F32, b1) for i in range(2)]

                n_t = 0
                for bi, (sq, tok0, nb) in enumerate(blocks_of(unit, 256)):
                    hT = hTb[bi % 2]
                    btiles = [tok0 // 128 + i for i in range(nb // 128)]
                    build_hT(hT, btiles, 0)
                    for ti, t in enumerate(btiles):
                        r = n_t % 2
                        n_t += 1
                        sv = ssb[r]
                        p1, p2 = pb[0], pb[1]
                        for kc in range(8):
                            P.mm(p1[:, 0:384], hT[:, kc, ti * 128:(ti + 1) * 128], wB[:, kc, 0:384], start=(kc == 0), stop=(kc == 7))
                        for kc in range(8):
                            P.mm(p2[:, 0:320], hT[:, kc, ti * 128:(ti + 1) * 128], wB[:, kc, 384:704], start=(kc == 0), stop=(kc == 7))
                        P.act(jk[r][:, 0:384], p1[:, 0:384], AF.Square, accum_out=sv[:, 0:1])
                        P.ts("dve", sv[:, 1:2], sv[:, 0:1], 1.0 / 384, EPS, op0=ALU.mult, op1=ALU.add)
                        P.act(sv[:, 2:3], sv[:, 1:2], AF.Sqrt)
                        P.recip(sv[:, 3:4], sv[:, 2:3])
                        P.act(cqn_b[r][:], p1[:, 0:384], AF.Copy, scale=sv[:, 3:4])
                        P.act(jk[r][:, 0:256], p2[:, 0:256], AF.Square, accum_out=sv[:, 4:5])
                        P.ts("dve", sv[:, 5:6], sv[:, 4:5], 1.0 / 256, EPS, op0=ALU.mult, op1=ALU.add)
                        P.act(sv[:, 6:7], sv[:, 5:6], AF.Sqrt)
                        P.recip(sv[:, 7:8], sv[:, 6:7])
                        P.act(kr_f[r][:], p2[:, 256:320], AF.Copy)
                        P.act(ckvn_f[r][:], p2[:, 0:256], AF.Copy, scale=sv[:, 7:8])
                        P.tt("dve", ckvn_f[r][:], ckvn_f[r][:], gkvb[:], ALU.mult)
                        P.copy("dve", ckvn_b[r][:], ckvn_f[r][:])
                        lt = t - sq["t0"]
                        if unit["prompt"]:
                            P.dma(nckv_out[sq["idx"], l, lt * 128:(lt + 1) * 128, :], ckvn_f[r][:])
                            P.dma(nkr_out[sq["idx"], l, lt * 128:(lt + 1) * 128, :], kr_f[r][:, 0:32])
                        P.dma(rKC[r][:], ropeKB_C[t * 128:(t + 1) * 128, :])
                        P.dma(rKS[r][:], ropeKB_S[t * 128:(t + 1) * 128, :])
                        P.tt("dve", kr_t[r][:, 0:32], kr_f[r][:, 0:32], rKC[r][:], ALU.mult)
                        P.tt("dve", kr_t[r][:, 32:64], kr_f[r][:, 32:64], rKS[r][:], ALU.mult)
                        P.tt("dve", kr_b[r][:], kr_t[r][:, 0:32], kr_t[r][:, 32:64], ALU.add)
                        for c in range(3):
                            P.tr(pbb[:, c * 128:(c + 1) * 128], cqn_b[r][:, c * 128:(c + 1) * 128], ident_b[:])
                        for c in range(2):
                            P.tr(pbb[:, (3 + c) * 128:(4 + c) * 128], ckvn_b[r][:, c * 128:(c + 1) * 128], ident_b[:])
                        P.tr(pbb[0:32, 640:768], kr_b[r][:], ident_b[:])
                        ut = t - tiles[0]
                        P.copy("dve", cqnT[:, :, ut * 128:(ut + 1) * 128], pbb[:, 0:384].rearrange("p (c n) -> p c n", c=3))
                        P.copy("dve", ckvT[:, :, ut * 128:(ut + 1) * 128], pbb[:, 384:640].rearrange("p (c n) -> p c n", c=2))
                        P.copy("dve", krT[:, ut * 128:(ut + 1) * 128], pbb[0:32, 640:768])

                P.fence()
                b1.close()
                wuq = sb("wuq", [128, 3, 1024], BF16, ph)
                wuqs = sb("wuqs", [128, 3, 1024], BF16, ph)
                wukA = sb("wukA", [128, 2, 8, 128], BF16, ph)
                wukV = sb("wukV", [128, 2, 512], BF16, ph)
                E32 = sb("E32", [32, 128], BF16, ph)
                gqT = sb("gqT", [128, 3], F32, ph)
                P.dma(wuq[:], b_w_uq_r[l].rearrange("(c p) n -> p c n", p=128), q="pool")
                P.dma(wuqs[:], b_w_uq_rs[l].rearrange("(c p) n -> p c n", p=128), q="pool")
                P.dma(gqT[:], b_gqT[l, :, :])
                for c in range(3):
                    P.ts("pool", wuq[:, c, :], wuq[:, c, :], gqT[:, c:c + 1], None, op0=ALU.mult)
                    P.ts("pool", wuqs[:, c, :], wuqs[:, c, :], gqT[:, c:c + 1], None, op0=ALU.mult)
                P.memset("pool", wukA[:], 0.0)
                P.memset("pool", E32[:], 0.0)
                P.copy("pool", E32[:, 32:64], ident_b[0:32, 0:32])
                with ExitStack() as tw:
                    wukv = sb("wukv", [128, 2, 1024], BF16, tw)
                    P.dma(wukv[:], b_w_ukv[l].rearrange("(c p) n -> p c n", p=128), q="pool")
                    wv = wukv[:].rearrange("p c (h e) -> p c h e", e=128)
                    for c in range(2):
                        P.copy("pool", wukA[:, c, :, 64:128], wv[:, c, :, 0:64])
                        P.copy("pool", wukV[:, c, :].rearrange("p (h e) -> p h e", e=64), wv[:, c, :, 64:128])
                    P.fence()
                KTh = sb("KTh", [128, maxkeys], BF16, ph)
                ksq = sb("ksq", [128, 512], BF16, ph)
                V1 = sb("V1", [128, maxkeys // 128, 2, 65], BF16, ph)
                P.memset("pool", V1[:, :, :, 64:65], 1.0)
                rQC = [sb("rQC%d" % i, [128, 512], F32, ph) for i in range(1)] * 2
                rQS = [sb("rQS%d" % i, [128, 512], F32, ph) for i in range(1)] * 2
                qt1 = [sb("qt1_%d" % i, [128, 512], F32, ph) for i in range(1)] * 2
                qt2 = [sb("qt2_%d" % i, [128, 512], F32, ph) for i in range(1)] * 2
                QT = [sb("QT%d" % i, [128, 512], BF16, ph) for i in range(2)]
                qsq = [sb("qsq%d" % i, [128, 512], BF16, ph) for i in range(1)] * 2
                sqr = [sb("sqr%d" % i, [1, 512], F32, ph) for i in range(1)] * 2
                kmx = sb("kmx", [1, 8], F32, ph)
                kmax = sb("kmax", [1, 1], F32, ph)
                PTb = [sb("PTb%d" % i, [128, 512], BF16, ph) for i in range(2)]
                otok = sb("otok", [128, maxnt, 128], BF16, ph)
                rden = [sb("rden%d" % i, [128, 1], F32, ph) for i in range(4)]
                ctx_b = sb("ctx_b", [128, 2, 288], BF16, ph)

                for sq in unit["seqs"]:
                    nk_own = sq["nt"] * 128
                    nkeys = nk_own + (256 if sq["ctx"] else 0)
                    nkt = nkeys // 128
                    kb = sq["t0"] * 128 - tok_base
                    if sq["ctx"]:
                        for c2 in range(2):
                            P.dma(ctx_b[:, c2, 0:256], cckv[l, c2 * 128:(c2 + 1) * 128, :], q="pool")
                            P.dma(ctx_b[:, c2, 256:288], ckr[l, c2 * 128:(c2 + 1) * 128, :], q="pool")
                        for c2 in range(2):
                            for c in range(2):
                                P.tr(pbb[:, c * 128:(c + 1) * 128], ctx_b[:, c2, c * 128:(c + 1) * 128], ident_b[:])
                            P.tr(pbb[0:32, 256:384], ctx_b[:, c2, 256:288], ident_b[:])
                            k0 = kb + nk_own + c2 * 128
                            P.copy("dve", ckvT[:, :, k0:k0 + 128], pbb[:, 0:256].rearrange("p (c n) -> p c n", c=2))
                            P.copy("dve", krT[:, k0:k0 + 128], pbb[0:32, 256:384])
                    qblocks = [(o, min(512, nk_own - o)) for o in range(0, nk_own, 512)]
                    for h in range(8):
                        if h % 2 == 0:
                            for kt in range(nkt):
                                pv = pb[kt % 2]
                                for c in range(2):
                                    P.mm(pv[:, 0:128], ckvT[:, c, kb + kt * 128:kb + (kt + 1) * 128], wukV[:, c, h * 64:(h + 2) * 64], start=(c == 0), stop=(c == 1))
                                P.copy("act", V1[:, kt, :, 0:64], pv[:, 0:128].rearrange("p (h e) -> p h e", e=64))
                        for k0 in range(0, nkeys, 512):
                            kn = min(512, nkeys - k0)
                            pk = pb[4 + (k0 // 512) % 2]
                            for c in range(2):
                                P.mm(pk[:, 0:kn], wukA[:, c, h, :], ckvT[:, c, kb + k0:kb + k0 + kn], start=(c == 0), stop=False)
                            P.mm(pk[:, 0:kn], E32[:], krT[:, kb + k0:kb + k0 + kn], start=False, stop=True)
                            P.copy("act", KTh[:, k0:k0 + kn], pk[:, 0:kn])
                            P.tt("pool", ksq[:, 0:kn], KTh[:, k0:k0 + kn], KTh[:, k0:k0 + kn], ALU.mult)
                            P.memset("dve", KTh[0:1, k0:k0 + kn], 1.0)
                            pn = pb[6]
                            P.mm(pn[0:1, 0:kn], ones_b[:, 0:1], ksq[:, 0:kn])
                            P.reduce(kmx[:, k0 // 512:k0 // 512 + 1], pn[0:1, 0:kn], ALU.max)
                        nkb = (nkeys + 511) // 512
                        P.reduce(kmax[:], kmx[:, 0:nkb], ALU.max)
                        P.act(kmax[:], kmax[:], AF.Sqrt)
                        P.ts("dve", kmax[:], kmax[:], -1.0, None, op0=ALU.mult)
                        def prepQ(qi, h=h):
                            qo, nq = qblocks[qi]
                            r = (h * len(qblocks) + qi) % 2
                            g0 = sq["t0"] * 128 + qo
                            u0 = g0 - tok_base
                            P.dma(rQC[r][:, 0:nq], ropeQB_C[:, g0:g0 + nq])
                            P.dma(rQS[r][:, 0:nq], ropeQB_S[:, g0:g0 + nq])
                            pq = pb[6]
                            for c in range(3):
                                P.mm(pq[:, 0:nq], wuq[:, c, h * 128:(h + 1) * 128], cqnT[:, c, u0:u0 + nq], start=(c == 0), stop=(c == 2))
                            P.tt("dve", qt1[r][:, 0:nq], pq[:, 0:nq], rQC[r][:, 0:nq], ALU.mult)
                            for c in range(3):
                                P.mm(pq[:, 0:nq], wuqs[:, c, h * 128:(h + 1) * 128], cqnT[:, c, u0:u0 + nq], start=(c == 0), stop=(c == 2))
                            P.tt("dve", qt2[r][:, 0:nq], pq[:, 0:nq], rQS[r][:, 0:nq], ALU.mult)
                            P.tt("pool", QT[r][:, 0:nq], qt1[r][:, 0:nq], qt2[r][:, 0:nq], ALU.add)
                            P.tt("pool", qsq[r][:, 0:nq], QT[r][:, 0:nq], QT[r][:, 0:nq], ALU.mult)
                            P.mm(pq[0:1, 0:nq], ones_b[:, 0:1], qsq[r][:, 0:nq])
                            P.act(sqr[r][:, 0:nq], pq[0:1, 0:nq], AF.Ln, bias=1e-30)
                            P.act(sqr[r][:, 0:nq], sqr[r][:, 0:nq], AF.Exp, scale=0.5)
                            P.ts("dve", QT[r][0:1, 0:nq], sqr[r][:, 0:nq], kmax[0:1, 0:1], None, op0=ALU.mult)

                        prepQ(0)
                        for qi, (qo, nq) in enumerate(qblocks):
                            r = (h * len(qblocks) + qi) % 2
                            if qi + 1 < len(qblocks):
                                prepQ(qi + 1)
                            nqt = nq // 128
                            obanks = [pb[qq][:, 0:65] for qq in range(nqt)]
                            attend(lambda kt: KTh[:, kt * 128:(kt + 1) * 128], lambda kt: V1[:, kt, h % 2, :],
                                   QT[r][:, 0:nq], nq, None, [(kt, None) for kt in range(nkt)],
                                   obanks, [pb[4], pb[5]], PTb, scale)
                            for qq in range(nqt):
                                lt = qo // 128 + qq
                                P.recip(rden[qq][:], pb[qq][:, 64:65])
                                P.ts("dve", otok[:, lt, (h % 2) * 64:(h % 2 + 1) * 64], pb[qq][:, 0:64],
                                     rden[qq][:, 0:1], None, op0=ALU.mult)
                        if h % 2 == 1:
                            for lt0 in range(0, sq["nt"], 8):
                                n8 = min(8, sq["nt"] - lt0)
                                for j in range(n8):
                                    P.tr(pbb[:, j * 128:(j + 1) * 128], otok[:, lt0 + j, :], ident_b[:])
                                uo = kb + lt0 * 128
                                P.copy("act", obT[:, h // 2, uo:uo + n8 * 128], pbb[:, 0:n8 * 128])
                P.fence()

        def mixer_A(l, unit, oaT):
            with ExitStack() as ph:
                tiles = unit["tiles"]
                ntok_u = len(tiles) * 128
                tok_base = tiles[0] * 128
                has_ctx = any(sq["ctx"] for sq in unit["seqs"])
                nkeys_u = ntok_u + (256 if has_ctx else 0)
                scale = 64.0 ** -0.5
                KTa = sb("KTa", [64, 2, nkeys_u], BF16, ph)
                Va = sb("Va", [128, nkeys_u // 128, 2, 65], BF16, ph)
                P.memset("pool", Va[:, :, :, 64:65], 1.0)
                rC = sb("rAC", [64, 256], F32, ph)
                rS = sb("rAS", [64, 256], F32, ph)
                t1 = sb("tA1", [64, 256], F32, ph)
                t2 = sb("tA2", [64, 256], F32, ph)
                hTb = [sb("hTbA0", [128, 8, 256], BF16, ph)]

                with ExitStack() as a1:
                    hTb.append(sb("hTbA1", [128, 8, 256], BF16, a1))
                    wAk = sb("wAk", [128, 8, 384], BF16, a1)
                    P.dma(wAk[:, :, 0:256], w_in[l, :, OFF_AK:OFF_AK + 256].rearrange("(c p) n -> p c n", p=128), q="pool")
                    P.dma(wAk[:, :, 256:384], w_in_sw[l, :, 512:640].rearrange("(c p) n -> p c n", p=128), q="pool")
                    kvf = [sb("kvf%d" % i, [128, 256], F32, a1) for i in range(2)]
                    for bi, (sq, tok0, nb) in enumerate(blocks_of(unit, 256)):
                        hT = hTb[bi % 2]
                        btiles = [tok0 // 128 + i for i in range(nb // 128)]
                        build_hT(hT, btiles, 0)
                        u0 = tok0 - tok_base
                        P.dma(rC[:, 0:nb], ropeA_C[:, tok0:tok0 + nb])
                        P.dma(rS[:, 0:nb], ropeA_S[:, tok0:tok0 + nb])
                        for g in range(2):
                            p1, p2 = pb[(2 * g) % 4], pb[(2 * g + 1) % 4]
                            for kc in range(8):
                                P.mm(p1[0:64, 0:nb], wAk[:, kc, g * 64:(g + 1) * 64], hT[:, kc, 0:nb], start=(kc == 0), stop=(kc == 7))
                            P.tt("dve", t1[:, 0:nb], p1[0:64, 0:nb], rC[:, 0:nb], ALU.mult)
                            for kc in range(8):
                                P.mm(p2[0:64, 0:nb], wAk[:, kc, 256 + g * 64:256 + (g + 1) * 64], hT[:, kc, 0:nb], start=(kc == 0), stop=(kc == 7))
                            P.tt("dve", t2[:, 0:nb], p2[0:64, 0:nb], rS[:, 0:nb], ALU.mult)
                            P.tt("pool", KTa[:, g, u0:u0 + nb], t1[:, 0:nb], t2[:, 0:nb], ALU.add)
                        for ti, t in enumerate(btiles):
                            pv = pb[4 + ti % 2]
                            for kc in range(8):
                                P.mm(pv[:, 0:256], hT[:, kc, ti * 128:(ti + 1) * 128], wAk[:, kc, 0:256], start=(kc == 0), stop=(kc == 7))
                            ut = t - tiles[0]
                            P.copy("act", Va[:, ut, :, 0:64], pv[:, 128:256].rearrange("p (g e) -> p g e", e=64))
                            if unit["prompt"]:
                                lt = t - sq["t0"]
                                kv = kvf[ti % 2]
                                P.copy("act", kv[:], pv[:, 0:256])
                                P.dma(nak_out[sq["idx"], l, lt * 128:(lt + 1) * 128, :], kv[:, 0:128])
                                P.dma(nav_out[sq["idx"], l, lt * 128:(lt + 1) * 128, :], kv[:, 128:256])
                    if has_ctx:
                        ctxkb = sb("ctxkb", [128, 2, 128], BF16, a1)
                        ctxvb = sb("ctxvb", [128, 2, 128], BF16, a1)
                        for c2 in range(2):
                            P.dma(ctxkb[:, c2, :], cak[l, c2 * 128:(c2 + 1) * 128, :], q="pool")
                            P.dma(ctxvb[:, c2, :], cav[l, c2 * 128:(c2 + 1) * 128, :], q="pool")
                        for c2 in range(2):
                            for g in range(2):
                                P.tr(pbb[0:64, (c2 * 2 + g) * 128:(c2 * 2 + g + 1) * 128], ctxkb[:, c2, g * 64:(g + 1) * 64], ident_b[:])
                            P.copy("dve", Va[:, ntok_u // 128 + c2, :, 0:64], ctxvb[:, c2, :].rearrange("p (g e) -> p g e", e=64))
                        for c2 in range(2):
                            for g in range(2):
                                P.copy("act", KTa[:, g, ntok_u + c2 * 128:ntok_u + (c2 + 1) * 128],
                                       pbb[0:64, (c2 * 2 + g) * 128:(c2 * 2 + g + 1) * 128])
                    P.fence()

                wAq = sb("wAq", [128, 8, 512], BF16, ph)
                wAqs = sb("wAqs", [128, 8, 512], BF16, ph)
                P.dma(wAq[:], w_in[l, :, 0:512].rearrange("(c p) n -> p c n", p=128), q="pool")
                P.dma(wAqs[:], w_in_sw[l, :, 0:512].rearrange("(c p) n -> p c n", p=128), q="pool")
                maskb = sb("maskb", [128, 8, 128], BF16, ph)
                P.dma(maskb[:], cst_masks[:, :, :], q="pool")
                sinkf = sb("sinkf", [1, 8], F32, ph)
                vsink = sb("vsink", [1, 65], BF16, ph)
                P.dma(sinkf[:], a_sink[l:l + 1, :])
                P.ts("dve", sinkf[:], sinkf[:], 1.0 / scale, None, op0=ALU.mult)
                P.memset("dve", vsink[:], 0.0)
                P.memset("dve", vsink[:, 64:65], 1.0)
                QTb = [sb("QTb%d" % i, [64, 8, 256], BF16, ph) for i in range(2)]
                ksqa = sb("ksqa", [64, 512], BF16, ph)
                qsqa = sb("qsqa", [64, 4, 128], BF16, ph)
                kmx = sb("kmxA", [1, 2, 8], F32, ph)
                kmax = sb("kmaxA", [1, 2], F32, ph)
                sqr = sb("sqrA", [1, 512], F32, ph)
                negf = sqr
                negm = [sb("negmA%d" % i, [1, 512], BF16, ph) for i in range(2)]
                srow = sb("srowA", [1, 512], F32, ph)
                psink = [sb("psinkA%d" % i, [1, 512], BF16, ph) for i in range(2)]
                PTb = [sb("PTbA%d" % i, [128, 512], BF16, ph) for i in range(2)]
                otok = [sb("otokA%d" % i, [128, 512], BF16, ph) for i in range(2)]
                rden = [sb("rdenA%d" % i, [128, 1], F32, ph) for i in range(4)]
                nq_ = 0
                nblk = 0
                for sq in unit["seqs"]:
                    kb = sq["t0"] * 128 - tok_base
                    nk_own = sq["nt"] * 128
                    kranges = [(kb, nk_own)] + ([(ntok_u, 256)] if sq["ctx"] else [])
                    for g in range(2):
                        nb_ = 0
                        for (k0, kn_all) in kranges:
                            for o in range(0, kn_all, 512):
                                kn = min(512, kn_all - o)
                                P.tt("pool", ksqa[:, 0:kn], KTa[:, g, k0 + o:k0 + o + kn], KTa[:, g, k0 + o:k0 + o + kn], ALU.mult)
                                P.mm(pb[6][0:1, 0:kn], ones_b[0:64, 0:1], ksqa[:, 0:kn])
                                P.reduce(kmx[:, g, nb_:nb_ + 1], pb[6][0:1, 0:kn], ALU.max)
                                nb_ += 1
                        P.reduce(kmax[:, g:g + 1], kmx[:, g, 0:nb_], ALU.max)
                    P.act(kmax[:], kmax[:], AF.Sqrt)
                    P.ts("dve", kmax[:], kmax[:], -1.0, None, op0=ALU.mult)
                    blocks = [(q0, min(2, sq["nt"] - q0)) for q0 in range(0, sq["nt"], 2)]

                    def qproj(b):
                        q0, nbt = blocks[b]
                        nb = nbt * 128
                        tok0 = (sq["t0"] + q0) * 128
                        hT = hTb[0]
                        QT = QTb[b % 2]
                        build_hT(hT, [sq["t0"] + q0 + i for i in range(nbt)], 0)
                        P.dma(rC[:, 0:nb], ropeA_C[:, tok0:tok0 + nb])
                        P.dma(rS[:, 0:nb], ropeA_S[:, tok0:tok0 + nb])
                        for h in range(8):
                            p1 = pb[6]
                            for kc in range(8):
                                P.mm(p1[0:64, 0:nb], wAq[:, kc, h * 64:(h + 1) * 64], hT[:, kc, 0:nb], start=(kc == 0), stop=(kc == 7))
                            for kc in range(8):
                                P.mm(p1[0:64, 256:256 + nb], wAqs[:, kc, h * 64:(h + 1) * 64], hT[:, kc, 0:nb], start=(kc == 0), stop=(kc == 7))
                            P.tt("dve", t1[:, 0:nb], p1[0:64, 0:nb], rC[:, 0:nb], ALU.mult)
                            P.tt("dve", t2[:, 0:nb], p1[0:64, 256:256 + nb], rS[:, 0:nb], ALU.mult)
                            P.tt("pool", QT[:, h, 0:nb], t1[:, 0:nb], t2[:, 0:nb], ALU.add)

                    items = [(b, qi, g) for b in range(len(blocks)) for qi in range(blocks[b][1]) for g in range(2)]

                    def prep_item(n):
                        b, qi, g = items[n]
                        r = n % 2
                        QTg = QTb[b % 2][:, 4 * g:4 * g + 4, qi * 128:(qi + 1) * 128]
                        P.tt("pool", qsqa[:], QTg, QTg, ALU.mult)
                        P.mm(pb[6][0:1, 0:512], ones_b[0:64, 0:1], qsqa[:])
                        P.act(sqr[:], pb[6][0:1, 0:512], AF.Ln, bias=1e-30)
                        P.act(sqr[:], sqr[:], AF.Exp, scale=0.5)
                        P.ts("dve", negf[:], sqr[:], kmax[0:1, g:g + 1], None, op0=ALU.mult)
                        P.copy("dve", negm[r][:], negf[:])
                        for hh in range(4):
                            P.ts("dve", srow[:, hh * 128:(hh + 1) * 128], negf[:, hh * 128:(hh + 1) * 128],
                                 sinkf[0:1, 4 * g + hh:4 * g + hh + 1], None, op0=ALU.add)
                        P.act(psink[r][:], srow[:], AF.Exp, scale=scale)

                    qproj(0)
                    prep_item(0)
                    for n, (b, qi, g) in enumerate(items):
                        r = n % 2
                        q0 = blocks[b][0]
                        qt = q0 + qi
                        uq = kb + qt * 128
                        ot = otok[qt % 2]
                        if qi == 0 and g == 0 and b + 1 < len(blocks):
                            qproj(b + 1)
                        if n + 1 < len(items):
                            prep_item(n + 1)
                        QTg = QTb[b % 2][:, 4 * g:4 * g + 4, qi * 128:(qi + 1) * 128]
                        if sq["ctx"]:
                            kts = []
                            if qt > 0:
                                kts.append((kb // 128 + qt - 1, maskb[:, 0:4, :]))
                            kts.append((kb // 128 + qt, None))
                            if qt + 1 < sq["nt"]:
                                kts.append((kb // 128 + qt + 1, maskb[:, 4:8, :]))
                            kts += [(ntok_u // 128, None), (ntok_u // 128 + 1, None)]
                        else:
                            kts = [(kb // 128 + j, None) for j in range(sq["nt"])]
                        obanks = [pb[hh][:, 0:65] for hh in range(4)]

                        def tail(r=r):
                            for hh in range(4):
                                P.mm(pb[hh][:, 0:65], psink[r][0:1, hh * 128:(hh + 1) * 128], vsink[:], start=False, stop=True)

                        attend(lambda kt: KTa[:, g, kt * 128:(kt + 1) * 128], lambda kt: Va[:, kt, g, :],
                               QTg, 512, negm[r][:], kts, obanks, [pb[4], pb[5]], PTb, scale, tail=tail)
                        for hh in range(4):
                            h = 4 * g + hh
                            P.recip(rden[hh][:], pb[hh][:, 64:65])
                            P.ts("dve", ot[:, h * 64:(h + 1) * 64], pb[hh][:, 0:64], rden[hh][:, 0:1], None, op0=ALU.mult)
                        if g == 1:
                            for c in range(4):
                                P.tr(pbb[:, c * 128:(c + 1) * 128], ot[:, c * 128:(c + 1) * 128], ident_b[:])
                            P.copy("act", oaT[:, :, uq:uq + 128], pbb[:, 0:512].rearrange("p (c n) -> p c n", c=4))
                P.fence()

        def mixer_C(l, unit, ocT):
            with ExitStack() as ph:
                tiles = unit["tiles"]
                ntok_u = len(tiles) * 128
                nt_u = len(tiles)
                tok_base = tiles[0] * 128
                trc = sb("trc", [128, 2, 2, 128], BF16, ph)
                trif = sb("trif", [128, 2, 128], F32, ph)
                lvN = sb("lvN", [128, 7, 2, 128], BF16, ph)
                lvA = sb("lvA", [128, 7, 2, 128], BF16, ph)
                convw = sb("convw", [128, 12, 3], F32, ph)
                alog = sb("alog", [128, 8], F32, ph)
                dtb = sb("dtb", [128, 8], F32, ph)
                ong = sb("ong", [128, 128], F32, ph)
                wab = sb("wab", [128, 8, 16], BF16, ph)
                gb = sb("gb", [128, nt_u, 16], F32, ph)
                P.dma(trc[:], cst_trc[:, :, :, :], q="pool")
                P.dma(lvN[:], cst_lvN[:, :, :, :], q="pool")
                P.dma(lvA[:], cst_lvA[:, :, :, :], q="pool")
                P.dma(trif[:, 0, :], cst_tri[:, 1, :])
                P.dma(trif[:, 1, :], cst_tri[:, 3, :])
                P.dma(convw[:], c_convT[l, :, :, :])
                P.dma(alog[:], c_a_log[l:l + 1, :].broadcast_to([128, 8]))
                P.dma(dtb[:], c_dt_bias[l:l + 1, :].broadcast_to([128, 8]))
                P.dma(ong[:], c_onorm_g[l:l + 1, :].broadcast_to([128, 128]))
                P.dma(wab[:], w_in[l, :, OFF_CA:OFF_CA + 16].rearrange("(c p) n -> p c n", p=128), q="pool")
                P.act(alog[:], alog[:], AF.Exp)
                P.ts("dve", alog[:], alog[:], -1.0, None, op0=ALU.mult)
                sm = [sb("smC%d" % i, [128, 8], F32, ph) for i in range(6)]

                for h in range(4):
                    with ExitStack() as hs:
                        qkvT = sb("qkvT", [128, 3, ntok_u], BF16, hs)
                        zs = sb("zs", [128, nt_u, 128], F32, hs)
                        oacc = sb("oacc", [128, nt_u, 128], F32, hs)
                        P.memset("pool", oacc[:], 0.0)
                        with ExitStack() as ps_:
                            wC = sb("wC", [128, 8, 512], BF16, ps_)
                            for j in range(3):
                                c0 = OFF_CQKV + j * 512 + h * 128
                                P.dma(wC[:, :, j * 128:(j + 1) * 128], w_in[l, :, c0:c0 + 128].rearrange("(c p) n -> p c n", p=128), q="pool")
                            P.dma(wC[:, :, 384:512], w_in[l, :, OFF_CZ + h * 128:OFF_CZ + (h + 1) * 128].rearrange("(c p) n -> p c n", p=128), q="pool")
                            hTb = [sb("hTbC%d" % i, [128, 8, 512], BF16, ps_) for i in range(2)]
                            rawT = sb("rawT", [128, 3, ntok_u], F32, ps_)
                            cv = [sb("cvC%d" % i, [128, 512], F32, ps_) for i in range(2)]
                            zf = [sb("zfC%d" % i, [128, 128], F32, ps_) for i in range(2)]
                            for bi, (sq, tok0, nb) in enumerate(blocks_of(unit)):
                                hT = hTb[bi % 2]
                                btiles = [tok0 // 128 + i for i in range(nb // 128)]
                                build_hT(hT, btiles, 0)
                                u0 = tok0 - tok_base
                                for j in range(3):
                                    pj = pb[j % 2]
                                    for kc in range(8):
                                        P.mm(pj[:, 0:nb], wC[:, kc, j * 128:(j + 1) * 128], hT[:, kc, 0:nb], start=(kc == 0), stop=(kc == 7))
                                    P.copy("act", rawT[:, j, u0:u0 + nb], pj[:, 0:nb])
                                for ti, t in enumerate(btiles):
                                    ut = t - tiles[0]
                                    pz = pb[2 + ti % 2]
                                    for kc in range(8):
                                        P.mm(pz[:, 0:128], hT[:, kc, ti * 128:(ti + 1) * 128], wC[:, kc, 384:512], start=(kc == 0), stop=(kc == 7))
                                    P.act(zf[ti % 2][:], pz[:, 0:128], AF.Silu)
                                    P.tt("pool", zs[:, ut, :], zf[ti % 2][:], ong[:], ALU.mult)
                                    if h == 0:
                                        pa_ = pb[4]
                                        for kc in range(8):
                                            P.mm(pa_[:, 0:16], hT[:, kc, ti * 128:(ti + 1) * 128], wab[:, kc, :], start=(kc == 0), stop=(kc == 7))
                                        P.tt("dve", sm[0][:], pa_[:, 0:8], dtb[:], ALU.add)
                                        P.act(sm[1][:], sm[0][:], AF.Abs)
                                        P.act(sm[2][:], sm[1][:], AF.Exp, scale=-1.0)
                                        P.act(sm[3][:], sm[2][:], AF.Ln, bias=1.0)
                                        P.ts("dve", sm[4][:], sm[0][:], 0.0, None, op0=ALU.max)
                                        P.tt("dve", sm[5][:], sm[4][:], sm[3][:], ALU.add)
                                        P.tt("dve", gb[:, ut, 0:8], sm[5][:], alog[:], ALU.mult)
                                        P.act(gb[:, ut, 8:16], pa_[:, 8:16], AF.Sigmoid)
                            for sq in unit["seqs"]:
                                s0 = sq["t0"] * 128 - tok_base
                                n = sq["nt"] * 128
                                for j in range(3):
                                    ci = j * 4 + h
                                    for o in range(0, n, 512):
                                        nb = min(512, n - o)
                                        a0 = s0 + o
                                        c_ = cv[(o // 512) % 2]
                                        P.act(c_[:, 0:nb], rawT[:, j, a0:a0 + nb], AF.Copy, scale=convw[:, ci, 1:2])
                                        lo = 1 if o == 0 else 0
                                        P.stt(c_[:, lo:nb], rawT[:, j, a0 + lo - 1:a0 + nb - 1], convw[:, ci, 0:1], c_[:, lo:nb], ALU.mult, ALU.add)
                                        hi = nb - 1 if o + nb == n else nb
                                        P.stt(c_[:, 0:hi], rawT[:, j, a0 + 1:a0 + hi + 1], convw[:, ci, 2:3], c_[:, 0:hi], ALU.mult, ALU.add)
                                        P.act(qkvT[:, j, a0:a0 + nb], c_[:, 0:nb], AF.Silu)
                            P.fence()

                        gc = sb("gc", [128, 2, nt_u], F32, hs)
                        egc = sb("egc", [128, 2, nt_u], F32, hs)
                        gl = sb("gl", [128, 2, nt_u], F32, hs)
                        edl = sb("edl", [128, 2, nt_u], F32, hs)
                        ge = sb("ge", [128, 2, nt_u], F32, hs)
                        gsel = sb("gsel", [128, 2, nt_u], F32, hs)
                        for d in range(2):
                            P.copy("dve", gsel[:, d, :], gb[:, :, d * 4 + h])
                        for d in range(2):
                            P.mm(pb[6][:, d * nt_u:(d + 1) * nt_u], trif[:, d, :], gsel[:, d, :])
                        P.copy("dve", gc[:], pb[6][:, 0:2 * nt_u].rearrange("p (d n) -> p d n", d=2))
                        P.mm(pb[5][:, 0:2 * nt_u], ones_f[:], gsel[:].rearrange("p d n -> p (d n)"))
                        P.copy("dve", gl[:], pb[5][:, 0:2 * nt_u].rearrange("p (d n) -> p d n", d=2))
                        P.act(egc[:], gc[:], AF.Exp)
                        P.act(ge[:], gl[:], AF.Exp)
                        P.tt("dve", edl[:], gl[:], gc[:], ALU.subtract)
                        P.act(edl[:], edl[:], AF.Exp)

                        Sst = sb("Sst", [128, 2, 128], F32, hs)
                        Sbf = sb("Sbf", [128, 2, 128], BF16, hs)
                        ident2 = ident_f[:].unsqueeze(1).broadcast_to([128, 2, 128])

                        def slotbufs(s_):
                            W = Ctx()
                            f32 = lambda n, shp=(128, 2, 128): sb("%s_s%d" % (n, s_), list(shp), F32, hs)
                            b16 = lambda n, shp=(128, 2, 128): sb("%s_s%d" % (n, s_), list(shp), BF16, hs)
                            W.ssq = f32("ssq", (128, 2, 2)); W.rkq = f32("rkq", (128, 2, 2))
                            W.qkv = b16("qkvtok", (128, 2, 3, 128)); W.kn = b16("kn"); W.qn = b16("qn")
                            W.vke = b16("vke", (128, 2, 256)); W.kqT = b16("kqT", (128, 2, 2, 128))
                            W.diag = f32("diag"); W.DT = f32("DT"); W.tmp1 = f32("tmp1"); W.AT = f32("AT"); W.tmp2 = f32("tmp2")
                            W.qkmT = b16("qkmT"); W.A = f32("A"); W.R = f32("R"); W.RT = f32("RT")
                            W.BkT = [f32("BkT0"), f32("BkT1")]; W.Y = f32("Y"); W.Rb = b16("Rb"); W.u = f32("u")
                            W.w = b16("w"); W.wT = b16("wT"); W.kd = b16("kd"); W.qd = b16("qd"); W.qdT = b16("qdT")
                            W.vnew = b16("vnew"); W.t3 = f32("t3"); W.t4 = f32("t4")
                            return W
                        SB_ = [slotbufs(0), slotbufs(1)]

                        def run_slot(sq, s_):
                            W = SB_[s_]
                            kb = sq["t0"] * 128 - tok_base
                            n = sq["nt"]
                            bA, bB, bC = pb[3 * s_], pb[3 * s_ + 1], pb[3 * s_ + 2]
                            tb0 = s_ * 4 * 128
                            tb = lambda j, cnt=1: pbb[:, tb0 + j * 128:tb0 + (j + cnt) * 128]
                            v3 = lambda ap: ap.rearrange("p (d n) -> p d n", d=2)
                            if s_ == 1:
                                for _ in range(3):
                                    yield
                            for st in range(s_, n, 2):
                                cs = [st, n - 1 - st]
                                uts = [kb // 128 + c for c in cs]
                                gcc = [gc[:, d, uts[d]:uts[d] + 1] for d in range(2)]
                                beta = [gb[:, uts[d], 8 + d * 4 + h:8 + d * 4 + h + 1] for d in range(2)]
                                for d in range(2):
                                    sl = slice(kb + cs[d] * 128, kb + (cs[d] + 1) * 128)
                                    for j in range(3):
                                        P.tr(tb(j), qkvT[:, j, sl], ident_b[:])
                                    P.copy("act", W.qkv[:, d, :, :], tb(0, 3).rearrange("p (j n) -> p j n", j=3))
                                yield
                                for d in range(2):
                                    P.act(W.t3[:, d, :], W.qkv[:, d, 0, :], AF.Square, accum_out=W.ssq[:, d, 0:1])
                                    P.act(W.t4[:, d, :], W.qkv[:, d, 1, :], AF.Square, accum_out=W.ssq[:, d, 1:2])
                                yield
                                P.act(W.rkq[:], W.ssq[:], AF.Ln, bias=EPS)
                                P.act(W.rkq[:], W.rkq[:], AF.Exp, scale=-0.5)
                                yield
                                for d in range(2):
                                    P.act(W.kn[:, d, :], W.qkv[:, d, 1, :], AF.Copy, scale=W.rkq[:, d, 1:2])
                                    P.ts("dve", W.qn[:, d, :], W.qkv[:, d, 0, :], W.rkq[:, d, 0:1], 128.0 ** -0.5, op0=ALU.mult, op1=ALU.mult)
                                    P.copy("pool", W.vke[:, d, 0:128], W.qkv[:, d, 2, :])
                                yield
                                for d in range(2):
                                    P.tr(tb(0), W.kn[:, d, :], ident_b[:])
                                    P.tr(tb(1), W.qn[:, d, :], ident_b[:])
                                    P.copy("act", W.kqT[:, d, :, :], tb(0, 2).rearrange("p (c n) -> p c n", c=2))
                                    P.act(W.vke[:, d, 128:256], W.kn[:, d, :], AF.Copy, scale=egc[:, d, uts[d]:uts[d] + 1])
                                    P.ts("dve", W.kd[:, d, :], W.kn[:, d, :], edl[:, d, uts[d]:uts[d] + 1], None, op0=ALU.mult)
                                    P.act(W.qd[:, d, :], W.qn[:, d, :], AF.Copy, scale=egc[:, d, uts[d]:uts[d] + 1])
                                    P.ts("dve", W.diag[:, d, :], ident_f[:], gcc[d], None, op0=ALU.mult)
                                yield
                                for d in range(2):
                                    P.mm(bA[:, d * 128:(d + 1) * 128], W.kqT[:, d, 0, :], W.kqT[:, d, 0, :])
                                    P.mm(bA[:, 256 + d * 128:256 + (d + 1) * 128], W.kqT[:, d, 0, :], W.kqT[:, d, 1, :])
                                    P.mm(bB[:, d * 128:(d + 1) * 128], ones_f[:], W.diag[:, d, :])
                                    P.tr(tb(2 + d), W.qd[:, d, :], ident_b[:])
                                for d in range(2):
                                    P.ts("dve", W.DT[:, d, :], bB[:, d * 128:(d + 1) * 128], gcc[d], 0.0, op0=ALU.subtract, op1=ALU.min)
                                P.act(W.DT[:], W.DT[:], AF.Exp)
                                P.copy("act", W.qdT[:], v3(tb(2, 2)))
                                yield
                                P.tt("dve", W.tmp1[:], v3(bA[:, 0:256]), W.DT[:], ALU.mult)
                                P.tt("dve", W.tmp2[:], v3(bA[:, 256:512]), W.DT[:], ALU.mult)
                                for d in range(2):
                                    P.stt(W.AT[:, d, :], W.tmp1[:, d, :], beta[d], trc[:, 0, d, :], ALU.mult, ALU.mult)
                                P.tt("pool", W.qkmT[:], W.tmp2[:], trc[:, 1, :, :], ALU.mult)
                                yield
                                for d in range(2):
                                    P.tr(bB[:, 256 + d * 128:256 + (d + 1) * 128], W.AT[:, d, :], ident_f[:])
                                P.copy("act", W.A[:], v3(bB[:, 256:512]))
                                P.tt("dve", W.t3[:], W.AT[:], lvN[:, 0, :, :], ALU.mult)
                                P.tt("dve", W.R[:], ident2, W.t3[:], ALU.subtract)
                                yield
                                P.tt("pool", W.t4[:], W.A[:], lvA[:, 0, :, :], ALU.mult)
                                P.tt("pool", W.RT[:], ident2, W.t4[:], ALU.subtract)
                                P.tt("pool", W.BkT[1][:], W.A[:], lvA[:, 1, :, :], ALU.mult)
                                yield
                                for k in range(1, 7):
                                    Bk = W.BkT[k % 2]
                                    for d in range(2):
                                        P.mm(bC[:, d * 128:(d + 1) * 128], Bk[:, d, :], W.R[:, d, :])
                                    if k < 6:
                                        P.tt("pool", W.BkT[(k + 1) % 2][:], W.A[:], lvA[:, k + 1, :, :], ALU.mult)
                                    P.copy("act", W.Y[:], v3(bC[:, 0:256]))
                                    yield
                                    for d in range(2):
                                        P.mm(bB[:, d * 128:(d + 1) * 128], W.RT[:, d, :], W.Y[:, d, :])
                                        P.mm(bB[:, 256 + d * 128:256 + (d + 1) * 128], W.Y[:, d, :], W.RT[:, d, :])
                                    yield
                                    P.tt("dve", W.R[:], W.R[:], v3(bB[:, 0:256]), ALU.subtract)
                                    P.tt("dve", W.RT[:], W.RT[:], v3(bB[:, 256:512]), ALU.subtract)
                                    yield
                                P.copy("act", W.Rb[:], W.R[:])
                                yield
                                for d in range(2):
                                    P.mm(bA[:, d * 256:(d + 1) * 256], W.Rb[:, d, :], W.vke[:, d, :])
                                for d in range(2):
                                    P.act(W.u[:, d, :], bA[:, d * 256:d * 256 + 128], AF.Copy, scale=beta[d])
                                    P.act(W.w[:, d, :], bA[:, d * 256 + 128:(d + 1) * 256], AF.Copy, scale=beta[d])
                                yield
                                for d in range(2):
                                    P.tr(tb(d), W.w[:, d, :], ident_b[:])
                                P.copy("act", W.wT[:], v3(tb(0, 2)))
                                yield
                                for d in range(2):
                                    P.mm(bC[:, d * 128:(d + 1) * 128], W.wT[:, d, :], Sbf[:, d, :])
                                P.tt("dve", W.vnew[:], W.u[:], v3(bC[:, 0:256]), ALU.subtract)
                                yield
                                for d in range(2):
                                    P.mm(bC[:, 256 + d * 128:256 + (d + 1) * 128], W.qdT[:, d, :], Sbf[:, d, :], start=True, stop=False)
                                    P.mm(bC[:, 256 + d * 128:256 + (d + 1) * 128], W.qkmT[:, d, :], W.vnew[:, d, :], start=False, stop=True)
                                for d in range(2):
                                    P.tt("dve", oacc[:, uts[d], :], oacc[:, uts[d], :], bC[:, 256 + d * 128:256 + (d + 1) * 128], ALU.add)
                                yield
                                for d in range(2):
                                    P.mm(bA[:, d * 128:(d + 1) * 128], W.kd[:, d, :], W.vnew[:, d, :])
                                for d in range(2):
                                    P.stt(Sst[:, d, :], Sst[:, d, :], ge[:, d, uts[d]:uts[d] + 1], bA[:, d * 128:(d + 1) * 128], ALU.mult, ALU.add)
                                P.copy("act", Sbf[:], Sst[:])
                                yield

                        for sq in unit["seqs"]:
                            for d in range(2):
                                if sq["ctx"]:
                                    P.dma(Sst[:, d, :], cstate[l, d, h, :, :])
                                else:
                                    P.memset("pool", Sst[:, d, :], 0.0)
                            P.copy("act", Sbf[:], Sst[:])
                            alive = [run_slot(sq, 0), run_slot(sq, 1)]
                            while alive:
                                for g_ in list(alive):
                                    try:
                                        next(g_)
                                    except StopIteration:
                                        alive.remove(g_)
                            if unit["prompt"]:
                                for d in range(2):
                                    P.dma(ncs_out[sq["idx"], l, d, h, :, :], Sst[:, d, :])
                        on_ss = sb("on_ss", [128, nt_u], F32, hs)
                        on_r = sb("on_r", [128, nt_u], F32, hs)
                        ojk = [sb("ojk%d" % i, [128, 128], F32, hs) for i in range(2)]
                        ob16 = sb("ob16", [128, nt_u, 128], BF16, hs)
                        for ut in range(nt_u):
                            P.act(ojk[ut % 2][:], oacc[:, ut, :], AF.Square, accum_out=on_ss[:, ut:ut + 1])
                        P.ts("dve", on_r[:], on_ss[:], 1.0 / 128, EPS, op0=ALU.mult, op1=ALU.add)
                        P.act(on_r[:], on_r[:], AF.Sqrt)
                        P.recip(on_r[:], on_r[:])
                        for ut in range(nt_u):
                            P.stt(ob16[:, ut, :], oacc[:, ut, :], on_r[:, ut:ut + 1], zs[:, ut, :], ALU.mult, ALU.mult)
                        for u0 in range(0, nt_u, 8):
                            n8 = min(8, nt_u - u0)
                            for j in range(n8):
                                P.tr(pbb[:, j * 128:(j + 1) * 128], ob16[:, u0 + j, :], ident_b[:])
                            P.copy("act", ocT[:, h, u0 * 128:(u0 + n8) * 128], pbb[:, 0:n8 * 128])
                        P.fence()
                P.fence()

        def merge(l, unit, oT):
            tiles = unit["tiles"]
            tok_base = tiles[0] * 128
            cd = unit["cond"]
            for sub0 in range(0, len(tiles), 8):
                stiles = tiles[sub0:sub0 + 8]
                with ExitStack() as ph:
                    nsub = len(stiles) * 128
                    mT = sb("mT", [128, 8, nsub], BF16, ph)
                    with ExitStack() as p2:
                        hTm = sb("hTm", [128, 8, 256], BF16, p2)
                        gt = [sb("gtM%d" % i, [128, 256], F32, p2) for i in range(2)]
                        tm = [sb("tmM%d" % i, [128, 256], F32, p2) for i in range(2)]
                        wg = sb("wg", [128, 8, D], BF16, p2)
                        wbr = sb("wbr", [128, 4, D], BF16, p2)
                        for j in range(3):
                            P.dma(wg[:], w_in[l, :, OFF_GATES + j * D:OFF_GATES + (j + 1) * D].rearrange("(c p) n -> p c n", p=128), q="pool")
                            P.dma(wbr[:], w_branch[l, j].rearrange("(c p) n -> p c n", p=128), q="pool")
                            for b0 in range(0, len(stiles), 2):
                                bt = stiles[b0:b0 + 2]
                                nb = len(bt) * 128
                                build_hT(hTm, bt, 0)
                                u0 = bt[0] * 128 - tok_base
                                m0 = b0 * 128
                                for fc in range(8):
                                    pg, py = pb[fc % 2], pb[2 + fc % 2]
                                    for kc in range(8):
                                        P.mm(pg[:, 0:nb], wg[:, kc, fc * 128:(fc + 1) * 128], hTm[:, kc, 0:nb], start=(kc == 0), stop=(kc == 7))
                                    P.act(gt[fc % 2][:, 0:nb], pg[:, 0:nb], AF.Sigmoid)
                                    for c4 in range(4):
                                        P.mm(py[:, 0:nb], wbr[:, c4, fc * 128:(fc + 1) * 128], oT[j][:, c4, u0:u0 + nb], start=(c4 == 0), stop=(c4 == 3))
                                    if j == 0:
                                        P.tt("dve", mT[:, fc, m0:m0 + nb], py[:, 0:nb], gt[fc % 2][:, 0:nb], ALU.mult)
                                    else:
                                        P.tt("dve", tm[fc % 2][:, 0:nb], py[:, 0:nb], gt[fc % 2][:, 0:nb], ALU.mult)
                                        P.tt("pool", mT[:, fc, m0:m0 + nb], mT[:, fc, m0:m0 + nb], tm[fc % 2][:, 0:nb], ALU.add)
                        P.fence()
                    with ExitStack() as p3:
                        wo = sb("wo", [128, 8, D], BF16, p3)
                        G1 = sb("G1", [128, D], F32, p3)
                        tmo = [sb("tmo%d" % i, [128, 512], F32, p3) for i in range(2)]
                        P.dma(wo[:], w_out[l].rearrange("(c p) n -> p c n", p=128), q="pool")
                        make_G(G1[:], 0, cd, p3)
                        n_ = 0
                        for ti, t in enumerate(stiles):
                            for half in range(2):
                                py = pb[2 + n_ % 4]
                                for kc in range(8):
                                    P.mm(py[:], mT[:, kc, ti * 128:(ti + 1) * 128], wo[:, kc, half * 512:(half + 1) * 512], start=(kc == 0), stop=(kc == 7))
                                tmp = tmo[n_ % 2]
                                n_ += 1
                                P.tt("dve", tmp[:], py[:], G1[:, half * 512:(half + 1) * 512], ALU.mult)
                                P.tt("pool", x[:, t, half * 512:(half + 1) * 512], x[:, t, half * 512:(half + 1) * 512], tmp[:], ALU.add)
                        P.fence()

        K.P, K.nc, K.x, K.pb, K.sb, K.psb = P, nc, x, pb, sb, psb
        def dump_bf16(name, src, n):
            with ExitStack() as ph:
                stg = sb("stg", [128, src.shape[1], n], F32, ph)
                P.copy("dve", stg[:], src)
                P.dma(dbg_out[name][:, :, :], stg[:])
                P.fence()

        for l in range(depth):
            if "adaln" in parts:
                adaln(l)
            calc_rstd(list(range(NT)))
            for unit in units:
                with ExitStack() as us:
                    ntu = len(unit["tiles"]) * 128
                    oT = {}
                    for m in ("C", "B", "A"):
                        oT[m] = sb("o%sT" % m.lower(), [128, 4, ntu], BF16, us)
                        if m in mixers:
                            K.phase = "mix%s_%s" % (m, unit["name"])
                            {"A": mixer_A, "B": mixer_B, "C": mixer_C}[m](l, unit, oT[m])
                            K.phase = "other"
                        else:
                            P.memset("pool", oT[m][:], 0.0)
                        nm = "o%sT_%s%d" % (m.lower(), unit["name"], l)
                        if nm in dbg_out:
                            dump_bf16(nm, oT[m][:], ntu)
                    if "merge" in parts:
                        K.phase = "merge_%s" % unit["name"]
                        merge(l, unit, [oT["A"], oT["B"], oT["C"]])
                        K.phase = "other"
                    P.fence()
            if "ffn" in parts:
                ffn(l)
        if "final" in parts:
            final_norm()
        P.finish()
        K.stats = (P.n_inst, P.n_wait)
    return nc, K


def _rope_feat(R, pos_row, pos_col):
    q = R // 4
    inv = np.power(10000.0, -np.arange(q, dtype=np.float32) / q).astype(np.float32)
    n = len(pos_row)
    C = np.zeros((R, n), np.float32)
    S = np.zeros((R, n), np.float32)
    for r in range(R):
        blk, i = r // q, r % q
        pos = pos_row if blk < 2 else pos_col
        ang = pos.astype(np.float32) * inv[i]
        C[r] = np.cos(ang)
        S[r] = (-1.0 if blk % 2 == 0 else 1.0) * np.sin(ang)
    return C, S


def _swap_perm(R):
    q = R // 4
    return np.array([r + q if (r // q) % 2 == 0 else r - q for r in range(R)])


def make_shared(inp, NTS, NSP):
    f = lambda a: np.ascontiguousarray(np.asarray(a, dtype=np.float32))
    depth = inp["w_in"].shape[0]
    NTP = NSP * 2
    NTOK = (NTP + NTS) * 128
    sh = {}
    for k in ("w_ada", "b_ada", "w_ff1", "w_ff2", "w_in", "b_gkv", "b_w_ukv", "a_sink",
              "w_branch", "w_out", "c_onorm_g"):
        if k in inp:
            sh[k] = f(inp[k])
    sh["b_adaT"] = f(np.asarray(inp["b_ada"]).reshape(depth, 48, 128).transpose(0, 2, 1))
    sh["norm1T"] = f(np.asarray(inp["norm1_g"]).reshape(depth, 8, 128).transpose(0, 2, 1))
    sh["norm2T"] = f(np.asarray(inp["norm2_g"]).reshape(depth, 8, 128).transpose(0, 2, 1))
    sh["final_g"] = f(np.asarray(inp["final_g"]).reshape(1, D))
    sh["b_gqT"] = f(np.asarray(inp["b_gq"]).reshape(depth, 3, 128).transpose(0, 2, 1))
    w_in = np.asarray(inp["w_in"])
    p64, p32 = _swap_perm(64), _swap_perm(32)
    cols = []
    for h in range(8):
        cols += list(OFF_AQ + h * 64 + p64)
    for h in range(2):
        cols += list(OFF_AK + h * 64 + p64)
    cols += list(OFF_BKR + p32)
    sh["w_in_sw"] = f(w_in[:, :, np.array(cols)])
    wuq_ = np.asarray(inp["b_w_uq"], dtype=np.float32).reshape(depth, 384, 8, 96)
    zpad = np.zeros((depth, 384, 8, 32), np.float32)
    sh["b_w_uq_r"] = f(np.concatenate([zpad, wuq_[..., 64:96], wuq_[..., 0:64]], -1).reshape(depth, 384, 1024))
    sh["b_w_uq_rs"] = f(np.concatenate([zpad, wuq_[..., 64 + p32], wuq_[..., 0:64]], -1).reshape(depth, 384, 1024))
    ts = np.arange(NTS * 128)
    row, col = ts // 64, ts % 64
    CA, SA = _rope_feat(64, row, col)
    CB, SB = _rope_feat(32, row, col)
    npad = NTP * 128
    padC = lambda C: np.concatenate([np.ones((C.shape[0], npad), np.float32), C], axis=1)
    padS = lambda S: np.concatenate([np.zeros((S.shape[0], npad), np.float32), S], axis=1)
    sh["ropeA_C"], sh["ropeA_S"] = f(padC(CA)), f(padS(SA))
    sh["ropeQB_C"] = f(np.concatenate([np.ones((32, NTOK), np.float32), padC(CB), np.ones((64, NTOK), np.float32)], axis=0))
    sh["ropeQB_S"] = f(np.concatenate([np.zeros((32, NTOK), np.float32), padS(SB), np.zeros((64, NTOK), np.float32)], axis=0))
    sh["ropeKB_C"], sh["ropeKB_S"] = f(padC(CB).T), f(padS(SB).T)
    sh["cst_ident"] = np.eye(128, dtype=np.float32)
    jj = np.arange(128)[:, None]
    ii = np.arange(128)[None, :]
    sh["cst_tri"] = f(np.stack([(jj < ii), (jj <= ii), (jj > ii), (jj >= ii)], axis=1))
    lv = []
    for k in range(1, 8):
        lv.append(((jj >> k) == (ii >> k)) & ((jj >> (k - 1)) != (ii >> (k - 1))) & (jj < ii))
    tri4 = [(jj < ii), (jj <= ii), (jj > ii), (jj >= ii)]
    sh["cst_trc"] = f(np.stack([np.stack([tri4[0], tri4[2]], 1), np.stack([tri4[1], tri4[3]], 1)], 1))
    sh["cst_lvN"] = f(np.stack([np.stack([m, m.T], 1) for m in lv], 1))
    sh["cst_lvA"] = f(np.stack([np.stack([m.T, m], 1) for m in lv], 1))
    sh["c_convT"] = f(np.asarray(inp["c_conv_w"]).reshape(depth, 3, 12, 128).transpose(0, 3, 2, 1))
    sh["c_a_log"] = f(np.asarray(inp["c_a_log"]).reshape(depth, 8))
    sh["c_dt_bias"] = f(np.asarray(inp["c_dt_bias"]).reshape(depth, 8))
    sel = np.zeros((2, 2, 128), np.float32)
    sel[0, 0] = 1
    sel[1, 1] = 1
    sh["cst_sel"] = sel
    b_ = np.arange(128)[:, None]
    a_ = np.arange(128)[None, :]
    mprev = np.where(a_ > b_, -30000.0, 0.0).astype(np.float32)
    mnext = np.where(b_ > a_, -30000.0, 0.0).astype(np.float32)
    sh["cst_masks"] = f(np.stack([mprev] * 4 + [mnext] * 4, axis=1))
    return sh


def make_core(inp, prompt_ids, sample_id, NTS):
    f = lambda a: np.ascontiguousarray(np.asarray(a, dtype=np.float32))
    xp = np.asarray(inp["x_prompt"])[prompt_ids].reshape(-1, D)
    xs = np.asarray(inp["x_sample"])[sample_id].reshape(-1, D)
    cond = np.stack([np.asarray(inp["c_ctx"]), np.asarray(inp["c"])[sample_id]], 0)
    c = {"xin": f(np.concatenate([xp, xs], 0)),
         "condT": f(cond.reshape(2, 8, 128).transpose(2, 1, 0)),
         "cak": f(np.asarray(inp["cache_a_k"])[sample_id].reshape(-1, 256, 128)),
         "cav": f(np.asarray(inp["cache_a_v"])[sample_id].reshape(-1, 256, 128)),
         "cckv": f(np.asarray(inp["cache_b_ckv"])[sample_id]),
         "ckr": f(np.asarray(inp["cache_b_krope"])[sample_id]),
         "cstate": f(np.asarray(inp["state_c"])[sample_id])}
    return c


_NTS, _NSP = 16, 2
_CACHE = {}


def kernel(**inputs):
    if "nc" not in _CACHE:
        _CACHE["nc"] = build(NTS=_NTS, NSP=_NSP, depth=DEPTH, parts=("adaln", "merge", "ffn", "final"))
    nc, K = _CACHE["nc"]
    names = [a.memorylocations[0].name for a in nc.m.functions[0].allocations
             if isinstance(a, mybir.MemoryLocationSet) and a.kind == "ExternalInput"]
    shared = make_shared(inputs, _NTS, _NSP)
    in_maps = []
    for core in range(8):
        d = dict(shared)
        d.update(make_core(inputs, [2 * core, 2 * core + 1], core // 2, _NTS))
        in_maps.append({k: d[k] for k in names if k in d})
    res = run_bass_kernel_spmd(nc, in_maps, core_ids=list(range(8))).results
    npt = _NSP * 256
    y_prompt = np.concatenate([res[c]["y"][:npt].reshape(_NSP, 256, D) for c in range(8)], 0)
    y_sample = np.stack([res[2 * b]["y"][npt:] for b in range(4)], 0)
    cat = lambda k: np.concatenate([res[c][k] for c in range(8)], 0)
    new_a_k = cat("nak").reshape(16, DEPTH, 256, 2, 64)
    new_a_v = cat("nav").reshape(16, DEPTH, 256, 2, 64)
    new_b_ckv = cat("nckv")
    new_b_krope = cat("nkr")
    new_c_state = cat("ncs")
    f = lambda a: np.ascontiguousarray(a, dtype=np.float32)
    return (f(y_prompt), f(y_sample), f(new_a_k), f(new_a_v), f(new_b_ckv), f(new_b_krope), f(new_c_state))
```

```python
import bisect
from contextlib import ExitStack

import numpy as np
import concourse.bass as bass
import concourse.mybir as mybir
from concourse.bass_utils import run_bass_kernel_spmd

F32 = mybir.dt.float32
BF16 = mybir.dt.bfloat16
AF = mybir.ActivationFunctionType
ALU = mybir.AluOpType
AX = mybir.AxisListType

COMPUTE = ("pe", "act", "dve", "pool")
NDS = 48
NDS_SP = 32


def _region(ap):
    t = ap.tensor
    dims = [(int(s), int(c)) for s, c in ap.ap]
    off = int(ap.offset)
    if type(t).__name__.startswith("DRam"):
        ext = sum((c - 1) * abs(s) for s, c in dims)
        return (t.name, 0, 1, off, off + ext + 1)
    rows = 1
    for d in list(t.shape)[1:]:
        rows *= int(d)
    p0 = off // rows
    f0 = off % rows
    pc = dims[0][1] if dims[0][0] != 0 else 1
    ext = sum((c - 1) * abs(s) for s, c in dims[1:])
    f1 = f0 + ext + 1
    if type(t).__name__.startswith("PSum"):
        be = 1024 if t.dtype == BF16 else 512
        f0 = (f0 // be) * be
        f1 = ((f1 + be - 1) // be) * be
        p0, pc = 0, 128
    return (t.name, p0, p0 + pc, f0, f1)


def _overlap(a, b):
    return a[1] < b[2] and b[1] < a[2] and a[3] < b[4] and b[3] < a[4]


def _contains(a, b):
    return a[1] <= b[1] and b[2] <= a[2] and a[3] <= b[3] and b[4] <= a[4]


class Op:
    __slots__ = ("eng", "fn", "deps", "signal", "is_dma", "sem", "semval", "seq", "idx")

    def __init__(self, eng, fn, idx, is_dma=False):
        self.eng = eng
        self.fn = fn
        self.deps = []
        self.signal = False
        self.is_dma = is_dma
        self.sem = None
        self.semval = 0
        self.seq = None
        self.idx = idx


class Prog:
    def __init__(self, nc, stack):
        self.nc = nc
        self.engs = {"pe": nc.tensor, "act": nc.scalar, "dve": nc.vector, "pool": nc.gpsimd, "sp": nc.sync}
        self.esem = {e: stack.enter_context(nc.semaphore("es_" + e)) for e in COMPUTE}
        self.dsems = [stack.enter_context(nc.semaphore("ds%d" % i)) for i in range(NDS)]
        self.dcount = [0] * NDS
        self.dlast = [None] * NDS
        self.dnext = {"sp": 0, "pool": 0}
        self.pending = []
        self.state = {}
        self.sigcount = {e: 0 for e in COMPUTE}
        self.sigidx = {e: [] for e in COMPUTE}
        self.sigseq = {e: [] for e in COMPUTE}
        self.waited = {f: {e: 0 for e in COMPUTE} for f in self.engs}
        self.waited_dma = {f: {} for f in self.engs}
        self.last_op = {}
        self.open_dmas = []
        self.nops = 0
        self.n_inst = 0
        self.n_wait = 0

    def add(self, eng, fn, reads=(), writes=(), dma=False):
        op = Op(eng, fn, self.nops, dma)
        self.nops += 1
        deps = {}
        rregs = [_region(a) for a in reads]
        wregs = [_region(a) for a in writes]
        for r in rregs:
            is_ps = r[0].startswith("pb")
            for (reg, o, isw) in self.state.get(r[0], ()):
                if _overlap(reg, r) and (isw or (is_ps and o.eng != eng)):
                    deps[o.idx] = o
        for w in wregs:
            for (reg, o, isw) in self.state.get(w[0], ()):
                if _overlap(reg, w):
                    if (not dma) and (not o.is_dma) and o.eng == eng and eng == "pe":
                        continue
                    deps[o.idx] = o
        if dma:
            if eng == "sp":
                s = self.dnext["sp"]
                self.dnext["sp"] = (s + 1) % NDS_SP
            else:
                s = NDS_SP + self.dnext["pool"]
                self.dnext["pool"] = (self.dnext["pool"] + 1) % (NDS - NDS_SP)
            if self.dlast[s] is not None:
                deps[self.dlast[s].idx] = self.dlast[s]
            self.dlast[s] = op
            self.dcount[s] += 1
            op.sem = s
            op.semval = 16 * self.dcount[s]
            self.open_dmas.append(op)
        for o in deps.values():
            if not o.is_dma:
                o.signal = True
        op.deps = list(deps.values())
        for w in wregs:
            lst = self.state.setdefault(w[0], [])
            lst[:] = [rec for rec in lst if not _contains(w, rec[0])]
            lst.append((w, op, True))
        for r in rregs:
            lst = self.state.setdefault(r[0], [])
            if not dma:
                lst[:] = [rec for rec in lst
                          if not ((not rec[2]) and (not rec[1].is_dma) and rec[1].eng == eng and _contains(r, rec[0]))]
            lst.append((r, op, False))
        self.pending.append(op)
        if not dma:
            self.last_op[eng] = op
        return op

    def fence(self):
        lasts = dict(self.last_op)
        dmas = list(self.open_dmas)
        self.open_dmas = []
        for f in self.engs:
            op = Op(f, None, self.nops, False)
            self.nops += 1
            for e, o in lasts.items():
                o.signal = True
                op.deps.append(o)
            op.deps.extend(dmas)
            self.pending.append(op)
        self.state = {}
        self.flush()

    def _seq_of(self, dep):
        if dep.seq is not None:
            return dep.seq
        e = dep.eng
        i = bisect.bisect_left(self.sigidx[e], dep.idx)
        return self.sigseq[e][i]

    def flush(self):
        lastp = {}
        for op in self.pending:
            if (not op.is_dma) and op.fn is not None and op.eng in COMPUTE:
                lastp[op.eng] = op
        for op in lastp.values():
            op.signal = True
        for op in self.pending:
            f = op.eng
            eng = self.engs[f]
            waits = []
            for dep in op.deps:
                if dep.is_dma:
                    if self.waited_dma[f].get(dep.sem, 0) >= dep.semval:
                        continue
                    waits.append((self.dsems[dep.sem], dep.semval))
                    self.waited_dma[f][dep.sem] = dep.semval
                else:
                    e = dep.eng
                    seq = self._seq_of(dep)
                    if self.waited[f][e] >= seq:
                        continue
                    waits.append((self.esem[e], seq))
                    self.waited[f][e] = seq
            self.n_wait += len(waits)
            embed = None
            if op.fn is not None and (not op.is_dma) and waits:
                embed = waits.pop()
            for (sem_, val_) in waits:
                eng.wait_ge(sem_, val_)
            if op.fn is None:
                continue
            ins = op.fn(eng)
            if embed is not None:
                ins._wait_ge(embed[0], embed[1])
            self.n_inst += 1
            if op.is_dma:
                ins.then_inc(self.dsems[op.sem], 16)
            elif op.signal:
                self.sigcount[f] += 1
                op.seq = self.sigcount[f]
                ins.then_inc(self.esem[f], 1)
                self.sigidx[f].append(op.idx)
                self.sigseq[f].append(op.seq)
                self.waited[f][f] = max(self.waited[f][f], 0)
        self.pending = []

    def finish(self):
        self.fence()

    def mm(self, out, lhsT, rhs, start=True, stop=True):
        return self.add("pe", lambda e: e.matmul(out, lhsT, rhs, start=start, stop=stop),
                        reads=[lhsT, rhs], writes=[out])

    def tr(self, out, in_, ident):
        return self.add("pe", lambda e: e.transpose(out, in_, ident), reads=[in_, ident], writes=[out])

    def act(self, out, in_, func, bias=None, scale=1.0, accum_out=None, eng="act"):
        reads = [in_]
        kw = {}
        if bias is not None:
            kw["bias"] = bias
            if not isinstance(bias, (int, float)):
                reads.append(bias)
        if not isinstance(scale, (int, float)):
            reads.append(scale)
        kw["scale"] = scale
        writes = [out]
        if accum_out is not None:
            kw["accum_out"] = accum_out
            writes.append(accum_out)
        return self.add("act", lambda e: e.activation(out, in_, func, **kw), reads=reads, writes=writes)

    def tt(self, eng, out, in0, in1, op):
        return self.add(eng, lambda e: e.tensor_tensor(out, in0, in1, op), reads=[in0, in1], writes=[out])

    def ts(self, eng, out, in0, s1, s2=None, op0=ALU.mult, op1=None, accum_out=None):
        reads = [in0]
        for s in (s1, s2):
            if s is not None and not isinstance(s, (int, float)):
                reads.append(s)
        writes = [out]
        kw = {}
        if op1 is not None:
            kw["op1"] = op1
        if accum_out is not None:
            kw["accum_out"] = accum_out
            writes.append(accum_out)
        return self.add(eng, lambda e: e.tensor_scalar(out, in0, s1, s2, op0, **kw), reads=reads, writes=writes)

    def stt(self, out, in0, scalar, in1, op0, op1, eng="dve"):
        reads = [in0, in1]
        if not isinstance(scalar, (int, float)):
            reads.append(scalar)
        return self.add(eng, lambda e: e.scalar_tensor_tensor(out, in0, scalar, in1, op0, op1),
                        reads=reads, writes=[out])

    def copy(self, eng, out, in_):
        if eng == "act":
            return self.add("act", lambda e: e.copy(out, in_), reads=[in_], writes=[out])
        return self.add(eng, lambda e: e.tensor_copy(out, in_), reads=[in_], writes=[out])

    def memset(self, eng, out, val):
        return self.add(eng, lambda e: e.memset(out, val), reads=[], writes=[out])

    def recip(self, out, in_):
        return self.add("dve", lambda e: e.reciprocal(out, in_), reads=[in_], writes=[out])

    def reduce(self, out, in_, op, axis=AX.X, eng="dve"):
        return self.add(eng, lambda e: e.tensor_reduce(out, in_, axis, op), reads=[in_], writes=[out])

    def dma(self, out, in_, q="sp"):
        return self.add(q, lambda e: e.dma_start(out=out, in_=in_), reads=[in_], writes=[out], dma=True)


D = 1024
DEPTH = 2
D_IN = 6576
OFF_AQ, OFF_AK, OFF_AV = 0, 512, 640
OFF_BCQ, OFF_BCKV, OFF_BKR = 768, 1152, 1408
OFF_CQKV, OFF_CA, OFF_CB, OFF_CZ, OFF_GATES = 1440, 2976, 2984, 2992, 3504
EPS = 1e-6


class Ctx:
    pass


def build(NTS=16, NSP=2, depth=DEPTH, mixers=("B", "A", "C"), dbg=(), parts=("adaln", "ffn", "final")):
    NTP = NSP * 2
    NT = NTP + NTS
    NTOK = NT * 128
    nc = bass.Bass("TRN2", target_bir_lowering=False)
    K = Ctx()

    def din(name, shape):
        return nc.dram_tensor(name, list(shape), F32, kind="ExternalInput").ap()

    def dout(name, shape):
        return nc.dram_tensor(name, list(shape), F32, kind="ExternalOutput").ap()

    xin = din("xin", [NTOK, D])
    condT = din("condT", [128, 8, 2])
    w_ada = din("w_ada", [depth, D, 6 * D])
    b_adaT = din("b_adaT", [depth, 128, 48])
    b_ada = din("b_ada", [depth, 6 * D])
    norm1T = din("norm1T", [depth, 128, 8])
    norm2T = din("norm2T", [depth, 128, 8])
    final_g = din("final_g", [1, D])
    w_ff1 = din("w_ff1", [depth, D, 4 * D])
    w_ff2 = din("w_ff2", [depth, 4 * D, D])
    cst_ident = din("cst_ident", [128, 128])
    w_branch = din("w_branch", [depth, 3, 512, D])
    w_out = din("w_out", [depth, D, D])
    w_in = din("w_in", [depth, D, D_IN])
    w_in_sw = din("w_in_sw", [depth, D, 672])
    b_w_uq_r = din("b_w_uq_r", [depth, 384, 1024])
    b_w_uq_rs = din("b_w_uq_rs", [depth, 384, 1024])
    b_gqT = din("b_gqT", [depth, 128, 3])
    b_gkv = din("b_gkv", [depth, 256])
    b_w_ukv = din("b_w_ukv", [depth, 256, 1024])
    a_sink = din("a_sink", [depth, 8])
    ropeQB_C = din("ropeQB_C", [128, NTOK])
    ropeQB_S = din("ropeQB_S", [128, NTOK])
    ropeKB_C = din("ropeKB_C", [NTOK, 32])
    ropeKB_S = din("ropeKB_S", [NTOK, 32])
    ropeA_C = din("ropeA_C", [64, NTOK])
    ropeA_S = din("ropeA_S", [64, NTOK])
    cak = din("cak", [depth, 256, 128])
    cav = din("cav", [depth, 256, 128])
    cckv = din("cckv", [depth, 256, 256])
    ckr = din("ckr", [depth, 256, 32])
    cstate = din("cstate", [depth, 2, 4, 128, 128])
    cst_masks = din("cst_masks", [128, 8, 128])
    cst_tri = din("cst_tri", [128, 4, 128])
    cst_trc = din("cst_trc", [128, 2, 2, 128])
    cst_lvN = din("cst_lvN", [128, 7, 2, 128])
    cst_lvA = din("cst_lvA", [128, 7, 2, 128])
    c_convT = din("c_convT", [depth, 128, 12, 3])
    c_a_log = din("c_a_log", [depth, 8])
    c_dt_bias = din("c_dt_bias", [depth, 8])
    c_onorm_g = din("c_onorm_g", [depth, 128])
    nak_out = dout("nak", [NSP, depth, 256, 128])
    nav_out = dout("nav", [NSP, depth, 256, 128])
    nckv_out = dout("nckv", [NSP, depth, 256, 256])
    nkr_out = dout("nkr", [NSP, depth, 256, 32])
    ncs_out = dout("ncs", [NSP, depth, 2, 4, 128, 128])
    y_out = dout("y", [NTOK, D])
    dbg_out = {}
    for name, shape in dbg:
        dbg_out[name] = dout(name, shape)

    with ExitStack() as st:
        P = Prog(nc, st)

        K.uid = 0

        K.phase = "init"
        K.minrem = {}

        def sb(name, shape, dt=F32, stack=st):
            K.uid += 1
            t_ = stack.enter_context(nc.sbuf_tensor("%s_%d" % (name, K.uid), list(shape), dt))
            K.minrem[K.phase] = min(K.minrem.get(K.phase, 1 << 30), nc.sbuf_bytes_remaining)
            return t_

        def psb(name, shape=(128, 512), dt=F32, stack=st):
            K.uid += 1
            return stack.enter_context(nc.psum_tensor("%s_%d" % (name, K.uid), list(shape), dt))

        x = sb("x", [128, NT, D])
        pb = [psb("pb%d" % i) for i in range(7)]
        pbb = psb("pbb", (128, 1024), BF16)
        ident_f = sb("ident_f", [128, 128])
        ident_b = sb("ident_b", [128, 128], BF16)
        condT_sb = sb("condT_sb", [128, 8, 2])
        scT = sb("scT", [128, 8, 2], BF16)
        modT = sb("modT", [128, 48, 2])
        affA = sb("affA", [128, 2, 8, 2])
        nrmT = sb("nrmT", [128, 2, 8])
        ss = sb("ss", [128, NT])
        rstd = sb("rstd", [128, NT])
        junkb = [sb("junk%d" % i, [128, D], BF16) for i in range(2)] * 2
        K.nj = 0
        xnb = [sb("xnb%d" % i, [128, D]) for i in range(2)]

        P.dma(ident_f[:], cst_ident[:, :])
        P.dma(condT_sb[:], condT[:, :, :])
        for t in range(NT):
            P.dma(x[:, t, :], xin[t * 128:(t + 1) * 128, :])
        P.copy("dve", ident_b[:], ident_f[:])
        ones_b = sb("ones_b", [128, 128], BF16)
        ones_f = sb("ones_f", [128, 128])
        P.memset("dve", ones_b[:], 1.0)
        P.memset("dve", ones_f[:], 1.0)
        P.act(scT[:], condT_sb[:], AF.Silu)

        def cond_of(t):
            return 0 if t < NTP else 1

        def calc_rstd(tiles):
            for t in tiles:
                K.nj += 1
                P.act(junkb[K.nj % 2][:], x[:, t, :], AF.Square, accum_out=ss[:, t:t + 1])
            t0, t1 = tiles[0], tiles[-1] + 1
            P.ts("dve", rstd[:, t0:t1], ss[:, t0:t1], 1.0 / D, EPS, op0=ALU.mult, op1=ALU.add)
            P.act(rstd[:, t0:t1], rstd[:, t0:t1], AF.Sqrt)
            P.recip(rstd[:, t0:t1], rstd[:, t0:t1])

        K.ntr = 0
        K.ew = "pool"

        def build_hT(dst, tiles, which):
            for ti, t in enumerate(tiles):
                cd = cond_of(t)
                xn = xnb[K.ntr % 2]
                K.ntr += 1
                P.ts("dve", xn[:], x[:, t, :], rstd[:, t:t + 1], None, op0=ALU.mult)
                for g in range(2):
                    pp = pb[5 + g]
                    for c in range(4):
                        P.tr(pp[:, c * 128:(c + 1) * 128], xn[:, (g * 4 + c) * 128:(g * 4 + c + 1) * 128], ident_f[:])
                    for c in range(4):
                        ch = g * 4 + c
                        a_ap = affA[:, which, ch, cd:cd + 1]
                        b_ap = modT[:, (3 * which) * 8 + ch, cd:cd + 1]
                        o_ap = dst[:, ch, ti * 128:(ti + 1) * 128]
                        if g % 2 == 0:
                            P.act(o_ap, pp[:, c * 128:(c + 1) * 128], AF.Identity, bias=b_ap, scale=a_ap)
                        else:
                            P.ts("dve", o_ap, pp[:, c * 128:(c + 1) * 128], a_ap, b_ap, op0=ALU.mult, op1=ALU.add)

        def make_G(dst, which, cd, ph):
            dg = [sb("dgG%d" % i, [128, 128], F32, ph) for i in range(2)]
            j = 2 + 3 * which
            for c in range(8):
                P.ts("dve", dg[c % 2][:], ident_f[:], modT[:, j * 8 + c, cd:cd + 1], None, op0=ALU.mult)
                pg = pb[c % 2]
                P.mm(pg[:, 0:128], ones_f[:], dg[c % 2][:])
                P.copy("act", dst[:, c * 128:(c + 1) * 128], pg[:, 0:128])

        def adaln(l):
            with ExitStack() as ph:
                wb = [sb("wada%d" % i, [128, 8, D], BF16, ph) for i in range(2)]
                badaT = sb("badaT", [128, 48], F32, ph)
                P.dma(badaT[:], b_adaT[l, :, :])
                P.dma(nrmT[:, 0, :], norm1T[l, :, :])
                P.dma(nrmT[:, 1, :], norm2T[l, :, :])
                for j in range(6):
                    w = wb[j % 2]
                    P.dma(w[:], w_ada[l, :, j * D:(j + 1) * D].rearrange("(c p) n -> p c n", p=128), q="pool")
                    pm = pb[j % 2]
                    for c in range(8):
                        for kc in range(8):
                            P.mm(pm[:, c * 2:c * 2 + 2], w[:, kc, c * 128:(c + 1) * 128], scT[:, kc, :],
                                 start=(kc == 0), stop=(kc == 7))
                    P.tt("dve", modT[:, j * 8:(j + 1) * 8, :], pm[:, 0:16].rearrange("p (c k) -> p c k", k=2),
                         badaT[:, j * 8:(j + 1) * 8].unsqueeze(2).broadcast_to([128, 8, 2]), ALU.add)
                for which in range(2):
                    j = 1 + 3 * which
                    P.ts("dve", affA[:, which, :, :], modT[:, j * 8:(j + 1) * 8, :], 1.0, None, op0=ALU.add)
                    P.tt("dve", affA[:, which, :, :], affA[:, which, :, :],
                         nrmT[:, which, :].unsqueeze(2).broadcast_to([128, 8, 2]), ALU.mult)
                P.fence()

        def ffn(l):
            with ExitStack() as ph:
                h2T = sb("h2T", [128, 8, NTOK], BF16, ph)
                w1b = [sb("w1b%d" % i, [128, 8, 512], BF16, ph) for i in range(2)]
                w2b = [sb("w2b%d" % i, [128, 4, D], BF16, ph) for i in range(2)]
                aTb = [sb("aTb%d" % i, [128, 4, 256], BF16, ph) for i in range(2)]
                rfb = [sb("rfb%d" % i, [128, 256], F32, ph) for i in range(2)]
                tmpb = [sb("tmpb%d" % i, [128, 512], F32, ph) for i in range(2)]
                G2 = sb("G2", [128, 2, D], F32, ph)
                for cd_ in range(2):
                    make_G(G2[:, cd_, :], 1, cd_, ph)

                def load_w(g):
                    P.dma(w1b[g % 2][:], w_ff1[l, :, g * 512:(g + 1) * 512].rearrange("(c p) n -> p c n", p=128), q="pool")
                    P.dma(w2b[g % 2][:], w_ff2[l, g * 512:(g + 1) * 512, :].rearrange("(c p) n -> p c n", p=128), q="pool")

                load_w(0)
                calc_rstd(list(range(NT)))
                build_hT(h2T, list(range(NT)), 1)
                K.cnt = 0
                NG = 8
                items = [(g, blk) for g in range(NG) for blk in range(NT // 2)]

                def ff1(g, blk):
                    w1 = w1b[g % 2]
                    aT = aTb[blk % 2]
                    for fc in range(4):
                        pa = pb[fc % 2]
                        for kc in range(8):
                            P.mm(pa[:, 0:256], w1[:, kc, fc * 128:(fc + 1) * 128],
                                 h2T[:, kc, blk * 256:(blk + 1) * 256], start=(kc == 0), stop=(kc == 7))
                        rf = rfb[fc % 2]
                        P.act(rf[:], pa[:, 0:256], AF.Relu)
                        P.tt("dve" if fc % 2 == 0 else "pool", aT[:, fc, :], rf[:], rf[:], ALU.mult)

                def ff2(g, blk):
                    w2 = w2b[g % 2]
                    aT = aTb[blk % 2]
                    for tt_ in range(2):
                        t = blk * 2 + tt_
                        cd = cond_of(t)
                        for half in range(2):
                            py = pb[2 + tt_ * 2 + half]
                            for fc in range(4):
                                P.mm(py[:], aT[:, fc, tt_ * 128:(tt_ + 1) * 128],
                                     w2[:, fc, half * 512:(half + 1) * 512], start=(fc == 0), stop=(fc == 3))
                            if cd == 0:
                                tmp = tmpb[K.cnt % 2]
                                K.cnt += 1
                                P.tt("dve", tmp[:], py[:], G2[:, cd, half * 512:(half + 1) * 512], ALU.mult)
                                P.tt(K.ew, x[:, t, half * 512:(half + 1) * 512],
                                     x[:, t, half * 512:(half + 1) * 512], tmp[:], ALU.add)
                            else:
                                P.tt("dve", x[:, t, half * 512:(half + 1) * 512],
                                     x[:, t, half * 512:(half + 1) * 512], py[:], ALU.add)
                    if blk == NTP // 2 - 1:
                        P.tt("pool", w2[:], w2[:], G2[:, 1, :].unsqueeze(1).broadcast_to([128, 4, D]), ALU.mult)

                ff1(*items[0])
                for n, (g, blk) in enumerate(items):
                    if blk == 0 and g + 1 < NG:
                        load_w(g + 1)
                    if n + 1 < len(items):
                        ff1(*items[n + 1])
                    ff2(g, blk)
                P.fence()

        def final_norm():
            with ExitStack() as ph:
                gf = sb("gf", [128, D], F32, ph)
                ob = [sb("ob%d" % i, [128, D], F32, ph) for i in range(2)]
                P.dma(gf[:], final_g.partition_broadcast(128) if False else final_g[0:1, :].broadcast_to([128, D]))
                calc_rstd(list(range(NT)))
                for t in range(NT):
                    o = ob[t % 2]
                    P.stt(o[:], x[:, t, :], rstd[:, t:t + 1], gf[:], ALU.mult, ALU.mult)
                    P.dma(y_out[t * 128:(t + 1) * 128, :], o[:])
                P.fence()


        units = [
            dict(name="P", tiles=list(range(0, NTP)), cond=0, prompt=True,
                 seqs=[dict(t0=2 * i, nt=2, ctx=False, idx=i) for i in range(NSP)]),
            dict(name="S", tiles=list(range(NTP, NT)), cond=1, prompt=False,
                 seqs=[dict(t0=NTP, nt=NTS, ctx=True, idx=0)]),
        ]
        K.rr = 0

        def blocks_of(unit, bs=512):
            out = []
            for sq in unit["seqs"]:
                n = sq["nt"] * 128
                o = 0
                while o < n:
                    b = min(bs, n - o)
                    out.append((sq, sq["t0"] * 128 + o, b))
                    o += b
            return out

        def attend(KT, V1, QT, nq, negm, key_tiles, obanks, sbanks, PTb, scale, tail=None):
            nk = len(key_tiles)
            slots = []

            def scores(i):
                kt, mask = key_tiles[i]
                sp_ = sbanks[K.rr % len(sbanks)]
                pt_ = PTb[K.rr % len(PTb)]
                K.rr += 1
                P.mm(sp_[:, 0:nq], KT(kt), QT, start=True, stop=(mask is None and negm is None))
                if mask is not None:
                    P.mm(sp_[:, 0:nq], ident_b[:], mask, start=False, stop=(negm is None))
                if negm is not None:
                    P.mm(sp_[:, 0:nq], ones_b[0:1, 0:128], negm, start=False, stop=True)
                slots.append((sp_, pt_))

            scores(0)
            for i, (kt, mask) in enumerate(key_tiles):
                if i + 1 < nk:
                    scores(i + 1)
                sp_, pt_ = slots[i]
                P.act(pt_[:, 0:nq], sp_[:, 0:nq], AF.Exp, scale=scale)
                for qi, ob in enumerate(obanks):
                    last = (i == nk - 1) and tail is None
                    P.mm(ob, pt_[:, qi * 128:(qi + 1) * 128], V1(kt), start=(i == 0), stop=last)
            if tail is not None:
                tail()

        def mixer_B(l, unit, obT):
            with ExitStack() as ph:
                tiles = unit["tiles"]
                ntok_u = len(tiles) * 128
                tok_base = tiles[0] * 128
                has_ctx = any(sq["ctx"] for sq in unit["seqs"])
                maxkeys = max(sq["nt"] * 128 + (256 if sq["ctx"] else 0) for sq in unit["seqs"])
                maxnt = max(sq["nt"] for sq in unit["seqs"])
                nkeys_u = ntok_u + (256 if has_ctx else 0)
                cqnT = sb("cqnT", [128, 3, ntok_u], BF16, ph)
                ckvT = sb("ckvT", [128, 2, nkeys_u], BF16, ph)
                krT = sb("krT", [32, nkeys_u], BF16, ph)
                scale = 96.0 ** -0.5
                b1 = ExitStack()
                wB = sb("wB", [128, 8, 704], BF16, b1)
                gkvb = sb("gkvb", [128, 256], F32, b1)
                P.dma(wB[:, :, 0:672], w_in[l, :, OFF_BCQ:OFF_BCQ + 672].rearrange("(c p) n -> p c n", p=128), q="pool")
                P.dma(wB[:, :, 672:704], w_in_sw[l, :, 640:672].rearrange("(c p) n -> p c n", p=128), q="pool")
                P.dma(gkvb[:], b_gkv[l:l + 1, :].broadcast_to([128, 256]))
                hTb = [sb("hTbB%d" % i, [128, 8, 256], BF16, b1) for i in range(2)]
                ssb = [sb("ssB%d" % i, [128, 8], F32, b1) for i in range(2)]
                jk = [sb("jkB%d" % i, [128, 384], F32, b1) for i in range(2)]
                cqn_b = [sb("cqnb%d" % i, [128, 384], BF16, b1) for i in range(2)]
                ckvn_f = [sb("ckvnf%d" % i, [128, 256], F32, b1) for i in range(2)]
                ckvn_b = [sb("ckvnb%d" % i, [128, 256], BF16, b1) for i in range(2)]
                kr_f = [sb("krf%d" % i, [128, 64], F32, b1) for i in range(2)]
                kr_t = [sb("krt%d" % i, [128, 64], F32, b1) for i in range(2)]
                kr_b = [sb("krb%d" % i, [128, 32], BF16, b1) for i in range(2)]
                rKC = [sb("rKC%d" % i, [128, 32], F32, b1) for i in range(2)]
                rKS = [sb("rKS%d" % i, [128, 32], F32, b1) for i in range(2)]

                n_t = 0
                for bi, (sq, tok0, nb) in enumerate(blocks_of(unit, 256)):
                    hT = hTb[bi % 2]
                    btiles = [tok0 // 128 + i for i in range(nb // 128)]
                    build_hT(hT, btiles, 0)
                    for ti, t in enumerate(btiles):
                        r = n_t % 2
                        n_t += 1
                        sv = ssb[r]
                        p1, p2 = pb[0], pb[1]
                        for kc in range(8):
                            P.mm(p1[:, 0:384], hT[:, kc, ti * 128:(ti + 1) * 128], wB[:, kc, 0:384], start=(kc == 0), stop=(kc == 7))
                        for kc in range(8):
                            P.mm(p2[:, 0:320], hT[:, kc, ti * 128:(ti + 1) * 128], wB[:, kc, 384:704], start=(kc == 0), stop=(kc == 7))
                        P.act(jk[r][:, 0:384], p1[:, 0:384], AF.Square, accum_out=sv[:, 0:1])
                        P.ts("dve", sv[:, 1:2], sv[:, 0:1], 1.0 / 384, EPS, op0=ALU.mult, op1=ALU.add)
                        P.act(sv[:, 2:3], sv[:, 1:2], AF.Sqrt)
                        P.recip(sv[:, 3:4], sv[:, 2:3])
                        P.act(cqn_b[r][:], p1[:, 0:384], AF.Copy, scale=sv[:, 3:4])
                        P.act(jk[r][:, 0:256], p2[:, 0:256], AF.Square, accum_out=sv[:, 4:5])
                        P.ts("dve", sv[:, 5:6], sv[:, 4:5], 1.0 / 256, EPS, op0=ALU.mult, op1=ALU.add)
                        P.act(sv[:, 6:7], sv[:, 5:6], AF.Sqrt)
                        P.recip(sv[:, 7:8], sv[:, 6:7])
                        P.act(kr_f[r][:], p2[:, 256:320], AF.Copy)
                        P.act(ckvn_f[r][:], p2[:, 0:256], AF.Copy, scale=sv[:, 7:8])
                        P.tt("dve", ckvn_f[r][:], ckvn_f[r][:], gkvb[:], ALU.mult)
                        P.copy("dve", ckvn_b[r][:], ckvn_f[r][:])
                        lt = t - sq["t0"]
                        if unit["prompt"]:
                            P.dma(nckv_out[sq["idx"], l, lt * 128:(lt + 1) * 128, :], ckvn_f[r][:])
                            P.dma(nkr_out[sq["idx"], l, lt * 128:(lt + 1) * 128, :], kr_f[r][:, 0:32])
                        P.dma(rKC[r][:], ropeKB_C[t * 128:(t + 1) * 128, :])
                        P.dma(rKS[r][:], ropeKB_S[t * 128:(t + 1) * 128, :])
                        P.tt("dve", kr_t[r][:, 0:32], kr_f[r][:, 0:32], rKC[r][:], ALU.mult)
                        P.tt("dve", kr_t[r][:, 32:64], kr_f[r][:, 32:64], rKS[r][:], ALU.mult)
                        P.tt("dve", kr_b[r][:], kr_t[r][:, 0:32], kr_t[r][:, 32:64], ALU.add)
                        for c in range(3):
                            P.tr(pbb[:, c * 128:(c + 1) * 128], cqn_b[r][:, c * 128:(c + 1) * 128], ident_b[:])
                        for c in range(2):
                            P.tr(pbb[:, (3 + c) * 128:(4 + c) * 128], ckvn_b[r][:, c * 128:(c + 1) * 128], ident_b[:])
                        P.tr(pbb[0:32, 640:768], kr_b[r][:], ident_b[:])
                        ut = t - tiles[0]
                        P.copy("dve", cqnT[:, :, ut * 128:(ut + 1) * 128], pbb[:, 0:384].rearrange("p (c n) -> p c n", c=3))
                        P.copy("dve", ckvT[:, :, ut * 128:(ut + 1) * 128], pbb[:, 384:640].rearrange("p (c n) -> p c n", c=2))
                        P.copy("dve", krT[:, ut * 128:(ut + 1) * 128], pbb[0:32, 640:768])

                P.fence()
                b1.close()
                wuq = sb("wuq", [128, 3, 1024], BF16, ph)
                wuqs = sb("wuqs", [128, 3, 1024], BF16, ph)
                wukA = sb("wukA", [128, 2, 8, 128], BF16, ph)
                wukV = sb("wukV", [128, 2, 512], BF16, ph)
                E32 = sb("E32", [32, 128], BF16, ph)
                gqT = sb("gqT", [128, 3], F32, ph)
                P.dma(wuq[:], b_w_uq_r[l].rearrange("(c p) n -> p c n", p=128), q="pool")
                P.dma(wuqs[:], b_w_uq_rs[l].rearrange("(c p) n -> p c n", p=128), q="pool")
                P.dma(gqT[:], b_gqT[l, :, :])
                for c in range(3):
                    P.act(wuq[:, c, :], wuq[:, c, :], AF.Copy, scale=gqT[:, c:c + 1])
                    P.ts("dve", wuqs[:, c, :], wuqs[:, c, :], gqT[:, c:c + 1], None, op0=ALU.mult)
                P.memset("pool", wukA[:], 0.0)
                P.memset("pool", E32[:], 0.0)
                P.copy("pool", E32[:, 32:64], ident_b[0:32, 0:32])
                with ExitStack() as tw:
                    wukv = sb("wukv", [128, 2, 1024], BF16, tw)
                    P.dma(wukv[:], b_w_ukv[l].rearrange("(c p) n -> p c n", p=128), q="pool")
                    wv = wukv[:].rearrange("p c (h e) -> p c h e", e=128)
                    for c in range(2):
                        P.copy("pool", wukA[:, c, :, 64:128], wv[:, c, :, 0:64])
                        P.copy("pool", wukV[:, c, :].rearrange("p (h e) -> p h e", e=64), wv[:, c, :, 64:128])
                    P.fence()
                KTh = sb("KTh", [128, maxkeys], BF16, ph)
                ksq = sb("ksq", [128, 512], BF16, ph)
                V1 = sb("V1", [128, maxkeys // 128, 2, 65], BF16, ph)
                P.memset("pool", V1[:, :, :, 64:65], 1.0)
                rQC = [sb("rQC%d" % i, [128, 512], F32, ph) for i in range(1)] * 2
                rQS = [sb("rQS%d" % i, [128, 512], F32, ph) for i in range(1)] * 2
                qt1 = [sb("qt1_%d" % i, [128, 512], F32, ph) for i in range(1)] * 2
                qt2 = [sb("qt2_%d" % i, [128, 512], F32, ph) for i in range(1)] * 2
                QT = [sb("QT%d" % i, [128, 512], BF16, ph) for i in range(2)]
                qsq = [sb("qsq%d" % i, [128, 512], BF16, ph) for i in range(1)] * 2
                sqr = [sb("sqr%d" % i, [1, 512], F32, ph) for i in range(1)] * 2
                kmx = sb("kmx", [1, 8], F32, ph)
                kmax = sb("kmax", [1, 1], F32, ph)
                PTb = [sb("PTb%d" % i, [128, 512], BF16, ph) for i in range(2)]
                otok = sb("otok", [128, maxnt, 128], BF16, ph)
                rden = [sb("rden%d" % i, [128, 1], F32, ph) for i in range(4)]
                ctx_b = sb("ctx_b", [128, 2, 288], BF16, ph)

                for sq in unit["seqs"]:
                    nk_own = sq["nt"] * 128
                    nkeys = nk_own + (256 if sq["ctx"] else 0)
                    nkt = nkeys // 128
                    kb = sq["t0"] * 128 - tok_base
                    if sq["ctx"]:
                        for c2 in range(2):
                            P.dma(ctx_b[:, c2, 0:256], cckv[l, c2 * 128:(c2 + 1) * 128, :], q="pool")
                            P.dma(ctx_b[:, c2, 256:288], ckr[l, c2 * 128:(c2 + 1) * 128, :], q="pool")
                        for c2 in range(2):
                            for c in range(2):
                                P.tr(pbb[:, c * 128:(c + 1) * 128], ctx_b[:, c2, c * 128:(c + 1) * 128], ident_b[:])
                            P.tr(pbb[0:32, 256:384], ctx_b[:, c2, 256:288], ident_b[:])
                            k0 = kb + nk_own + c2 * 128
                            P.copy("dve", ckvT[:, :, k0:k0 + 128], pbb[:, 0:256].rearrange("p (c n) -> p c n", c=2))
                            P.copy("dve", krT[:, k0:k0 + 128], pbb[0:32, 256:384])
                    qblocks = [(o, min(512, nk_own - o)) for o in range(0, nk_own, 512)]
                    for h in range(8):
                        if h % 2 == 0:
                            for kt in range(nkt):
                                pv = pb[kt % 2]
                                for c in range(2):
                                    P.mm(pv[:, 0:128], ckvT[:, c, kb + kt * 128:kb + (kt + 1) * 128], wukV[:, c, h * 64:(h + 2) * 64], start=(c == 0), stop=(c == 1))
                                P.copy("act", V1[:, kt, :, 0:64], pv[:, 0:128].rearrange("p (h e) -> p h e", e=64))
                        for k0 in range(0, nkeys, 512):
                            kn = min(512, nkeys - k0)
                            pk = pb[4 + (k0 // 512) % 2]
                            for c in range(2):
                                P.mm(pk[:, 0:kn], wukA[:, c, h, :], ckvT[:, c, kb + k0:kb + k0 + kn], start=(c == 0), stop=False)
                            P.mm(pk[:, 0:kn], E32[:], krT[:, kb + k0:kb + k0 + kn], start=False, stop=True)
                            P.copy("act", KTh[:, k0:k0 + kn], pk[:, 0:kn])
                            P.tt("pool", ksq[:, 0:kn], KTh[:, k0:k0 + kn], KTh[:, k0:k0 + kn], ALU.mult)
                            P.memset("dve", KTh[0:1, k0:k0 + kn], 1.0)
                            pn = pb[6]
                            P.mm(pn[0:1, 0:kn], ones_b[:, 0:1], ksq[:, 0:kn])
                            P.reduce(kmx[:, k0 // 512:k0 // 512 + 1], pn[0:1, 0:kn], ALU.max)
                        nkb = (nkeys + 511) // 512
                        P.reduce(kmax[:], kmx[:, 0:nkb], ALU.max)
                        P.act(kmax[:], kmax[:], AF.Sqrt)
                        P.ts("dve", kmax[:], kmax[:], -1.0, None, op0=ALU.mult)
                        def prepQ(qi, h=h):
                            qo, nq = qblocks[qi]
                            r = (h * len(qblocks) + qi) % 2
                            g0 = sq["t0"] * 128 + qo
                            u0 = g0 - tok_base
                            P.dma(rQC[r][:, 0:nq], ropeQB_C[:, g0:g0 + nq])
                            P.dma(rQS[r][:, 0:nq], ropeQB_S[:, g0:g0 + nq])
                            pq = pb[6]
                            for c in range(3):
                                P.mm(pq[:, 0:nq], wuq[:, c, h * 128:(h + 1) * 128], cqnT[:, c, u0:u0 + nq], start=(c == 0), stop=(c == 2))
                            P.tt("dve", qt1[r][:, 0:nq], pq[:, 0:nq], rQC[r][:, 0:nq], ALU.mult)
                            for c in range(3):
                                P.mm(pq[:, 0:nq], wuqs[:, c, h * 128:(h + 1) * 128], cqnT[:, c, u0:u0 + nq], start=(c == 0), stop=(c == 2))
                            P.tt("dve", qt2[r][:, 0:nq], pq[:, 0:nq], rQS[r][:, 0:nq], ALU.mult)
                            P.tt("pool", QT[r][:, 0:nq], qt1[r][:, 0:nq], qt2[r][:, 0:nq], ALU.add)
                            P.tt("pool", qsq[r][:, 0:nq], QT[r][:, 0:nq], QT[r][:, 0:nq], ALU.mult)
                            P.mm(pq[0:1, 0:nq], ones_b[:, 0:1], qsq[r][:, 0:nq])
                            P.act(sqr[r][:, 0:nq], pq[0:1, 0:nq], AF.Ln, bias=1e-30)
                            P.act(sqr[r][:, 0:nq], sqr[r][:, 0:nq], AF.Exp, scale=0.5)
                            P.ts("dve", QT[r][0:1, 0:nq], sqr[r][:, 0:nq], kmax[0:1, 0:1], None, op0=ALU.mult)

                        prepQ(0)
                        for qi, (qo, nq) in enumerate(qblocks):
                            r = (h * len(qblocks) + qi) % 2
                            if qi + 1 < len(qblocks):
                                prepQ(qi + 1)
                            nqt = nq // 128
                            obanks = [pb[qq][:, 0:65] for qq in range(nqt)]
                            attend(lambda kt: KTh[:, kt * 128:(kt + 1) * 128], lambda kt: V1[:, kt, h % 2, :],
                                   QT[r][:, 0:nq], nq, None, [(kt, None) for kt in range(nkt)],
                                   obanks, [pb[4], pb[5]], PTb, scale)
                            for qq in range(nqt):
                                lt = qo // 128 + qq
                                P.recip(rden[qq][:], pb[qq][:, 64:65])
                                P.ts("dve", otok[:, lt, (h % 2) * 64:(h % 2 + 1) * 64], pb[qq][:, 0:64],
                                     rden[qq][:, 0:1], None, op0=ALU.mult)
                        if h % 2 == 1:
                            for lt0 in range(0, sq["nt"], 8):
                                n8 = min(8, sq["nt"] - lt0)
                                for j in range(n8):
                                    P.tr(pbb[:, j * 128:(j + 1) * 128], otok[:, lt0 + j, :], ident_b[:])
                                uo = kb + lt0 * 128
                                P.copy("act", obT[:, h // 2, uo:uo + n8 * 128], pbb[:, 0:n8 * 128])
                P.fence()

        def mixer_A(l, unit, oaT):
            with ExitStack() as ph:
                tiles = unit["tiles"]
                ntok_u = len(tiles) * 128
                tok_base = tiles[0] * 128
                has_ctx = any(sq["ctx"] for sq in unit["seqs"])
                nkeys_u = ntok_u + (256 if has_ctx else 0)
                scale = 64.0 ** -0.5
                KTa = sb("KTa", [64, 2, nkeys_u], BF16, ph)
                Va = sb("Va", [128, nkeys_u // 128, 2, 65], BF16, ph)
                P.memset("pool", Va[:, :, :, 64:65], 1.0)
                rC = sb("rAC", [64, 256], F32, ph)
                rS = sb("rAS", [64, 256], F32, ph)
                t1 = sb("tA1", [64, 256], F32, ph)
                t2 = sb("tA2", [64, 256], F32, ph)
                hTb = [sb("hTbA0", [128, 8, 256], BF16, ph)]

                with ExitStack() as a1:
                    hTb.append(sb("hTbA1", [128, 8, 256], BF16, a1))
                    wAk = sb("wAk", [128, 8, 384], BF16, a1)
                    P.dma(wAk[:, :, 0:256], w_in[l, :, OFF_AK:OFF_AK + 256].rearrange("(c p) n -> p c n", p=128), q="pool")
                    P.dma(wAk[:, :, 256:384], w_in_sw[l, :, 512:640].rearrange("(c p) n -> p c n", p=128), q="pool")
                    kvf = [sb("kvf%d" % i, [128, 256], F32, a1) for i in range(2)]
                    for bi, (sq, tok0, nb) in enumerate(blocks_of(unit, 256)):
                        hT = hTb[bi % 2]
                        btiles = [tok0 // 128 + i for i in range(nb // 128)]
                        build_hT(hT, btiles, 0)
                        u0 = tok0 - tok_base
                        P.dma(rC[:, 0:nb], ropeA_C[:, tok0:tok0 + nb])
                        P.dma(rS[:, 0:nb], ropeA_S[:, tok0:tok0 + nb])
                        for g in range(2):
                            p1, p2 = pb[(2 * g) % 4], pb[(2 * g + 1) % 4]
                            for kc in range(8):
                                P.mm(p1[0:64, 0:nb], wAk[:, kc, g * 64:(g + 1) * 64], hT[:, kc, 0:nb], start=(kc == 0), stop=(kc == 7))
                            P.tt("dve", t1[:, 0:nb], p1[0:64, 0:nb], rC[:, 0:nb], ALU.mult)
                            for kc in range(8):
                                P.mm(p2[0:64, 0:nb], wAk[:, kc, 256 + g * 64:256 + (g + 1) * 64], hT[:, kc, 0:nb], start=(kc == 0), stop=(kc == 7))
                            P.tt("dve", t2[:, 0:nb], p2[0:64, 0:nb], rS[:, 0:nb], ALU.mult)
                            P.tt("pool", KTa[:, g, u0:u0 + nb], t1[:, 0:nb], t2[:, 0:nb], ALU.add)
                        for ti, t in enumerate(btiles):
                            pv = pb[4 + ti % 2]
                            for kc in range(8):
                                P.mm(pv[:, 0:256], hT[:, kc, ti * 128:(ti + 1) * 128], wAk[:, kc, 0:256], start=(kc == 0), stop=(kc == 7))
                            ut = t - tiles[0]
                            P.copy("act", Va[:, ut, :, 0:64], pv[:, 128:256].rearrange("p (g e) -> p g e", e=64))
                            if unit["prompt"]:
                                lt = t - sq["t0"]
                                kv = kvf[ti % 2]
                                P.copy("act", kv[:], pv[:, 0:256])
                                P.dma(nak_out[sq["idx"], l, lt * 128:(lt + 1) * 128, :], kv[:, 0:128])
                                P.dma(nav_out[sq["idx"], l, lt * 128:(lt + 1) * 128, :], kv[:, 128:256])
                    if has_ctx:
                        ctxkb = sb("ctxkb", [128, 2, 128], BF16, a1)
                        ctxvb = sb("ctxvb", [128, 2, 128], BF16, a1)
                        for c2 in range(2):
                            P.dma(ctxkb[:, c2, :], cak[l, c2 * 128:(c2 + 1) * 128, :], q="pool")
                            P.dma(ctxvb[:, c2, :], cav[l, c2 * 128:(c2 + 1) * 128, :], q="pool")
                        for c2 in range(2):
                            for g in range(2):
                                P.tr(pbb[0:64, (c2 * 2 + g) * 128:(c2 * 2 + g + 1) * 128], ctxkb[:, c2, g * 64:(g + 1) * 64], ident_b[:])
                            P.copy("dve", Va[:, ntok_u // 128 + c2, :, 0:64], ctxvb[:, c2, :].rearrange("p (g e) -> p g e", e=64))
                        for c2 in range(2):
                            for g in range(2):
                                P.copy("act", KTa[:, g, ntok_u + c2 * 128:ntok_u + (c2 + 1) * 128],
                                       pbb[0:64, (c2 * 2 + g) * 128:(c2 * 2 + g + 1) * 128])
                    P.fence()

                wAq = sb("wAq", [128, 8, 512], BF16, ph)
                wAqs = sb("wAqs", [128, 8, 512], BF16, ph)
                P.dma(wAq[:], w_in[l, :, 0:512].rearrange("(c p) n -> p c n", p=128), q="pool")
                P.dma(wAqs[:], w_in_sw[l, :, 0:512].rearrange("(c p) n -> p c n", p=128), q="pool")
                maskb = sb("maskb", [128, 8, 128], BF16, ph)
                P.dma(maskb[:], cst_masks[:, :, :], q="pool")
                sinkf = sb("sinkf", [1, 8], F32, ph)
                vsink = sb("vsink", [1, 65], BF16, ph)
                P.dma(sinkf[:], a_sink[l:l + 1, :])
                P.ts("dve", sinkf[:], sinkf[:], 1.0 / scale, None, op0=ALU.mult)
                P.memset("dve", vsink[:], 0.0)
                P.memset("dve", vsink[:, 64:65], 1.0)
                QTb = [sb("QTb%d" % i, [64, 8, 256], BF16, ph) for i in range(2)]
                ksqa = sb("ksqa", [64, 512], BF16, ph)
                qsqa = sb("qsqa", [64, 4, 128], BF16, ph)
                kmx = sb("kmxA", [1, 2, 8], F32, ph)
                kmax = sb("kmaxA", [1, 2], F32, ph)
                sqr = sb("sqrA", [1, 512], F32, ph)
                negf = sqr
                negm = [sb("negmA%d" % i, [1, 512], BF16, ph) for i in range(2)]
                srow = sb("srowA", [1, 512], F32, ph)
                psink = [sb("psinkA%d" % i, [1, 512], BF16, ph) for i in range(2)]
                PTb = [sb("PTbA%d" % i, [128, 512], BF16, ph) for i in range(2)]
                otok = [sb("otokA%d" % i, [128, 512], BF16, ph) for i in range(2)]
                rden = [sb("rdenA%d" % i, [128, 1], F32, ph) for i in range(4)]
                nq_ = 0
                nblk = 0
                for sq in unit["seqs"]:
                    kb = sq["t0"] * 128 - tok_base
                    nk_own = sq["nt"] * 128
                    kranges = [(kb, nk_own)] + ([(ntok_u, 256)] if sq["ctx"] else [])
                    for g in range(2):
                        nb_ = 0
                        for (k0, kn_all) in kranges:
                            for o in range(0, kn_all, 512):
                                kn = min(512, kn_all - o)
                                P.tt("pool", ksqa[:, 0:kn], KTa[:, g, k0 + o:k0 + o + kn], KTa[:, g, k0 + o:k0 + o + kn], ALU.mult)
                                P.mm(pb[6][0:1, 0:kn], ones_b[0:64, 0:1], ksqa[:, 0:kn])
                                P.reduce(kmx[:, g, nb_:nb_ + 1], pb[6][0:1, 0:kn], ALU.max)
                                nb_ += 1
                        P.reduce(kmax[:, g:g + 1], kmx[:, g, 0:nb_], ALU.max)
                    P.act(kmax[:], kmax[:], AF.Sqrt)
                    P.ts("dve", kmax[:], kmax[:], -1.0, None, op0=ALU.mult)
                    blocks = [(q0, min(2, sq["nt"] - q0)) for q0 in range(0, sq["nt"], 2)]

                    def qproj(b):
                        q0, nbt = blocks[b]
                        nb = nbt * 128
                        tok0 = (sq["t0"] + q0) * 128
                        hT = hTb[0]
                        QT = QTb[b % 2]
                        build_hT(hT, [sq["t0"] + q0 + i for i in range(nbt)], 0)
                        P.dma(rC[:, 0:nb], ropeA_C[:, tok0:tok0 + nb])
                        P.dma(rS[:, 0:nb], ropeA_S[:, tok0:tok0 + nb])
                        for h in range(8):
                            p1 = pb[6]
                            for kc in range(8):
                                P.mm(p1[0:64, 0:nb], wAq[:, kc, h * 64:(h + 1) * 64], hT[:, kc, 0:nb], start=(kc == 0), stop=(kc == 7))
                            for kc in range(8):
                                P.mm(p1[0:64, 256:256 + nb], wAqs[:, kc, h * 64:(h + 1) * 64], hT[:, kc, 0:nb], start=(kc == 0), stop=(kc == 7))
                            P.tt("dve", t1[:, 0:nb], p1[0:64, 0:nb], rC[:, 0:nb], ALU.mult)
                            P.tt("dve", t2[:, 0:nb], p1[0:64, 256:256 + nb], rS[:, 0:nb], ALU.mult)
                            P.tt("pool", QT[:, h, 0:nb], t1[:, 0:nb], t2[:, 0:nb], ALU.add)

                    items = [(b, qi, g) for b in range(len(blocks)) for qi in range(blocks[b][1]) for g in range(2)]

                    def prep_item(n):
                        b, qi, g = items[n]
                        r = n % 2
                        QTg = QTb[b % 2][:, 4 * g:4 * g + 4, qi * 128:(qi + 1) * 128]
                        P.tt("pool", qsqa[:], QTg, QTg, ALU.mult)
                        P.mm(pb[6][0:1, 0:512], ones_b[0:64, 0:1], qsqa[:])
                        P.act(sqr[:], pb[6][0:1, 0:512], AF.Ln, bias=1e-30)
                        P.act(sqr[:], sqr[:], AF.Exp, scale=0.5)
                        P.ts("dve", negf[:], sqr[:], kmax[0:1, g:g + 1], None, op0=ALU.mult)
                        P.copy("dve", negm[r][:], negf[:])
                        for hh in range(4):
                            P.ts("dve", srow[:, hh * 128:(hh + 1) * 128], negf[:, hh * 128:(hh + 1) * 128],
                                 sinkf[0:1, 4 * g + hh:4 * g + hh + 1], None, op0=ALU.add)
                        P.act(psink[r][:], srow[:], AF.Exp, scale=scale)

                    qproj(0)
                    prep_item(0)
                    for n, (b, qi, g) in enumerate(items):
                        r = n % 2
                        q0 = blocks[b][0]
                        qt = q0 + qi
                        uq = kb + qt * 128
                        ot = otok[qt % 2]
                        if qi == 0 and g == 0 and b + 1 < len(blocks):
                            qproj(b + 1)
                        if n + 1 < len(items):
                            prep_item(n + 1)
                        QTg = QTb[b % 2][:, 4 * g:4 * g + 4, qi * 128:(qi + 1) * 128]
                        if sq["ctx"]:
                            kts = []
                            if qt > 0:
                                kts.append((kb // 128 + qt - 1, maskb[:, 0:4, :]))
                            kts.append((kb // 128 + qt, None))
                            if qt + 1 < sq["nt"]:
                                kts.append((kb // 128 + qt + 1, maskb[:, 4:8, :]))
                            kts += [(ntok_u // 128, None), (ntok_u // 128 + 1, None)]
                        else:
                            kts = [(kb // 128 + j, None) for j in range(sq["nt"])]
                        obanks = [pb[hh][:, 0:65] for hh in range(4)]

                        def tail(r=r):
                            for hh in range(4):
                                P.mm(pb[hh][:, 0:65], psink[r][0:1, hh * 128:(hh + 1) * 128], vsink[:], start=False, stop=True)

                        attend(lambda kt: KTa[:, g, kt * 128:(kt + 1) * 128], lambda kt: Va[:, kt, g, :],
                               QTg, 512, negm[r][:], kts, obanks, [pb[4], pb[5]], PTb, scale, tail=tail)
                        for hh in range(4):
                            h = 4 * g + hh
                            P.recip(rden[hh][:], pb[hh][:, 64:65])
                            P.ts("dve", ot[:, h * 64:(h + 1) * 64], pb[hh][:, 0:64], rden[hh][:, 0:1], None, op0=ALU.mult)
                        if g == 1:
                            for c in range(4):
                                P.tr(pbb[:, c * 128:(c + 1) * 128], ot[:, c * 128:(c + 1) * 128], ident_b[:])
                            P.copy("act", oaT[:, :, uq:uq + 128], pbb[:, 0:512].rearrange("p (c n) -> p c n", c=4))
                P.fence()

        def mixer_C(l, unit, ocT):
            with ExitStack() as ph:
                tiles = unit["tiles"]
                ntok_u = len(tiles) * 128
                nt_u = len(tiles)
                tok_base = tiles[0] * 128
                trc = sb("trc", [128, 2, 2, 128], BF16, ph)
                trif = sb("trif", [128, 2, 128], F32, ph)
                lvN = sb("lvN", [128, 7, 2, 128], BF16, ph)
                lvA = sb("lvA", [128, 7, 2, 128], BF16, ph)
                convw = sb("convw", [128, 12, 3], F32, ph)
                alog = sb("alog", [128, 8], F32, ph)
                dtb = sb("dtb", [128, 8], F32, ph)
                ong = sb("ong", [128, 128], F32, ph)
                wab = sb("wab", [128, 8, 16], BF16, ph)
                gb = sb("gb", [128, nt_u, 16], F32, ph)
                P.dma(trc[:], cst_trc[:, :, :, :], q="pool")
                P.dma(lvN[:], cst_lvN[:, :, :, :], q="pool")
                P.dma(lvA[:], cst_lvA[:, :, :, :], q="pool")
                P.dma(trif[:, 0, :], cst_tri[:, 1, :])
                P.dma(trif[:, 1, :], cst_tri[:, 3, :])
                P.dma(convw[:], c_convT[l, :, :, :])
                P.dma(alog[:], c_a_log[l:l + 1, :].broadcast_to([128, 8]))
                P.dma(dtb[:], c_dt_bias[l:l + 1, :].broadcast_to([128, 8]))
                P.dma(ong[:], c_onorm_g[l:l + 1, :].broadcast_to([128, 128]))
                P.dma(wab[:], w_in[l, :, OFF_CA:OFF_CA + 16].rearrange("(c p) n -> p c n", p=128), q="pool")
                P.act(alog[:], alog[:], AF.Exp)
                P.ts("dve", alog[:], alog[:], -1.0, None, op0=ALU.mult)
                sm = [sb("smC%d" % i, [128, 8], F32, ph) for i in range(6)]

                for h in range(4):
                    with ExitStack() as hs:
                        qkvT = sb("qkvT", [128, 3, ntok_u], BF16, hs)
                        zs = sb("zs", [128, nt_u, 128], F32, hs)
                        oacc = sb("oacc", [128, nt_u, 128], F32, hs)
                        P.memset("pool", oacc[:], 0.0)
                        with ExitStack() as ps_:
                            wC = sb("wC", [128, 8, 512], BF16, ps_)
                            for j in range(3):
                                c0 = OFF_CQKV + j * 512 + h * 128
                                P.dma(wC[:, :, j * 128:(j + 1) * 128], w_in[l, :, c0:c0 + 128].rearrange("(c p) n -> p c n", p=128), q="pool")
                            P.dma(wC[:, :, 384:512], w_in[l, :, OFF_CZ + h * 128:OFF_CZ + (h + 1) * 128].rearrange("(c p) n -> p c n", p=128), q="pool")
                            hTb = [sb("hTbC%d" % i, [128, 8, 512], BF16, ps_) for i in range(2)]
                            rawT = sb("rawT", [128, 3, ntok_u], F32, ps_)
                            cv = [sb("cvC%d" % i, [128, 512], F32, ps_) for i in range(2)]
                            zf = [sb("zfC%d" % i, [128, 128], F32, ps_) for i in range(2)]
                            for bi, (sq, tok0, nb) in enumerate(blocks_of(unit)):
                                hT = hTb[bi % 2]
                                btiles = [tok0 // 128 + i for i in range(nb // 128)]
                                build_hT(hT, btiles, 0)
                                u0 = tok0 - tok_base
                                for j in range(3):
                                    pj = pb[j % 2]
                                    for kc in range(8):
                                        P.mm(pj[:, 0:nb], wC[:, kc, j * 128:(j + 1) * 128], hT[:, kc, 0:nb], start=(kc == 0), stop=(kc == 7))
                                    P.copy("act", rawT[:, j, u0:u0 + nb], pj[:, 0:nb])
                                for ti, t in enumerate(btiles):
                                    ut = t - tiles[0]
                                    pz = pb[2 + ti % 2]
                                    for kc in range(8):
                                        P.mm(pz[:, 0:128], hT[:, kc, ti * 128:(ti + 1) * 128], wC[:, kc, 384:512], start=(kc == 0), stop=(kc == 7))
                                    P.act(zf[ti % 2][:], pz[:, 0:128], AF.Silu)
                                    P.tt("pool", zs[:, ut, :], zf[ti % 2][:], ong[:], ALU.mult)
                                    if h == 0:
                                        pa_ = pb[4]
                                        for kc in range(8):
                                            P.mm(pa_[:, 0:16], hT[:, kc, ti * 128:(ti + 1) * 128], wab[:, kc, :], start=(kc == 0), stop=(kc == 7))
                                        P.tt("dve", sm[0][:], pa_[:, 0:8], dtb[:], ALU.add)
                                        P.act(sm[1][:], sm[0][:], AF.Abs)
                                        P.act(sm[2][:], sm[1][:], AF.Exp, scale=-1.0)
                                        P.act(sm[3][:], sm[2][:], AF.Ln, bias=1.0)
                                        P.ts("dve", sm[4][:], sm[0][:], 0.0, None, op0=ALU.max)
                                        P.tt("dve", sm[5][:], sm[4][:], sm[3][:], ALU.add)
                                        P.tt("dve", gb[:, ut, 0:8], sm[5][:], alog[:], ALU.mult)
                                        P.act(gb[:, ut, 8:16], pa_[:, 8:16], AF.Sigmoid)
                            for sq in unit["seqs"]:
                                s0 = sq["t0"] * 128 - tok_base
                                n = sq["nt"] * 128
                                for j in range(3):
                                    ci = j * 4 + h
                                    for o in range(0, n, 512):
                                        nb = min(512, n - o)
                                        a0 = s0 + o
                                        c_ = cv[(o // 512) % 2]
                                        P.act(c_[:, 0:nb], rawT[:, j, a0:a0 + nb], AF.Copy, scale=convw[:, ci, 1:2])
                                        lo = 1 if o == 0 else 0
                                        P.stt(c_[:, lo:nb], rawT[:, j, a0 + lo - 1:a0 + nb - 1], convw[:, ci, 0:1], c_[:, lo:nb], ALU.mult, ALU.add)
                                        hi = nb - 1 if o + nb == n else nb
                                        P.stt(c_[:, 0:hi], rawT[:, j, a0 + 1:a0 + hi + 1], convw[:, ci, 2:3], c_[:, 0:hi], ALU.mult, ALU.add)
                                        P.act(qkvT[:, j, a0:a0 + nb], c_[:, 0:nb], AF.Silu)
                            P.fence()

                        gc = sb("gc", [128, 2, nt_u], F32, hs)
                        egc = sb("egc", [128, 2, nt_u], F32, hs)
                        gl = sb("gl", [128, 2, nt_u], F32, hs)
                        edl = sb("edl", [128, 2, nt_u], F32, hs)
                        ge = sb("ge", [128, 2, nt_u], F32, hs)
                        gsel = sb("gsel", [128, 2, nt_u], F32, hs)
                        for d in range(2):
                            P.copy("dve", gsel[:, d, :], gb[:, :, d * 4 + h])
                        for d in range(2):
                            P.mm(pb[6][:, d * nt_u:(d + 1) * nt_u], trif[:, d, :], gsel[:, d, :])
                        P.copy("dve", gc[:], pb[6][:, 0:2 * nt_u].rearrange("p (d n) -> p d n", d=2))
                        P.mm(pb[5][:, 0:2 * nt_u], ones_f[:], gsel[:].rearrange("p d n -> p (d n)"))
                        P.copy("dve", gl[:], pb[5][:, 0:2 * nt_u].rearrange("p (d n) -> p d n", d=2))
                        P.act(egc[:], gc[:], AF.Exp)
                        P.act(ge[:], gl[:], AF.Exp)
                        P.tt("dve", edl[:], gl[:], gc[:], ALU.subtract)
                        P.act(edl[:], edl[:], AF.Exp)

                        Sst = sb("Sst", [128, 2, 128], F32, hs)
                        Sbf = sb("Sbf", [128, 2, 128], BF16, hs)
                        ident2 = ident_f[:].unsqueeze(1).broadcast_to([128, 2, 128])

                        def slotbufs(s_):
                            W = Ctx()
                            f32 = lambda n, shp=(128, 2, 128): sb("%s_s%d" % (n, s_), list(shp), F32, hs)
                            b16 = lambda n, shp=(128, 2, 128): sb("%s_s%d" % (n, s_), list(shp), BF16, hs)
                            W.ssq = f32("ssq", (128, 2, 2)); W.rkq = f32("rkq", (128, 2, 2))
                            W.qkv = b16("qkvtok", (128, 2, 3, 128)); W.kn = b16("kn"); W.qn = b16("qn")
                            W.vke = b16("vke", (128, 2, 256)); W.kqT = b16("kqT", (128, 2, 2, 128))
                            W.diag = f32("diag"); W.DT = f32("DT"); W.tmp1 = f32("tmp1"); W.AT = f32("AT"); W.tmp2 = f32("tmp2")
                            W.qkmT = b16("qkmT"); W.A = f32("A"); W.R = f32("R"); W.RT = f32("RT")
                            W.BkT = [f32("BkT0"), f32("BkT1")]; W.Y = f32("Y"); W.Rb = b16("Rb"); W.u = f32("u")
                            W.w = b16("w"); W.wT = b16("wT"); W.kd = b16("kd"); W.qd = b16("qd"); W.qdT = b16("qdT")
                            W.vnew = b16("vnew"); W.t3 = f32("t3"); W.t4 = f32("t4")
                            return W
                        SB_ = [slotbufs(0), slotbufs(1)]

                        def run_slot(sq, s_):
                            W = SB_[s_]
                            kb = sq["t0"] * 128 - tok_base
                            n = sq["nt"]
                            bA, bB, bC = pb[3 * s_], pb[3 * s_ + 1], pb[3 * s_ + 2]
                            tb0 = s_ * 4 * 128
                            tb = lambda j, cnt=1: pbb[:, tb0 + j * 128:tb0 + (j + cnt) * 128]
                            v3 = lambda ap: ap.rearrange("p (d n) -> p d n", d=2)
                            if s_ == 1:
                                for _ in range(3):
                                    yield
                            for st in range(s_, n, 2):
                                cs = [st, n - 1 - st]
                                uts = [kb // 128 + c for c in cs]
                                gcc = [gc[:, d, uts[d]:uts[d] + 1] for d in range(2)]
                                beta = [gb[:, uts[d], 8 + d * 4 + h:8 + d * 4 + h + 1] for d in range(2)]
                                for d in range(2):
                                    sl = slice(kb + cs[d] * 128, kb + (cs[d] + 1) * 128)
                                    for j in range(3):
                                        P.tr(tb(j), qkvT[:, j, sl], ident_b[:])
                                    P.copy("act", W.qkv[:, d, :, :], tb(0, 3).rearrange("p (j n) -> p j n", j=3))
                                yield
                                for d in range(2):
                                    P.act(W.t3[:, d, :], W.qkv[:, d, 0, :], AF.Square, accum_out=W.ssq[:, d, 0:1])
                                    P.act(W.t4[:, d, :], W.qkv[:, d, 1, :], AF.Square, accum_out=W.ssq[:, d, 1:2])
                                yield
                                P.act(W.rkq[:], W.ssq[:], AF.Ln, bias=EPS)
                                P.act(W.rkq[:], W.rkq[:], AF.Exp, scale=-0.5)
                                yield
                                for d in range(2):
                                    P.act(W.kn[:, d, :], W.qkv[:, d, 1, :], AF.Copy, scale=W.rkq[:, d, 1:2])
                                    P.ts("dve", W.qn[:, d, :], W.qkv[:, d, 0, :], W.rkq[:, d, 0:1], 128.0 ** -0.5, op0=ALU.mult, op1=ALU.mult)
                                    P.copy("pool", W.vke[:, d, 0:128], W.qkv[:, d, 2, :])
                                yield
                                for d in range(2):
                                    P.tr(tb(0), W.kn[:, d, :], ident_b[:])
                                    P.tr(tb(1), W.qn[:, d, :], ident_b[:])
                                    P.copy("act", W.kqT[:, d, :, :], tb(0, 2).rearrange("p (c n) -> p c n", c=2))
                                    P.act(W.vke[:, d, 128:256], W.kn[:, d, :], AF.Copy, scale=egc[:, d, uts[d]:uts[d] + 1])
                                    P.ts("dve", W.kd[:, d, :], W.kn[:, d, :], edl[:, d, uts[d]:uts[d] + 1], None, op0=ALU.mult)
                                    P.act(W.qd[:, d, :], W.qn[:, d, :], AF.Copy, scale=egc[:, d, uts[d]:uts[d] + 1])
                                    P.ts("dve", W.diag[:, d, :], ident_f[:], gcc[d], None, op0=ALU.mult)
                                yield
                                for d in range(2):
                                    P.mm(bA[:, d * 128:(d + 1) * 128], W.kqT[:, d, 0, :], W.kqT[:, d, 0, :])
                                    P.mm(bA[:, 256 + d * 128:256 + (d + 1) * 128], W.kqT[:, d, 0, :], W.kqT[:, d, 1, :])
                                    P.mm(bB[:, d * 128:(d + 1) * 128], ones_f[:], W.diag[:, d, :])
                                    P.tr(tb(2 + d), W.qd[:, d, :], ident_b[:])
                                for d in range(2):
                                    P.ts("dve", W.DT[:, d, :], bB[:, d * 128:(d + 1) * 128], gcc[d], 0.0, op0=ALU.subtract, op1=ALU.min)
                                P.act(W.DT[:], W.DT[:], AF.Exp)
                                P.copy("act", W.qdT[:], v3(tb(2, 2)))
                                yield
                                P.tt("dve", W.tmp1[:], v3(bA[:, 0:256]), W.DT[:], ALU.mult)
                                P.tt("dve", W.tmp2[:], v3(bA[:, 256:512]), W.DT[:], ALU.mult)
                                for d in range(2):
                                    P.stt(W.AT[:, d, :], W.tmp1[:, d, :], beta[d], trc[:, 0, d, :], ALU.mult, ALU.mult)
                                P.tt("pool", W.qkmT[:], W.tmp2[:], trc[:, 1, :, :], ALU.mult)
                                yield
                                for d in range(2):
                                    P.tr(bB[:, 256 + d * 128:256 + (d + 1) * 128], W.AT[:, d, :], ident_f[:])
                                P.copy("act", W.A[:], v3(bB[:, 256:512]))
                                P.tt("dve", W.t3[:], W.AT[:], lvN[:, 0, :, :], ALU.mult)
                                P.tt("dve", W.R[:], ident2, W.t3[:], ALU.subtract)
                                yield
                                P.tt("pool", W.t4[:], W.A[:], lvA[:, 0, :, :], ALU.mult)
                                P.tt("pool", W.RT[:], ident2, W.t4[:], ALU.subtract)
                                P.tt("pool", W.BkT[1][:], W.A[:], lvA[:, 1, :, :], ALU.mult)
                                yield
                                for k in range(1, 7):
                                    Bk = W.BkT[k % 2]
                                    for d in range(2):
                                        P.mm(bC[:, d * 128:(d + 1) * 128], Bk[:, d, :], W.R[:, d, :])
                                    if k < 6:
                                        P.tt("pool", W.BkT[(k + 1) % 2][:], W.A[:], lvA[:, k + 1, :, :], ALU.mult)
                                    P.copy("act", W.Y[:], v3(bC[:, 0:256]))
                                    yield
                                    for d in range(2):
                                        P.mm(bB[:, d * 128:(d + 1) * 128], W.RT[:, d, :], W.Y[:, d, :])
                                        P.mm(bB[:, 256 + d * 128:256 + (d + 1) * 128], W.Y[:, d, :], W.RT[:, d, :])
                                    yield
                                    P.tt("dve", W.R[:], W.R[:], v3(bB[:, 0:256]), ALU.subtract)
                                    P.tt("dve", W.RT[:], W.RT[:], v3(bB[:, 256:512]), ALU.subtract)
                                    yield
                                P.copy("act", W.Rb[:], W.R[:])
                                yield
                                for d in range(2):
                                    P.mm(bA[:, d * 256:(d + 1) * 256], W.Rb[:, d, :], W.vke[:, d, :])
                                for d in range(2):
                                    P.act(W.u[:, d, :], bA[:, d * 256:d * 256 + 128], AF.Copy, scale=beta[d])
                                    P.act(W.w[:, d, :], bA[:, d * 256 + 128:(d + 1) * 256], AF.Copy, scale=beta[d])
                                yield
                                for d in range(2):
                                    P.tr(tb(d), W.w[:, d, :], ident_b[:])
                                P.copy("act", W.wT[:], v3(tb(0, 2)))
                                yield
                                for d in range(2):
                                    P.mm(bC[:, d * 128:(d + 1) * 128], W.wT[:, d, :], Sbf[:, d, :])
                                P.tt("dve", W.vnew[:], W.u[:], v3(bC[:, 0:256]), ALU.subtract)
                                yield
                                for d in range(2):
                                    P.mm(bC[:, 256 + d * 128:256 + (d + 1) * 128], W.qdT[:, d, :], Sbf[:, d, :], start=True, stop=False)
                                    P.mm(bC[:, 256 + d * 128:256 + (d + 1) * 128], W.qkmT[:, d, :], W.vnew[:, d, :], start=False, stop=True)
                                for d in range(2):
                                    P.tt("dve", oacc[:, uts[d], :], oacc[:, uts[d], :], bC[:, 256 + d * 128:256 + (d + 1) * 128], ALU.add)
                                yield
                                for d in range(2):
                                    P.mm(bA[:, d * 128:(d + 1) * 128], W.kd[:, d, :], W.vnew[:, d, :])
                                for d in range(2):
                                    P.stt(Sst[:, d, :], Sst[:, d, :], ge[:, d, uts[d]:uts[d] + 1], bA[:, d * 128:(d + 1) * 128], ALU.mult, ALU.add)
                                P.copy("act", Sbf[:], Sst[:])
                                yield

                        for sq in unit["seqs"]:
                            for d in range(2):
                                if sq["ctx"]:
                                    P.dma(Sst[:, d, :], cstate[l, d, h, :, :])
                                else:
                                    P.memset("pool", Sst[:, d, :], 0.0)
                            P.copy("act", Sbf[:], Sst[:])
                            alive = [run_slot(sq, 0), run_slot(sq, 1)]
                            while alive:
                                for g_ in list(alive):
                                    try:
                                        next(g_)
                                    except StopIteration:
                                        alive.remove(g_)
                            if unit["prompt"]:
                                for d in range(2):
                                    P.dma(ncs_out[sq["idx"], l, d, h, :, :], Sst[:, d, :])
                        on_ss = sb("on_ss", [128, nt_u], F32, hs)
                        on_r = sb("on_r", [128, nt_u], F32, hs)
                        ojk = [sb("ojk%d" % i, [128, 128], F32, hs) for i in range(2)]
                        ob16 = sb("ob16", [128, nt_u, 128], BF16, hs)
                        for ut in range(nt_u):
                            P.act(ojk[ut % 2][:], oacc[:, ut, :], AF.Square, accum_out=on_ss[:, ut:ut + 1])
                        P.ts("dve", on_r[:], on_ss[:], 1.0 / 128, EPS, op0=ALU.mult, op1=ALU.add)
                        P.act(on_r[:], on_r[:], AF.Sqrt)
                        P.recip(on_r[:], on_r[:])
                        for ut in range(nt_u):
                            P.stt(ob16[:, ut, :], oacc[:, ut, :], on_r[:, ut:ut + 1], zs[:, ut, :], ALU.mult, ALU.mult)
                        for u0 in range(0, nt_u, 8):
                            n8 = min(8, nt_u - u0)
                            for j in range(n8):
                                P.tr(pbb[:, j * 128:(j + 1) * 128], ob16[:, u0 + j, :], ident_b[:])
                            P.copy("act", ocT[:, h, u0 * 128:(u0 + n8) * 128], pbb[:, 0:n8 * 128])
                        P.fence()
                P.fence()

        def merge(l, unit, oT):
            tiles = unit["tiles"]
            tok_base = tiles[0] * 128
            cd = unit["cond"]
            for sub0 in range(0, len(tiles), 8):
                stiles = tiles[sub0:sub0 + 8]
                with ExitStack() as ph:
                    nsub = len(stiles) * 128
                    mT = sb("mT", [128, 8, nsub], BF16, ph)
                    with ExitStack() as p2:
                        hTm = sb("hTm", [128, 8, 256], BF16, p2)
                        gt = [sb("gtM%d" % i, [128, 256], F32, p2) for i in range(2)]
                        tm = [sb("tmM%d" % i, [128, 256], F32, p2) for i in range(2)]
                        wg = sb("wg", [128, 8, D], BF16, p2)
                        wbr = sb("wbr", [128, 4, D], BF16, p2)
                        for j in range(3):
                            P.dma(wg[:], w_in[l, :, OFF_GATES + j * D:OFF_GATES + (j + 1) * D].rearrange("(c p) n -> p c n", p=128), q="pool")
                            P.dma(wbr[:], w_branch[l, j].rearrange("(c p) n -> p c n", p=128), q="pool")
                            for b0 in range(0, len(stiles), 2):
                                bt = stiles[b0:b0 + 2]
                                nb = len(bt) * 128
                                build_hT(hTm, bt, 0)
                                u0 = bt[0] * 128 - tok_base
                                m0 = b0 * 128
                                for fc in range(8):
                                    pg, py = pb[fc % 2], pb[2 + fc % 2]
                                    for kc in range(8):
                                        P.mm(pg[:, 0:nb], wg[:, kc, fc * 128:(fc + 1) * 128], hTm[:, kc, 0:nb], start=(kc == 0), stop=(kc == 7))
                                    P.act(gt[fc % 2][:, 0:nb], pg[:, 0:nb], AF.Sigmoid)
                                    for c4 in range(4):
                                        P.mm(py[:, 0:nb], wbr[:, c4, fc * 128:(fc + 1) * 128], oT[j][:, c4, u0:u0 + nb], start=(c4 == 0), stop=(c4 == 3))
                                    if j == 0:
                                        P.tt("dve", mT[:, fc, m0:m0 + nb], py[:, 0:nb], gt[fc % 2][:, 0:nb], ALU.mult)
                                    else:
                                        P.tt("dve", tm[fc % 2][:, 0:nb], py[:, 0:nb], gt[fc % 2][:, 0:nb], ALU.mult)
                                        P.tt("pool", mT[:, fc, m0:m0 + nb], mT[:, fc, m0:m0 + nb], tm[fc % 2][:, 0:nb], ALU.add)
                        P.fence()
                    with ExitStack() as p3:
                        wo = sb("wo", [128, 8, D], BF16, p3)
                        G1 = sb("G1", [128, D], F32, p3)
                        tmo = [sb("tmo%d" % i, [128, 512], F32, p3) for i in range(2)]
                        P.dma(wo[:], w_out[l].rearrange("(c p) n -> p c n", p=128), q="pool")
                        make_G(G1[:], 0, cd, p3)
                        n_ = 0
                        for ti, t in enumerate(stiles):
                            for half in range(2):
                                py = pb[2 + n_ % 4]
                                for kc in range(8):
                                    P.mm(py[:], mT[:, kc, ti * 128:(ti + 1) * 128], wo[:, kc, half * 512:(half + 1) * 512], start=(kc == 0), stop=(kc == 7))
                                tmp = tmo[n_ % 2]
                                n_ += 1
                                P.tt("dve", tmp[:], py[:], G1[:, half * 512:(half + 1) * 512], ALU.mult)
                                P.tt("pool", x[:, t, half * 512:(half + 1) * 512], x[:, t, half * 512:(half + 1) * 512], tmp[:], ALU.add)
                        P.fence()

        K.P, K.nc, K.x, K.pb, K.sb, K.psb = P, nc, x, pb, sb, psb
        def dump_bf16(name, src, n):
            with ExitStack() as ph:
                stg = sb("stg", [128, src.shape[1], n], F32, ph)
                P.copy("dve", stg[:], src)
                P.dma(dbg_out[name][:, :, :], stg[:])
                P.fence()

        for l in range(depth):
            if "adaln" in parts:
                adaln(l)
            calc_rstd(list(range(NT)))
            for unit in units:
                with ExitStack() as us:
                    ntu = len(unit["tiles"]) * 128
                    oT = {}
                    for m in ("C", "B", "A"):
                        oT[m] = sb("o%sT" % m.lower(), [128, 4, ntu], BF16, us)
                        if m in mixers:
                            K.phase = "mix%s_%s" % (m, unit["name"])
                            {"A": mixer_A, "B": mixer_B, "C": mixer_C}[m](l, unit, oT[m])
                            K.phase = "other"
                        else:
                            P.memset("pool", oT[m][:], 0.0)
                        nm = "o%sT_%s%d" % (m.lower(), unit["name"], l)
                        if nm in dbg_out:
                            dump_bf16(nm, oT[m][:], ntu)
                    if "merge" in parts:
                        K.phase = "merge_%s" % unit["name"]
                        merge(l, unit, [oT["A"], oT["B"], oT["C"]])
                        K.phase = "other"
                    P.fence()
            if "ffn" in parts:
                ffn(l)
        if "final" in parts:
            final_norm()
        P.finish()
        K.stats = (P.n_inst, P.n_wait)
    return nc, K


def _rope_feat(R, pos_row, pos_col):
    q = R // 4
    inv = np.power(10000.0, -np.arange(q, dtype=np.float32) / q).astype(np.float32)
    n = len(pos_row)
    C = np.zeros((R, n), np.float32)
    S = np.zeros((R, n), np.float32)
    for r in range(R):
        blk, i = r // q, r % q
        pos = pos_row if blk < 2 else pos_col
        ang = pos.astype(np.float32) * inv[i]
        C[r] = np.cos(ang)
        S[r] = (-1.0 if blk % 2 == 0 else 1.0) * np.sin(ang)
    return C, S


def _swap_perm(R):
    q = R // 4
    return np.array([r + q if (r // q) % 2 == 0 else r - q for r in range(R)])


def make_shared(inp, NTS, NSP):
    f = lambda a: np.ascontiguousarray(np.asarray(a, dtype=np.float32))
    depth = inp["w_in"].shape[0]
    NTP = NSP * 2
    NTOK = (NTP + NTS) * 128
    sh = {}
    for k in ("w_ada", "b_ada", "w_ff1", "w_ff2", "w_in", "b_gkv", "b_w_ukv", "a_sink",
              "w_branch", "w_out", "c_onorm_g"):
        if k in inp:
            sh[k] = f(inp[k])
    sh["b_adaT"] = f(np.asarray(inp["b_ada"]).reshape(depth, 48, 128).transpose(0, 2, 1))
    sh["norm1T"] = f(np.asarray(inp["norm1_g"]).reshape(depth, 8, 128).transpose(0, 2, 1))
    sh["norm2T"] = f(np.asarray(inp["norm2_g"]).reshape(depth, 8, 128).transpose(0, 2, 1))
    sh["final_g"] = f(np.asarray(inp["final_g"]).reshape(1, D))
    sh["b_gqT"] = f(np.asarray(inp["b_gq"]).reshape(depth, 3, 128).transpose(0, 2, 1))
    w_in = np.asarray(inp["w_in"])
    p64, p32 = _swap_perm(64), _swap_perm(32)
    cols = []
    for h in range(8):
        cols += list(OFF_AQ + h * 64 + p64)
    for h in range(2):
        cols += list(OFF_AK + h * 64 + p64)
    cols += list(OFF_BKR + p32)
    sh["w_in_sw"] = f(w_in[:, :, np.array(cols)])
    wuq_ = np.asarray(inp["b_w_uq"], dtype=np.float32).reshape(depth, 384, 8, 96)
    zpad = np.zeros((depth, 384, 8, 32), np.float32)
    sh["b_w_uq_r"] = f(np.concatenate([zpad, wuq_[..., 64:96], wuq_[..., 0:64]], -1).reshape(depth, 384, 1024))
    sh["b_w_uq_rs"] = f(np.concatenate([zpad, wuq_[..., 64 + p32], wuq_[..., 0:64]], -1).reshape(depth, 384, 1024))
    ts = np.arange(NTS * 128)
    row, col = ts // 64, ts % 64
    CA, SA = _rope_feat(64, row, col)
    CB, SB = _rope_feat(32, row, col)
    npad = NTP * 128
    padC = lambda C: np.concatenate([np.ones((C.shape[0], npad), np.float32), C], axis=1)
    padS = lambda S: np.concatenate([np.zeros((S.shape[0], npad), np.float32), S], axis=1)
    sh["ropeA_C"], sh["ropeA_S"] = f(padC(CA)), f(padS(SA))
    sh["ropeQB_C"] = f(np.concatenate([np.ones((32, NTOK), np.float32), padC(CB), np.ones((64, NTOK), np.float32)], axis=0))
    sh["ropeQB_S"] = f(np.concatenate([np.zeros((32, NTOK), np.float32), padS(SB), np.zeros((64, NTOK), np.float32)], axis=0))
    sh["ropeKB_C"], sh["ropeKB_S"] = f(padC(CB).T), f(padS(SB).T)
    sh["cst_ident"] = np.eye(128, dtype=np.float32)
    jj = np.arange(128)[:, None]
    ii = np.arange(128)[None, :]
    sh["cst_tri"] = f(np.stack([(jj < ii), (jj <= ii), (jj > ii), (jj >= ii)], axis=1))
    lv = []
    for k in range(1, 8):
        lv.append(((jj >> k) == (ii >> k)) & ((jj >> (k - 1)) != (ii >> (k - 1))) & (jj < ii))
    tri4 = [(jj < ii), (jj <= ii), (jj > ii), (jj >= ii)]
    sh["cst_trc"] = f(np.stack([np.stack([tri4[0], tri4[2]], 1), np.stack([tri4[1], tri4[3]], 1)], 1))
    sh["cst_lvN"] = f(np.stack([np.stack([m, m.T], 1) for m in lv], 1))
    sh["cst_lvA"] = f(np.stack([np.stack([m.T, m], 1) for m in lv], 1))
    sh["c_convT"] = f(np.asarray(inp["c_conv_w"]).reshape(depth, 3, 12, 128).transpose(0, 3, 2, 1))
    sh["c_a_log"] = f(np.asarray(inp["c_a_log"]).reshape(depth, 8))
    sh["c_dt_bias"] = f(np.asarray(inp["c_dt_bias"]).reshape(depth, 8))
    sel = np.zeros((2, 2, 128), np.float32)
    sel[0, 0] = 1
    sel[1, 1] = 1
    sh["cst_sel"] = sel
    b_ = np.arange(128)[:, None]
    a_ = np.arange(128)[None, :]
    mprev = np.where(a_ > b_, -30000.0, 0.0).astype(np.float32)
    mnext = np.where(b_ > a_, -30000.0, 0.0).astype(np.float32)
    sh["cst_masks"] = f(np.stack([mprev] * 4 + [mnext] * 4, axis=1))
    return sh


def make_core(inp, prompt_ids, sample_id, NTS):
    f = lambda a: np.ascontiguousarray(np.asarray(a, dtype=np.float32))
    xp = np.asarray(inp["x_prompt"])[prompt_ids].reshape(-1, D)
    xs = np.asarray(inp["x_sample"])[sample_id].reshape(-1, D)
    cond = np.stack([np.asarray(inp["c_ctx"]), np.asarray(inp["c"])[sample_id]], 0)
    c = {"xin": f(np.concatenate([xp, xs], 0)),
         "condT": f(cond.reshape(2, 8, 128).transpose(2, 1, 0)),
         "cak": f(np.asarray(inp["cache_a_k"])[sample_id].reshape(-1, 256, 128)),
         "cav": f(np.asarray(inp["cache_a_v"])[sample_id].reshape(-1, 256, 128)),
         "cckv": f(np.asarray(inp["cache_b_ckv"])[sample_id]),
         "ckr": f(np.asarray(inp["cache_b_krope"])[sample_id]),
         "cstate": f(np.asarray(inp["state_c"])[sample_id])}
    return c


_NTS, _NSP = 16, 2
_CACHE = {}


def kernel(**inputs):
    if "nc" not in _CACHE:
        _CACHE["nc"] = build(NTS=_NTS, NSP=_NSP, depth=DEPTH, parts=("adaln", "merge", "ffn", "final"))
    nc, K = _CACHE["nc"]
    names = [a.memorylocations[0].name for a in nc.m.functions[0].allocations
             if isinstance(a, mybir.MemoryLocationSet) and a.kind == "ExternalInput"]
    shared = make_shared(inputs, _NTS, _NSP)
    in_maps = []
    for core in range(8):
        d = dict(shared)
        d.update(make_core(inputs, [2 * core, 2 * core + 1], core // 2, _NTS))
        in_maps.append({k: d[k] for k in names if k in d})
    res = run_bass_kernel_spmd(nc, in_maps, core_ids=list(range(8))).results
    npt = _NSP * 256
    y_prompt = np.concatenate([res[c]["y"][:npt].reshape(_NSP, 256, D) for c in range(8)], 0)
    y_sample = np.stack([res[2 * b]["y"][npt:] for b in range(4)], 0)
    cat = lambda k: np.concatenate([res[c][k] for c in range(8)], 0)
    new_a_k = cat("nak").reshape(16, DEPTH, 256, 2, 64)
    new_a_v = cat("nav").reshape(16, DEPTH, 256, 2, 64)
    new_b_ckv = cat("nckv")
    new_b_krope = cat("nkr")
    new_c_state = cat("ncs")
    f = lambda a: np.ascontiguousarray(a, dtype=np.float32)
    return (f(y_prompt), f(y_sample), f(new_a_k), f(new_a_v), f(new_b_ckv), f(new_b_krope), f(new_c_state))
```

```python
import bisect
from contextlib import ExitStack

import numpy as np
import concourse.bass as bass
import concourse.mybir as mybir
from concourse.bass_utils import run_bass_kernel_spmd

F32 = mybir.dt.float32
BF16 = mybir.dt.bfloat16
AF = mybir.ActivationFunctionType
ALU = mybir.AluOpType
AX = mybir.AxisListType

COMPUTE = ("pe", "act", "dve", "pool")
NDS = 48
NDS_SP = 32


def _region(ap):
    t = ap.tensor
    dims = [(int(s), int(c)) for s, c in ap.ap]
    off = int(ap.offset)
    if type(t).__name__.startswith("DRam"):
        ext = sum((c - 1) * abs(s) for s, c in dims)
        return (t.name, 0, 1, off, off + ext + 1)
    rows = 1
    for d in list(t.shape)[1:]:
        rows *= int(d)
    p0 = off // rows
    f0 = off % rows
    pc = dims[0][1] if dims[0][0] != 0 else 1
    ext = sum((c - 1) * abs(s) for s, c in dims[1:])
    f1 = f0 + ext + 1
    if type(t).__name__.startswith("PSum"):
        be = 1024 if t.dtype == BF16 else 512
        f0 = (f0 // be) * be
        f1 = ((f1 + be - 1) // be) * be
        p0, pc = 0, 128
    return (t.name, p0, p0 + pc, f0, f1)


def _overlap(a, b):
    return a[1] < b[2] and b[1] < a[2] and a[3] < b[4] and b[3] < a[4]


def _contains(a, b):
    return a[1] <= b[1] and b[2] <= a[2] and a[3] <= b[3] and b[4] <= a[4]


class Op:
    __slots__ = ("eng", "fn", "deps", "signal", "is_dma", "sem", "semval", "seq", "idx")

    def __init__(self, eng, fn, idx, is_dma=False):
        self.eng = eng
        self.fn = fn
        self.deps = []
        self.signal = False
        self.is_dma = is_dma
        self.sem = None
        self.semval = 0
        self.seq = None
        self.idx = idx


class Prog:
    def __init__(self, nc, stack):
        self.nc = nc
        self.engs = {"pe": nc.tensor, "act": nc.scalar, "dve": nc.vector, "pool": nc.gpsimd, "sp": nc.sync}
        self.esem = {e: stack.enter_context(nc.semaphore("es_" + e)) for e in COMPUTE}
        self.dsems = [stack.enter_context(nc.semaphore("ds%d" % i)) for i in range(NDS)]
        self.dcount = [0] * NDS
        self.dlast = [None] * NDS
        self.dnext = {"sp": 0, "pool": 0}
        self.pending = []
        self.state = {}
        self.sigcount = {e: 0 for e in COMPUTE}
        self.sigidx = {e: [] for e in COMPUTE}
        self.sigseq = {e: [] for e in COMPUTE}
        self.waited = {f: {e: 0 for e in COMPUTE} for f in self.engs}
        self.waited_dma = {f: {} for f in self.engs}
        self.last_op = {}
        self.open_dmas = []
        self.nops = 0
        self.n_inst = 0
        self.n_wait = 0

    def add(self, eng, fn, reads=(), writes=(), dma=False):
        op = Op(eng, fn, self.nops, dma)
        self.nops += 1
        deps = {}
        rregs = [_region(a) for a in reads]
        wregs = [_region(a) for a in writes]
        for r in rregs:
            is_ps = r[0].startswith("pb")
            for (reg, o, isw) in self.state.get(r[0], ()):
                if _overlap(reg, r) and (isw or (is_ps and o.eng != eng)):
                    deps[o.idx] = o
        for w in wregs:
            for (reg, o, isw) in self.state.get(w[0], ()):
                if _overlap(reg, w):
                    if (not dma) and (not o.is_dma) and o.eng == eng and eng == "pe":
                        continue
                    deps[o.idx] = o
        if dma:
            if eng == "sp":
                s = self.dnext["sp"]
                self.dnext["sp"] = (s + 1) % NDS_SP
            else:
                s = NDS_SP + self.dnext["pool"]
                self.dnext["pool"] = (self.dnext["pool"] + 1) % (NDS - NDS_SP)
            if self.dlast[s] is not None:
                deps[self.dlast[s].idx] = self.dlast[s]
            self.dlast[s] = op
            self.dcount[s] += 1
            op.sem = s
            op.semval = 16 * self.dcount[s]
            self.open_dmas.append(op)
        for o in deps.values():
            if not o.is_dma:
                o.signal = True
        op.deps = list(deps.values())
        for w in wregs:
            lst = self.state.setdefault(w[0], [])
            lst[:] = [rec for rec in lst if not _contains(w, rec[0])]
            lst.append((w, op, True))
        for r in rregs:
            lst = self.state.setdefault(r[0], [])
            if not dma:
                lst[:] = [rec for rec in lst
                          if not ((not rec[2]) and (not rec[1].is_dma) and rec[1].eng == eng and _contains(r, rec[0]))]
            lst.append((r, op, False))
        self.pending.append(op)
        if not dma:
            self.last_op[eng] = op
        return op

    def fence(self):
        lasts = dict(self.last_op)
        dmas = list(self.open_dmas)
        self.open_dmas = []
        for f in self.engs:
            op = Op(f, None, self.nops, False)
            self.nops += 1
            for e, o in lasts.items():
                o.signal = True
                op.deps.append(o)
            op.deps.extend(dmas)
            self.pending.append(op)
        self.state = {}
        self.flush()

    def _seq_of(self, dep):
        if dep.seq is not None:
            return dep.seq
        e = dep.eng
        i = bisect.bisect_left(self.sigidx[e], dep.idx)
        return self.sigseq[e][i]

    def flush(self):
        lastp = {}
        for op in self.pending:
            if (not op.is_dma) and op.fn is not None and op.eng in COMPUTE:
                lastp[op.eng] = op
        for op in lastp.values():
            op.signal = True
        for op in self.pending:
            f = op.eng
            eng = self.engs[f]
            waits = []
            for dep in op.deps:
                if dep.is_dma:
                    if self.waited_dma[f].get(dep.sem, 0) >= dep.semval:
                        continue
                    waits.append((self.dsems[dep.sem], dep.semval))
                    self.waited_dma[f][dep.sem] = dep.semval
                else:
                    e = dep.eng
                    seq = self._seq_of(dep)
                    if self.waited[f][e] >= seq:
                        continue
                    waits.append((self.esem[e], seq))
                    self.waited[f][e] = seq
            self.n_wait += len(waits)
            embed = None
            if op.fn is not None and (not op.is_dma) and waits:
                embed = waits.pop()
            for (sem_, val_) in waits:
                eng.wait_ge(sem_, val_)
            if op.fn is None:
                continue
            ins = op.fn(eng)
            if embed is not None:
                ins._wait_ge(embed[0], embed[1])
            self.n_inst += 1
            if op.is_dma:
                ins.then_inc(self.dsems[op.sem], 16)
            elif op.signal:
                self.sigcount[f] += 1
                op.seq = self.sigcount[f]
                ins.then_inc(self.esem[f], 1)
                self.sigidx[f].append(op.idx)
                self.sigseq[f].append(op.seq)
                self.waited[f][f] = max(self.waited[f][f], 0)
        self.pending = []

    def finish(self):
        self.fence()

    def mm(self, out, lhsT, rhs, start=True, stop=True):
        return self.add("pe", lambda e: e.matmul(out, lhsT, rhs, start=start, stop=stop),
                        reads=[lhsT, rhs], writes=[out])

    def tr(self, out, in_, ident):
        return self.add("pe", lambda e: e.transpose(out, in_, ident), reads=[in_, ident], writes=[out])

    def act(self, out, in_, func, bias=None, scale=1.0, accum_out=None, eng="act"):
        reads = [in_]
        kw = {}
        if bias is not None:
            kw["bias"] = bias
            if not isinstance(bias, (int, float)):
                reads.append(bias)
        if not isinstance(scale, (int, float)):
            reads.append(scale)
        kw["scale"] = scale
        writes = [out]
        if accum_out is not None:
            kw["accum_out"] = accum_out
            writes.append(accum_out)
        return self.add("act", lambda e: e.activation(out, in_, func, **kw), reads=reads, writes=writes)

    def tt(self, eng, out, in0, in1, op):
        return self.add(eng, lambda e: e.tensor_tensor(out, in0, in1, op), reads=[in0, in1], writes=[out])

    def ts(self, eng, out, in0, s1, s2=None, op0=ALU.mult, op1=None, accum_out=None):
        reads = [in0]
        for s in (s1, s2):
            if s is not None and not isinstance(s, (int, float)):
                reads.append(s)
        writes = [out]
        kw = {}
        if op1 is not None:
            kw["op1"] = op1
        if accum_out is not None:
            kw["accum_out"] = accum_out
            writes.append(accum_out)
        return self.add(eng, lambda e: e.tensor_scalar(out, in0, s1, s2, op0, **kw), reads=reads, writes=writes)

    def stt(self, out, in0, scalar, in1, op0, op1, eng="dve"):
        reads = [in0, in1]
        if not isinstance(scalar, (int, float)):
            reads.append(scalar)
        return self.add(eng, lambda e: e.scalar_tensor_tensor(out, in0, scalar, in1, op0, op1),
                        reads=reads, writes=[out])

    def copy(self, eng, out, in_):
        if eng == "act":
            return self.add("act", lambda e: e.copy(out, in_), reads=[in_], writes=[out])
        return self.add(eng, lambda e: e.tensor_copy(out, in_), reads=[in_], writes=[out])

    def memset(self, eng, out, val):
        return self.add(eng, lambda e: e.memset(out, val), reads=[], writes=[out])

    def recip(self, out, in_):
        return self.add("dve", lambda e: e.reciprocal(out, in_), reads=[in_], writes=[out])

    def reduce(self, out, in_, op, axis=AX.X, eng="dve"):
        return self.add(eng, lambda e: e.tensor_reduce(out, in_, axis, op), reads=[in_], writes=[out])

    def dma(self, out, in_, q="sp"):
        return self.add(q, lambda e: e.dma_start(out=out, in_=in_), reads=[in_], writes=[out], dma=True)


D = 1024
DEPTH = 2
D_IN = 6576
OFF_AQ, OFF_AK, OFF_AV = 0, 512, 640
OFF_BCQ, OFF_BCKV, OFF_BKR = 768, 1152, 1408
OFF_CQKV, OFF_CA, OFF_CB, OFF_CZ, OFF_GATES = 1440, 2976, 2984, 2992, 3504
EPS = 1e-6


class Ctx:
    pass


def build(NTS=16, NSP=2, depth=DEPTH, mixers=("B", "A", "C"), dbg=(), parts=("adaln", "ffn", "final")):
    NTP = NSP * 2
    NT = NTP + NTS
    NTOK = NT * 128
    nc = bass.Bass("TRN2", target_bir_lowering=False)
    K = Ctx()

    def din(name, shape):
        return nc.dram_tensor(name, list(shape), F32, kind="ExternalInput").ap()

    def dout(name, shape):
        return nc.dram_tensor(name, list(shape), F32, kind="ExternalOutput").ap()

    xin = din("xin", [NTOK, D])
    condT = din("condT", [128, 8, 2])
    w_ada = din("w_ada", [depth, D, 6 * D])
    b_adaT = din("b_adaT", [depth, 128, 48])
    b_ada = din("b_ada", [depth, 6 * D])
    norm1T = din("norm1T", [depth, 128, 8])
    norm2T = din("norm2T", [depth, 128, 8])
    final_g = din("final_g", [1, D])
    w_ff1 = din("w_ff1", [depth, D, 4 * D])
    w_ff2 = din("w_ff2", [depth, 4 * D, D])
    cst_ident = din("cst_ident", [128, 128])
    w_branch = din("w_branch", [depth, 3, 512, D])
    w_out = din("w_out", [depth, D, D])
    w_in = din("w_in", [depth, D, D_IN])
    w_in_sw = din("w_in_sw", [depth, D, 672])
    b_w_uq_r = din("b_w_uq_r", [depth, 384, 1024])
    b_w_uq_rs = din("b_w_uq_rs", [depth, 384, 1024])
    b_gqT = din("b_gqT", [depth, 128, 3])
    b_gkv = din("b_gkv", [depth, 256])
    b_w_ukv = din("b_w_ukv", [depth, 256, 1024])
    a_sink = din("a_sink", [depth, 8])
    ropeQB_C = din("ropeQB_C", [128, NTOK])
    ropeQB_S = din("ropeQB_S", [128, NTOK])
    ropeKB_C = din("ropeKB_C", [NTOK, 32])
    ropeKB_S = din("ropeKB_S", [NTOK, 32])
    ropeA_C = din("ropeA_C", [64, NTOK])
    ropeA_S = din("ropeA_S", [64, NTOK])
    cak = din("cak", [depth, 256, 128])
    cav = din("cav", [depth, 256, 128])
    cckv = din("cckv", [depth, 256, 256])
    ckr = din("ckr", [depth, 256, 32])
    cstate = din("cstate", [depth, 2, 4, 128, 128])
    cst_masks = din("cst_masks", [128, 8, 128])
    cst_tri = din("cst_tri", [128, 4, 128])
    cst_trc = din("cst_trc", [128, 2, 2, 128])
    cst_lvN = din("cst_lvN", [128, 7, 2, 128])
    cst_lvA = din("cst_lvA", [128, 7, 2, 128])
    c_convT = din("c_convT", [depth, 128, 12, 3])
    c_a_log = din("c_a_log", [depth, 8])
    c_dt_bias = din("c_dt_bias", [depth, 8])
    c_onorm_g = din("c_onorm_g", [depth, 128])
    nak_out = dout("nak", [NSP, depth, 256, 128])
    nav_out = dout("nav", [NSP, depth, 256, 128])
    nckv_out = dout("nckv", [NSP, depth, 256, 256])
    nkr_out = dout("nkr", [NSP, depth, 256, 32])
    ncs_out = dout("ncs", [NSP, depth, 2, 4, 128, 128])
    y_out = dout("y", [NTOK, D])
    dbg_out = {}
    for name, shape in dbg:
        dbg_out[name] = dout(name, shape)

    with ExitStack() as st:
        P = Prog(nc, st)

        K.uid = 0

        K.phase = "init"
        K.minrem = {}

        def sb(name, shape, dt=F32, stack=st):
            K.uid += 1
            t_ = stack.enter_context(nc.sbuf_tensor("%s_%d" % (name, K.uid), list(shape), dt))
            K.minrem[K.phase] = min(K.minrem.get(K.phase, 1 << 30), nc.sbuf_bytes_remaining)
            return t_

        def psb(name, shape=(128, 512), dt=F32, stack=st):
            K.uid += 1
            return stack.enter_context(nc.psum_tensor("%s_%d" % (name, K.uid), list(shape), dt))

        x = sb("x", [128, NT, D])
        pb = [psb("pb%d" % i) for i in range(7)]
        pbb = psb("pbb", (128, 1024), BF16)
        ident_f = sb("ident_f", [128, 128])
        ident_b = sb("ident_b", [128, 128], BF16)
        condT_sb = sb("condT_sb", [128, 8, 2])
        scT = sb("scT", [128, 8, 2], BF16)
        modT = sb("modT", [128, 48, 2])
        affA = sb("affA", [128, 2, 8, 2])
        nrmT = sb("nrmT", [128, 2, 8])
        ss = sb("ss", [128, NT])
        rstd = sb("rstd", [128, NT])
        junkb = [sb("junk%d" % i, [128, D], BF16) for i in range(2)] * 2
        K.nj = 0
        xnb = [sb("xnb%d" % i, [128, D]) for i in range(2)]

        P.dma(ident_f[:], cst_ident[:, :])
        P.dma(condT_sb[:], condT[:, :, :])
        for t in range(NT):
            P.dma(x[:, t, :], xin[t * 128:(t + 1) * 128, :])
        P.copy("dve", ident_b[:], ident_f[:])
        ones_b = sb("ones_b", [128, 128], BF16)
        ones_f = sb("ones_f", [128, 128])
        P.memset("dve", ones_b[:], 1.0)
        P.memset("dve", ones_f[:], 1.0)
        P.act(scT[:], condT_sb[:], AF.Silu)

        def cond_of(t):
            return 0 if t < NTP else 1

        def calc_rstd(tiles):
            for t in tiles:
                K.nj += 1
                P.act(junkb[K.nj % 2][:], x[:, t, :], AF.Square, accum_out=ss[:, t:t + 1])
            t0, t1 = tiles[0], tiles[-1] + 1
            P.ts("dve", rstd[:, t0:t1], ss[:, t0:t1], 1.0 / D, EPS, op0=ALU.mult, op1=ALU.add)
            P.act(rstd[:, t0:t1], rstd[:, t0:t1], AF.Sqrt)
            P.recip(rstd[:, t0:t1], rstd[:, t0:t1])

        K.ntr = 0
        K.ew = "pool"

        def build_hT(dst, tiles, which):
            for ti, t in enumerate(tiles):
                cd = cond_of(t)
                xn = xnb[K.ntr % 2]
                K.ntr += 1
                P.ts("dve", xn[:], x[:, t, :], rstd[:, t:t + 1], None, op0=ALU.mult)
                for g in range(2):
                    pp = pb[5 + g]
                    for c in range(4):
                        P.tr(pp[:, c * 128:(c + 1) * 128], xn[:, (g * 4 + c) * 128:(g * 4 + c + 1) * 128], ident_f[:])
                    for c in range(4):
                        ch = g * 4 + c
                        a_ap = affA[:, which, ch, cd:cd + 1]
                        b_ap = modT[:, (3 * which) * 8 + ch, cd:cd + 1]
                        o_ap = dst[:, ch, ti * 128:(ti + 1) * 128]
                        if g % 2 == 0:
                            P.act(o_ap, pp[:, c * 128:(c + 1) * 128], AF.Identity, bias=b_ap, scale=a_ap)
                        else:
                            P.ts("dve", o_ap, pp[:, c * 128:(c + 1) * 128], a_ap, b_ap, op0=ALU.mult, op1=ALU.add)

        def make_G(dst, which, cd, ph):
            dg = [sb("dgG%d" % i, [128, 128], F32, ph) for i in range(2)]
            j = 2 + 3 * which
            for c in range(8):
                P.ts("dve", dg[c % 2][:], ident_f[:], modT[:, j * 8 + c, cd:cd + 1], None, op0=ALU.mult)
                pg = pb[c % 2]
                P.mm(pg[:, 0:128], ones_f[:], dg[c % 2][:])
                P.copy("act", dst[:, c * 128:(c + 1) * 128], pg[:, 0:128])

        def adaln(l):
            with ExitStack() as ph:
                wb = [sb("wada%d" % i, [128, 8, D], BF16, ph) for i in range(2)]
                badaT = sb("badaT", [128, 48], F32, ph)
                P.dma(badaT[:], b_adaT[l, :, :])
                P.dma(nrmT[:, 0, :], norm1T[l, :, :])
                P.dma(nrmT[:, 1, :], norm2T[l, :, :])
                for j in range(6):
                    w = wb[j % 2]
                    P.dma(w[:], w_ada[l, :, j * D:(j + 1) * D].rearrange("(c p) n -> p c n", p=128), q="pool")
                    pm = pb[j % 2]
                    for c in range(8):
                        for kc in range(8):
                            P.mm(pm[:, c * 2:c * 2 + 2], w[:, kc, c * 128:(c + 1) * 128], scT[:, kc, :],
                                 start=(kc == 0), stop=(kc == 7))
                    P.tt("dve", modT[:, j * 8:(j + 1) * 8, :], pm[:, 0:16].rearrange("p (c k) -> p c k", k=2),
                         badaT[:, j * 8:(j + 1) * 8].unsqueeze(2).broadcast_to([128, 8, 2]), ALU.add)
                for which in range(2):
                    j = 1 + 3 * which
                    P.ts("dve", affA[:, which, :, :], modT[:, j * 8:(j + 1) * 8, :], 1.0, None, op0=ALU.add)
                    P.tt("dve", affA[:, which, :, :], affA[:, which, :, :],
                         nrmT[:, which, :].unsqueeze(2).broadcast_to([128, 8, 2]), ALU.mult)
                P.fence()

        def ffn(l):
            with ExitStack() as ph:
                h2T = sb("h2T", [128, 8, NTOK], BF16, ph)
                w1b = [sb("w1b%d" % i, [128, 8, 512], BF16, ph) for i in range(2)]
                w2b = [sb("w2b%d" % i, [128, 4, D], BF16, ph) for i in range(2)]
                aTb = [sb("aTb%d" % i, [128, 4, 256], BF16, ph) for i in range(2)]
                rfb = [sb("rfb%d" % i, [128, 256], F32, ph) for i in range(2)]
                tmpb = [sb("tmpb%d" % i, [128, 512], F32, ph) for i in range(2)]
                G2 = sb("G2", [128, 2, D], F32, ph)
                for cd_ in range(2):
                    make_G(G2[:, cd_, :], 1, cd_, ph)

                def load_w(g):
                    P.dma(w1b[g % 2][:], w_ff1[l, :, g * 512:(g + 1) * 512].rearrange("(c p) n -> p c n", p=128), q="pool")
                    P.dma(w2b[g % 2][:], w_ff2[l, g * 512:(g + 1) * 512, :].rearrange("(c p) n -> p c n", p=128), q="pool")

                load_w(0)
                calc_rstd(list(range(NT)))
                build_hT(h2T, list(range(NT)), 1)
                K.cnt = 0
                NG = 8
                items = [(g, blk) for g in range(NG) for blk in range(NT // 2)]

                def ff1(g, blk):
                    w1 = w1b[g % 2]
                    aT = aTb[blk % 2]
                    for fc in range(4):
                        pa = pb[fc % 2]
                        for kc in range(8):
                            P.mm(pa[:, 0:256], w1[:, kc, fc * 128:(fc + 1) * 128],
                                 h2T[:, kc, blk * 256:(blk + 1) * 256], start=(kc == 0), stop=(kc == 7))
                        rf = rfb[fc % 2]
                        P.act(rf[:], pa[:, 0:256], AF.Relu)
                        P.tt("dve" if fc % 2 == 0 else "pool", aT[:, fc, :], rf[:], rf[:], ALU.mult)

                def ff2(g, blk):
                    w2 = w2b[g % 2]
                    aT = aTb[blk % 2]
                    for tt_ in range(2):
                        t = blk * 2 + tt_
                        cd = cond_of(t)
                        for half in range(2):
                            py = pb[2 + tt_ * 2 + half]
                            for fc in range(4):
                                P.mm(py[:], aT[:, fc, tt_ * 128:(tt_ + 1) * 128],
                                     w2[:, fc, half * 512:(half + 1) * 512], start=(fc == 0), stop=(fc == 3))
                            if cd == 0:
                                tmp = tmpb[K.cnt % 2]
                                K.cnt += 1
                                P.tt("dve", tmp[:], py[:], G2[:, cd, half * 512:(half + 1) * 512], ALU.mult)
                                P.tt(K.ew, x[:, t, half * 512:(half + 1) * 512],
                                     x[:, t, half * 512:(half + 1) * 512], tmp[:], ALU.add)
                            else:
                                P.tt("dve", x[:, t, half * 512:(half + 1) * 512],
                                     x[:, t, half * 512:(half + 1) * 512], py[:], ALU.add)
                    if blk == NTP // 2 - 1:
                        P.tt("pool", w2[:], w2[:], G2[:, 1, :].unsqueeze(1).broadcast_to([128, 4, D]), ALU.mult)

                ff1(*items[0])
                for n, (g, blk) in enumerate(items):
                    if blk == 0 and g + 1 < NG:
                        load_w(g + 1)
                    if n + 1 < len(items):
                        ff1(*items[n + 1])
                    ff2(g, blk)
                P.fence()

        def final_norm():
            with ExitStack() as ph:
                gf = sb("gf", [128, D], F32, ph)
                ob = [sb("ob%d" % i, [128, D], F32, ph) for i in range(2)]
                P.dma(gf[:], final_g.partition_broadcast(128) if False else final_g[0:1, :].broadcast_to([128, D]))
                calc_rstd(list(range(NT)))
                for t in range(NT):
                    o = ob[t % 2]
                    P.stt(o[:], x[:, t, :], rstd[:, t:t + 1], gf[:], ALU.mult, ALU.mult)
                    P.dma(y_out[t * 128:(t + 1) * 128, :], o[:])
                P.fence()


        units = [
            dict(name="P", tiles=list(range(0, NTP)), cond=0, prompt=True,
                 seqs=[dict(t0=2 * i, nt=2, ctx=False, idx=i) for i in range(NSP)]),
            dict(name="S", tiles=list(range(NTP, NT)), cond=1, prompt=False,
                 seqs=[dict(t0=NTP, nt=NTS, ctx=True, idx=0)]),
        ]
        K.rr = 0

        def blocks_of(unit, bs=512):
            out = []
            for sq in unit["seqs"]:
                n = sq["nt"] * 128
                o = 0
                while o < n:
                    b = min(bs, n - o)
                    out.append((sq, sq["t0"] * 128 + o, b))
                    o += b
            return out

        def attend(KT, V1, QT, nq, negm, key_tiles, obanks, sbanks, PTb, scale, tail=None):
            nk = len(key_tiles)
            slots = []

            def scores(i):
                kt, mask = key_tiles[i]
                sp_ = sbanks[K.rr % len(sbanks)]
                pt_ = PTb[K.rr % len(PTb)]
                K.rr += 1
                P.mm(sp_[:, 0:nq], KT(kt), QT, start=True, stop=(mask is None and negm is None))
                if mask is not None:
                    P.mm(sp_[:, 0:nq], ident_b[:], mask, start=False, stop=(negm is None))
                if negm is not None:
                    P.mm(sp_[:, 0:nq], ones_b[0:1, 0:128], negm, start=False, stop=True)
                slots.append((sp_, pt_))

            scores(0)
            for i, (kt, mask) in enumerate(key_tiles):
                if i + 1 < nk:
                    scores(i + 1)
                sp_, pt_ = slots[i]
                P.act(pt_[:, 0:nq], sp_[:, 0:nq], AF.Exp, scale=scale)
                for qi, ob in enumerate(obanks):
                    last = (i == nk - 1) and tail is None
                    P.mm(ob, pt_[:, qi * 128:(qi + 1) * 128], V1(kt), start=(i == 0), stop=last)
            if tail is not None:
                tail()

        def mixer_B(l, unit, obT):
            with ExitStack() as ph:
                tiles = unit["tiles"]
                ntok_u = len(tiles) * 128
                tok_base = tiles[0] * 128
                has_ctx = any(sq["ctx"] for sq in unit["seqs"])
                maxkeys = max(sq["nt"] * 128 + (256 if sq["ctx"] else 0) for sq in unit["seqs"])
                maxnt = max(sq["nt"] for sq in unit["seqs"])
                nkeys_u = ntok_u + (256 if has_ctx else 0)
                cqnT = sb("cqnT", [128, 3, ntok_u], BF16, ph)
                ckvT = sb("ckvT", [128, 2, nkeys_u], BF16, ph)
                krT = sb("krT", [32, nkeys_u], BF16, ph)
                scale = 96.0 ** -0.5
                b1 = ExitStack()
                wB = sb("wB", [128, 8, 704], BF16, b1)
                gkvb = sb("gkvb", [128, 256], F32, b1)
                P.dma(wB[:, :, 0:672], w_in[l, :, OFF_BCQ:OFF_BCQ + 672].rearrange("(c p) n -> p c n", p=128), q="pool")
                P.dma(wB[:, :, 672:704], w_in_sw[l, :, 640:672].rearrange("(c p) n -> p c n", p=128), q="pool")
                P.dma(gkvb[:], b_gkv[l:l + 1, :].broadcast_to([128, 256]))
                hTb = [sb("hTbB%d" % i, [128, 8, 256], BF16, b1) for i in range(2)]
                ssb = [sb("ssB%d" % i, [128, 8], F32, b1) for i in range(2)]
                jk = [sb("jkB%d" % i, [128, 384], F32, b1) for i in range(2)]
                cqn_b = [sb("cqnb%d" % i, [128, 384], BF16, b1) for i in range(2)]
                ckvn_f = [sb("ckvnf%d" % i, [128, 256], F32, b1) for i in range(2)]
                ckvn_b = [sb("ckvnb%d" % i, [128, 256], BF16, b1) for i in range(2)]
                kr_f = [sb("krf%d" % i, [128, 64], F32, b1) for i in range(2)]
                kr_t = [sb("krt%d" % i, [128, 64], F32, b1) for i in range(2)]
                kr_b = [sb("krb%d" % i, [128, 32], BF16, b1) for i in range(2)]
                rKC = [sb("rKC%d" % i, [128, 32], F32, b1) for i in range(2)]
                rKS = [sb("rKS%d" % i, [128, 32], F32, b1) for i in range(2)]

                n_t = 0
                for bi, (sq, tok0, nb) in enumerate(blocks_of(unit, 256)):
                    hT = hTb[bi % 2]
                    btiles = [tok0 // 128 + i for i in range(nb // 128)]
                    build_hT(hT, btiles, 0)
                    for ti, t in enumerate(btiles):
                        r = n_t % 2
                        n_t += 1
                        sv = ssb[r]
                        p1, p2 = pb[0], pb[1]
                        for kc in range(8):
                            P.mm(p1[:, 0:384], hT[:, kc, ti * 128:(ti + 1) * 128], wB[:, kc, 0:384], start=(kc == 0), stop=(kc == 7))
                        for kc in range(8):
                            P.mm(p2[:, 0:320], hT[:, kc, ti * 128:(ti + 1) * 128], wB[:, kc, 384:704], start=(kc == 0), stop=(kc == 7))
                        P.act(jk[r][:, 0:384], p1[:, 0:384], AF.Square, accum_out=sv[:, 0:1])
                        P.act(sv[:, 2:3], sv[:, 0:1], AF.Ln, bias=EPS, scale=1.0 / 384)
                        P.act(sv[:, 3:4], sv[:, 2:3], AF.Exp, scale=-0.5)
                        P.act(cqn_b[r][:], p1[:, 0:384], AF.Copy, scale=sv[:, 3:4])
                        P.act(jk[r][:, 0:256], p2[:, 0:256], AF.Square, accum_out=sv[:, 4:5])
                        P.act(sv[:, 6:7], sv[:, 4:5], AF.Ln, bias=EPS, scale=1.0 / 256)
                        P.act(sv[:, 7:8], sv[:, 6:7], AF.Exp, scale=-0.5)
                        P.act(kr_f[r][:], p2[:, 256:320], AF.Copy)
                        P.act(ckvn_f[r][:], p2[:, 0:256], AF.Copy, scale=sv[:, 7:8])
                        P.tt("dve", ckvn_f[r][:], ckvn_f[r][:], gkvb[:], ALU.mult)
                        P.copy("dve", ckvn_b[r][:], ckvn_f[r][:])
                        lt = t - sq["t0"]
                        if unit["prompt"]:
                            P.dma(nckv_out[sq["idx"], l, lt * 128:(lt + 1) * 128, :], ckvn_f[r][:])
                            P.dma(nkr_out[sq["idx"], l, lt * 128:(lt + 1) * 128, :], kr_f[r][:, 0:32])
                        P.dma(rKC[r][:], ropeKB_C[t * 128:(t + 1) * 128, :])
                        P.dma(rKS[r][:], ropeKB_S[t * 128:(t + 1) * 128, :])
                        P.tt("dve", kr_t[r][:, 0:32], kr_f[r][:, 0:32], rKC[r][:], ALU.mult)
                        P.tt("dve", kr_t[r][:, 32:64], kr_f[r][:, 32:64], rKS[r][:], ALU.mult)
                        P.tt("dve", kr_b[r][:], kr_t[r][:, 0:32], kr_t[r][:, 32:64], ALU.add)
                        for c in range(3):
                            P.tr(pbb[:, c * 128:(c + 1) * 128], cqn_b[r][:, c * 128:(c + 1) * 128], ident_b[:])
                        for c in range(2):
                            P.tr(pbb[:, (3 + c) * 128:(4 + c) * 128], ckvn_b[r][:, c * 128:(c + 1) * 128], ident_b[:])
                        P.tr(pbb[0:32, 640:768], kr_b[r][:], ident_b[:])
                        ut = t - tiles[0]
                        P.copy("dve", cqnT[:, :, ut * 128:(ut + 1) * 128], pbb[:, 0:384].rearrange("p (c n) -> p c n", c=3))
                        P.copy("dve", ckvT[:, :, ut * 128:(ut + 1) * 128], pbb[:, 384:640].rearrange("p (c n) -> p c n", c=2))
                        P.copy("dve", krT[:, ut * 128:(ut + 1) * 128], pbb[0:32, 640:768])

                P.fence()
                b1.close()
                wuq = sb("wuq", [128, 3, 1024], BF16, ph)
                wuqs = sb("wuqs", [128, 3, 1024], BF16, ph)
                wukA = sb("wukA", [128, 2, 8, 128], BF16, ph)
                wukV = sb("wukV", [128, 2, 512], BF16, ph)
                E32 = sb("E32", [32, 128], BF16, ph)
                gqT = sb("gqT", [128, 3], F32, ph)
                P.dma(wuq[:], b_w_uq_r[l].rearrange("(c p) n -> p c n", p=128), q="pool")
                P.dma(wuqs[:], b_w_uq_rs[l].rearrange("(c p) n -> p c n", p=128), q="pool")
                P.dma(gqT[:], b_gqT[l, :, :])
                for c in range(3):
                    P.act(wuq[:, c, :], wuq[:, c, :], AF.Copy, scale=gqT[:, c:c + 1])
                    P.ts("dve", wuqs[:, c, :], wuqs[:, c, :], gqT[:, c:c + 1], None, op0=ALU.mult)
                P.memset("pool", wukA[:], 0.0)
                P.memset("pool", E32[:], 0.0)
                P.copy("pool", E32[:, 32:64], ident_b[0:32, 0:32])
                with ExitStack() as tw:
                    wukv = sb("wukv", [128, 2, 1024], BF16, tw)
                    P.dma(wukv[:], b_w_ukv[l].rearrange("(c p) n -> p c n", p=128), q="pool")
                    wv = wukv[:].rearrange("p c (h e) -> p c h e", e=128)
                    for c in range(2):
                        P.copy("pool", wukA[:, c, :, 64:128], wv[:, c, :, 0:64])
                        P.copy("pool", wukV[:, c, :].rearrange("p (h e) -> p h e", e=64), wv[:, c, :, 64:128])
                    P.fence()
                KTh = sb("KTh", [128, maxkeys], BF16, ph)
                ksq = sb("ksq", [128, 512], BF16, ph)
                V1 = sb("V1", [128, maxkeys // 128, 2, 65], BF16, ph)
                P.memset("pool", V1[:, :, :, 64:65], 1.0)
                rQC = [sb("rQC%d" % i, [128, 512], F32, ph) for i in range(1)] * 2
                rQS = [sb("rQS%d" % i, [128, 512], F32, ph) for i in range(1)] * 2
                qt1 = [sb("qt1_%d" % i, [128, 512], F32, ph) for i in range(1)] * 2
                qt2 = [sb("qt2_%d" % i, [128, 512], F32, ph) for i in range(1)] * 2
                QT = [sb("QT%d" % i, [128, 512], BF16, ph) for i in range(2)]
                qsq = [sb("qsq%d" % i, [128, 512], BF16, ph) for i in range(1)] * 2
                sqr = [sb("sqr%d" % i, [1, 512], F32, ph) for i in range(1)] * 2
                kmx = sb("kmx", [1, 8], F32, ph)
                kmax = sb("kmax", [1, 1], F32, ph)
                PTb = [sb("PTb%d" % i, [128, 512], BF16, ph) for i in range(2)]
                otok = sb("otok", [128, maxnt, 128], BF16, ph)
                rden = [sb("rden%d" % i, [128, 1], F32, ph) for i in range(4)]
                ctx_b = sb("ctx_b", [128, 2, 288], BF16, ph)

                for sq in unit["seqs"]:
                    nk_own = sq["nt"] * 128
                    nkeys = nk_own + (256 if sq["ctx"] else 0)
                    nkt = nkeys // 128
                    kb = sq["t0"] * 128 - tok_base
                    if sq["ctx"]:
                        for c2 in range(2):
                            P.dma(ctx_b[:, c2, 0:256], cckv[l, c2 * 128:(c2 + 1) * 128, :], q="pool")
                            P.dma(ctx_b[:, c2, 256:288], ckr[l, c2 * 128:(c2 + 1) * 128, :], q="pool")
                        for c2 in range(2):
                            for c in range(2):
                                P.tr(pbb[:, c * 128:(c + 1) * 128], ctx_b[:, c2, c * 128:(c + 1) * 128], ident_b[:])
                            P.tr(pbb[0:32, 256:384], ctx_b[:, c2, 256:288], ident_b[:])
                            k0 = kb + nk_own + c2 * 128
                            P.copy("dve", ckvT[:, :, k0:k0 + 128], pbb[:, 0:256].rearrange("p (c n) -> p c n", c=2))
                            P.copy("dve", krT[:, k0:k0 + 128], pbb[0:32, 256:384])
                    qblocks = [(o, min(512, nk_own - o)) for o in range(0, nk_own, 512)]
                    for h in range(8):
                        if h % 2 == 0:
                            for kt in range(nkt):
                                pv = pb[kt % 2]
                                for c in range(2):
                                    P.mm(pv[:, 0:128], ckvT[:, c, kb + kt * 128:kb + (kt + 1) * 128], wukV[:, c, h * 64:(h + 2) * 64], start=(c == 0), stop=(c == 1))
                                P.copy("act", V1[:, kt, :, 0:64], pv[:, 0:128].rearrange("p (h e) -> p h e", e=64))
                        for k0 in range(0, nkeys, 512):
                            kn = min(512, nkeys - k0)
                            pk = pb[4 + (k0 // 512) % 2]
                            for c in range(2):
                                P.mm(pk[:, 0:kn], wukA[:, c, h, :], ckvT[:, c, kb + k0:kb + k0 + kn], start=(c == 0), stop=False)
                            P.mm(pk[:, 0:kn], E32[:], krT[:, kb + k0:kb + k0 + kn], start=False, stop=True)
                            P.copy("act", KTh[:, k0:k0 + kn], pk[:, 0:kn])
                            P.tt("pool", ksq[:, 0:kn], KTh[:, k0:k0 + kn], KTh[:, k0:k0 + kn], ALU.mult)
                            P.memset("dve", KTh[0:1, k0:k0 + kn], 1.0)
                            pn = pb[6]
                            P.mm(pn[0:1, 0:kn], ones_b[:, 0:1], ksq[:, 0:kn])
                            P.reduce(kmx[:, k0 // 512:k0 // 512 + 1], pn[0:1, 0:kn], ALU.max)
                        nkb = (nkeys + 511) // 512
                        P.reduce(kmax[:], kmx[:, 0:nkb], ALU.max)
                        P.act(kmax[:], kmax[:], AF.Sqrt)
                        P.ts("dve", kmax[:], kmax[:], -1.0, None, op0=ALU.mult)
                        def prepQ(qi, h=h):
                            qo, nq = qblocks[qi]
                            r = (h * len(qblocks) + qi) % 2
                            g0 = sq["t0"] * 128 + qo
                            u0 = g0 - tok_base
                            P.dma(rQC[r][:, 0:nq], ropeQB_C[:, g0:g0 + nq])
                            P.dma(rQS[r][:, 0:nq], ropeQB_S[:, g0:g0 + nq])
                            pq = pb[6]
                            for c in range(3):
                                P.mm(pq[:, 0:nq], wuq[:, c, h * 128:(h + 1) * 128], cqnT[:, c, u0:u0 + nq], start=(c == 0), stop=(c == 2))
                            P.tt("dve", qt1[r][:, 0:nq], pq[:, 0:nq], rQC[r][:, 0:nq], ALU.mult)
                            for c in range(3):
                                P.mm(pq[:, 0:nq], wuqs[:, c, h * 128:(h + 1) * 128], cqnT[:, c, u0:u0 + nq], start=(c == 0), stop=(c == 2))
                            P.tt("dve", qt2[r][:, 0:nq], pq[:, 0:nq], rQS[r][:, 0:nq], ALU.mult)
                            P.tt("pool", QT[r][:, 0:nq], qt1[r][:, 0:nq], qt2[r][:, 0:nq], ALU.add)
                            P.tt("pool", qsq[r][:, 0:nq], QT[r][:, 0:nq], QT[r][:, 0:nq], ALU.mult)
                            P.mm(pq[0:1, 0:nq], ones_b[:, 0:1], qsq[r][:, 0:nq])
                            P.act(sqr[r][:, 0:nq], pq[0:1, 0:nq], AF.Ln, bias=1e-30)
                            P.act(sqr[r][:, 0:nq], sqr[r][:, 0:nq], AF.Exp, scale=0.5)
                            P.ts("dve", QT[r][0:1, 0:nq], sqr[r][:, 0:nq], kmax[0:1, 0:1], None, op0=ALU.mult)

                        prepQ(0)
                        for qi, (qo, nq) in enumerate(qblocks):
                            r = (h * len(qblocks) + qi) % 2
                            if qi + 1 < len(qblocks):
                                prepQ(qi + 1)
                            nqt = nq // 128
                            obanks = [pb[qq][:, 0:65] for qq in range(nqt)]
                            attend(lambda kt: KTh[:, kt * 128:(kt + 1) * 128], lambda kt: V1[:, kt, h % 2, :],
                                   QT[r][:, 0:nq], nq, None, [(kt, None) for kt in range(nkt)],
                                   obanks, [pb[4], pb[5]], PTb, scale)
                            for qq in range(nqt):
                                lt = qo // 128 + qq
                                P.recip(rden[qq][:], pb[qq][:, 64:65])
                                P.ts("dve", otok[:, lt, (h % 2) * 64:(h % 2 + 1) * 64], pb[qq][:, 0:64],
                                     rden[qq][:, 0:1], None, op0=ALU.mult)
                        if h % 2 == 1:
                            for lt0 in range(0, sq["nt"], 8):
                                n8 = min(8, sq["nt"] - lt0)
                                for j in range(n8):
                                    P.tr(pbb[:, j * 128:(j + 1) * 128], otok[:, lt0 + j, :], ident_b[:])
                                uo = kb + lt0 * 128
                                P.copy("act", obT[:, h // 2, uo:uo + n8 * 128], pbb[:, 0:n8 * 128])
                P.fence()

        def mixer_A(l, unit, oaT):
            with ExitStack() as ph:
                tiles = unit["tiles"]
                ntok_u = len(tiles) * 128
                tok_base = tiles[0] * 128
                has_ctx = any(sq["ctx"] for sq in unit["seqs"])
                nkeys_u = ntok_u + (256 if has_ctx else 0)
                scale = 64.0 ** -0.5
                KTa = sb("KTa", [64, 2, nkeys_u], BF16, ph)
                Va = sb("Va", [128, nkeys_u // 128, 2, 65], BF16, ph)
                P.memset("pool", Va[:, :, :, 64:65], 1.0)
                rC = sb("rAC", [64, 256], F32, ph)
                rS = sb("rAS", [64, 256], F32, ph)
                t1 = sb("tA1", [64, 256], F32, ph)
                t2 = sb("tA2", [64, 256], F32, ph)
                hTb = [sb("hTbA0", [128, 8, 256], BF16, ph)]

                with ExitStack() as a1:
                    hTb.append(sb("hTbA1", [128, 8, 256], BF16, a1))
                    wAk = sb("wAk", [128, 8, 384], BF16, a1)
                    P.dma(wAk[:, :, 0:256], w_in[l, :, OFF_AK:OFF_AK + 256].rearrange("(c p) n -> p c n", p=128), q="pool")
                    P.dma(wAk[:, :, 256:384], w_in_sw[l, :, 512:640].rearrange("(c p) n -> p c n", p=128), q="pool")
                    kvf = [sb("kvf%d" % i, [128, 256], F32, a1) for i in range(2)]
                    for bi, (sq, tok0, nb) in enumerate(blocks_of(unit, 256)):
                        hT = hTb[bi % 2]
                        btiles = [tok0 // 128 + i for i in range(nb // 128)]
                        build_hT(hT, btiles, 0)
                        u0 = tok0 - tok_base
                        P.dma(rC[:, 0:nb], ropeA_C[:, tok0:tok0 + nb])
                        P.dma(rS[:, 0:nb], ropeA_S[:, tok0:tok0 + nb])
                        for g in range(2):
                            p1, p2 = pb[(2 * g) % 4], pb[(2 * g + 1) % 4]
                            for kc in range(8):
                                P.mm(p1[0:64, 0:nb], wAk[:, kc, g * 64:(g + 1) * 64], hT[:, kc, 0:nb], start=(kc == 0), stop=(kc == 7))
                            P.tt("dve", t1[:, 0:nb], p1[0:64, 0:nb], rC[:, 0:nb], ALU.mult)
                            for kc in range(8):
                                P.mm(p2[0:64, 0:nb], wAk[:, kc, 256 + g * 64:256 + (g + 1) * 64], hT[:, kc, 0:nb], start=(kc == 0), stop=(kc == 7))
                            P.tt("dve", t2[:, 0:nb], p2[0:64, 0:nb], rS[:, 0:nb], ALU.mult)
                            P.tt("pool", KTa[:, g, u0:u0 + nb], t1[:, 0:nb], t2[:, 0:nb], ALU.add)
                        for ti, t in enumerate(btiles):
                            pv = pb[4 + ti % 2]
                            for kc in range(8):
                                P.mm(pv[:, 0:256], hT[:, kc, ti * 128:(ti + 1) * 128], wAk[:, kc, 0:256], start=(kc == 0), stop=(kc == 7))
                            ut = t - tiles[0]
                            P.copy("act", Va[:, ut, :, 0:64], pv[:, 128:256].rearrange("p (g e) -> p g e", e=64))
                            if unit["prompt"]:
                                lt = t - sq["t0"]
                                kv = kvf[ti % 2]
                                P.copy("act", kv[:], pv[:, 0:256])
                                P.dma(nak_out[sq["idx"], l, lt * 128:(lt + 1) * 128, :], kv[:, 0:128])
                                P.dma(nav_out[sq["idx"], l, lt * 128:(lt + 1) * 128, :], kv[:, 128:256])
                    if has_ctx:
                        ctxkb = sb("ctxkb", [128, 2, 128], BF16, a1)
                        ctxvb = sb("ctxvb", [128, 2, 128], BF16, a1)
                        for c2 in range(2):
                            P.dma(ctxkb[:, c2, :], cak[l, c2 * 128:(c2 + 1) * 128, :], q="pool")
                            P.dma(ctxvb[:, c2, :], cav[l, c2 * 128:(c2 + 1) * 128, :], q="pool")
                        for c2 in range(2):
                            for g in range(2):
                                P.tr(pbb[0:64, (c2 * 2 + g) * 128:(c2 * 2 + g + 1) * 128], ctxkb[:, c2, g * 64:(g + 1) * 64], ident_b[:])
                            P.copy("dve", Va[:, ntok_u // 128 + c2, :, 0:64], ctxvb[:, c2, :].rearrange("p (g e) -> p g e", e=64))
                        for c2 in range(2):
                            for g in range(2):
                                P.copy("act", KTa[:, g, ntok_u + c2 * 128:ntok_u + (c2 + 1) * 128],
                                       pbb[0:64, (c2 * 2 + g) * 128:(c2 * 2 + g + 1) * 128])
                    P.fence()

                wAq = sb("wAq", [128, 8, 512], BF16, ph)
                wAqs = sb("wAqs", [128, 8, 512], BF16, ph)
                P.dma(wAq[:], w_in[l, :, 0:512].rearrange("(c p) n -> p c n", p=128), q="pool")
                P.dma(wAqs[:], w_in_sw[l, :, 0:512].rearrange("(c p) n -> p c n", p=128), q="pool")
                maskb = sb("maskb", [128, 8, 128], BF16, ph)
                P.dma(maskb[:], cst_masks[:, :, :], q="pool")
                sinkf = sb("sinkf", [1, 8], F32, ph)
                vsink = sb("vsink", [1, 65], BF16, ph)
                P.dma(sinkf[:], a_sink[l:l + 1, :])
                P.ts("dve", sinkf[:], sinkf[:], 1.0 / scale, None, op0=ALU.mult)
                P.memset("dve", vsink[:], 0.0)
                P.memset("dve", vsink[:, 64:65], 1.0)
                QTb = [sb("QTb%d" % i, [64, 8, 256], BF16, ph) for i in range(2)]
                ksqa = sb("ksqa", [64, 512], BF16, ph)
                qsqa = sb("qsqa", [64, 4, 128], BF16, ph)
                kmx = sb("kmxA", [1, 2, 8], F32, ph)
                kmax = sb("kmaxA", [1, 2], F32, ph)
                sqr = sb("sqrA", [1, 512], F32, ph)
                negf = sqr
                negm = [sb("negmA%d" % i, [1, 512], BF16, ph) for i in range(2)]
                srow = sb("srowA", [1, 512], F32, ph)
                psink = [sb("psinkA%d" % i, [1, 512], BF16, ph) for i in range(2)]
                PTb = [sb("PTbA%d" % i, [128, 512], BF16, ph) for i in range(2)]
                otok = [sb("otokA%d" % i, [128, 512], BF16, ph) for i in range(2)]
                rden = [sb("rdenA%d" % i, [128, 1], F32, ph) for i in range(4)]
                nq_ = 0
                nblk = 0
                for sq in unit["seqs"]:
                    kb = sq["t0"] * 128 - tok_base
                    nk_own = sq["nt"] * 128
                    kranges = [(kb, nk_own)] + ([(ntok_u, 256)] if sq["ctx"] else [])
                    for g in range(2):
                        nb_ = 0
                        for (k0, kn_all) in kranges:
                            for o in range(0, kn_all, 512):
                                kn = min(512, kn_all - o)
                                P.tt("pool", ksqa[:, 0:kn], KTa[:, g, k0 + o:k0 + o + kn], KTa[:, g, k0 + o:k0 + o + kn], ALU.mult)
                                P.mm(pb[6][0:1, 0:kn], ones_b[0:64, 0:1], ksqa[:, 0:kn])
                                P.reduce(kmx[:, g, nb_:nb_ + 1], pb[6][0:1, 0:kn], ALU.max)
                                nb_ += 1
                        P.reduce(kmax[:, g:g + 1], kmx[:, g, 0:nb_], ALU.max)
                    P.act(kmax[:], kmax[:], AF.Sqrt)
                    P.ts("dve", kmax[:], kmax[:], -1.0, None, op0=ALU.mult)
                    blocks = [(q0, min(2, sq["nt"] - q0)) for q0 in range(0, sq["nt"], 2)]

                    def qproj(b):
                        q0, nbt = blocks[b]
                        nb = nbt * 128
                        tok0 = (sq["t0"] + q0) * 128
                        hT = hTb[0]
                        QT = QTb[b % 2]
                        build_hT(hT, [sq["t0"] + q0 + i for i in range(nbt)], 0)
                        P.dma(rC[:, 0:nb], ropeA_C[:, tok0:tok0 + nb])
                        P.dma(rS[:, 0:nb], ropeA_S[:, tok0:tok0 + nb])
                        for h in range(8):
                            p1 = pb[6]
                            for kc in range(8):
                                P.mm(p1[0:64, 0:nb], wAq[:, kc, h * 64:(h + 1) * 64], hT[:, kc, 0:nb], start=(kc == 0), stop=(kc == 7))
                            for kc in range(8):
                                P.mm(p1[0:64, 256:256 + nb], wAqs[:, kc, h * 64:(h + 1) * 64], hT[:, kc, 0:nb], start=(kc == 0), stop=(kc == 7))
                            P.tt("dve", t1[:, 0:nb], p1[0:64, 0:nb], rC[:, 0:nb], ALU.mult)
                            P.tt("dve", t2[:, 0:nb], p1[0:64, 256:256 + nb], rS[:, 0:nb], ALU.mult)
                            P.tt("pool", QT[:, h, 0:nb], t1[:, 0:nb], t2[:, 0:nb], ALU.add)

                    items = [(b, qi, g) for b in range(len(blocks)) for qi in range(blocks[b][1]) for g in range(2)]

                    def prep_item(n):
                        b, qi, g = items[n]
                        r = n % 2
                        QTg = QTb[b % 2][:, 4 * g:4 * g + 4, qi * 128:(qi + 1) * 128]
                        P.tt("pool", qsqa[:], QTg, QTg, ALU.mult)
                        P.mm(pb[6][0:1, 0:512], ones_b[0:64, 0:1], qsqa[:])
                        P.act(sqr[:], pb[6][0:1, 0:512], AF.Ln, bias=1e-30)
                        P.act(sqr[:], sqr[:], AF.Exp, scale=0.5)
                        P.ts("dve", negf[:], sqr[:], kmax[0:1, g:g + 1], None, op0=ALU.mult)
                        P.copy("dve", negm[r][:], negf[:])
                        for hh in range(4):
                            P.ts("dve", srow[:, hh * 128:(hh + 1) * 128], negf[:, hh * 128:(hh + 1) * 128],
                                 sinkf[0:1, 4 * g + hh:4 * g + hh + 1], None, op0=ALU.add)
                        P.act(psink[r][:], srow[:], AF.Exp, scale=scale)

                    qproj(0)
                    prep_item(0)
                    for n, (b, qi, g) in enumerate(items):
                        r = n % 2
                        q0 = blocks[b][0]
                        qt = q0 + qi
                        uq = kb + qt * 128
                        ot = otok[qt % 2]
                        if qi == 0 and g == 0 and b + 1 < len(blocks):
                            qproj(b + 1)
                        if n + 1 < len(items):
                            prep_item(n + 1)
                        QTg = QTb[b % 2][:, 4 * g:4 * g + 4, qi * 128:(qi + 1) * 128]
                        if sq["ctx"]:
                            kts = []
                            if qt > 0:
                                kts.append((kb // 128 + qt - 1, maskb[:, 0:4, :]))
                            kts.append((kb // 128 + qt, None))
                            if qt + 1 < sq["nt"]:
                                kts.append((kb // 128 + qt + 1, maskb[:, 4:8, :]))
                            kts += [(ntok_u // 128, None), (ntok_u // 128 + 1, None)]
                        else:
                            kts = [(kb // 128 + j, None) for j in range(sq["nt"])]
                        obanks = [pb[hh][:, 0:65] for hh in range(4)]

                        def tail(r=r):
                            for hh in range(4):
                                P.mm(pb[hh][:, 0:65], psink[r][0:1, hh * 128:(hh + 1) * 128], vsink[:], start=False, stop=True)

                        attend(lambda kt: KTa[:, g, kt * 128:(kt + 1) * 128], lambda kt: Va[:, kt, g, :],
                               QTg, 512, negm[r][:], kts, obanks, [pb[4], pb[5]], PTb, scale, tail=tail)
                        for hh in range(4):
                            h = 4 * g + hh
                            P.recip(rden[hh][:], pb[hh][:, 64:65])
                            P.ts("dve", ot[:, h * 64:(h + 1) * 64], pb[hh][:, 0:64], rden[hh][:, 0:1], None, op0=ALU.mult)
                        if g == 1:
                            for c in range(4):
                                P.tr(pbb[:, c * 128:(c + 1) * 128], ot[:, c * 128:(c + 1) * 128], ident_b[:])
                            P.copy("act", oaT[:, :, uq:uq + 128], pbb[:, 0:512].rearrange("p (c n) -> p c n", c=4))
                P.fence()

        def mixer_C(l, unit, ocT):
            with ExitStack() as ph:
                tiles = unit["tiles"]
                ntok_u = len(tiles) * 128
                nt_u = len(tiles)
                tok_base = tiles[0] * 128
                trc = sb("trc", [128, 2, 2, 128], BF16, ph)
                trif = sb("trif", [128, 2, 128], F32, ph)
                lvN = sb("lvN", [128, 7, 2, 128], BF16, ph)
                lvA = sb("lvA", [128, 7, 2, 128], BF16, ph)
                convw = sb("convw", [128, 12, 3], F32, ph)
                alog = sb("alog", [128, 8], F32, ph)
                dtb = sb("dtb", [128, 8], F32, ph)
                ong = sb("ong", [128, 128], F32, ph)
                wab = sb("wab", [128, 8, 16], BF16, ph)
                gb = sb("gb", [128, nt_u, 16], F32, ph)
                P.dma(trc[:], cst_trc[:, :, :, :], q="pool")
                P.dma(lvN[:], cst_lvN[:, :, :, :], q="pool")
                P.dma(lvA[:], cst_lvA[:, :, :, :], q="pool")
                P.dma(trif[:, 0, :], cst_tri[:, 1, :])
                P.dma(trif[:, 1, :], cst_tri[:, 3, :])
                P.dma(convw[:], c_convT[l, :, :, :])
                P.dma(alog[:], c_a_log[l:l + 1, :].broadcast_to([128, 8]))
                P.dma(dtb[:], c_dt_bias[l:l + 1, :].broadcast_to([128, 8]))
                P.dma(ong[:], c_onorm_g[l:l + 1, :].broadcast_to([128, 128]))
                P.dma(wab[:], w_in[l, :, OFF_CA:OFF_CA + 16].rearrange("(c p) n -> p c n", p=128), q="pool")
                P.act(alog[:], alog[:], AF.Exp)
                P.ts("dve", alog[:], alog[:], -1.0, None, op0=ALU.mult)
                sm = [sb("smC%d" % i, [128, 8], F32, ph) for i in range(6)]

                for h in range(4):
                    with ExitStack() as hs:
                        qkvT = sb("qkvT", [128, 3, ntok_u], BF16, hs)
                        zs = sb("zs", [128, nt_u, 128], F32, hs)
                        oacc = sb("oacc", [128, nt_u, 128], F32, hs)
                        P.memset("pool", oacc[:], 0.0)
                        with ExitStack() as ps_:
                            wC = sb("wC", [128, 8, 512], BF16, ps_)
                            for j in range(3):
                                c0 = OFF_CQKV + j * 512 + h * 128
                                P.dma(wC[:, :, j * 128:(j + 1) * 128], w_in[l, :, c0:c0 + 128].rearrange("(c p) n -> p c n", p=128), q="pool")
                            P.dma(wC[:, :, 384:512], w_in[l, :, OFF_CZ + h * 128:OFF_CZ + (h + 1) * 128].rearrange("(c p) n -> p c n", p=128), q="pool")
                            hTb = [sb("hTbC%d" % i, [128, 8, 512], BF16, ps_) for i in range(2)]
                            rawT = sb("rawT", [128, 3, ntok_u], F32, ps_)
                            cv = [sb("cvC%d" % i, [128, 512], F32, ps_) for i in range(2)]
                            zf = [sb("zfC%d" % i, [128, 128], F32, ps_) for i in range(2)]
                            for bi, (sq, tok0, nb) in enumerate(blocks_of(unit)):
                                hT = hTb[bi % 2]
                                btiles = [tok0 // 128 + i for i in range(nb // 128)]
                                build_hT(hT, btiles, 0)
                                u0 = tok0 - tok_base
                                for j in range(3):
                                    pj = pb[j % 2]
                                    for kc in range(8):
                                        P.mm(pj[:, 0:nb], wC[:, kc, j * 128:(j + 1) * 128], hT[:, kc, 0:nb], start=(kc == 0), stop=(kc == 7))
                                    P.copy("act", rawT[:, j, u0:u0 + nb], pj[:, 0:nb])
                                for ti, t in enumerate(btiles):
                                    ut = t - tiles[0]
                                    pz = pb[2 + ti % 2]
                                    for kc in range(8):
                                        P.mm(pz[:, 0:128], hT[:, kc, ti * 128:(ti + 1) * 128], wC[:, kc, 384:512], start=(kc == 0), stop=(kc == 7))
                                    P.act(zf[ti % 2][:], pz[:, 0:128], AF.Silu)
                                    P.tt("pool", zs[:, ut, :], zf[ti % 2][:], ong[:], ALU.mult)
                                    if h == 0:
                                        pa_ = pb[4]
                                        for kc in range(8):
                                            P.mm(pa_[:, 0:16], hT[:, kc, ti * 128:(ti + 1) * 128], wab[:, kc, :], start=(kc == 0), stop=(kc == 7))
                                        P.tt("dve", sm[0][:], pa_[:, 0:8], dtb[:], ALU.add)
                                        P.act(sm[1][:], sm[0][:], AF.Abs)
                                        P.act(sm[2][:], sm[1][:], AF.Exp, scale=-1.0)
                                        P.act(sm[3][:], sm[2][:], AF.Ln, bias=1.0)
                                        P.ts("dve", sm[4][:], sm[0][:], 0.0, None, op0=ALU.max)
                                        P.tt("dve", sm[5][:], sm[4][:], sm[3][:], ALU.add)
                                        P.tt("dve", gb[:, ut, 0:8], sm[5][:], alog[:], ALU.mult)
                                        P.act(gb[:, ut, 8:16], pa_[:, 8:16], AF.Sigmoid)
                            for sq in unit["seqs"]:
                                s0 = sq["t0"] * 128 - tok_base
                                n = sq["nt"] * 128
                                for j in range(3):
                                    ci = j * 4 + h
                                    for o in range(0, n, 512):
                                        nb = min(512, n - o)
                                        a0 = s0 + o
                                        c_ = cv[(o // 512) % 2]
                                        P.act(c_[:, 0:nb], rawT[:, j, a0:a0 + nb], AF.Copy, scale=convw[:, ci, 1:2])
                                        lo = 1 if o == 0 else 0
                                        P.stt(c_[:, lo:nb], rawT[:, j, a0 + lo - 1:a0 + nb - 1], convw[:, ci, 0:1], c_[:, lo:nb], ALU.mult, ALU.add)
                                        hi = nb - 1 if o + nb == n else nb
                                        P.stt(c_[:, 0:hi], rawT[:, j, a0 + 1:a0 + hi + 1], convw[:, ci, 2:3], c_[:, 0:hi], ALU.mult, ALU.add)
                                        P.act(qkvT[:, j, a0:a0 + nb], c_[:, 0:nb], AF.Silu)
                            P.fence()

                        gc = sb("gc", [128, 2, nt_u], F32, hs)
                        egc = sb("egc", [128, 2, nt_u], F32, hs)
                        gl = sb("gl", [128, 2, nt_u], F32, hs)
                        edl = sb("edl", [128, 2, nt_u], F32, hs)
                        ge = sb("ge", [128, 2, nt_u], F32, hs)
                        gsel = sb("gsel", [128, 2, nt_u], F32, hs)
                        for d in range(2):
                            P.copy("dve", gsel[:, d, :], gb[:, :, d * 4 + h])
                        for d in range(2):
                            P.mm(pb[6][:, d * nt_u:(d + 1) * nt_u], trif[:, d, :], gsel[:, d, :])
                        P.copy("dve", gc[:], pb[6][:, 0:2 * nt_u].rearrange("p (d n) -> p d n", d=2))
                        P.mm(pb[5][:, 0:2 * nt_u], ones_f[:], gsel[:].rearrange("p d n -> p (d n)"))
                        P.copy("dve", gl[:], pb[5][:, 0:2 * nt_u].rearrange("p (d n) -> p d n", d=2))
                        P.act(egc[:], gc[:], AF.Exp)
                        P.act(ge[:], gl[:], AF.Exp)
                        P.tt("dve", edl[:], gl[:], gc[:], ALU.subtract)
                        P.act(edl[:], edl[:], AF.Exp)

                        Sst = sb("Sst", [128, 2, 128], F32, hs)
                        Sbf = sb("Sbf", [128, 2, 128], BF16, hs)
                        ident2 = ident_f[:].unsqueeze(1).broadcast_to([128, 2, 128])

                        def slotbufs(s_):
                            W = Ctx()
                            f32 = lambda n, shp=(128, 2, 128): sb("%s_s%d" % (n, s_), list(shp), F32, hs)
                            b16 = lambda n, shp=(128, 2, 128): sb("%s_s%d" % (n, s_), list(shp), BF16, hs)
                            W.ssq = f32("ssq", (128, 2, 2)); W.rkq = f32("rkq", (128, 2, 2))
                            W.qkv = b16("qkvtok", (128, 2, 3, 128)); W.kn = b16("kn"); W.qn = b16("qn")
                            W.vke = b16("vke", (128, 2, 256)); W.kqT = b16("kqT", (128, 2, 2, 128))
                            W.diag = f32("diag"); W.DT = f32("DT"); W.tmp1 = f32("tmp1"); W.AT = f32("AT"); W.tmp2 = f32("tmp2")
                            W.qkmT = b16("qkmT"); W.A = f32("A"); W.R = f32("R"); W.RT = f32("RT")
                            W.BkT = [f32("BkT0"), f32("BkT1")]; W.Y = f32("Y"); W.Rb = b16("Rb"); W.u = f32("u")
                            W.w = b16("w"); W.wT = b16("wT"); W.kd = b16("kd"); W.qd = b16("qd"); W.qdT = b16("qdT")
                            W.vnew = b16("vnew"); W.t3 = f32("t3"); W.t4 = f32("t4")
                            return W
                        SB_ = [slotbufs(0), slotbufs(1)]

                        def run_slot(sq, s_):
                            W = SB_[s_]
                            kb = sq["t0"] * 128 - tok_base
                            n = sq["nt"]
                            bA, bB, bC = pb[3 * s_], pb[3 * s_ + 1], pb[3 * s_ + 2]
                            tb0 = s_ * 4 * 128
                            tb = lambda j, cnt=1: pbb[:, tb0 + j * 128:tb0 + (j + cnt) * 128]
                            v3 = lambda ap: ap.rearrange("p (d n) -> p d n", d=2)
                            if s_ == 1:
                                for _ in range(3):
                                    yield
                            for st in range(s_, n, 2):
                                cs = [st, n - 1 - st]
                                uts = [kb // 128 + c for c in cs]
                                gcc = [gc[:, d, uts[d]:uts[d] + 1] for d in range(2)]
                                beta = [gb[:, uts[d], 8 + d * 4 + h:8 + d * 4 + h + 1] for d in range(2)]
                                for d in range(2):
                                    sl = slice(kb + cs[d] * 128, kb + (cs[d] + 1) * 128)
                                    for j in range(3):
                                        P.tr(tb(j), qkvT[:, j, sl], ident_b[:])
                                    P.copy("act", W.qkv[:, d, :, :], tb(0, 3).rearrange("p (j n) -> p j n", j=3))
                                yield
                                for d in range(2):
                                    P.add("dve", lambda e, d=d: e.scalar_tensor_tensor(W.t3[:, d, :], W.qkv[:, d, 0, :], 1.0, W.qkv[:, d, 0, :],
                                                                                       ALU.mult, ALU.mult, accum_out=W.ssq[:, d, 0:1]),
                                          reads=[W.qkv[:, d, 0, :]], writes=[W.t3[:, d, :], W.ssq[:, d, 0:1]])
                                    P.add("dve", lambda e, d=d: e.scalar_tensor_tensor(W.t4[:, d, :], W.qkv[:, d, 1, :], 1.0, W.qkv[:, d, 1, :],
                                                                                       ALU.mult, ALU.mult, accum_out=W.ssq[:, d, 1:2]),
                                          reads=[W.qkv[:, d, 1, :]], writes=[W.t4[:, d, :], W.ssq[:, d, 1:2]])
                                yield
                                P.act(W.rkq[:], W.ssq[:], AF.Ln, bias=EPS)
                                P.act(W.rkq[:], W.rkq[:], AF.Exp, scale=-0.5)
                                yield
                                for d in range(2):
                                    P.act(W.kn[:, d, :], W.qkv[:, d, 1, :], AF.Copy, scale=W.rkq[:, d, 1:2])
                                    P.ts("dve", W.qn[:, d, :], W.qkv[:, d, 0, :], W.rkq[:, d, 0:1], 128.0 ** -0.5, op0=ALU.mult, op1=ALU.mult)
                                    P.copy("pool", W.vke[:, d, 0:128], W.qkv[:, d, 2, :])
                                yield
                                for d in range(2):
                                    P.tr(tb(0), W.kn[:, d, :], ident_b[:])
                                    P.tr(tb(1), W.qn[:, d, :], ident_b[:])
                                    P.copy("act", W.kqT[:, d, :, :], tb(0, 2).rearrange("p (c n) -> p c n", c=2))
                                    P.act(W.vke[:, d, 128:256], W.kn[:, d, :], AF.Copy, scale=egc[:, d, uts[d]:uts[d] + 1])
                                    P.ts("dve", W.kd[:, d, :], W.kn[:, d, :], edl[:, d, uts[d]:uts[d] + 1], None, op0=ALU.mult)
                                    P.act(W.qd[:, d, :], W.qn[:, d, :], AF.Copy, scale=egc[:, d, uts[d]:uts[d] + 1])
                                    P.ts("dve", W.diag[:, d, :], ident_f[:], gcc[d], None, op0=ALU.mult)
                                yield
                                for d in range(2):
                                    P.mm(bA[:, d * 128:(d + 1) * 128], W.kqT[:, d, 0, :], W.kqT[:, d, 0, :])
                                    P.mm(bA[:, 256 + d * 128:256 + (d + 1) * 128], W.kqT[:, d, 0, :], W.kqT[:, d, 1, :])
                                    P.mm(bB[:, d * 128:(d + 1) * 128], ones_f[:], W.diag[:, d, :])
                                    P.tr(tb(2 + d), W.qd[:, d, :], ident_b[:])
                                for d in range(2):
                                    P.ts("dve", W.DT[:, d, :], bB[:, d * 128:(d + 1) * 128], gcc[d], 0.0, op0=ALU.subtract, op1=ALU.min)
                                P.act(W.DT[:], W.DT[:], AF.Exp)
                                P.copy("act", W.qdT[:], v3(tb(2, 2)))
                                yield
                                P.tt("dve", W.tmp1[:], v3(bA[:, 0:256]), W.DT[:], ALU.mult)
                                P.tt("dve", W.tmp2[:], v3(bA[:, 256:512]), W.DT[:], ALU.mult)
                                for d in range(2):
                                    P.stt(W.AT[:, d, :], W.tmp1[:, d, :], beta[d], trc[:, 0, d, :], ALU.mult, ALU.mult)
                                P.tt("pool", W.qkmT[:], W.tmp2[:], trc[:, 1, :, :], ALU.mult)
                                yield
                                for d in range(2):
                                    P.tr(bB[:, 256 + d * 128:256 + (d + 1) * 128], W.AT[:, d, :], ident_f[:])
                                P.copy("act", W.A[:], v3(bB[:, 256:512]))
                                P.tt("dve", W.t3[:], W.AT[:], lvN[:, 0, :, :], ALU.mult)
                                P.tt("dve", W.R[:], ident2, W.t3[:], ALU.subtract)
                                yield
                                P.tt("pool", W.t4[:], W.A[:], lvA[:, 0, :, :], ALU.mult)
                                P.tt("pool", W.RT[:], ident2, W.t4[:], ALU.subtract)
                                P.tt("pool", W.BkT[1][:], W.A[:], lvA[:, 1, :, :], ALU.mult)
                                yield
                                for k in range(1, 7):
                                    Bk = W.BkT[k % 2]
                                    for d in range(2):
                                        P.mm(bC[:, d * 128:(d + 1) * 128], Bk[:, d, :], W.R[:, d, :])
                                    if k < 6:
                                        P.tt("pool", W.BkT[(k + 1) % 2][:], W.A[:], lvA[:, k + 1, :, :], ALU.mult)
                                    P.copy("act", W.Y[:], v3(bC[:, 0:256]))
                                    yield
                                    for d in range(2):
                                        P.mm(bB[:, d * 128:(d + 1) * 128], W.RT[:, d, :], W.Y[:, d, :])
                                        P.mm(bB[:, 256 + d * 128:256 + (d + 1) * 128], W.Y[:, d, :], W.RT[:, d, :])
                                    yield
                                    P.tt("dve", W.R[:], W.R[:], v3(bB[:, 0:256]), ALU.subtract)
                                    P.tt("dve", W.RT[:], W.RT[:], v3(bB[:, 256:512]), ALU.subtract)
                                    yield
                                P.copy("act", W.Rb[:], W.R[:])
                                yield
                                for d in range(2):
                                    P.mm(bA[:, d * 256:(d + 1) * 256], W.Rb[:, d, :], W.vke[:, d, :])
                                for d in range(2):
                                    P.act(W.u[:, d, :], bA[:, d * 256:d * 256 + 128], AF.Copy, scale=beta[d])
                                    P.act(W.w[:, d, :], bA[:, d * 256 + 128:(d + 1) * 256], AF.Copy, scale=beta[d])
                                yield
                                for d in range(2):
                                    P.tr(tb(d), W.w[:, d, :], ident_b[:])
                                P.copy("act", W.wT[:], v3(tb(0, 2)))
                                yield
                                for d in range(2):
                                    P.mm(bC[:, d * 128:(d + 1) * 128], W.wT[:, d, :], Sbf[:, d, :])
                                P.tt("dve", W.vnew[:], W.u[:], v3(bC[:, 0:256]), ALU.subtract)
                                yield
                                for d in range(2):
                                    P.mm(bC[:, 256 + d * 128:256 + (d + 1) * 128], W.qdT[:, d, :], Sbf[:, d, :], start=True, stop=False)
                                    P.mm(bC[:, 256 + d * 128:256 + (d + 1) * 128], W.qkmT[:, d, :], W.vnew[:, d, :], start=False, stop=True)
                                for d in range(2):
                                    P.tt("dve", oacc[:, uts[d], :], oacc[:, uts[d], :], bC[:, 256 + d * 128:256 + (d + 1) * 128], ALU.add)
                                yield
                                for d in range(2):
                                    P.mm(bA[:, d * 128:(d + 1) * 128], W.kd[:, d, :], W.vnew[:, d, :])
                                for d in range(2):
                                    P.stt(Sst[:, d, :], Sst[:, d, :], ge[:, d, uts[d]:uts[d] + 1], bA[:, d * 128:(d + 1) * 128], ALU.mult, ALU.add)
                                P.copy("act", Sbf[:], Sst[:])
                                yield

                        for sq in unit["seqs"]:
                            for d in range(2):
                                if sq["ctx"]:
                                    P.dma(Sst[:, d, :], cstate[l, d, h, :, :])
                                else:
                                    P.memset("pool", Sst[:, d, :], 0.0)
                            P.copy("act", Sbf[:], Sst[:])
                            alive = [run_slot(sq, 0), run_slot(sq, 1)]
                            while alive:
                                for g_ in list(alive):
                                    try:
                                        next(g_)
                                    except StopIteration:
                                        alive.remove(g_)
                            if unit["prompt"]:
                                for d in range(2):
                                    P.dma(ncs_out[sq["idx"], l, d, h, :, :], Sst[:, d, :])
                        on_ss = sb("on_ss", [128, nt_u], F32, hs)
                        on_r = sb("on_r", [128, nt_u], F32, hs)
                        ojk = [sb("ojk%d" % i, [128, 128], F32, hs) for i in range(2)]
                        ob16 = sb("ob16", [128, nt_u, 128], BF16, hs)
                        for ut in range(nt_u):
                            P.act(ojk[ut % 2][:], oacc[:, ut, :], AF.Square, accum_out=on_ss[:, ut:ut + 1])
                        P.ts("dve", on_r[:], on_ss[:], 1.0 / 128, EPS, op0=ALU.mult, op1=ALU.add)
                        P.act(on_r[:], on_r[:], AF.Sqrt)
                        P.recip(on_r[:], on_r[:])
                        for ut in range(nt_u):
                            P.stt(ob16[:, ut, :], oacc[:, ut, :], on_r[:, ut:ut + 1], zs[:, ut, :], ALU.mult, ALU.mult)
                        for u0 in range(0, nt_u, 8):
                            n8 = min(8, nt_u - u0)
                            for j in range(n8):
                                P.tr(pbb[:, j * 128:(j + 1) * 128], ob16[:, u0 + j, :], ident_b[:])
                            P.copy("act", ocT[:, h, u0 * 128:(u0 + n8) * 128], pbb[:, 0:n8 * 128])
                        P.fence()
                P.fence()

        def merge(l, unit, oT):
            tiles = unit["tiles"]
            tok_base = tiles[0] * 128
            cd = unit["cond"]
            for sub0 in range(0, len(tiles), 8):
                stiles = tiles[sub0:sub0 + 8]
                with ExitStack() as ph:
                    nsub = len(stiles) * 128
                    mT = sb("mT", [128, 8, nsub], BF16, ph)
                    with ExitStack() as p2:
                        hTm = sb("hTm", [128, 8, 256], BF16, p2)
                        gt = [sb("gtM%d" % i, [128, 256], F32, p2) for i in range(2)]
                        tm = [sb("tmM%d" % i, [128, 256], F32, p2) for i in range(2)]
                        wg = sb("wg", [128, 8, D], BF16, p2)
                        wbr = sb("wbr", [128, 4, D], BF16, p2)
                        for j in range(3):
                            P.dma(wg[:], w_in[l, :, OFF_GATES + j * D:OFF_GATES + (j + 1) * D].rearrange("(c p) n -> p c n", p=128), q="pool")
                            P.dma(wbr[:], w_branch[l, j].rearrange("(c p) n -> p c n", p=128), q="pool")
                            for b0 in range(0, len(stiles), 2):
                                bt = stiles[b0:b0 + 2]
                                nb = len(bt) * 128
                                build_hT(hTm, bt, 0)
                                u0 = bt[0] * 128 - tok_base
                                m0 = b0 * 128
                                for fc in range(8):
                                    pg, py = pb[fc % 2], pb[2 + fc % 2]
                                    for kc in range(8):
                                        P.mm(pg[:, 0:nb], wg[:, kc, fc * 128:(fc + 1) * 128], hTm[:, kc, 0:nb], start=(kc == 0), stop=(kc == 7))
                                    P.act(gt[fc % 2][:, 0:nb], pg[:, 0:nb], AF.Sigmoid)
                                    for c4 in range(4):
                                        P.mm(py[:, 0:nb], wbr[:, c4, fc * 128:(fc + 1) * 128], oT[j][:, c4, u0:u0 + nb], start=(c4 == 0), stop=(c4 == 3))
                                    if j == 0:
                                        P.tt("dve", mT[:, fc, m0:m0 + nb], py[:, 0:nb], gt[fc % 2][:, 0:nb], ALU.mult)
                                    else:
                                        P.tt("dve", tm[fc % 2][:, 0:nb], py[:, 0:nb], gt[fc % 2][:, 0:nb], ALU.mult)
                                        P.tt("pool", mT[:, fc, m0:m0 + nb], mT[:, fc, m0:m0 + nb], tm[fc % 2][:, 0:nb], ALU.add)
                        P.fence()
                    with ExitStack() as p3:
                        wo = sb("wo", [128, 8, D], BF16, p3)
                        G1 = sb("G1", [128, D], F32, p3)
                        tmo = [sb("tmo%d" % i, [128, 512], F32, p3) for i in range(2)]
                        P.dma(wo[:], w_out[l].rearrange("(c p) n -> p c n", p=128), q="pool")
                        make_G(G1[:], 0, cd, p3)
                        n_ = 0
                        for ti, t in enumerate(stiles):
                            for half in range(2):
                                py = pb[2 + n_ % 4]
                                for kc in range(8):
                                    P.mm(py[:], mT[:, kc, ti * 128:(ti + 1) * 128], wo[:, kc, half * 512:(half + 1) * 512], start=(kc == 0), stop=(kc == 7))
                                tmp = tmo[n_ % 2]
                                n_ += 1
                                P.tt("dve", tmp[:], py[:], G1[:, half * 512:(half + 1) * 512], ALU.mult)
                                P.tt("pool", x[:, t, half * 512:(half + 1) * 512], x[:, t, half * 512:(half + 1) * 512], tmp[:], ALU.add)
                        P.fence()

        K.P, K.nc, K.x, K.pb, K.sb, K.psb = P, nc, x, pb, sb, psb
        def dump_bf16(name, src, n):
            with ExitStack() as ph:
                stg = sb("stg", [128, src.shape[1], n], F32, ph)
                P.copy("dve", stg[:], src)
                P.dma(dbg_out[name][:, :, :], stg[:])
                P.fence()

        for l in range(depth):
            if "adaln" in parts:
                adaln(l)
            calc_rstd(list(range(NT)))
            for unit in units:
                with ExitStack() as us:
                    ntu = len(unit["tiles"]) * 128
                    oT = {}
                    for m in ("C", "B", "A"):
                        oT[m] = sb("o%sT" % m.lower(), [128, 4, ntu], BF16, us)
                        if m in mixers:
                            K.phase = "mix%s_%s" % (m, unit["name"])
                            {"A": mixer_A, "B": mixer_B, "C": mixer_C}[m](l, unit, oT[m])
                            K.phase = "other"
                        else:
                            P.memset("pool", oT[m][:], 0.0)
                        nm = "o%sT_%s%d" % (m.lower(), unit["name"], l)
                        if nm in dbg_out:
                            dump_bf16(nm, oT[m][:], ntu)
                    if "merge" in parts:
                        K.phase = "merge_%s" % unit["name"]
                        merge(l, unit, [oT["A"], oT["B"], oT["C"]])
                        K.phase = "other"
                    P.fence()
            if "ffn" in parts:
                ffn(l)
        if "final" in parts:
            final_norm()
        P.finish()
        K.stats = (P.n_inst, P.n_wait)
    return nc, K


def _rope_feat(R, pos_row, pos_col):
    q = R // 4
    inv = np.power(10000.0, -np.arange(q, dtype=np.float32) / q).astype(np.float32)
    n = len(pos_row)
    C = np.zeros((R, n), np.float32)
    S = np.zeros((R, n), np.float32)
    for r in range(R):
        blk, i = r // q, r % q
        pos = pos_row if blk < 2 else pos_col
        ang = pos.astype(np.float32) * inv[i]
        C[r] = np.cos(ang)
        S[r] = (-1.0 if blk % 2 == 0 else 1.0) * np.sin(ang)
    return C, S


def _swap_perm(R):
    q = R // 4
    return np.array([r + q if (r // q) % 2 == 0 else r - q for r in range(R)])


def make_shared(inp, NTS, NSP):
    f = lambda a: np.ascontiguousarray(np.asarray(a, dtype=np.float32))
    depth = inp["w_in"].shape[0]
    NTP = NSP * 2
    NTOK = (NTP + NTS) * 128
    sh = {}
    for k in ("w_ada", "b_ada", "w_ff1", "w_ff2", "w_in", "b_gkv", "b_w_ukv", "a_sink",
              "w_branch", "w_out", "c_onorm_g"):
        if k in inp:
            sh[k] = f(inp[k])
    sh["b_adaT"] = f(np.asarray(inp["b_ada"]).reshape(depth, 48, 128).transpose(0, 2, 1))
    sh["norm1T"] = f(np.asarray(inp["norm1_g"]).reshape(depth, 8, 128).transpose(0, 2, 1))
    sh["norm2T"] = f(np.asarray(inp["norm2_g"]).reshape(depth, 8, 128).transpose(0, 2, 1))
    sh["final_g"] = f(np.asarray(inp["final_g"]).reshape(1, D))
    sh["b_gqT"] = f(np.asarray(inp["b_gq"]).reshape(depth, 3, 128).transpose(0, 2, 1))
    w_in = np.asarray(inp["w_in"])
    p64, p32 = _swap_perm(64), _swap_perm(32)
    cols = []
    for h in range(8):
        cols += list(OFF_AQ + h * 64 + p64)
    for h in range(2):
        cols += list(OFF_AK + h * 64 + p64)
    cols += list(OFF_BKR + p32)
    sh["w_in_sw"] = f(w_in[:, :, np.array(cols)])
    wuq_ = np.asarray(inp["b_w_uq"], dtype=np.float32).reshape(depth, 384, 8, 96)
    zpad = np.zeros((depth, 384, 8, 32), np.float32)
    sh["b_w_uq_r"] = f(np.concatenate([zpad, wuq_[..., 64:96], wuq_[..., 0:64]], -1).reshape(depth, 384, 1024))
    sh["b_w_uq_rs"] = f(np.concatenate([zpad, wuq_[..., 64 + p32], wuq_[..., 0:64]], -1).reshape(depth, 384, 1024))
    ts = np.arange(NTS * 128)
    row, col = ts // 64, ts % 64
    CA, SA = _rope_feat(64, row, col)
    CB, SB = _rope_feat(32, row, col)
    npad = NTP * 128
    padC = lambda C: np.concatenate([np.ones((C.shape[0], npad), np.float32), C], axis=1)
    padS = lambda S: np.concatenate([np.zeros((S.shape[0], npad), np.float32), S], axis=1)
    sh["ropeA_C"], sh["ropeA_S"] = f(padC(CA)), f(padS(SA))
    sh["ropeQB_C"] = f(np.concatenate([np.ones((32, NTOK), np.float32), padC(CB), np.ones((64, NTOK), np.float32)], axis=0))
    sh["ropeQB_S"] = f(np.concatenate([np.zeros((32, NTOK), np.float32), padS(SB), np.zeros((64, NTOK), np.float32)], axis=0))
    sh["ropeKB_C"], sh["ropeKB_S"] = f(padC(CB).T), f(padS(SB).T)
    sh["cst_ident"] = np.eye(128, dtype=np.float32)
    jj = np.arange(128)[:, None]
    ii = np.arange(128)[None, :]
    sh["cst_tri"] = f(np.stack([(jj < ii), (jj <= ii), (jj > ii), (jj >= ii)], axis=1))
    lv = []
    for k in range(1, 8):
        lv.append(((jj >> k) == (ii >> k)) & ((jj >> (k - 1)) != (ii >> (k - 1))) & (jj < ii))
    tri4 = [(jj < ii), (jj <= ii), (jj > ii), (jj >= ii)]
    sh["cst_trc"] = f(np.stack([np.stack([tri4[0], tri4[2]], 1), np.stack([tri4[1], tri4[3]], 1)], 1))
    sh["cst_lvN"] = f(np.stack([np.stack([m, m.T], 1) for m in lv], 1))
    sh["cst_lvA"] = f(np.stack([np.stack([m.T, m], 1) for m in lv], 1))
    sh["c_convT"] = f(np.asarray(inp["c_conv_w"]).reshape(depth, 3, 12, 128).transpose(0, 3, 2, 1))
    sh["c_a_log"] = f(np.asarray(inp["c_a_log"]).reshape(depth, 8))
    sh["c_dt_bias"] = f(np.asarray(inp["c_dt_bias"]).reshape(depth, 8))
    sel = np.zeros((2, 2, 128), np.float32)
    sel[0, 0] = 1
    sel[1, 1] = 1
    sh["cst_sel"] = sel
    b_ = np.arange(128)[:, None]
    a_ = np.arange(128)[None, :]
    mprev = np.where(a_ > b_, -30000.0, 0.0).astype(np.float32)
    mnext = np.where(b_ > a_, -30000.0, 0.0).astype(np.float32)
    sh["cst_masks"] = f(np.stack([mprev] * 4 + [mnext] * 4, axis=1))
    return sh


def make_core(inp, prompt_ids, sample_id, NTS):
    f = lambda a: np.ascontiguousarray(np.asarray(a, dtype=np.float32))
    xp = np.asarray(inp["x_prompt"])[prompt_ids].reshape(-1, D)
    xs = np.asarray(inp["x_sample"])[sample_id].reshape(-1, D)
    cond = np.stack([np.asarray(inp["c_ctx"]), np.asarray(inp["c"])[sample_id]], 0)
    c = {"xin": f(np.concatenate([xp, xs], 0)),
         "condT": f(cond.reshape(2, 8, 128).transpose(2, 1, 0)),
         "cak": f(np.asarray(inp["cache_a_k"])[sample_id].reshape(-1, 256, 128)),
         "cav": f(np.asarray(inp["cache_a_v"])[sample_id].reshape(-1, 256, 128)),
         "cckv": f(np.asarray(inp["cache_b_ckv"])[sample_id]),
         "ckr": f(np.asarray(inp["cache_b_krope"])[sample_id]),
         "cstate": f(np.asarray(inp["state_c"])[sample_id])}
    return c


_NTS, _NSP = 16, 2
_CACHE = {}


def kernel(**inputs):
    if "nc" not in _CACHE:
        _CACHE["nc"] = build(NTS=_NTS, NSP=_NSP, depth=DEPTH, parts=("adaln", "merge", "ffn", "final"))
    nc, K = _CACHE["nc"]
    names = [a.memorylocations[0].name for a in nc.m.functions[0].allocations
             if isinstance(a, mybir.MemoryLocationSet) and a.kind == "ExternalInput"]
    shared = make_shared(inputs, _NTS, _NSP)
    in_maps = []
    for core in range(8):
        d = dict(shared)
        d.update(make_core(inputs, [2 * core, 2 * core + 1], core // 2, _NTS))
        in_maps.append({k: d[k] for k in names if k in d})
    res = run_bass_kernel_spmd(nc, in_maps, core_ids=list(range(8))).results
    npt = _NSP * 256
    y_prompt = np.concatenate([res[c]["y"][:npt].reshape(_NSP, 256, D) for c in range(8)], 0)
    y_sample = np.stack([res[2 * b]["y"][npt:] for b in range(4)], 0)
    cat = lambda k: np.concatenate([res[c][k] for c in range(8)], 0)
    new_a_k = cat("nak").reshape(16, DEPTH, 256, 2, 64)
    new_a_v = cat("nav").reshape(16, DEPTH, 256, 2, 64)
    new_b_ckv = cat("nckv")
    new_b_krope = cat("nkr")
    new_c_state = cat("ncs")
    f = lambda a: np.ascontiguousarray(a, dtype=np.float32)
    return (f(y_prompt), f(y_sample), f(new_a_k), f(new_a_v), f(new_b_ckv), f(new_b_krope), f(new_c_state))
```

```python
import bisect
from contextlib import ExitStack

import numpy as np
import concourse.bass as bass
import concourse.mybir as mybir
from concourse.bass_utils import run_bass_kernel_spmd

F32 = mybir.dt.float32
BF16 = mybir.dt.bfloat16
AF = mybir.ActivationFunctionType
ALU = mybir.AluOpType
AX = mybir.AxisListType

COMPUTE = ("pe", "act", "dve", "pool")
NDS = 48
NDS_SP = 32


def _region(ap):
    t = ap.tensor
    dims = [(int(s), int(c)) for s, c in ap.ap]
    off = int(ap.offset)
    if type(t).__name__.startswith("DRam"):
        ext = sum((c - 1) * abs(s) for s, c in dims)
        return (t.name, 0, 1, off, off + ext + 1)
    rows = 1
    for d in list(t.shape)[1:]:
        rows *= int(d)
    p0 = off // rows
    f0 = off % rows
    pc = dims[0][1] if dims[0][0] != 0 else 1
    ext = sum((c - 1) * abs(s) for s, c in dims[1:])
    f1 = f0 + ext + 1
    if type(t).__name__.startswith("PSum"):
        be = 1024 if t.dtype == BF16 else 512
        f0 = (f0 // be) * be
        f1 = ((f1 + be - 1) // be) * be
        p0, pc = 0, 128
    return (t.name, p0, p0 + pc, f0, f1)


def _overlap(a, b):
    return a[1] < b[2] and b[1] < a[2] and a[3] < b[4] and b[3] < a[4]


def _contains(a, b):
    return a[1] <= b[1] and b[2] <= a[2] and a[3] <= b[3] and b[4] <= a[4]


class Op:
    __slots__ = ("eng", "fn", "deps", "signal", "is_dma", "sem", "semval", "seq", "idx")

    def __init__(self, eng, fn, idx, is_dma=False):
        self.eng = eng
        self.fn = fn
        self.deps = []
        self.signal = False
        self.is_dma = is_dma
        self.sem = None
        self.semval = 0
        self.seq = None
        self.idx = idx


class Prog:
    def __init__(self, nc, stack):
        self.nc = nc
        self.engs = {"pe": nc.tensor, "act": nc.scalar, "dve": nc.vector, "pool": nc.gpsimd, "sp": nc.sync}
        self.esem = {e: stack.enter_context(nc.semaphore("es_" + e)) for e in COMPUTE}
        self.dsems = [stack.enter_context(nc.semaphore("ds%d" % i)) for i in range(NDS)]
        self.dcount = [0] * NDS
        self.dlast = [None] * NDS
        self.dnext = {"sp": 0, "pool": 0}
        self.pending = []
        self.state = {}
        self.sigcount = {e: 0 for e in COMPUTE}
        self.sigidx = {e: [] for e in COMPUTE}
        self.sigseq = {e: [] for e in COMPUTE}
        self.waited = {f: {e: 0 for e in COMPUTE} for f in self.engs}
        self.waited_dma = {f: {} for f in self.engs}
        self.last_op = {}
        self.open_dmas = []
        self.nops = 0
        self.n_inst = 0
        self.n_wait = 0

    def add(self, eng, fn, reads=(), writes=(), dma=False):
        op = Op(eng, fn, self.nops, dma)
        self.nops += 1
        deps = {}
        rregs = [_region(a) for a in reads]
        wregs = [_region(a) for a in writes]
        for r in rregs:
            is_ps = r[0].startswith("pb")
            for (reg, o, isw) in self.state.get(r[0], ()):
                if _overlap(reg, r) and (isw or (is_ps and o.eng != eng)):
                    deps[o.idx] = o
        for w in wregs:
            for (reg, o, isw) in self.state.get(w[0], ()):
                if _overlap(reg, w):
                    if (not dma) and (not o.is_dma) and o.eng == eng and eng == "pe":
                        continue
                    deps[o.idx] = o
        if dma:
            if eng == "sp":
                s = self.dnext["sp"]
                self.dnext["sp"] = (s + 1) % NDS_SP
            else:
                s = NDS_SP + self.dnext["pool"]
                self.dnext["pool"] = (self.dnext["pool"] + 1) % (NDS - NDS_SP)
            if self.dlast[s] is not None:
                deps[self.dlast[s].idx] = self.dlast[s]
            self.dlast[s] = op
            self.dcount[s] += 1
            op.sem = s
            op.semval = 16 * self.dcount[s]
            self.open_dmas.append(op)
        for o in deps.values():
            if not o.is_dma:
                o.signal = True
        op.deps = list(deps.values())
        for w in wregs:
            lst = self.state.setdefault(w[0], [])
            lst[:] = [rec for rec in lst if not _contains(w, rec[0])]
            lst.append((w, op, True))
        for r in rregs:
            lst = self.state.setdefault(r[0], [])
            if not dma:
                lst[:] = [rec for rec in lst
                          if not ((not rec[2]) and (not rec[1].is_dma) and rec[1].eng == eng and _contains(r, rec[0]))]
            lst.append((r, op, False))
        self.pending.append(op)
        if not dma:
            self.last_op[eng] = op
        return op

    def fence(self):
        lasts = dict(self.last_op)
        dmas = list(self.open_dmas)
        self.open_dmas = []
        for f in self.engs:
            op = Op(f, None, self.nops, False)
            self.nops += 1
            for e, o in lasts.items():
                o.signal = True
                op.deps.append(o)
            op.deps.extend(dmas)
            self.pending.append(op)
        self.state = {}
        self.flush()

    def _seq_of(self, dep):
        if dep.seq is not None:
            return dep.seq
        e = dep.eng
        i = bisect.bisect_left(self.sigidx[e], dep.idx)
        return self.sigseq[e][i]

    def flush(self):
        lastp = {}
        for op in self.pending:
            if (not op.is_dma) and op.fn is not None and op.eng in COMPUTE:
                lastp[op.eng] = op
        for op in lastp.values():
            op.signal = True
        for op in self.pending:
            f = op.eng
            eng = self.engs[f]
            waits = []
            for dep in op.deps:
                if dep.is_dma:
                    if self.waited_dma[f].get(dep.sem, 0) >= dep.semval:
                        continue
                    waits.append((self.dsems[dep.sem], dep.semval))
                    self.waited_dma[f][dep.sem] = dep.semval
                else:
                    e = dep.eng
                    seq = self._seq_of(dep)
                    if self.waited[f][e] >= seq:
                        continue
                    waits.append((self.esem[e], seq))
                    self.waited[f][e] = seq
            self.n_wait += len(waits)
            embed = None
            if op.fn is not None and (not op.is_dma) and waits:
                embed = waits.pop()
            for (sem_, val_) in waits:
                eng.wait_ge(sem_, val_)
            if op.fn is None:
                continue
            ins = op.fn(eng)
            if embed is not None:
                ins._wait_ge(embed[0], embed[1])
            self.n_inst += 1
            if op.is_dma:
                ins.then_inc(self.dsems[op.sem], 16)
            elif op.signal:
                self.sigcount[f] += 1
                op.seq = self.sigcount[f]
                ins.then_inc(self.esem[f], 1)
                self.sigidx[f].append(op.idx)
                self.sigseq[f].append(op.seq)
                self.waited[f][f] = max(self.waited[f][f], 0)
        self.pending = []

    def finish(self):
        self.fence()

    def mm(self, out, lhsT, rhs, start=True, stop=True):
        return self.add("pe", lambda e: e.matmul(out, lhsT, rhs, start=start, stop=stop),
                        reads=[lhsT, rhs], writes=[out])

    def tr(self, out, in_, ident):
        return self.add("pe", lambda e: e.transpose(out, in_, ident), reads=[in_, ident], writes=[out])

    def act(self, out, in_, func, bias=None, scale=1.0, accum_out=None, eng="act"):
        reads = [in_]
        kw = {}
        if bias is not None:
            kw["bias"] = bias
            if not isinstance(bias, (int, float)):
                reads.append(bias)
        if not isinstance(scale, (int, float)):
            reads.append(scale)
        kw["scale"] = scale
        writes = [out]
        if accum_out is not None:
            kw["accum_out"] = accum_out
            writes.append(accum_out)
        return self.add("act", lambda e: e.activation(out, in_, func, **kw), reads=reads, writes=writes)

    def tt(self, eng, out, in0, in1, op):
        return self.add(eng, lambda e: e.tensor_tensor(out, in0, in1, op), reads=[in0, in1], writes=[out])

    def ts(self, eng, out, in0, s1, s2=None, op0=ALU.mult, op1=None, accum_out=None):
        reads = [in0]
        for s in (s1, s2):
            if s is not None and not isinstance(s, (int, float)):
                reads.append(s)
        writes = [out]
        kw = {}
        if op1 is not None:
            kw["op1"] = op1
        if accum_out is not None:
            kw["accum_out"] = accum_out
            writes.append(accum_out)
        return self.add(eng, lambda e: e.tensor_scalar(out, in0, s1, s2, op0, **kw), reads=reads, writes=writes)

    def stt(self, out, in0, scalar, in1, op0, op1, eng="dve"):
        reads = [in0, in1]
        if not isinstance(scalar, (int, float)):
            reads.append(scalar)
        return self.add(eng, lambda e: e.scalar_tensor_tensor(out, in0, scalar, in1, op0, op1),
                        reads=reads, writes=[out])

    def copy(self, eng, out, in_):
        if eng == "act":
            return self.add("act", lambda e: e.copy(out, in_), reads=[in_], writes=[out])
        return self.add(eng, lambda e: e.tensor_copy(out, in_), reads=[in_], writes=[out])

    def memset(self, eng, out, val):
        return self.add(eng, lambda e: e.memset(out, val), reads=[], writes=[out])

    def recip(self, out, in_):
        return self.add("dve", lambda e: e.reciprocal(out, in_), reads=[in_], writes=[out])

    def reduce(self, out, in_, op, axis=AX.X, eng="dve"):
        return self.add(eng, lambda e: e.tensor_reduce(out, in_, axis, op), reads=[in_], writes=[out])

    def dma(self, out, in_, q="sp"):
        return self.add(q, lambda e: e.dma_start(out=out, in_=in_), reads=[in_], writes=[out], dma=True)


D = 1024
DEPTH = 2
D_IN = 6576
OFF_AQ, OFF_AK, OFF_AV = 0, 512, 640
OFF_BCQ, OFF_BCKV, OFF_BKR = 768, 1152, 1408
OFF_CQKV, OFF_CA, OFF_CB, OFF_CZ, OFF_GATES = 1440, 2976, 2984, 2992, 3504
EPS = 1e-6


class Ctx:
    pass


def build(NTS=16, NSP=2, depth=DEPTH, mixers=("B", "A", "C"), dbg=(), parts=("adaln", "ffn", "final")):
    NTP = NSP * 2
    NT = NTP + NTS
    NTOK = NT * 128
    nc = bass.Bass("TRN2", target_bir_lowering=False)
    K = Ctx()

    def din(name, shape):
        return nc.dram_tensor(name, list(shape), F32, kind="ExternalInput").ap()

    def dout(name, shape):
        return nc.dram_tensor(name, list(shape), F32, kind="ExternalOutput").ap()

    xin = din("xin", [NTOK, D])
    condT = din("condT", [128, 8, 2])
    w_ada = din("w_ada", [depth, D, 6 * D])
    b_adaT = din("b_adaT", [depth, 128, 48])
    b_ada = din("b_ada", [depth, 6 * D])
    norm1T = din("norm1T", [depth, 128, 8])
    norm2T = din("norm2T", [depth, 128, 8])
    final_g = din("final_g", [1, D])
    w_ff1 = din("w_ff1", [depth, D, 4 * D])
    w_ff2 = din("w_ff2", [depth, 4 * D, D])
    cst_ident = din("cst_ident", [128, 128])
    w_branch = din("w_branch", [depth, 3, 512, D])
    w_out = din("w_out", [depth, D, D])
    w_in = din("w_in", [depth, D, D_IN])
    w_in_sw = din("w_in_sw", [depth, D, 672])
    b_w_uq_r = din("b_w_uq_r", [depth, 384, 1024])
    b_w_uq_rs = din("b_w_uq_rs", [depth, 384, 1024])
    b_gqT = din("b_gqT", [depth, 128, 3])
    b_gkv = din("b_gkv", [depth, 256])
    b_w_ukv = din("b_w_ukv", [depth, 256, 1024])
    a_sink = din("a_sink", [depth, 8])
    ropeQB_C = din("ropeQB_C", [128, NTOK])
    ropeQB_S = din("ropeQB_S", [128, NTOK])
    ropeKB_C = din("ropeKB_C", [NTOK, 32])
    ropeKB_S = din("ropeKB_S", [NTOK, 32])
    ropeA_C = din("ropeA_C", [64, NTOK])
    ropeA_S = din("ropeA_S", [64, NTOK])
    cak = din("cak", [depth, 256, 128])
    cav = din("cav", [depth, 256, 128])
    cckv = din("cckv", [depth, 256, 256])
    ckr = din("ckr", [depth, 256, 32])
    cstate = din("cstate", [depth, 2, 4, 128, 128])
    cst_masks = din("cst_masks", [128, 8, 128])
    cst_tri = din("cst_tri", [128, 4, 128])
    cst_trc = din("cst_trc", [128, 2, 2, 128])
    cst_lvN = din("cst_lvN", [128, 7, 2, 128])
    cst_lvA = din("cst_lvA", [128, 7, 2, 128])
    c_convT = din("c_convT", [depth, 128, 12, 3])
    c_a_log = din("c_a_log", [depth, 8])
    c_dt_bias = din("c_dt_bias", [depth, 8])
    c_onorm_g = din("c_onorm_g", [depth, 128])
    nak_out = dout("nak", [NSP, depth, 256, 128])
    nav_out = dout("nav", [NSP, depth, 256, 128])
    nckv_out = dout("nckv", [NSP, depth, 256, 256])
    nkr_out = dout("nkr", [NSP, depth, 256, 32])
    ncs_out = dout("ncs", [NSP, depth, 2, 4, 128, 128])
    y_out = dout("y", [NTOK, D])
    dbg_out = {}
    for name, shape in dbg:
        dbg_out[name] = dout(name, shape)

    with ExitStack() as st:
        P = Prog(nc, st)

        K.uid = 0

        K.phase = "init"
        K.minrem = {}

        def sb(name, shape, dt=F32, stack=st):
            K.uid += 1
            t_ = stack.enter_context(nc.sbuf_tensor("%s_%d" % (name, K.uid), list(shape), dt))
            K.minrem[K.phase] = min(K.minrem.get(K.phase, 1 << 30), nc.sbuf_bytes_remaining)
            return t_

        def psb(name, shape=(128, 512), dt=F32, stack=st):
            K.uid += 1
            return stack.enter_context(nc.psum_tensor("%s_%d" % (name, K.uid), list(shape), dt))

        x = sb("x", [128, NT, D])
        pb = [psb("pb%d" % i) for i in range(7)]
        pbb = psb("pbb", (128, 1024), BF16)
        ident_f = sb("ident_f", [128, 128])
        ident_b = sb("ident_b", [128, 128], BF16)
        condT_sb = sb("condT_sb", [128, 8, 2])
        scT = sb("scT", [128, 8, 2], BF16)
        modT = sb("modT", [128, 48, 2])
        affA = sb("affA", [128, 2, 8, 2])
        nrmT = sb("nrmT", [128, 2, 8])
        ss = sb("ss", [128, NT])
        rstd = sb("rstd", [128, NT])
        junkb = [sb("junk%d" % i, [128, D], BF16) for i in range(2)] * 2
        K.nj = 0
        xnb = [sb("xnb%d" % i, [128, D]) for i in range(2)]

        P.dma(ident_f[:], cst_ident[:, :])
        P.dma(condT_sb[:], condT[:, :, :])
        for t in range(NT):
            P.dma(x[:, t, :], xin[t * 128:(t + 1) * 128, :])
        P.copy("dve", ident_b[:], ident_f[:])
        ones_b = sb("ones_b", [128, 128], BF16)
        ones_f = sb("ones_f", [128, 128])
        P.memset("dve", ones_b[:], 1.0)
        P.memset("dve", ones_f[:], 1.0)
        P.act(scT[:], condT_sb[:], AF.Silu)

        def cond_of(t):
            return 0 if t < NTP else 1

        def calc_rstd(tiles):
            for t in tiles:
                K.nj += 1
                P.act(junkb[K.nj % 2][:], x[:, t, :], AF.Square, accum_out=ss[:, t:t + 1])
            t0, t1 = tiles[0], tiles[-1] + 1
            P.ts("dve", rstd[:, t0:t1], ss[:, t0:t1], 1.0 / D, EPS, op0=ALU.mult, op1=ALU.add)
            P.act(rstd[:, t0:t1], rstd[:, t0:t1], AF.Sqrt)
            P.recip(rstd[:, t0:t1], rstd[:, t0:t1])

        K.ntr = 0
        K.ew = "pool"

        def build_hT(dst, tiles, which):
            for ti, t in enumerate(tiles):
                cd = cond_of(t)
                xn = xnb[K.ntr % 2]
                K.ntr += 1
                P.ts("dve", xn[:], x[:, t, :], rstd[:, t:t + 1], None, op0=ALU.mult)
                for g in range(2):
                    pp = pb[5 + g]
                    for c in range(4):
                        P.tr(pp[:, c * 128:(c + 1) * 128], xn[:, (g * 4 + c) * 128:(g * 4 + c + 1) * 128], ident_f[:])
                    for c in range(4):
                        ch = g * 4 + c
                        a_ap = affA[:, which, ch, cd:cd + 1]
                        b_ap = modT[:, (3 * which) * 8 + ch, cd:cd + 1]
                        o_ap = dst[:, ch, ti * 128:(ti + 1) * 128]
                        if g % 2 == 0:
                            P.act(o_ap, pp[:, c * 128:(c + 1) * 128], AF.Identity, bias=b_ap, scale=a_ap)
                        else:
                            P.ts("dve", o_ap, pp[:, c * 128:(c + 1) * 128], a_ap, b_ap, op0=ALU.mult, op1=ALU.add)

        def make_G(dst, which, cd, ph):
            dg = [sb("dgG%d" % i, [128, 128], F32, ph) for i in range(2)]
            j = 2 + 3 * which
            for c in range(8):
                P.ts("dve", dg[c % 2][:], ident_f[:], modT[:, j * 8 + c, cd:cd + 1], None, op0=ALU.mult)
                pg = pb[c % 2]
                P.mm(pg[:, 0:128], ones_f[:], dg[c % 2][:])
                P.copy("act", dst[:, c * 128:(c + 1) * 128], pg[:, 0:128])

        def adaln(l):
            with ExitStack() as ph:
                wb = [sb("wada%d" % i, [128, 8, D], BF16, ph) for i in range(2)]
                badaT = sb("badaT", [128, 48], F32, ph)
                P.dma(badaT[:], b_adaT[l, :, :])
                P.dma(nrmT[:, 0, :], norm1T[l, :, :])
                P.dma(nrmT[:, 1, :], norm2T[l, :, :])
                for j in range(6):
                    w = wb[j % 2]
                    P.dma(w[:], w_ada[l, :, j * D:(j + 1) * D].rearrange("(c p) n -> p c n", p=128), q="pool")
                    pm = pb[j % 2]
                    for c in range(8):
                        for kc in range(8):
                            P.mm(pm[:, c * 2:c * 2 + 2], w[:, kc, c * 128:(c + 1) * 128], scT[:, kc, :],
                                 start=(kc == 0), stop=(kc == 7))
                    P.tt("dve", modT[:, j * 8:(j + 1) * 8, :], pm[:, 0:16].rearrange("p (c k) -> p c k", k=2),
                         badaT[:, j * 8:(j + 1) * 8].unsqueeze(2).broadcast_to([128, 8, 2]), ALU.add)
                for which in range(2):
                    j = 1 + 3 * which
                    P.ts("dve", affA[:, which, :, :], modT[:, j * 8:(j + 1) * 8, :], 1.0, None, op0=ALU.add)
                    P.tt("dve", affA[:, which, :, :], affA[:, which, :, :],
                         nrmT[:, which, :].unsqueeze(2).broadcast_to([128, 8, 2]), ALU.mult)
                P.fence()

        def ffn(l):
            with ExitStack() as ph:
                h2T = sb("h2T", [128, 8, NTOK], BF16, ph)
                w1b = [sb("w1b%d" % i, [128, 8, 512], BF16, ph) for i in range(2)]
                w2b = [sb("w2b%d" % i, [128, 4, D], BF16, ph) for i in range(2)]
                aTb = [sb("aTb%d" % i, [128, 4, 256], BF16, ph) for i in range(2)]
                rfb = [sb("rfb%d" % i, [128, 256], F32, ph) for i in range(2)]
                tmpb = [sb("tmpb%d" % i, [128, 512], F32, ph) for i in range(2)]
                G2 = sb("G2", [128, 2, D], F32, ph)
                for cd_ in range(2):
                    make_G(G2[:, cd_, :], 1, cd_, ph)

                def load_w(g):
                    P.dma(w1b[g % 2][:], w_ff1[l, :, g * 512:(g + 1) * 512].rearrange("(c p) n -> p c n", p=128), q="pool")
                    P.dma(w2b[g % 2][:], w_ff2[l, g * 512:(g + 1) * 512, :].rearrange("(c p) n -> p c n", p=128), q="pool")

                load_w(0)
                calc_rstd(list(range(NT)))
                build_hT(h2T, list(range(NT)), 1)
                K.cnt = 0
                NG = 8
                items = [(g, blk) for g in range(NG) for blk in range(NT // 2)]

                def ff1(g, blk):
                    w1 = w1b[g % 2]
                    aT = aTb[blk % 2]
                    for fc in range(4):
                        pa = pb[fc % 2]
                        for kc in range(8):
                            P.mm(pa[:, 0:256], w1[:, kc, fc * 128:(fc + 1) * 128],
                                 h2T[:, kc, blk * 256:(blk + 1) * 256], start=(kc == 0), stop=(kc == 7))
                        rf = rfb[fc % 2]
                        P.act(rf[:], pa[:, 0:256], AF.Relu)
                        P.tt("dve" if fc % 2 == 0 else "pool", aT[:, fc, :], rf[:], rf[:], ALU.mult)

                def ff2(g, blk):
                    w2 = w2b[g % 2]
                    aT = aTb[blk % 2]
                    for tt_ in range(2):
                        t = blk * 2 + tt_
                        cd = cond_of(t)
                        for half in range(2):
                            py = pb[2 + tt_ * 2 + half]
                            for fc in range(4):
                                P.mm(py[:], aT[:, fc, tt_ * 128:(tt_ + 1) * 128],
                                     w2[:, fc, half * 512:(half + 1) * 512], start=(fc == 0), stop=(fc == 3))
                            if cd == 0:
                                tmp = tmpb[K.cnt % 2]
                                K.cnt += 1
                                P.tt("dve", tmp[:], py[:], G2[:, cd, half * 512:(half + 1) * 512], ALU.mult)
                                P.tt(K.ew, x[:, t, half * 512:(half + 1) * 512],
                                     x[:, t, half * 512:(half + 1) * 512], tmp[:], ALU.add)
                            else:
                                P.tt("dve", x[:, t, half * 512:(half + 1) * 512],
                                     x[:, t, half * 512:(half + 1) * 512], py[:], ALU.add)
                    if blk == NTP // 2 - 1:
                        P.tt("pool", w2[:], w2[:], G2[:, 1, :].unsqueeze(1).broadcast_to([128, 4, D]), ALU.mult)

                ff1(*items[0])
                for n, (g, blk) in enumerate(items):
                    if blk == 0 and g + 1 < NG:
                        load_w(g + 1)
                    if n + 1 < len(items):
                        ff1(*items[n + 1])
                    ff2(g, blk)
                P.fence()

        def final_norm():
            with ExitStack() as ph:
                gf = sb("gf", [128, D], F32, ph)
                ob = [sb("ob%d" % i, [128, D], F32, ph) for i in range(2)]
                P.dma(gf[:], final_g.partition_broadcast(128) if False else final_g[0:1, :].broadcast_to([128, D]))
                calc_rstd(list(range(NT)))
                for t in range(NT):
                    o = ob[t % 2]
                    P.stt(o[:], x[:, t, :], rstd[:, t:t + 1], gf[:], ALU.mult, ALU.mult)
                    P.dma(y_out[t * 128:(t + 1) * 128, :], o[:])
                P.fence()


        units = [
            dict(name="P", tiles=list(range(0, NTP)), cond=0, prompt=True,
                 seqs=[dict(t0=2 * i, nt=2, ctx=False, idx=i) for i in range(NSP)]),
            dict(name="S", tiles=list(range(NTP, NT)), cond=1, prompt=False,
                 seqs=[dict(t0=NTP, nt=NTS, ctx=True, idx=0)]),
        ]
        K.rr = 0

        def blocks_of(unit, bs=512):
            out = []
            for sq in unit["seqs"]:
                n = sq["nt"] * 128
                o = 0
                while o < n:
                    b = min(bs, n - o)
                    out.append((sq, sq["t0"] * 128 + o, b))
                    o += b
            return out

        def attend(KT, V1, QT, nq, negm, key_tiles, obanks, sbanks, PTb, scale, tail=None):
            nk = len(key_tiles)
            slots = []

            def scores(i):
                kt, mask = key_tiles[i]
                sp_ = sbanks[K.rr % len(sbanks)]
                pt_ = PTb[K.rr % len(PTb)]
                K.rr += 1
                P.mm(sp_[:, 0:nq], KT(kt), QT, start=True, stop=(mask is None and negm is None))
                if mask is not None:
                    P.mm(sp_[:, 0:nq], ident_b[:], mask, start=False, stop=(negm is None))
                if negm is not None:
                    P.mm(sp_[:, 0:nq], ones_b[0:1, 0:128], negm, start=False, stop=True)
                slots.append((sp_, pt_))

            scores(0)
            for i, (kt, mask) in enumerate(key_tiles):
                if i + 1 < nk:
                    scores(i + 1)
                sp_, pt_ = slots[i]
                P.act(pt_[:, 0:nq], sp_[:, 0:nq], AF.Exp, scale=scale)
                for qi, ob in enumerate(obanks):
                    last = (i == nk - 1) and tail is None
                    P.mm(ob, pt_[:, qi * 128:(qi + 1) * 128], V1(kt), start=(i == 0), stop=last)
            if tail is not None:
                tail()

        def mixer_B(l, unit, obT):
            with ExitStack() as ph:
                tiles = unit["tiles"]
                ntok_u = len(tiles) * 128
                tok_base = tiles[0] * 128
                has_ctx = any(sq["ctx"] for sq in unit["seqs"])
                maxkeys = max(sq["nt"] * 128 + (256 if sq["ctx"] else 0) for sq in unit["seqs"])
                maxnt = max(sq["nt"] for sq in unit["seqs"])
                nkeys_u = ntok_u + (256 if has_ctx else 0)
                cqnT = sb("cqnT", [128, 3, ntok_u], BF16, ph)
                ckvT = sb("ckvT", [128, 2, nkeys_u], BF16, ph)
                krT = sb("krT", [32, nkeys_u], BF16, ph)
                scale = 96.0 ** -0.5
                b1 = ExitStack()
                wB = sb("wB", [128, 8, 704], BF16, b1)
                gkvb = sb("gkvb", [128, 256], F32, b1)
                P.dma(wB[:, :, 0:672], w_in[l, :, OFF_BCQ:OFF_BCQ + 672].rearrange("(c p) n -> p c n", p=128), q="pool")
                P.dma(wB[:, :, 672:704], w_in_sw[l, :, 640:672].rearrange("(c p) n -> p c n", p=128), q="pool")
                P.dma(gkvb[:], b_gkv[l:l + 1, :].broadcast_to([128, 256]))
                hTb = [sb("hTbB%d" % i, [128, 8, 256], BF16, b1) for i in range(2)]
                ssb = [sb("ssB%d" % i, [128, 8], F32, b1) for i in range(2)]
                jk = [sb("jkB%d" % i, [128, 384], F32, b1) for i in range(2)]
                cqn_b = [sb("cqnb%d" % i, [128, 384], BF16, b1) for i in range(2)]
                ckvn_f = [sb("ckvnf%d" % i, [128, 256], F32, b1) for i in range(2)]
                ckvn_b = [sb("ckvnb%d" % i, [128, 256], BF16, b1) for i in range(2)]
                kr_f = [sb("krf%d" % i, [128, 64], F32, b1) for i in range(2)]
                kr_t = [sb("krt%d" % i, [128, 64], F32, b1) for i in range(2)]
                kr_b = [sb("krb%d" % i, [128, 32], BF16, b1) for i in range(2)]
                rKC = [sb("rKC%d" % i, [128, 32], F32, b1) for i in range(2)]
                rKS = [sb("rKS%d" % i, [128, 32], F32, b1) for i in range(2)]

                n_t = 0
                for bi, (sq, tok0, nb) in enumerate(blocks_of(unit, 256)):
                    hT = hTb[bi % 2]
                    btiles = [tok0 // 128 + i for i in range(nb // 128)]
                    build_hT(hT, btiles, 0)
                    for ti, t in enumerate(btiles):
                        r = n_t % 2
                        n_t += 1
                        sv = ssb[r]
                        p1, p2 = pb[0], pb[1]
                        for kc in range(8):
                            P.mm(p1[:, 0:384], hT[:, kc, ti * 128:(ti + 1) * 128], wB[:, kc, 0:384], start=(kc == 0), stop=(kc == 7))
                        for kc in range(8):
                            P.mm(p2[:, 0:320], hT[:, kc, ti * 128:(ti + 1) * 128], wB[:, kc, 384:704], start=(kc == 0), stop=(kc == 7))
                        P.act(jk[r][:, 0:384], p1[:, 0:384], AF.Square, accum_out=sv[:, 0:1])
                        P.ts("dve", sv[:, 1:2], sv[:, 0:1], 1.0 / 384, EPS, op0=ALU.mult, op1=ALU.add)
                        P.act(sv[:, 2:3], sv[:, 1:2], AF.Sqrt)
                        P.recip(sv[:, 3:4], sv[:, 2:3])
                        P.act(cqn_b[r][:], p1[:, 0:384], AF.Copy, scale=sv[:, 3:4])
                        P.act(jk[r][:, 0:256], p2[:, 0:256], AF.Square, accum_out=sv[:, 4:5])
                        P.ts("dve", sv[:, 5:6], sv[:, 4:5], 1.0 / 256, EPS, op0=ALU.mult, op1=ALU.add)
                        P.act(sv[:, 6:7], sv[:, 5:6], AF.Sqrt)
                        P.recip(sv[:, 7:8], sv[:, 6:7])
                        P.act(kr_f[r][:], p2[:, 256:320], AF.Copy)
                        P.act(ckvn_f[r][:], p2[:, 0:256], AF.Copy, scale=sv[:, 7:8])
                        P.tt("dve", ckvn_f[r][:], ckvn_f[r][:], gkvb[:], ALU.mult)
                        P.copy("dve", ckvn_b[r][:], ckvn_f[r][:])
                        lt = t - sq["t0"]
                        if unit["prompt"]:
                            P.dma(nckv_out[sq["idx"], l, lt * 128:(lt + 1) * 128, :], ckvn_f[r][:])
                            P.dma(nkr_out[sq["idx"], l, lt * 128:(lt + 1) * 128, :], kr_f[r][:, 0:32])
                        P.dma(rKC[r][:], ropeKB_C[t * 128:(t + 1) * 128, :])
                        P.dma(rKS[r][:], ropeKB_S[t * 128:(t + 1) * 128, :])
                        P.tt("dve", kr_t[r][:, 0:32], kr_f[r][:, 0:32], rKC[r][:], ALU.mult)
                        P.tt("dve", kr_t[r][:, 32:64], kr_f[r][:, 32:64], rKS[r][:], ALU.mult)
                        P.tt("dve", kr_b[r][:], kr_t[r][:, 0:32], kr_t[r][:, 32:64], ALU.add)
                        for c in range(3):
                            P.tr(pbb[:, c * 128:(c + 1) * 128], cqn_b[r][:, c * 128:(c + 1) * 128], ident_b[:])
                        for c in range(2):
                            P.tr(pbb[:, (3 + c) * 128:(4 + c) * 128], ckvn_b[r][:, c * 128:(c + 1) * 128], ident_b[:])
                        P.tr(pbb[0:32, 640:768], kr_b[r][:], ident_b[:])
                        ut = t - tiles[0]
                        P.copy("dve", cqnT[:, :, ut * 128:(ut + 1) * 128], pbb[:, 0:384].rearrange("p (c n) -> p c n", c=3))
                        P.copy("dve", ckvT[:, :, ut * 128:(ut + 1) * 128], pbb[:, 384:640].rearrange("p (c n) -> p c n", c=2))
                        P.copy("dve", krT[:, ut * 128:(ut + 1) * 128], pbb[0:32, 640:768])

                P.fence()
                b1.close()
                wuq = sb("wuq", [128, 3, 1024], BF16, ph)
                wuqs = sb("wuqs", [128, 3, 1024], BF16, ph)
                wukA = sb("wukA", [128, 2, 8, 128], BF16, ph)
                wukV = sb("wukV", [128, 2, 512], BF16, ph)
                E32 = sb("E32", [32, 128], BF16, ph)
                gqT = sb("gqT", [128, 3], F32, ph)
                P.dma(wuq[:], b_w_uq_r[l].rearrange("(c p) n -> p c n", p=128), q="pool")
                P.dma(wuqs[:], b_w_uq_rs[l].rearrange("(c p) n -> p c n", p=128), q="pool")
                P.dma(gqT[:], b_gqT[l, :, :])
                for c in range(3):
                    P.act(wuq[:, c, :], wuq[:, c, :], AF.Copy, scale=gqT[:, c:c + 1])
                    P.ts("dve", wuqs[:, c, :], wuqs[:, c, :], gqT[:, c:c + 1], None, op0=ALU.mult)
                P.memset("pool", wukA[:], 0.0)
                P.memset("pool", E32[:], 0.0)
                P.copy("pool", E32[:, 32:64], ident_b[0:32, 0:32])
                with ExitStack() as tw:
                    wukv = sb("wukv", [128, 2, 1024], BF16, tw)
                    P.dma(wukv[:], b_w_ukv[l].rearrange("(c p) n -> p c n", p=128), q="pool")
                    wv = wukv[:].rearrange("p c (h e) -> p c h e", e=128)
                    for c in range(2):
                        P.copy("pool", wukA[:, c, :, 64:128], wv[:, c, :, 0:64])
                        P.copy("pool", wukV[:, c, :].rearrange("p (h e) -> p h e", e=64), wv[:, c, :, 64:128])
                    P.fence()
                KTh = sb("KTh", [128, maxkeys], BF16, ph)
                ksq = sb("ksq", [128, 512], BF16, ph)
                V1 = sb("V1", [128, maxkeys // 128, 2, 65], BF16, ph)
                P.memset("pool", V1[:, :, :, 64:65], 1.0)
                rQC = [sb("rQC%d" % i, [128, 512], F32, ph) for i in range(1)] * 2
                rQS = [sb("rQS%d" % i, [128, 512], F32, ph) for i in range(1)] * 2
                qt1 = [sb("qt1_%d" % i, [128, 512], F32, ph) for i in range(1)] * 2
                qt2 = [sb("qt2_%d" % i, [128, 512], F32, ph) for i in range(1)] * 2
                QT = [sb("QT%d" % i, [128, 512], BF16, ph) for i in range(2)]
                qsq = [sb("qsq%d" % i, [128, 512], BF16, ph) for i in range(1)] * 2
                sqr = [sb("sqr%d" % i, [1, 512], F32, ph) for i in range(1)] * 2
                kmx = sb("kmx", [1, 8], F32, ph)
                kmax = sb("kmax", [1, 1], F32, ph)
                PTb = [sb("PTb%d" % i, [128, 512], BF16, ph) for i in range(2)]
                otok = sb("otok", [128, maxnt, 128], BF16, ph)
                rden = [sb("rden%d" % i, [128, 1], F32, ph) for i in range(4)]
                ctx_b = sb("ctx_b", [128, 2, 288], BF16, ph)

                for sq in unit["seqs"]:
                    nk_own = sq["nt"] * 128
                    nkeys = nk_own + (256 if sq["ctx"] else 0)
                    nkt = nkeys // 128
                    kb = sq["t0"] * 128 - tok_base
                    if sq["ctx"]:
                        for c2 in range(2):
                            P.dma(ctx_b[:, c2, 0:256], cckv[l, c2 * 128:(c2 + 1) * 128, :], q="pool")
                            P.dma(ctx_b[:, c2, 256:288], ckr[l, c2 * 128:(c2 + 1) * 128, :], q="pool")
                        for c2 in range(2):
                            for c in range(2):
                                P.tr(pbb[:, c * 128:(c + 1) * 128], ctx_b[:, c2, c * 128:(c + 1) * 128], ident_b[:])
                            P.tr(pbb[0:32, 256:384], ctx_b[:, c2, 256:288], ident_b[:])
                            k0 = kb + nk_own + c2 * 128
                            P.copy("dve", ckvT[:, :, k0:k0 + 128], pbb[:, 0:256].rearrange("p (c n) -> p c n", c=2))
                            P.copy("dve", krT[:, k0:k0 + 128], pbb[0:32, 256:384])
                    qblocks = [(o, min(512, nk_own - o)) for o in range(0, nk_own, 512)]
                    for h in range(8):
                        if h % 2 == 0:
                            for kt in range(nkt):
                                pv = pb[kt % 2]
                                for c in range(2):
                                    P.mm(pv[:, 0:128], ckvT[:, c, kb + kt * 128:kb + (kt + 1) * 128], wukV[:, c, h * 64:(h + 2) * 64], start=(c == 0), stop=(c == 1))
                                P.copy("act", V1[:, kt, :, 0:64], pv[:, 0:128].rearrange("p (h e) -> p h e", e=64))
                        for k0 in range(0, nkeys, 512):
                            kn = min(512, nkeys - k0)
                            pk = pb[4 + (k0 // 512) % 2]
                            for c in range(2):
                                P.mm(pk[:, 0:kn], wukA[:, c, h, :], ckvT[:, c, kb + k0:kb + k0 + kn], start=(c == 0), stop=False)
                            P.mm(pk[:, 0:kn], E32[:], krT[:, kb + k0:kb + k0 + kn], start=False, stop=True)
                            P.copy("act", KTh[:, k0:k0 + kn], pk[:, 0:kn])
                            P.tt("pool", ksq[:, 0:kn], KTh[:, k0:k0 + kn], KTh[:, k0:k0 + kn], ALU.mult)
                            P.memset("dve", KTh[0:1, k0:k0 + kn], 1.0)
                            pn = pb[6]
                            P.mm(pn[0:1, 0:kn], ones_b[:, 0:1], ksq[:, 0:kn])
                            P.reduce(kmx[:, k0 // 512:k0 // 512 + 1], pn[0:1, 0:kn], ALU.max)
                        nkb = (nkeys + 511) // 512
                        P.reduce(kmax[:], kmx[:, 0:nkb], ALU.max)
                        P.act(kmax[:], kmax[:], AF.Sqrt)
                        P.ts("dve", kmax[:], kmax[:], -1.0, None, op0=ALU.mult)
                        def prepQ(qi, h=h):
                            qo, nq = qblocks[qi]
                            r = (h * len(qblocks) + qi) % 2
                            g0 = sq["t0"] * 128 + qo
                            u0 = g0 - tok_base
                            P.dma(rQC[r][:, 0:nq], ropeQB_C[:, g0:g0 + nq])
                            P.dma(rQS[r][:, 0:nq], ropeQB_S[:, g0:g0 + nq])
                            pq = pb[6]
                            for c in range(3):
                                P.mm(pq[:, 0:nq], wuq[:, c, h * 128:(h + 1) * 128], cqnT[:, c, u0:u0 + nq], start=(c == 0), stop=(c == 2))
                            P.tt("dve", qt1[r][:, 0:nq], pq[:, 0:nq], rQC[r][:, 0:nq], ALU.mult)
                            for c in range(3):
                                P.mm(pq[:, 0:nq], wuqs[:, c, h * 128:(h + 1) * 128], cqnT[:, c, u0:u0 + nq], start=(c == 0), stop=(c == 2))
                            P.tt("dve", qt2[r][:, 0:nq], pq[:, 0:nq], rQS[r][:, 0:nq], ALU.mult)
                            P.tt("pool", QT[r][:, 0:nq], qt1[r][:, 0:nq], qt2[r][:, 0:nq], ALU.add)
                            P.tt("pool", qsq[r][:, 0:nq], QT[r][:, 0:nq], QT[r][:, 0:nq], ALU.mult)
                            P.mm(pq[0:1, 0:nq], ones_b[:, 0:1], qsq[r][:, 0:nq])
                            P.act(sqr[r][:, 0:nq], pq[0:1, 0:nq], AF.Ln, bias=1e-30)
                            P.act(sqr[r][:, 0:nq], sqr[r][:, 0:nq], AF.Exp, scale=0.5)
                            P.ts("dve", QT[r][0:1, 0:nq], sqr[r][:, 0:nq], kmax[0:1, 0:1], None, op0=ALU.mult)

                        prepQ(0)
                        for qi, (qo, nq) in enumerate(qblocks):
                            r = (h * len(qblocks) + qi) % 2
                            if qi + 1 < len(qblocks):
                                prepQ(qi + 1)
                            nqt = nq // 128
                            obanks = [pb[qq][:, 0:65] for qq in range(nqt)]
                            attend(lambda kt: KTh[:, kt * 128:(kt + 1) * 128], lambda kt: V1[:, kt, h % 2, :],
                                   QT[r][:, 0:nq], nq, None, [(kt, None) for kt in range(nkt)],
                                   obanks, [pb[4], pb[5]], PTb, scale)
                            for qq in range(nqt):
                                lt = qo // 128 + qq
                                P.recip(rden[qq][:], pb[qq][:, 64:65])
                                P.ts("dve", otok[:, lt, (h % 2) * 64:(h % 2 + 1) * 64], pb[qq][:, 0:64],
                                     rden[qq][:, 0:1], None, op0=ALU.mult)
                        if h % 2 == 1:
                            for lt0 in range(0, sq["nt"], 8):
                                n8 = min(8, sq["nt"] - lt0)
                                for j in range(n8):
                                    P.tr(pbb[:, j * 128:(j + 1) * 128], otok[:, lt0 + j, :], ident_b[:])
                                uo = kb + lt0 * 128
                                P.copy("act", obT[:, h // 2, uo:uo + n8 * 128], pbb[:, 0:n8 * 128])
                P.fence()

        def mixer_A(l, unit, oaT):
            with ExitStack() as ph:
                tiles = unit["tiles"]
                ntok_u = len(tiles) * 128
                tok_base = tiles[0] * 128
                has_ctx = any(sq["ctx"] for sq in unit["seqs"])
                nkeys_u = ntok_u + (256 if has_ctx else 0)
                scale = 64.0 ** -0.5
                KTa = sb("KTa", [64, 2, nkeys_u], BF16, ph)
                Va = sb("Va", [128, nkeys_u // 128, 2, 65], BF16, ph)
                P.memset("pool", Va[:, :, :, 64:65], 1.0)
                rC = sb("rAC", [64, 256], F32, ph)
                rS = sb("rAS", [64, 256], F32, ph)
                t1 = sb("tA1", [64, 256], F32, ph)
                t2 = sb("tA2", [64, 256], F32, ph)
                hTb = [sb("hTbA0", [128, 8, 256], BF16, ph)]

                with ExitStack() as a1:
                    hTb.append(sb("hTbA1", [128, 8, 256], BF16, a1))
                    wAk = sb("wAk", [128, 8, 384], BF16, a1)
                    P.dma(wAk[:, :, 0:256], w_in[l, :, OFF_AK:OFF_AK + 256].rearrange("(c p) n -> p c n", p=128), q="pool")
                    P.dma(wAk[:, :, 256:384], w_in_sw[l, :, 512:640].rearrange("(c p) n -> p c n", p=128), q="pool")
                    kvf = [sb("kvf%d" % i, [128, 256], F32, a1) for i in range(2)]
                    for bi, (sq, tok0, nb) in enumerate(blocks_of(unit, 256)):
                        hT = hTb[bi % 2]
                        btiles = [tok0 // 128 + i for i in range(nb // 128)]
                        build_hT(hT, btiles, 0)
                        u0 = tok0 - tok_base
                        P.dma(rC[:, 0:nb], ropeA_C[:, tok0:tok0 + nb])
                        P.dma(rS[:, 0:nb], ropeA_S[:, tok0:tok0 + nb])
                        for g in range(2):
                            p1, p2 = pb[(2 * g) % 4], pb[(2 * g + 1) % 4]
                            for kc in range(8):
                                P.mm(p1[0:64, 0:nb], wAk[:, kc, g * 64:(g + 1) * 64], hT[:, kc, 0:nb], start=(kc == 0), stop=(kc == 7))
                            P.tt("dve", t1[:, 0:nb], p1[0:64, 0:nb], rC[:, 0:nb], ALU.mult)
                            for kc in range(8):
                                P.mm(p2[0:64, 0:nb], wAk[:, kc, 256 + g * 64:256 + (g + 1) * 64], hT[:, kc, 0:nb], start=(kc == 0), stop=(kc == 7))
                            P.tt("dve", t2[:, 0:nb], p2[0:64, 0:nb], rS[:, 0:nb], ALU.mult)
                            P.tt("pool", KTa[:, g, u0:u0 + nb], t1[:, 0:nb], t2[:, 0:nb], ALU.add)
                        for ti, t in enumerate(btiles):
                            pv = pb[4 + ti % 2]
                            for kc in range(8):
                                P.mm(pv[:, 0:256], hT[:, kc, ti * 128:(ti + 1) * 128], wAk[:, kc, 0:256], start=(kc == 0), stop=(kc == 7))
                            ut = t - tiles[0]
                            P.copy("act", Va[:, ut, :, 0:64], pv[:, 128:256].rearrange("p (g e) -> p g e", e=64))
                            if unit["prompt"]:
                                lt = t - sq["t0"]
                                kv = kvf[ti % 2]
                                P.copy("act", kv[:], pv[:, 0:256])
                                P.dma(nak_out[sq["idx"], l, lt * 128:(lt + 1) * 128, :], kv[:, 0:128])
                                P.dma(nav_out[sq["idx"], l, lt * 128:(lt + 1) * 128, :], kv[:, 128:256])
                    if has_ctx:
                        ctxkb = sb("ctxkb", [128, 2, 128], BF16, a1)
                        ctxvb = sb("ctxvb", [128, 2, 128], BF16, a1)
                        for c2 in range(2):
                            P.dma(ctxkb[:, c2, :], cak[l, c2 * 128:(c2 + 1) * 128, :], q="pool")
                            P.dma(ctxvb[:, c2, :], cav[l, c2 * 128:(c2 + 1) * 128, :], q="pool")
                        for c2 in range(2):
                            for g in range(2):
                                P.tr(pbb[0:64, (c2 * 2 + g) * 128:(c2 * 2 + g + 1) * 128], ctxkb[:, c2, g * 64:(g + 1) * 64], ident_b[:])
                            P.copy("dve", Va[:, ntok_u // 128 + c2, :, 0:64], ctxvb[:, c2, :].rearrange("p (g e) -> p g e", e=64))
                        for c2 in range(2):
                            for g in range(2):
                                P.copy("act", KTa[:, g, ntok_u + c2 * 128:ntok_u + (c2 + 1) * 128],
                                       pbb[0:64, (c2 * 2 + g) * 128:(c2 * 2 + g + 1) * 128])
                    P.fence()

                wAq = sb("wAq", [128, 8, 512], BF16, ph)
                wAqs = sb("wAqs", [128, 8, 512], BF16, ph)
                P.dma(wAq[:], w_in[l, :, 0:512].rearrange("(c p) n -> p c n", p=128), q="pool")
                P.dma(wAqs[:], w_in_sw[l, :, 0:512].rearrange("(c p) n -> p c n", p=128), q="pool")
                maskb = sb("maskb", [128, 8, 128], BF16, ph)
                P.dma(maskb[:], cst_masks[:, :, :], q="pool")
                sinkf = sb("sinkf", [1, 8], F32, ph)
                vsink = sb("vsink", [1, 65], BF16, ph)
                P.dma(sinkf[:], a_sink[l:l + 1, :])
                P.ts("dve", sinkf[:], sinkf[:], 1.0 / scale, None, op0=ALU.mult)
                P.memset("dve", vsink[:], 0.0)
                P.memset("dve", vsink[:, 64:65], 1.0)
                QTb = [sb("QTb%d" % i, [64, 8, 256], BF16, ph) for i in range(2)]
                ksqa = sb("ksqa", [64, 512], BF16, ph)
                qsqa = sb("qsqa", [64, 4, 128], BF16, ph)
                kmx = sb("kmxA", [1, 2, 8], F32, ph)
                kmax = sb("kmaxA", [1, 2], F32, ph)
                sqr = sb("sqrA", [1, 512], F32, ph)
                negf = sqr
                negm = [sb("negmA%d" % i, [1, 512], BF16, ph) for i in range(2)]
                srow = sb("srowA", [1, 512], F32, ph)
                psink = [sb("psinkA%d" % i, [1, 512], BF16, ph) for i in range(2)]
                PTb = [sb("PTbA%d" % i, [128, 512], BF16, ph) for i in range(2)]
                otok = [sb("otokA%d" % i, [128, 512], BF16, ph) for i in range(2)]
                rden = [sb("rdenA%d" % i, [128, 1], F32, ph) for i in range(4)]
                nq_ = 0
                nblk = 0
                for sq in unit["seqs"]:
                    kb = sq["t0"] * 128 - tok_base
                    nk_own = sq["nt"] * 128
                    kranges = [(kb, nk_own)] + ([(ntok_u, 256)] if sq["ctx"] else [])
                    for g in range(2):
                        nb_ = 0
                        for (k0, kn_all) in kranges:
                            for o in range(0, kn_all, 512):
                                kn = min(512, kn_all - o)
                                P.tt("pool", ksqa[:, 0:kn], KTa[:, g, k0 + o:k0 + o + kn], KTa[:, g, k0 + o:k0 + o + kn], ALU.mult)
                                P.mm(pb[6][0:1, 0:kn], ones_b[0:64, 0:1], ksqa[:, 0:kn])
                                P.reduce(kmx[:, g, nb_:nb_ + 1], pb[6][0:1, 0:kn], ALU.max)
                                nb_ += 1
                        P.reduce(kmax[:, g:g + 1], kmx[:, g, 0:nb_], ALU.max)
                    P.act(kmax[:], kmax[:], AF.Sqrt)
                    P.ts("dve", kmax[:], kmax[:], -1.0, None, op0=ALU.mult)
                    blocks = [(q0, min(2, sq["nt"] - q0)) for q0 in range(0, sq["nt"], 2)]

                    def qproj(b):
                        q0, nbt = blocks[b]
                        nb = nbt * 128
                        tok0 = (sq["t0"] + q0) * 128
                        hT = hTb[0]
                        QT = QTb[b % 2]
                        build_hT(hT, [sq["t0"] + q0 + i for i in range(nbt)], 0)
                        P.dma(rC[:, 0:nb], ropeA_C[:, tok0:tok0 + nb])
                        P.dma(rS[:, 0:nb], ropeA_S[:, tok0:tok0 + nb])
                        for h in range(8):
                            p1 = pb[6]
                            for kc in range(8):
                                P.mm(p1[0:64, 0:nb], wAq[:, kc, h * 64:(h + 1) * 64], hT[:, kc, 0:nb], start=(kc == 0), stop=(kc == 7))
                            for kc in range(8):
                                P.mm(p1[0:64, 256:256 + nb], wAqs[:, kc, h * 64:(h + 1) * 64], hT[:, kc, 0:nb], start=(kc == 0), stop=(kc == 7))
                            P.tt("dve", t1[:, 0:nb], p1[0:64, 0:nb], rC[:, 0:nb], ALU.mult)
                            P.tt("dve", t2[:, 0:nb], p1[0:64, 256:256 + nb], rS[:, 0:nb], ALU.mult)
                            P.tt("pool", QT[:, h, 0:nb], t1[:, 0:nb], t2[:, 0:nb], ALU.add)

                    items = [(b, qi, g) for b in range(len(blocks)) for qi in range(blocks[b][1]) for g in range(2)]

                    def prep_item(n):
                        b, qi, g = items[n]
                        r = n % 2
                        QTg = QTb[b % 2][:, 4 * g:4 * g + 4, qi * 128:(qi + 1) * 128]
                        P.tt("pool", qsqa[:], QTg, QTg, ALU.mult)
                        P.mm(pb[6][0:1, 0:512], ones_b[0:64, 0:1], qsqa[:])
                        P.act(sqr[:], pb[6][0:1, 0:512], AF.Ln, bias=1e-30)
                        P.act(sqr[:], sqr[:], AF.Exp, scale=0.5)
                        P.ts("dve", negf[:], sqr[:], kmax[0:1, g:g + 1], None, op0=ALU.mult)
                        P.copy("dve", negm[r][:], negf[:])
                        for hh in range(4):
                            P.ts("dve", srow[:, hh * 128:(hh + 1) * 128], negf[:, hh * 128:(hh + 1) * 128],
                                 sinkf[0:1, 4 * g + hh:4 * g + hh + 1], None, op0=ALU.add)
                        P.act(psink[r][:], srow[:], AF.Exp, scale=scale)

                    qproj(0)
                    prep_item(0)
                    for n, (b, qi, g) in enumerate(items):
                        r = n % 2
                        q0 = blocks[b][0]
                        qt = q0 + qi
                        uq = kb + qt * 128
                        ot = otok[qt % 2]
                        if qi == 0 and g == 0 and b + 1 < len(blocks):
                            qproj(b + 1)
                        if n + 1 < len(items):
                            prep_item(n + 1)
                        QTg = QTb[b % 2][:, 4 * g:4 * g + 4, qi * 128:(qi + 1) * 128]
                        if sq["ctx"]:
                            kts = []
                            if qt > 0:
                                kts.append((kb // 128 + qt - 1, maskb[:, 0:4, :]))
                            kts.append((kb // 128 + qt, None))
                            if qt + 1 < sq["nt"]:
                                kts.append((kb // 128 + qt + 1, maskb[:, 4:8, :]))
                            kts += [(ntok_u // 128, None), (ntok_u // 128 + 1, None)]
                        else:
                            kts = [(kb // 128 + j, None) for j in range(sq["nt"])]
                        obanks = [pb[hh][:, 0:65] for hh in range(4)]

                        def tail(r=r):
                            for hh in range(4):
                                P.mm(pb[hh][:, 0:65], psink[r][0:1, hh * 128:(hh + 1) * 128], vsink[:], start=False, stop=True)

                        attend(lambda kt: KTa[:, g, kt * 128:(kt + 1) * 128], lambda kt: Va[:, kt, g, :],
                               QTg, 512, negm[r][:], kts, obanks, [pb[4], pb[5]], PTb, scale, tail=tail)
                        for hh in range(4):
                            h = 4 * g + hh
                            P.recip(rden[hh][:], pb[hh][:, 64:65])
                            P.ts("dve", ot[:, h * 64:(h + 1) * 64], pb[hh][:, 0:64], rden[hh][:, 0:1], None, op0=ALU.mult)
                        if g == 1:
                            for c in range(4):
                                P.tr(pbb[:, c * 128:(c + 1) * 128], ot[:, c * 128:(c + 1) * 128], ident_b[:])
                            P.copy("act", oaT[:, :, uq:uq + 128], pbb[:, 0:512].rearrange("p (c n) -> p c n", c=4))
                P.fence()

        def mixer_C(l, unit, ocT):
            with ExitStack() as ph:
                tiles = unit["tiles"]
                ntok_u = len(tiles) * 128
                nt_u = len(tiles)
                tok_base = tiles[0] * 128
                trc = sb("trc", [128, 2, 2, 128], BF16, ph)
                trif = sb("trif", [128, 2, 128], F32, ph)
                lvN = sb("lvN", [128, 7, 2, 128], BF16, ph)
                lvA = sb("lvA", [128, 7, 2, 128], BF16, ph)
                convw = sb("convw", [128, 12, 3], F32, ph)
                alog = sb("alog", [128, 8], F32, ph)
                dtb = sb("dtb", [128, 8], F32, ph)
                ong = sb("ong", [128, 128], F32, ph)
                wab = sb("wab", [128, 8, 16], BF16, ph)
                gb = sb("gb", [128, nt_u, 16], F32, ph)
                P.dma(trc[:], cst_trc[:, :, :, :], q="pool")
                P.dma(lvN[:], cst_lvN[:, :, :, :], q="pool")
                P.dma(lvA[:], cst_lvA[:, :, :, :], q="pool")
                P.dma(trif[:, 0, :], cst_tri[:, 1, :])
                P.dma(trif[:, 1, :], cst_tri[:, 3, :])
                P.dma(convw[:], c_convT[l, :, :, :])
                P.dma(alog[:], c_a_log[l:l + 1, :].broadcast_to([128, 8]))
                P.dma(dtb[:], c_dt_bias[l:l + 1, :].broadcast_to([128, 8]))
                P.dma(ong[:], c_onorm_g[l:l + 1, :].broadcast_to([128, 128]))
                P.dma(wab[:], w_in[l, :, OFF_CA:OFF_CA + 16].rearrange("(c p) n -> p c n", p=128), q="pool")
                P.act(alog[:], alog[:], AF.Exp)
                P.ts("dve", alog[:], alog[:], -1.0, None, op0=ALU.mult)
                sm = [sb("smC%d" % i, [128, 8], F32, ph) for i in range(6)]

                for h in range(4):
                    with ExitStack() as hs:
                        qkvT = sb("qkvT", [128, 3, ntok_u], BF16, hs)
                        zs = sb("zs", [128, nt_u, 128], F32, hs)
                        oacc = sb("oacc", [128, nt_u, 128], F32, hs)
                        P.memset("pool", oacc[:], 0.0)
                        with ExitStack() as ps_:
                            wC = sb("wC", [128, 8, 512], BF16, ps_)
                            for j in range(3):
                                c0 = OFF_CQKV + j * 512 + h * 128
                                P.dma(wC[:, :, j * 128:(j + 1) * 128], w_in[l, :, c0:c0 + 128].rearrange("(c p) n -> p c n", p=128), q="pool")
                            P.dma(wC[:, :, 384:512], w_in[l, :, OFF_CZ + h * 128:OFF_CZ + (h + 1) * 128].rearrange("(c p) n -> p c n", p=128), q="pool")
                            hTb = [sb("hTbC%d" % i, [128, 8, 512], BF16, ps_) for i in range(2)]
                            rawT = sb("rawT", [128, 3, ntok_u], F32, ps_)
                            cv = [sb("cvC%d" % i, [128, 512], F32, ps_) for i in range(2)]
                            zf = [sb("zfC%d" % i, [128, 128], F32, ps_) for i in range(2)]
                            for bi, (sq, tok0, nb) in enumerate(blocks_of(unit)):
                                hT = hTb[bi % 2]
                                btiles = [tok0 // 128 + i for i in range(nb // 128)]
                                build_hT(hT, btiles, 0)
                                u0 = tok0 - tok_base
                                for j in range(3):
                                    pj = pb[j % 2]
                                    for kc in range(8):
                                        P.mm(pj[:, 0:nb], wC[:, kc, j * 128:(j + 1) * 128], hT[:, kc, 0:nb], start=(kc == 0), stop=(kc == 7))
                                    P.copy("act", rawT[:, j, u0:u0 + nb], pj[:, 0:nb])
                                for ti, t in enumerate(btiles):
                                    ut = t - tiles[0]
                                    pz = pb[2 + ti % 2]
                                    for kc in range(8):
                                        P.mm(pz[:, 0:128], hT[:, kc, ti * 128:(ti + 1) * 128], wC[:, kc, 384:512], start=(kc == 0), stop=(kc == 7))
                                    P.act(zf[ti % 2][:], pz[:, 0:128], AF.Silu)
                                    P.tt("pool", zs[:, ut, :], zf[ti % 2][:], ong[:], ALU.mult)
                                    if h == 0:
                                        pa_ = pb[4]
                                        for kc in range(8):
                                            P.mm(pa_[:, 0:16], hT[:, kc, ti * 128:(ti + 1) * 128], wab[:, kc, :], start=(kc == 0), stop=(kc == 7))
                                        P.tt("dve", sm[0][:], pa_[:, 0:8], dtb[:], ALU.add)
                                        P.act(sm[1][:], sm[0][:], AF.Abs)
                                        P.act(sm[2][:], sm[1][:], AF.Exp, scale=-1.0)
                                        P.act(sm[3][:], sm[2][:], AF.Ln, bias=1.0)
                                        P.ts("dve", sm[4][:], sm[0][:], 0.0, None, op0=ALU.max)
                                        P.tt("dve", sm[5][:], sm[4][:], sm[3][:], ALU.add)
                                        P.tt("dve", gb[:, ut, 0:8], sm[5][:], alog[:], ALU.mult)
                                        P.act(gb[:, ut, 8:16], pa_[:, 8:16], AF.Sigmoid)
                            for sq in unit["seqs"]:
                                s0 = sq["t0"] * 128 - tok_base
                                n = sq["nt"] * 128
                                for j in range(3):
                                    ci = j * 4 + h
                                    for o in range(0, n, 512):
                                        nb = min(512, n - o)
                                        a0 = s0 + o
                                        c_ = cv[(o // 512) % 2]
                                        P.act(c_[:, 0:nb], rawT[:, j, a0:a0 + nb], AF.Copy, scale=convw[:, ci, 1:2])
                                        lo = 1 if o == 0 else 0
                                        P.stt(c_[:, lo:nb], rawT[:, j, a0 + lo - 1:a0 + nb - 1], convw[:, ci, 0:1], c_[:, lo:nb], ALU.mult, ALU.add)
                                        hi = nb - 1 if o + nb == n else nb
                                        P.stt(c_[:, 0:hi], rawT[:, j, a0 + 1:a0 + hi + 1], convw[:, ci, 2:3], c_[:, 0:hi], ALU.mult, ALU.add)
                                        P.act(qkvT[:, j, a0:a0 + nb], c_[:, 0:nb], AF.Silu)
                            P.fence()

                        gc = sb("gc", [128, 2, nt_u], F32, hs)
                        egc = sb("egc", [128, 2, nt_u], F32, hs)
                        gl = sb("gl", [128, 2, nt_u], F32, hs)
                        edl = sb("edl", [128, 2, nt_u], F32, hs)
                        ge = sb("ge", [128, 2, nt_u], F32, hs)
                        gsel = sb("gsel", [128, 2, nt_u], F32, hs)
                        for d in range(2):
                            P.copy("dve", gsel[:, d, :], gb[:, :, d * 4 + h])
                        for d in range(2):
                            P.mm(pb[6][:, d * nt_u:(d + 1) * nt_u], trif[:, d, :], gsel[:, d, :])
                        P.copy("dve", gc[:], pb[6][:, 0:2 * nt_u].rearrange("p (d n) -> p d n", d=2))
                        P.mm(pb[5][:, 0:2 * nt_u], ones_f[:], gsel[:].rearrange("p d n -> p (d n)"))
                        P.copy("dve", gl[:], pb[5][:, 0:2 * nt_u].rearrange("p (d n) -> p d n", d=2))
                        P.act(egc[:], gc[:], AF.Exp)
                        P.act(ge[:], gl[:], AF.Exp)
                        P.tt("dve", edl[:], gl[:], gc[:], ALU.subtract)
                        P.act(edl[:], edl[:], AF.Exp)

                        Sst = sb("Sst", [128, 2, 128], F32, hs)
                        Sbf = sb("Sbf", [128, 2, 128], BF16, hs)
                        ident2 = ident_f[:].unsqueeze(1).broadcast_to([128, 2, 128])

                        def slotbufs(s_):
                            W = Ctx()
                            f32 = lambda n, shp=(128, 2, 128): sb("%s_s%d" % (n, s_), list(shp), F32, hs)
                            b16 = lambda n, shp=(128, 2, 128): sb("%s_s%d" % (n, s_), list(shp), BF16, hs)
                            W.ssq = f32("ssq", (128, 2, 2)); W.rkq = f32("rkq", (128, 2, 2))
                            W.qkv = b16("qkvtok", (128, 2, 3, 128)); W.kn = b16("kn"); W.qn = b16("qn")
                            W.vke = b16("vke", (128, 2, 256)); W.kqT = b16("kqT", (128, 2, 2, 128))
                            W.diag = f32("diag"); W.DT = f32("DT"); W.tmp1 = f32("tmp1"); W.AT = f32("AT"); W.tmp2 = f32("tmp2")
                            W.qkmT = b16("qkmT"); W.A = f32("A"); W.R = f32("R"); W.RT = f32("RT")
                            W.BkT = [f32("BkT0"), f32("BkT1")]; W.Y = f32("Y"); W.Rb = b16("Rb"); W.u = f32("u")
                            W.w = b16("w"); W.wT = b16("wT"); W.kd = b16("kd"); W.qd = b16("qd"); W.qdT = b16("qdT")
                            W.vnew = b16("vnew"); W.t3 = f32("t3"); W.t4 = f32("t4")
                            return W
                        SB_ = [slotbufs(0), slotbufs(1)]

                        def run_slot(sq, s_):
                            W = SB_[s_]
                            kb = sq["t0"] * 128 - tok_base
                            n = sq["nt"]
                            bA, bB, bC = pb[3 * s_], pb[3 * s_ + 1], pb[3 * s_ + 2]
                            tb0 = s_ * 4 * 128
                            tb = lambda j, cnt=1: pbb[:, tb0 + j * 128:tb0 + (j + cnt) * 128]
                            v3 = lambda ap: ap.rearrange("p (d n) -> p d n", d=2)
                            if s_ == 1:
                                for _ in range(3):
                                    yield
                            for st in range(s_, n, 2):
                                cs = [st, n - 1 - st]
                                uts = [kb // 128 + c for c in cs]
                                gcc = [gc[:, d, uts[d]:uts[d] + 1] for d in range(2)]
                                beta = [gb[:, uts[d], 8 + d * 4 + h:8 + d * 4 + h + 1] for d in range(2)]
                                for d in range(2):
                                    sl = slice(kb + cs[d] * 128, kb + (cs[d] + 1) * 128)
                                    for j in range(3):
                                        P.tr(tb(j), qkvT[:, j, sl], ident_b[:])
                                    P.copy("act", W.qkv[:, d, :, :], tb(0, 3).rearrange("p (j n) -> p j n", j=3))
                                yield
                                for d in range(2):
                                    P.act(W.t3[:, d, :], W.qkv[:, d, 0, :], AF.Square, accum_out=W.ssq[:, d, 0:1])
                                    P.act(W.t4[:, d, :], W.qkv[:, d, 1, :], AF.Square, accum_out=W.ssq[:, d, 1:2])
                                yield
                                P.act(W.rkq[:], W.ssq[:], AF.Ln, bias=EPS)
                                P.act(W.rkq[:], W.rkq[:], AF.Exp, scale=-0.5)
                                yield
                                for d in range(2):
                                    P.act(W.kn[:, d, :], W.qkv[:, d, 1, :], AF.Copy, scale=W.rkq[:, d, 1:2])
                                    P.ts("dve", W.qn[:, d, :], W.qkv[:, d, 0, :], W.rkq[:, d, 0:1], 128.0 ** -0.5, op0=ALU.mult, op1=ALU.mult)
                                    P.copy("pool", W.vke[:, d, 0:128], W.qkv[:, d, 2, :])
                                yield
                                for d in range(2):
                                    P.tr(tb(0), W.kn[:, d, :], ident_b[:])
                                    P.tr(tb(1), W.qn[:, d, :], ident_b[:])
                                    P.copy("act", W.kqT[:, d, :, :], tb(0, 2).rearrange("p (c n) -> p c n", c=2))
                                    P.act(W.vke[:, d, 128:256], W.kn[:, d, :], AF.Copy, scale=egc[:, d, uts[d]:uts[d] + 1])
                                    P.ts("dve", W.kd[:, d, :], W.kn[:, d, :], edl[:, d, uts[d]:uts[d] + 1], None, op0=ALU.mult)
                                    P.act(W.qd[:, d, :], W.qn[:, d, :], AF.Copy, scale=egc[:, d, uts[d]:uts[d] + 1])
                                    P.ts("dve", W.diag[:, d, :], ident_f[:], gcc[d], None, op0=ALU.mult)
                                yield
                                for d in range(2):
                                    P.mm(bA[:, d * 128:(d + 1) * 128], W.kqT[:, d, 0, :], W.kqT[:, d, 0, :])
                                    P.mm(bA[:, 256 + d * 128:256 + (d + 1) * 128], W.kqT[:, d, 0, :], W.kqT[:, d, 1, :])
                                    P.mm(bB[:, d * 128:(d + 1) * 128], ones_f[:], W.diag[:, d, :])
                                    P.tr(tb(2 + d), W.qd[:, d, :], ident_b[:])
                                for d in range(2):
                                    P.ts("dve", W.DT[:, d, :], bB[:, d * 128:(d + 1) * 128], gcc[d], 0.0, op0=ALU.subtract, op1=ALU.min)
                                P.act(W.DT[:], W.DT[:], AF.Exp)
                                P.copy("act", W.qdT[:], v3(tb(2, 2)))
                                yield
                                P.tt("dve", W.tmp1[:], v3(bA[:, 0:256]), W.DT[:], ALU.mult)
                                P.tt("dve", W.tmp2[:], v3(bA[:, 256:512]), W.DT[:], ALU.mult)
                                for d in range(2):
                                    P.stt(W.AT[:, d, :], W.tmp1[:, d, :], beta[d], trc[:, 0, d, :], ALU.mult, ALU.mult)
                                P.tt("pool", W.qkmT[:], W.tmp2[:], trc[:, 1, :, :], ALU.mult)
                                yield
                                for d in range(2):
                                    P.tr(bB[:, 256 + d * 128:256 + (d + 1) * 128], W.AT[:, d, :], ident_f[:])
                                P.copy("act", W.A[:], v3(bB[:, 256:512]))
                                P.tt("dve", W.t3[:], W.AT[:], lvN[:, 0, :, :], ALU.mult)
                                P.tt("dve", W.R[:], ident2, W.t3[:], ALU.subtract)
                                yield
                                P.tt("pool", W.t4[:], W.A[:], lvA[:, 0, :, :], ALU.mult)
                                P.tt("pool", W.RT[:], ident2, W.t4[:], ALU.subtract)
                                P.tt("pool", W.BkT[1][:], W.A[:], lvA[:, 1, :, :], ALU.mult)
                                yield
                                for k in range(1, 7):
                                    Bk = W.BkT[k % 2]
                                    for d in range(2):
                                        P.mm(bC[:, d * 128:(d + 1) * 128], Bk[:, d, :], W.R[:, d, :])
                                    if k < 6:
                                        P.tt("pool", W.BkT[(k + 1) % 2][:], W.A[:], lvA[:, k + 1, :, :], ALU.mult)
                                    P.copy("act", W.Y[:], v3(bC[:, 0:256]))
                                    yield
                                    for d in range(2):
                                        P.mm(bB[:, d * 128:(d + 1) * 128], W.RT[:, d, :], W.Y[:, d, :])
                                        P.mm(bB[:, 256 + d * 128:256 + (d + 1) * 128], W.Y[:, d, :], W.RT[:, d, :])
                                    yield
                                    P.tt("dve", W.R[:], W.R[:], v3(bB[:, 0:256]), ALU.subtract)
                                    P.tt("dve", W.RT[:], W.RT[:], v3(bB[:, 256:512]), ALU.subtract)
                                    yield
                                P.copy("pool", W.Rb[:], W.R[:])
                                yield
                                for d in range(2):
                                    P.mm(bA[:, d * 256:(d + 1) * 256], W.Rb[:, d, :], W.vke[:, d, :])
                                for d in range(2):
                                    P.act(W.u[:, d, :], bA[:, d * 256:d * 256 + 128], AF.Copy, scale=beta[d])
                                    P.act(W.w[:, d, :], bA[:, d * 256 + 128:(d + 1) * 256], AF.Copy, scale=beta[d])
                                yield
                                for d in range(2):
                                    P.tr(tb(d), W.w[:, d, :], ident_b[:])
                                P.copy("act", W.wT[:], v3(tb(0, 2)))
                                yield
                                for d in range(2):
                                    P.mm(bC[:, d * 128:(d + 1) * 128], W.wT[:, d, :], Sbf[:, d, :])
                                P.tt("dve", W.vnew[:], W.u[:], v3(bC[:, 0:256]), ALU.subtract)
                                yield
                                for d in range(2):
                                    P.mm(bC[:, 256 + d * 128:256 + (d + 1) * 128], W.qdT[:, d, :], Sbf[:, d, :], start=True, stop=False)
                                    P.mm(bC[:, 256 + d * 128:256 + (d + 1) * 128], W.qkmT[:, d, :], W.vnew[:, d, :], start=False, stop=True)
                                for d in range(2):
                                    P.tt("dve", oacc[:, uts[d], :], oacc[:, uts[d], :], bC[:, 256 + d * 128:256 + (d + 1) * 128], ALU.add)
                                yield
                                for d in range(2):
                                    P.mm(bA[:, d * 128:(d + 1) * 128], W.kd[:, d, :], W.vnew[:, d, :])
                                for d in range(2):
                                    P.stt(Sst[:, d, :], Sst[:, d, :], ge[:, d, uts[d]:uts[d] + 1], bA[:, d * 128:(d + 1) * 128], ALU.mult, ALU.add)
                                P.copy("pool", Sbf[:], Sst[:])
                                yield

                        for sq in unit["seqs"]:
                            for d in range(2):
                                if sq["ctx"]:
                                    P.dma(Sst[:, d, :], cstate[l, d, h, :, :])
                                else:
                                    P.memset("pool", Sst[:, d, :], 0.0)
                            P.copy("act", Sbf[:], Sst[:])
                            alive = [run_slot(sq, 0), run_slot(sq, 1)]
                            while alive:
                                for g_ in list(alive):
                                    try:
                                        next(g_)
                                    except StopIteration:
                                        alive.remove(g_)
                            if unit["prompt"]:
                                for d in range(2):
                                    P.dma(ncs_out[sq["idx"], l, d, h, :, :], Sst[:, d, :])
                        on_ss = sb("on_ss", [128, nt_u], F32, hs)
                        on_r = sb("on_r", [128, nt_u], F32, hs)
                        ojk = [sb("ojk%d" % i, [128, 128], F32, hs) for i in range(2)]
                        ob16 = sb("ob16", [128, nt_u, 128], BF16, hs)
                        for ut in range(nt_u):
                            P.act(ojk[ut % 2][:], oacc[:, ut, :], AF.Square, accum_out=on_ss[:, ut:ut + 1])
                        P.ts("dve", on_r[:], on_ss[:], 1.0 / 128, EPS, op0=ALU.mult, op1=ALU.add)
                        P.act(on_r[:], on_r[:], AF.Sqrt)
                        P.recip(on_r[:], on_r[:])
                        for ut in range(nt_u):
                            P.stt(ob16[:, ut, :], oacc[:, ut, :], on_r[:, ut:ut + 1], zs[:, ut, :], ALU.mult, ALU.mult)
                        for u0 in range(0, nt_u, 8):
                            n8 = min(8, nt_u - u0)
                            for j in range(n8):
                                P.tr(pbb[:, j * 128:(j + 1) * 128], ob16[:, u0 + j, :], ident_b[:])
                            P.copy("act", ocT[:, h, u0 * 128:(u0 + n8) * 128], pbb[:, 0:n8 * 128])
                        P.fence()
                P.fence()

        def merge(l, unit, oT):
            tiles = unit["tiles"]
            tok_base = tiles[0] * 128
            cd = unit["cond"]
            for sub0 in range(0, len(tiles), 8):
                stiles = tiles[sub0:sub0 + 8]
                with ExitStack() as ph:
                    nsub = len(stiles) * 128
                    mT = sb("mT", [128, 8, nsub], BF16, ph)
                    with ExitStack() as p2:
                        hTm = sb("hTm", [128, 8, 256], BF16, p2)
                        gt = [sb("gtM%d" % i, [128, 256], F32, p2) for i in range(2)]
                        tm = [sb("tmM%d" % i, [128, 256], F32, p2) for i in range(2)]
                        wg = sb("wg", [128, 8, D], BF16, p2)
                        wbr = sb("wbr", [128, 4, D], BF16, p2)
                        for j in range(3):
                            P.dma(wg[:], w_in[l, :, OFF_GATES + j * D:OFF_GATES + (j + 1) * D].rearrange("(c p) n -> p c n", p=128), q="pool")
                            P.dma(wbr[:], w_branch[l, j].rearrange("(c p) n -> p c n", p=128), q="pool")
                            for b0 in range(0, len(stiles), 2):
                                bt = stiles[b0:b0 + 2]
                                nb = len(bt) * 128
                                build_hT(hTm, bt, 0)
                                u0 = bt[0] * 128 - tok_base
                                m0 = b0 * 128
                                for fc in range(8):
                                    pg, py = pb[fc % 2], pb[2 + fc % 2]
                                    for kc in range(8):
                                        P.mm(pg[:, 0:nb], wg[:, kc, fc * 128:(fc + 1) * 128], hTm[:, kc, 0:nb], start=(kc == 0), stop=(kc == 7))
                                    P.act(gt[fc % 2][:, 0:nb], pg[:, 0:nb], AF.Sigmoid)
                                    for c4 in range(4):
                                        P.mm(py[:, 0:nb], wbr[:, c4, fc * 128:(fc + 1) * 128], oT[j][:, c4, u0:u0 + nb], start=(c4 == 0), stop=(c4 == 3))
                                    if j == 0:
                                        P.tt("dve", mT[:, fc, m0:m0 + nb], py[:, 0:nb], gt[fc % 2][:, 0:nb], ALU.mult)
                                    else:
                                        P.tt("dve", tm[fc % 2][:, 0:nb], py[:, 0:nb], gt[fc % 2][:, 0:nb], ALU.mult)
                                        P.tt("pool", mT[:, fc, m0:m0 + nb], mT[:, fc, m0:m0 + nb], tm[fc % 2][:, 0:nb], ALU.add)
                        P.fence()
                    with ExitStack() as p3:
                        wo = sb("wo", [128, 8, D], BF16, p3)
                        G1 = sb("G1", [128, D], F32, p3)
                        tmo = [sb("tmo%d" % i, [128, 512], F32, p3) for i in range(2)]
                        P.dma(wo[:], w_out[l].rearrange("(c p) n -> p c n", p=128), q="pool")
                        make_G(G1[:], 0, cd, p3)
                        n_ = 0
                        for ti, t in enumerate(stiles):
                            for half in range(2):
                                py = pb[2 + n_ % 4]
                                for kc in range(8):
                                    P.mm(py[:], mT[:, kc, ti * 128:(ti + 1) * 128], wo[:, kc, half * 512:(half + 1) * 512], start=(kc == 0), stop=(kc == 7))
                                tmp = tmo[n_ % 2]
                                n_ += 1
                                P.tt("dve", tmp[:], py[:], G1[:, half * 512:(half + 1) * 512], ALU.mult)
                                P.tt("pool", x[:, t, half * 512:(half + 1) * 512], x[:, t, half * 512:(half + 1) * 512], tmp[:], ALU.add)
                        P.fence()

        K.P, K.nc, K.x, K.pb, K.sb, K.psb = P, nc, x, pb, sb, psb
        def dump_bf16(name, src, n):
            with ExitStack() as ph:
                stg = sb("stg", [128, src.shape[1], n], F32, ph)
                P.copy("dve", stg[:], src)
                P.dma(dbg_out[name][:, :, :], stg[:])
                P.fence()

        for l in range(depth):
            if "adaln" in parts:
                adaln(l)
            calc_rstd(list(range(NT)))
            for unit in units:
                with ExitStack() as us:
                    ntu = len(unit["tiles"]) * 128
                    oT = {}
                    for m in ("C", "B", "A"):
                        oT[m] = sb("o%sT" % m.lower(), [128, 4, ntu], BF16, us)
                        if m in mixers:
                            K.phase = "mix%s_%s" % (m, unit["name"])
                            {"A": mixer_A, "B": mixer_B, "C": mixer_C}[m](l, unit, oT[m])
                            K.phase = "other"
                        else:
                            P.memset("pool", oT[m][:], 0.0)
                        nm = "o%sT_%s%d" % (m.lower(), unit["name"], l)
                        if nm in dbg_out:
                            dump_bf16(nm, oT[m][:], ntu)
                    if "merge" in parts:
                        K.phase = "merge_%s" % unit["name"]
                        merge(l, unit, [oT["A"], oT["B"], oT["C"]])
                        K.phase = "other"
                    P.fence()
            if "ffn" in parts:
                ffn(l)
        if "final" in parts:
            final_norm()
        P.finish()
        K.stats = (P.n_inst, P.n_wait)
    return nc, K


def _rope_feat(R, pos_row, pos_col):
    q = R // 4
    inv = np.power(10000.0, -np.arange(q, dtype=np.float32) / q).astype(np.float32)
    n = len(pos_row)
    C = np.zeros((R, n), np.float32)
    S = np.zeros((R, n), np.float32)
    for r in range(R):
        blk, i = r // q, r % q
        pos = pos_row if blk < 2 else pos_col
        ang = pos.astype(np.float32) * inv[i]
        C[r] = np.cos(ang)
        S[r] = (-1.0 if blk % 2 == 0 else 1.0) * np.sin(ang)
    return C, S


def _swap_perm(R):
    q = R // 4
    return np.array([r + q if (r // q) % 2 == 0 else r - q for r in range(R)])


def make_shared(inp, NTS, NSP):
    f = lambda a: np.ascontiguousarray(np.asarray(a, dtype=np.float32))
    depth = inp["w_in"].shape[0]
    NTP = NSP * 2
    NTOK = (NTP + NTS) * 128
    sh = {}
    for k in ("w_ada", "b_ada", "w_ff1", "w_ff2", "w_in", "b_gkv", "b_w_ukv", "a_sink",
              "w_branch", "w_out", "c_onorm_g"):
        if k in inp:
            sh[k] = f(inp[k])
    sh["b_adaT"] = f(np.asarray(inp["b_ada"]).reshape(depth, 48, 128).transpose(0, 2, 1))
    sh["norm1T"] = f(np.asarray(inp["norm1_g"]).reshape(depth, 8, 128).transpose(0, 2, 1))
    sh["norm2T"] = f(np.asarray(inp["norm2_g"]).reshape(depth, 8, 128).transpose(0, 2, 1))
    sh["final_g"] = f(np.asarray(inp["final_g"]).reshape(1, D))
    sh["b_gqT"] = f(np.asarray(inp["b_gq"]).reshape(depth, 3, 128).transpose(0, 2, 1))
    w_in = np.asarray(inp["w_in"])
    p64, p32 = _swap_perm(64), _swap_perm(32)
    cols = []
    for h in range(8):
        cols += list(OFF_AQ + h * 64 + p64)
    for h in range(2):
        cols += list(OFF_AK + h * 64 + p64)
    cols += list(OFF_BKR + p32)
    sh["w_in_sw"] = f(w_in[:, :, np.array(cols)])
    wuq_ = np.asarray(inp["b_w_uq"], dtype=np.float32).reshape(depth, 384, 8, 96)
    zpad = np.zeros((depth, 384, 8, 32), np.float32)
    sh["b_w_uq_r"] = f(np.concatenate([zpad, wuq_[..., 64:96], wuq_[..., 0:64]], -1).reshape(depth, 384, 1024))
    sh["b_w_uq_rs"] = f(np.concatenate([zpad, wuq_[..., 64 + p32], wuq_[..., 0:64]], -1).reshape(depth, 384, 1024))
    ts = np.arange(NTS * 128)
    row, col = ts // 64, ts % 64
    CA, SA = _rope_feat(64, row, col)
    CB, SB = _rope_feat(32, row, col)
    npad = NTP * 128
    padC = lambda C: np.concatenate([np.ones((C.shape[0], npad), np.float32), C], axis=1)
    padS = lambda S: np.concatenate([np.zeros((S.shape[0], npad), np.float32), S], axis=1)
    sh["ropeA_C"], sh["ropeA_S"] = f(padC(CA)), f(padS(SA))
    sh["ropeQB_C"] = f(np.concatenate([np.ones((32, NTOK), np.float32), padC(CB), np.ones((64, NTOK), np.float32)], axis=0))
    sh["ropeQB_S"] = f(np.concatenate([np.zeros((32, NTOK), np.float32), padS(SB), np.zeros((64, NTOK), np.float32)], axis=0))
    sh["ropeKB_C"], sh["ropeKB_S"] = f(padC(CB).T), f(padS(SB).T)
    sh["cst_ident"] = np.eye(128, dtype=np.float32)
    jj = np.arange(128)[:, None]
    ii = np.arange(128)[None, :]
    sh["cst_tri"] = f(np.stack([(jj < ii), (jj <= ii), (jj > ii), (jj >= ii)], axis=1))
    lv = []
    for k in range(1, 8):
        lv.append(((jj >> k) == (ii >> k)) & ((jj >> (k - 1)) != (ii >> (k - 1))) & (jj < ii))
    tri4 = [(jj < ii), (jj <= ii), (jj > ii), (jj >= ii)]
    sh["cst_trc"] = f(np.stack([np.stack([tri4[0], tri4[2]], 1), np.stack([tri4[1], tri4[3]], 1)], 1))
    sh["cst_lvN"] = f(np.stack([np.stack([m, m.T], 1) for m in lv], 1))
    sh["cst_lvA"] = f(np.stack([np.stack([m.T, m], 1) for m in lv], 1))
    sh["c_convT"] = f(np.asarray(inp["c_conv_w"]).reshape(depth, 3, 12, 128).transpose(0, 3, 2, 1))
    sh["c_a_log"] = f(np.asarray(inp["c_a_log"]).reshape(depth, 8))
    sh["c_dt_bias"] = f(np.asarray(inp["c_dt_bias"]).reshape(depth, 8))
    sel = np.zeros((2, 2, 128), np.float32)
    sel[0, 0] = 1
    sel[1, 1] = 1
    sh["cst_sel"] = sel
    b_ = np.arange(128)[:, None]
    a_ = np.arange(128)[None, :]
    mprev = np.where(a_ > b_, -30000.0, 0.0).astype(np.float32)
    mnext = np.where(b_ > a_, -30000.0, 0.0).astype(np.float32)
    sh["cst_masks"] = f(np.stack([mprev] * 4 + [mnext] * 4, axis=1))
    return sh


def make_core(inp, prompt_ids, sample_id, NTS):
    f = lambda a: np.ascontiguousarray(np.asarray(a, dtype=np.float32))
    xp = np.asarray(inp["x_prompt"])[prompt_ids].reshape(-1, D)
    xs = np.asarray(inp["x_sample"])[sample_id].reshape(-1, D)
    cond = np.stack([np.asarray(inp["c_ctx"]), np.asarray(inp["c"])[sample_id]], 0)
    c = {"xin": f(np.concatenate([xp, xs], 0)),
         "condT": f(cond.reshape(2, 8, 128).transpose(2, 1, 0)),
         "cak": f(np.asarray(inp["cache_a_k"])[sample_id].reshape(-1, 256, 128)),
         "cav": f(np.asarray(inp["cache_a_v"])[sample_id].reshape(-1, 256, 128)),
         "cckv": f(np.asarray(inp["cache_b_ckv"])[sample_id]),
         "ckr": f(np.asarray(inp["cache_b_krope"])[sample_id]),
         "cstate": f(np.asarray(inp["state_c"])[sample_id])}
    return c


_NTS, _NSP = 16, 2
_CACHE = {}


def kernel(**inputs):
    if "nc" not in _CACHE:
        _CACHE["nc"] = build(NTS=_NTS, NSP=_NSP, depth=DEPTH, parts=("adaln", "merge", "ffn", "final"))
    nc, K = _CACHE["nc"]
    names = [a.memorylocations[0].name for a in nc.m.functions[0].allocations
             if isinstance(a, mybir.MemoryLocationSet) and a.kind == "ExternalInput"]
    shared = make_shared(inputs, _NTS, _NSP)
    in_maps = []
    for core in range(8):
        d = dict(shared)
        d.update(make_core(inputs, [2 * core, 2 * core + 1], core // 2, _NTS))
        in_maps.append({k: d[k] for k in names if k in d})
    res = run_bass_kernel_spmd(nc, in_maps, core_ids=list(range(8))).results
    npt = _NSP * 256
    y_prompt = np.concatenate([res[c]["y"][:npt].reshape(_NSP, 256, D) for c in range(8)], 0)
    y_sample = np.stack([res[2 * b]["y"][npt:] for b in range(4)], 0)
    cat = lambda k: np.concatenate([res[c][k] for c in range(8)], 0)
    new_a_k = cat("nak").reshape(16, DEPTH, 256, 2, 64)
    new_a_v = cat("nav").reshape(16, DEPTH, 256, 2, 64)
    new_b_ckv = cat("nckv")
    new_b_krope = cat("nkr")
    new_c_state = cat("ncs")
    f = lambda a: np.ascontiguousarray(a, dtype=np.float32)
    return (f(y_prompt), f(y_sample), f(new_a_k), f(new_a_v), f(new_b_ckv), f(new_b_krope), f(new_c_state))
```
